# Optimizing a Trainium2 kernel written in Bass

```python
import jax, jax.numpy as jnp
from jax import lax
import numpy as np

D_MODEL = 2048
BATCH = 2
SEQ = 4096
DEPTH = 1

HEAD_DIM = 128
N_HEADS = D_MODEL // HEAD_DIM
N_HEADS_NA = N_HEADS // 2
N_HEADS_DIL = N_HEADS - N_HEADS_NA
D_NA = N_HEADS_NA * HEAD_DIM
D_DIL = N_HEADS_DIL * HEAD_DIM
GRID_W = 64
NA_ROWS_MAX = 8
NA_COLS = 16
DIL_PATTERNS = ((128, 1), (512, 4), (2048, 16))
DIL_BLOCK = 128
ROPE_THETA = 500000.0
ROPE_DIM = HEAD_DIM // 4
D_FF = 5632
N_MOD = 9
EPS = 1e-6
NEG = -1e30

kernel_name = 'hybrid_natten_dilated_macaron_block'


def rms_norm(x, g):
    xf = x.astype(jnp.float32)
    y = xf * lax.rsqrt(jnp.mean(xf * xf, axis=-1, keepdims=True) + EPS)
    return (y * g.astype(jnp.float32)).astype(x.dtype)


def modulate(n, shift, scale):
    return n * (1 + scale[:, None, :]) + shift[:, None, :]


def swiglu(x, w_gate, w_up, w_down):
    return (jax.nn.silu(x @ w_gate) * (x @ w_up)) @ w_down


def rope_tables(seq):
    pos = jnp.arange(seq, dtype=jnp.float32)
    inv = jnp.power(ROPE_THETA, -jnp.arange(0, ROPE_DIM, 2, dtype=jnp.float32) / ROPE_DIM)
    ang = pos[:, None] * inv[None, :]
    return jnp.cos(ang)[:, None, :], jnp.sin(ang)[:, None, :]


def partial_rope(x, cos, sin):
    half = ROPE_DIM // 2
    xf = x.astype(jnp.float32)
    x1, x2, rest = xf[..., :half], xf[..., half:ROPE_DIM], xf[..., ROPE_DIM:]
    out = jnp.concatenate([x1 * cos - x2 * sin, x2 * cos + x1 * sin, rest], axis=-1)
    return out.astype(x.dtype)


def neighbourhood_attention(q, k, v, rpb):
    B, S, H, D = q.shape
    rows = S // GRID_W
    kr = min(NA_ROWS_MAX, rows)
    qg = q.reshape(B, rows, GRID_W, H, D)
    kg = k.reshape(B, rows, GRID_W, H, D)
    vg = v.reshape(B, rows, GRID_W, H, D)
    col = jnp.arange(GRID_W)
    col_start = jnp.clip(col - NA_COLS // 2, 0, GRID_W - NA_COLS)
    col_idx = col_start[:, None] + jnp.arange(NA_COLS)[None, :]
    col_off = col_idx - col[:, None] + (NA_COLS - 1)
    scale = D ** -0.5

    def row_block(r):
        r_start = jnp.clip(r - kr // 2, 0, rows - kr)
        k_rows = lax.dynamic_slice_in_dim(kg, r_start, kr, axis=1)
        v_rows = lax.dynamic_slice_in_dim(vg, r_start, kr, axis=1)
        k_nb = k_rows[:, :, col_idx]
        v_nb = v_rows[:, :, col_idx]
        row_off = r_start + jnp.arange(kr) - r + (NA_ROWS_MAX - 1)
        bias = rpb[:, row_off][:, :, col_off]
        bias = bias.transpose(0, 2, 1, 3)
        q_r = lax.dynamic_index_in_dim(qg, r, axis=1, keepdims=False)
        s = jnp.einsum('bqhd,bkqnhd->bhqkn', q_r, k_nb,
                       preferred_element_type=jnp.float32) * scale
        s = s + bias[None].astype(jnp.float32)
        p = jax.nn.softmax(s.reshape(B, H, GRID_W, kr * NA_COLS), axis=-1)
        p = p.reshape(B, H, GRID_W, kr, NA_COLS).astype(v.dtype)
        return jnp.einsum('bhqkn,bkqnhd->bqhd', p, v_nb)

    out = lax.map(row_block, jnp.arange(rows))
    return out.transpose(1, 0, 2, 3, 4).reshape(B, S, H, D)


def dilated_attention(q, k, v):
    B, S, H, D = q.shape
    nb = S // DIL_BLOCK
    scale = D ** -0.5
    qpos = jnp.arange(DIL_BLOCK)

    def q_block(i):
        t = i * DIL_BLOCK + qpos
        q_b = lax.dynamic_slice_in_dim(q, i * DIL_BLOCK, DIL_BLOCK, axis=1)
        outs, lses = [], []
        for window, dil in DIL_PATTERNS:
            half = window // 2 // dil
            offs = dil * jnp.arange(-half, half + 1)
            idx = t[:, None] + offs[None, :]
            valid = (idx >= 0) & (idx < S)
            idx_c = jnp.clip(idx, 0, S - 1).reshape(-1)
            k_g = jnp.take(k, idx_c, axis=1).reshape(B, DIL_BLOCK, -1, H, D)
            v_g = jnp.take(v, idx_c, axis=1).reshape(B, DIL_BLOCK, -1, H, D)
            s = jnp.einsum('bqhd,bqkhd->bhqk', q_b, k_g,
                           preferred_element_type=jnp.float32) * scale
            s = jnp.where(valid[None, None], s, NEG)
            m = jnp.max(s, axis=-1, keepdims=True)
            e = jnp.exp(s - m)
            den = jnp.sum(e, axis=-1, keepdims=True)
            o = jnp.einsum('bhqk,bqkhd->bqhd', (e / den).astype(v.dtype), v_g)
            outs.append(o)
            lses.append((m + jnp.log(den))[..., 0])
        w = jax.nn.softmax(jnp.stack(lses, axis=0), axis=0)
        w = w.transpose(0, 1, 3, 2).astype(v.dtype)
        return jnp.einsum('pbqh,pbqhd->bqhd', w, jnp.stack(outs, axis=0))

    out = lax.map(q_block, jnp.arange(nb))
    return out.transpose(1, 0, 2, 3, 4).reshape(B, S, H, D)


def setup_inputs(seed: int = 0) -> dict:
    key = jax.random.key(seed)
    ks = jax.random.split(key, 24)
    f32 = jnp.float32
    n = lambda k, shape, s: jax.random.normal(k, shape, f32) * s
    gain = lambda k, shape: 1.0 + 0.02 * jax.random.normal(k, shape, f32)
    L, D = DEPTH, D_MODEL
    return {
        'x': n(ks[0], (BATCH, SEQ, D), 1.0),
        'c': n(ks[1], (BATCH, D), 1.0),
        'w_ada': n(ks[2], (L, D, N_MOD * D), 0.5 * D ** -0.5),
        'b_ada': n(ks[3], (L, N_MOD * D), 0.02),
        'g_ffn1': gain(ks[4], (L, D)),
        'w1_gate': n(ks[5], (L, D, D_FF), D ** -0.5),
        'w1_up': n(ks[6], (L, D, D_FF), D ** -0.5),
        'w1_down': n(ks[7], (L, D_FF, D), D_FF ** -0.5),
        'g_mix': gain(ks[8], (L, D)),
        'w_qkv': n(ks[9], (L, D, 3 * D), D ** -0.5),
        'qn_na': gain(ks[10], (L, HEAD_DIM)),
        'kn_na': gain(ks[11], (L, HEAD_DIM)),
        'qn_dil': gain(ks[12], (L, HEAD_DIM)),
        'kn_dil': gain(ks[13], (L, HEAD_DIM)),
        'rpb_na': n(ks[14], (L, N_HEADS_NA, 2 * NA_ROWS_MAX - 1, 2 * NA_COLS - 1), 0.1),
        'g_out_na': gain(ks[15], (L, D_NA)),
        'g_out_dil': gain(ks[16], (L, D_DIL)),
        'w_o': n(ks[17], (L, D, D), D ** -0.5),
        'g_ffn2': gain(ks[18], (L, D)),
        'w2_gate': n(ks[19], (L, D, D_FF), D ** -0.5),
        'w2_up': n(ks[20], (L, D, D_FF), D ** -0.5),
        'w2_down': n(ks[21], (L, D_FF, D), D_FF ** -0.5),
    }


def reference(x, c, w_ada, b_ada, g_ffn1, w1_gate, w1_up, w1_down, g_mix, w_qkv,
              qn_na, kn_na, qn_dil, kn_dil, rpb_na, g_out_na, g_out_dil, w_o,
              g_ffn2, w2_gate, w2_up, w2_down):
    B, S, D = x.shape
    cos, sin = rope_tables(S)
    h = x
    for l in range(DEPTH):
        mod = jax.nn.silu(c) @ w_ada[l] + b_ada[l]
        sh1, sc1, gt1, sh2, sc2, gt2, sh3, sc3, gt3 = jnp.split(mod, N_MOD, axis=-1)

        f = swiglu(modulate(rms_norm(h, g_ffn1[l]), sh1, sc1), w1_gate[l], w1_up[l], w1_down[l])
        h = h + 0.5 * gt1[:, None, :] * f

        nrm = modulate(rms_norm(h, g_mix[l]), sh2, sc2)
        qkv = nrm @ w_qkv[l]
        q, k, v = jnp.split(qkv, 3, axis=-1)
        q = q.reshape(B, S, N_HEADS, HEAD_DIM)
        k = k.reshape(B, S, N_HEADS, HEAD_DIM)
        v = v.reshape(B, S, N_HEADS, HEAD_DIM)

        qa = rms_norm(q[:, :, :N_HEADS_NA], qn_na[l])
        ka = rms_norm(k[:, :, :N_HEADS_NA], kn_na[l])
        o_na = neighbourhood_attention(qa, ka, v[:, :, :N_HEADS_NA], rpb_na[l])

        qb = partial_rope(rms_norm(q[:, :, N_HEADS_NA:], qn_dil[l]), cos, sin)
        kb = partial_rope(rms_norm(k[:, :, N_HEADS_NA:], kn_dil[l]), cos, sin)
        o_dil = dilated_attention(qb, kb, v[:, :, N_HEADS_NA:])

        o_na = rms_norm(o_na.reshape(B, S, D_NA), g_out_na[l])
        o_dil = rms_norm(o_dil.reshape(B, S, D_DIL), g_out_dil[l])
        mix = jnp.concatenate([o_na, o_dil], axis=-1) @ w_o[l]
        h = h + gt2[:, None, :] * mix

        f = swiglu(modulate(rms_norm(h, g_ffn2[l]), sh3, sc3), w2_gate[l], w2_up[l], w2_down[l])
        h = h + 0.5 * gt3[:, None, :] * f
    return h
```

```python
import numpy as np
from contextlib import ExitStack
import concourse.bass as bass
import concourse.mybir as mybir
from concourse.bass_utils import run_bass_kernel_spmd

F32 = mybir.dt.float32
BF16 = mybir.dt.bfloat16
AF = mybir.ActivationFunctionType
ALU = mybir.AluOpType

D = 2048
NT = 1024
S = 4096
DFF = 5632
NFC = 44
EPS = 1e-6
SCALE = 128.0 ** -0.5
NEGM = -30000.0
PADN = 64
DEBUG = False
STAGE = 9
ATTSTOP = 0
CC_QOS = "P2"
NRM_NP = 4
LITE = False


class Op:
    __slots__ = ("eng", "fn", "deps", "is_dma", "semkey", "inc", "cum", "targets", "signal", "seq", "idx")


class _Stop(Exception):
    pass


def ckpt(n):
    if ATTSTOP == n:
        raise _Stop()


class Prog:
    ENGS = ("pe", "act", "dve", "pool", "sp")

    def __init__(self):
        self.ops = {e: [] for e in self.ENGS}
        self.lastw = {}
        self.readers = {}
        self.dma_count = {}
        self.last_dma = {}
        self.n = 0

    def add(self, eng, fn, reads=(), writes=(), dma=None, inc=16, extra_deps=()):
        op = Op()
        op.eng = eng; op.fn = fn; op.is_dma = dma is not None; op.semkey = dma; op.inc = inc
        op.signal = False; op.seq = 0; op.idx = self.n; self.n += 1
        deps = []
        for k in reads:
            w = self.lastw.get(k)
            if w is not None:
                deps.append((w, "raw"))
        for k in writes:
            w = self.lastw.get(k)
            if w is not None:
                deps.append((w, "waw"))
            for r in self.readers.get(k, ()):
                deps.append((r, "war"))
        for d in extra_deps:
            deps.append((d, "raw"))
        for k in reads:
            self.readers.setdefault(k, []).append(op)
        for k in writes:
            self.lastw[k] = op
            self.readers[k] = []
        op.deps = deps
        op.targets = {d.semkey: self.dma_count[d.semkey] for d, _ in deps if d.is_dma}
        if op.is_dma:
            self.dma_count[dma] = self.dma_count.get(dma, 0) + inc
            op.cum = self.dma_count[dma]
            self.last_dma[dma] = op
        self.ops[eng].append(op)
        return op

    def barrier(self, mk_nop):
        firsts = []
        for e in ("pe", "act", "dve", "pool"):
            firsts.append(self.add(e, mk_nop[e], writes=[("bar1", e, self.n)]))
        dmas = [op for key, op in self.last_dma.items() if not str(key).startswith("cc_")]
        for e in self.ENGS:
            self.add(e, mk_nop[e], writes=[("bar2", e, self.n)], extra_deps=firsts + dmas)

    def needs_wait(self, op, d, kind):
        if d.is_dma:
            return True
        if d.eng != op.eng:
            return True
        if op.eng == "pe":
            return False
        if op.is_dma:
            return True
        return kind == "raw"

    def finalize(self):
        for e in self.ENGS:
            for op in self.ops[e]:
                latest = {}
                for d, kind in op.deps:
                    if (not d.is_dma) and self.needs_wait(op, d, kind):
                        if d.eng not in latest or d.idx > latest[d.eng].idx:
                            latest[d.eng] = d
                for d in latest.values():
                    d.signal = True
                op.deps = [(d, k) for d, k in op.deps if d.is_dma or latest.get(d.eng) is d]
        for e in self.ENGS:
            s = 0
            for op in self.ops[e]:
                if op.signal and not op.is_dma:
                    s += 1
                    op.seq = s

    def emit(self, nc, block, engsem, dmasem):
        self.finalize()
        decos = {"pe": block.tensor, "act": block.scalar, "dve": block.vector, "pool": block.gpsimd, "sp": block.sync}
        for e in self.ENGS:
            ops = self.ops[e]

            def body(eng, ops=ops, e=e):
                waited = {}
                for op in ops:
                    need = {}
                    for d, kind in op.deps:
                        if not self.needs_wait(op, d, kind):
                            continue
                        if d.is_dma:
                            key = ("d", d.semkey); val = op.targets[d.semkey]; sem = dmasem[d.semkey]
                        else:
                            key = ("e", d.eng); val = d.seq; sem = engsem[d.eng]
                        if val > need.get(key, (0, None))[0]:
                            need[key] = (val, sem)
                    for key, (val, sem) in need.items():
                        if waited.get(key, 0) >= val:
                            continue
                        eng.wait_ge(sem, val)
                        waited[key] = val
                    inst = op.fn(eng)
                    if inst is None:
                        continue
                    if op.is_dma:
                        inst.then_inc(dmasem[op.semkey], op.inc)
                    elif op.signal:
                        inst.then_inc(engsem[e], 1)
            decos[e](body)


def na_tile_tables():
    types = {}
    tiles = []
    for n in range(32):
        rows = []
        for rq in (2 * n, 2 * n + 1):
            rs = min(max(rq - 4, 0), 56)
            rows.append((rq, rs))
        lo = min(r[1] for r in rows) // 2
        hi = (max(r[1] for r in rows) + 7) // 2
        lst = []
        for c in range(lo, hi + 1):
            key = []
            for rkl in range(2):
                for rql in range(2):
                    rk = 2 * c + rkl
                    rq, rs = rows[rql]
                    if rs <= rk < rs + 8:
                        key.append(rk - rq + 7)
                    else:
                        key.append(-1)
            key = tuple(key)
            if all(k < 0 for k in key):
                continue
            if key not in types:
                types[key] = len(types)
            lst.append((c, types[key]))
        tiles.append(lst)
    return tiles, types


NA_TILES, NA_TYPES = na_tile_tables()
NTYPES = len(NA_TYPES)


def build_bias_tiles(rpb_h):
    out = np.full((128, NTYPES, 128), NEGM, np.float32)
    ck = np.arange(64)[:, None]
    cq = np.arange(64)[None, :]
    cs = np.clip(cq - 8, 0, 48)
    cvalid = (ck >= cs) & (ck < cs + 16)
    coff = np.clip(ck - cq + 15, 0, 30)
    for key, t in NA_TYPES.items():
        i = 0
        for rkl in range(2):
            for rql in range(2):
                a = key[i]; i += 1
                if a < 0:
                    continue
                blk = np.where(cvalid, rpb_h[a][coff], np.float32(NEGM))
                out[64 * rkl:64 * rkl + 64, t, 64 * rql:64 * rql + 64] = blk
    return out.reshape(128, NTYPES * 128)


def dil_masks():
    i = np.arange(128)[:, None]
    j = np.arange(128)[None, :]
    A = np.where(j <= i, 0.0, NEGM)
    B = np.where(j >= i, 0.0, NEGM)
    A1 = A.copy(); A1[:64, :] = NEGM
    B1 = B.copy(); B1[64:, :] = NEGM
    return np.concatenate([A, A1, B, B1], axis=1).astype(np.float32)


def rope_tabs():
    pos = np.arange(S, dtype=np.float32)
    inv = np.power(np.float32(500000.0), -np.arange(0, 32, 2, dtype=np.float32) / np.float32(32)).astype(np.float32)
    ang = (pos[None, :] * inv[:, None]).astype(np.float32)
    c = np.cos(ang).astype(np.float32); s = np.sin(ang).astype(np.float32)
    C = np.concatenate([c, c], axis=0)
    Sg = np.concatenate([-s, s], axis=0)
    return C, Sg


def build_nc():
    nc = bass.Bass("TRN2", target_bir_lowering=False)
    P = Prog()

    def din(name, shape, dt=F32):
        return nc.dram_tensor(name, list(shape), dt, kind="ExternalInput").ap()

    xT = din("xT", [D, NT]); cT = din("cT", [128, 32]); wada = din("wada", [36, 128, 2048]); bada = din("bada", [128, 144])
    sel = din("sel", [2, 1])
    if not LITE:
        w1g = din("w1g", [NFC, 128, 2048]); w1u = din("w1u", [NFC, 128, 2048]); w1d = din("w1d", [NFC, 128, 2048])
        w2g = din("w2g", [NFC, 128, 2048]); w2u = din("w2u", [NFC, 128, 2048]); w2d = din("w2d", [NFC, 128, 2048])
    wh = din("wh", [4, 3, 128, 2048]); wo = din("wo", [16, 128, 2048])
    gv = din("gv", [128, 48]); goutd = din("gout", [128, 16]); qknd = din("qkn", [128, 4])
    btd = din("bt", [2, 128, NTYPES * 128]); ropeCd = din("ropeC", [32, S]); ropeSd = din("ropeS", [32, S])
    dmaskd = din("dmask", [128, 512]); identd = din("ident", [128, 128]); permd = din("perm", [32, 32])
    outT = nc.dram_tensor("outT", [D, NT], F32, kind="ExternalOutput").ap()
    dbg = {}
    if DEBUG:
        dbg["h1"] = nc.dram_tensor("dbg_h1", [D, NT], F32, kind="ExternalOutput").ap()
        dbg["modT"] = nc.dram_tensor("dbg_modT", [128, 144], F32, kind="ExternalOutput").ap()
        dbg["oT"] = nc.dram_tensor("dbg_oT", [4, 128, S], BF16, kind="ExternalOutput").ap()
        dbg["nrm"] = nc.dram_tensor("dbg_nrm", [D, NT], BF16, kind="ExternalOutput").ap()
        dbg["qT"] = nc.dram_tensor("dbg_qT", [4, 128, S], BF16, kind="ExternalOutput").ap()
        dbg["kT"] = nc.dram_tensor("dbg_kT", [4, 128, S], BF16, kind="ExternalOutput").ap()
        dbg["h2"] = nc.dram_tensor("dbg_h2", [D, NT], F32, kind="ExternalOutput").ap()

    mod_bA = nc.dram_tensor("mod_bA", [2, 1536], F32).ap()
    mod_gA = nc.dram_tensor("mod_gA", [8, 1536], F32).ap()
    mod_bB = nc.dram_tensor("mod_bB", [2, 3072], F32).ap()
    mod_gB = nc.dram_tensor("mod_gB", [8, 3072], F32).ap()
    nrm_b = nc.dram_tensor("nrm_b", [4, 128, 2048], F32).ap()
    nrm_g = nc.dram_tensor("nrm_g", [4, 512, 2048], F32).ap()
    vdram = nc.dram_tensor("vdram", [S, 128], BF16).ap()
    o_b = nc.dram_tensor("o_b", [4, 512, 512], F32).ap()
    o_g = nc.dram_tensor("o_g", [16 * 512, 512], F32).ap()

    es = ExitStack()
    ARENA = 212480
    es.enter_context(nc.sbuf_tensor("arena", [128, ARENA + 64], mybir.dt.uint8))
    base0 = (nc.sbuf_base - (ARENA + 64) + 31) // 32 * 32
    cur = [base0]

    def salloc(name, shape, dt, at=None):
        nbytes = int(np.prod(shape[1:])) * (4 if dt == F32 else 2)
        nbytes = (nbytes + 31) // 32 * 32
        if at is None:
            off = cur[0]; cur[0] += nbytes
        else:
            off = at
        assert off + nbytes <= base0 + ARENA, (name, off + nbytes - base0, ARENA)
        return nc.alloc_sbuf_tensor_at(name, list(shape), dt, offset=off), off + nbytes

    h, _ = salloc("h", [128, 16, NT], F32)
    modT, _ = salloc("modT", [128, 144], F32)
    gvs, _ = salloc("gvs", [128, 48], F32)
    gouts, _ = salloc("gouts", [128, 16], F32)
    qkns, _ = salloc("qkns", [128, 4], F32)
    qknsc, _ = salloc("qknsc", [128, 4], F32)
    Avec, _ = salloc("Avec", [128, 16], F32)
    tmpA, _ = salloc("tmpA", [128, 16], F32)
    gth, _ = salloc("gth", [128, 16], F32)
    ident, _ = salloc("ident", [128, 128], BF16)
    ones, _ = salloc("ones", [128, 128], BF16)
    rstd, _ = salloc("rstd", [128, NT], F32)
    sels, _ = salloc("sels", [2, 1], F32)
    scr, _ = salloc("scr", [128, 8], F32)
    R0 = cur[0]
    xn, e1 = salloc("xn", [128, 16, NT], BF16)
    wgu = []
    for i in range(6):
        t, _ = salloc(f"wgu{i}", [128, 2048], BF16); wgu.append(t)
    wdr = []
    for i in range(6):
        t, _ = salloc(f"wd{i}", [128, 2048], BF16); wdr.append(t)
    hid = []
    hid_off = cur[0]
    for i in range(3):
        t, _ = salloc(f"hid{i}", [128, 4, NT], BF16); hid.append(t)
    sgt = []
    for i in range(2):
        t, _ = salloc(f"sgt{i}", [128, 512], F32); sgt.append(t)
    sqr = []
    for i in range(2):
        t, _ = salloc(f"sqr{i}", [128, NT], BF16); sqr.append(t)
    tmpf = []
    for i in range(2):
        t, _ = salloc(f"tmpf{i}", [128, NT], F32); tmpf.append(t)
    cTf, _ = salloc("cTf", [128, 32], F32)
    csb, _ = salloc("csb", [128, 32], BF16)
    badaT, _ = salloc("badaT", [128, 144], F32)
    stg = []
    for i in range(2):
        t, _ = salloc(f"stg{i}", [2, 512], F32); stg.append(t)
    stgr = []
    for i in range(2):
        t, _ = salloc(f"stgr{i}", [2, 512], F32); stgr.append(t)
    ffn_end = cur[0]
    cur[0] = R0
    nrmr = []
    for i in range(2):
        t, _ = salloc(f"nrmr{i}", [128, 16, 256], BF16); nrmr.append(t)
    whs = []
    for i in range(3):
        t, _ = salloc(f"whs{i}", [128, 2048], BF16); whs.append(t)
    qT, _ = salloc("qT", [128, S], BF16)
    kTp, _ = salloc("kTp", [128, S + 2 * PADN], BF16)
    vS, _ = salloc("vS", [128, 32, 128], BF16)
    vD, _ = salloc("vD", [128, 33, 128], BF16)
    acc, _ = salloc("acc", [128, 2, S], F32)
    PT = []
    for i in range(2):
        t, _ = salloc(f"PT{i}", [128, 640], BF16); PT.append(t)
    Bt, _ = salloc("Bt", [128, NTYPES * 128], BF16)
    oT, _ = salloc("oT", [128, S], BF16)
    ropeC, _ = salloc("ropeCt", [32, S], BF16)
    ropeS, _ = salloc("ropeSt", [32, S], BF16)
    kD, _ = salloc("kD", [128, S + 128], BF16)
    perm, _ = salloc("perm", [32, 32], BF16)
    rtmp = []
    for i in range(2):
        t, _ = salloc(f"rtmp{i}", [32, 512], F32); rtmp.append(t)
    sqt = []
    for i in range(2):
        t, _ = salloc(f"sqt{i}", [128, 512], BF16); sqt.append(t)
    dmask, _ = salloc("dmask", [128, 512], BF16)
    att_end = cur[0]
    assert max(att_end, ffn_end) <= base0 + ARENA

    ps = [es.enter_context(nc.psum_tensor(f"ps{i}", [128, 512], F32)) for i in range(8)]
    psS = []
    engsem = {e: es.enter_context(nc.semaphore(f"sem_{e}")) for e in ("pe", "act", "dve", "pool")}
    dmasem = {}

    def dsem(key):
        if key not in dmasem:
            dmasem[key] = es.enter_context(nc.semaphore("d_" + str(key)))
        return key

    PSK = lambda b: ("ps", b)
    HK = lambda kc: [("h", kc, 0), ("h", kc, 1)]
    HALL = [("h", kc, t) for kc in range(16) for t in range(2)]

    def dma(eng, out, in_, reads, writes, key, **kw):
        dsem(key)
        return P.add(eng, lambda e: e.dma_start(out=out, in_=in_, **kw), reads, writes, dma=key)

    def castdma(out, in_, reads, writes, key):
        return dma("pool", out, in_, reads, writes, key, max_dma_last_dim=8192)

    def mm(out, lhsT, rhs, start, stop, reads, writes, **kw):
        return P.add("pe", lambda e: e.matmul(out, lhsT, rhs, start=start, stop=stop, **kw), reads, writes)

    def act(out, in_, func, reads, writes, bias=None, scale=None):
        kw = {}
        if bias is not None:
            kw["bias"] = bias
        if scale is not None:
            kw["scale"] = scale
        return P.add("act", lambda e: e.activation(out, in_, func, **kw), reads, writes)

    def tt(eng, out, in0, in1, op, reads, writes):
        return P.add(eng, lambda e: e.tensor_tensor(out, in0, in1, op), reads, writes)

    def stt(out, in0, scalar, in1, op0, op1, reads, writes):
        return P.add("dve", lambda e: e.scalar_tensor_tensor(out, in0, scalar, in1, op0, op1), reads, writes)

    def ts(eng, out, in0, s1, s2, op0, op1, reads, writes):
        return P.add(eng, lambda e: e.tensor_scalar(out, in0, s1, s2, op0, op1), reads, writes)

    def cp(eng, out, in_, reads, writes):
        return P.add(eng, lambda e: e.tensor_copy(out, in_), reads, writes)

    def memset(eng, ap, val, writes):
        return P.add(eng, lambda e: e.memset(ap, val), (), writes)

    dma("sp", h[:, :, :], xT.rearrange("(k p) t -> p k t", p=128), (), HALL, "ld_h")
    dma("sp", cTf[:, :], cT, (), ["cTf"], "ld_c")
    dma("sp", badaT[:, :], bada, (), ["badaT"], "ld_c")
    dma("sp", sels[:, :], sel, (), ["sels"], "ld_c")
    dma("sp", gvs[:, :], gv, (), ["gvs"], "ld_c")
    dma("sp", gouts[:, :], goutd, (), ["gouts"], "ld_c")
    dma("sp", qkns[:, :], qknd, (), ["qkns"], "ld_c")
    castdma(ident[:, :], identd, (), ["ident"], "ld_id")
    memset("dve", ones[:, :], 1.0, ["ones"])

    act(csb[:, :], cTf[:, :], AF.Silu, ["cTf"], ["csb"])
    wgu_i = [0]

    def next_wgu():
        i = wgu_i[0] % 6; wgu_i[0] += 1
        return i

    stg_i = [0]; stgr_i = [0]

    def ada_chunk(c, c0, bank, dst, dkey):
        sl = next_wgu()
        castdma(wgu[sl][:, :], wada[c], (), [("wgu", sl)], ("wgu", sl))
        lc = c - c0
        for kc in range(16):
            mm(ps[bank][0:2, (lc % 4) * 128:(lc % 4) * 128 + 128], csb[:, kc * 2:kc * 2 + 2], wgu[sl][:, kc * 128:kc * 128 + 128],
               kc == 0, kc == 15, ["csb", ("wgu", sl)], [PSK(bank)])
        if lc % 4 == 3:
            s_ = stg_i[0] % 2; stg_i[0] += 1
            cp("dve", stg[s_][:, :], ps[bank][0:2, 0:512], [PSK(bank)], [("stg", s_)])
            dma("sp", dst[:, (lc // 4) * 512:(lc // 4) * 512 + 512], stg[s_][:, :], [("stg", s_)], [dkey], ("stgd", s_))

    def ada_gather(src, dst, skey, gkey, sem):
        dsem(sem)
        P.add("pool", lambda e: e.collective_compute("AllGather", ALU.bypass, replica_groups=[[0, 1, 2, 3], [4, 5, 6, 7]], dma_qos=CC_QOS,
                                                     ins=[src.opt()], outs=[dst.opt()]),
              [skey], [gkey], dma=sem, inc=1)

    def ada_transposes(gathered, gkey, nchunk, gi_base, bank):
        for r in range(4):
            for grp in range(nchunk // 4):
                s_ = stgr_i[0] % 2; stgr_i[0] += 1
                dma("sp", stgr[s_][:, :], gathered[2 * r:2 * r + 2, grp * 512:grp * 512 + 512], [gkey], [("stgr", s_)], ("stgrd", s_))
                for k in range(4):
                    gi = gi_base + r * nchunk + grp * 4 + k
                    mm(ps[bank][:, gi:gi + 1], stgr[s_][:, k * 128:k * 128 + 128], sels[:, :], True, True,
                       [("stgr", s_), "sels"], [PSK(bank)])
        lo, hi = gi_base, gi_base + 4 * nchunk
        tt("dve", modT[:, lo:hi], ps[bank][:, lo:hi], badaT[:, lo:hi], ALU.add, [PSK(bank), "badaT"], ["modT"])

    for c in range(12):
        ada_chunk(c, 0, 0, mod_bA, "mod_bA")
    ada_gather(mod_bA, mod_gA, "mod_bA", "mod_gA", "cc_modA")
    ada_transposes(mod_gA, "mod_gA", 12, 0, 1)
    adaB = {fc: (lambda c=12 + fc: ada_chunk(c, 12, 7, mod_bB, "mod_bB")) for fc in range(24)}
    adaB[24] = lambda: ada_gather(mod_bB, mod_gB, "mod_bB", "mod_gB", "cc_modB")
    if LITE:
        for fc in range(25):
            adaB[fc]()
        ada_transposes(mod_gB, "mod_gB", 24, 48, 7)
    if DEBUG:
        dma("sp", dbg["modT"], modT[:, :], ["modT"], [], "dbg")
    cp("dve", qknsc[:, :], qkns[:, :], ["qkns"], ["qknsc"])
    ts("dve", qknsc[:, 0:1], qkns[:, 0:1], SCALE, None, ALU.mult, ALU.bypass, ["qkns", "qknsc"], ["qknsc"])
    ts("dve", qknsc[:, 2:3], qkns[:, 2:3], SCALE, None, ALU.mult, ALU.bypass, ["qkns", "qknsc"], ["qknsc"])

    MS = lambda m, kc: modT[:, m * 16 + kc:m * 16 + kc + 1]

    sq_i = [0]; tf_i = [0]

    xnB = xn[:, :, :].rearrange("p k t -> p (k t)").rearrange("p (b k t) -> p b k t", b=4, k=16)

    def norm_mod(gidx, m_sh, m_sc, blockmajor=False):
        ts("dve", tmpA[:, :], modT[:, m_sc * 16:m_sc * 16 + 16], 1.0, None, ALU.add, ALU.bypass, ["modT"], ["tmpA"])
        tt("dve", Avec[:, :], tmpA[:, :], gvs[:, gidx * 16:gidx * 16 + 16], ALU.mult, ["tmpA", "gvs"], ["Avec"])
        for kc in range(16):
            s = sq_i[0] % 2; sq_i[0] += 1
            tt("pool", sqr[s][:, :], h[:, kc, :], h[:, kc, :], ALU.mult, HK(kc), [("sqr", s)])
            for t in range(2):
                mm(ps[2 + t][:, :], ones[:, :], sqr[s][:, t * 512:t * 512 + 512], kc == 0, kc == 15,
                   ["ones", ("sqr", s)], [PSK(2 + t)])
        for t in range(2):
            act(rstd[:, t * 512:t * 512 + 512], ps[2 + t][:, :], AF.Ln, [PSK(2 + t)], ["rstd"], bias=EPS, scale=1.0 / D)
        act(rstd[:, :], rstd[:, :], AF.Exp, ["rstd"], ["rstd"], scale=-0.5)
        for kc in range(16):
            s = tf_i[0] % 2; tf_i[0] += 1
            stt(tmpf[s][:, :], h[:, kc, :], Avec[:, kc:kc + 1], rstd[:, :], ALU.mult, ALU.mult,
                HK(kc) + ["Avec", "rstd"], [("tmpf", s)])
            if blockmajor:
                act(xnB[:, :, kc, :], tmpf[s][:, :].rearrange("p (b t) -> p b t", b=4), AF.Identity,
                    [("tmpf", s), "modT"], [("xn", kc)], bias=MS(m_sh, kc), scale=1.0)
            else:
                act(xn[:, kc, :], tmpf[s][:, :], AF.Identity, [("tmpf", s), "modT"], [("xn", kc)], bias=MS(m_sh, kc), scale=1.0)

    wd_i = [0]
    pd_i = [0]

    def down_group(wsrc, fc0, rhs_fn, rhs_keys, scal):
        slots = []
        for fl in range(4):
            sl = wd_i[0] % 6; wd_i[0] += 1
            castdma(wdr[sl][:, :], wsrc[fc0 + fl], (), [("wd", sl)], ("wd", sl))
            slots.append(sl)
        for dc in range(16):
            for t in range(2):
                b = 4 + pd_i[0] % 4; pd_i[0] += 1
                for fl in range(4):
                    mm(ps[b][:, :], wdr[slots[fl]][:, dc * 128:dc * 128 + 128], rhs_fn(fl, t), fl == 0, fl == 3,
                       [("wd", slots[fl])] + rhs_keys(fl), [PSK(b)])
                stt(h[:, dc, t * 512:t * 512 + 512], ps[b][:, :], scal[:, dc:dc + 1], h[:, dc, t * 512:t * 512 + 512],
                    ALU.mult, ALU.add, [PSK(b), "gth", ("h", dc, t)], [("h", dc, t)])

    def ffn(wg, wu, wd_, m_gt, side=None):
        ts("dve", gth[:, :], modT[:, m_gt * 16:m_gt * 16 + 16], 0.5, None, ALU.mult, ALU.bypass, ["modT"], ["gth"])
        gu_i = 0
        for fc in range(NFC):
            if side and fc in side:
                side[fc]()
            g = fc // 4
            hs = g % 3
            sg_ = next_wgu(); su_ = next_wgu()
            castdma(wgu[sg_][:, :], wg[fc], (), [("wgu", sg_)], ("wgu", sg_))
            castdma(wgu[su_][:, :], wu[fc], (), [("wgu", su_)], ("wgu", su_))
            for t in range(2):
                bg = (gu_i % 2) * 2; bu = bg + 1; gu_i += 1
                for kc in range(16):
                    mm(ps[bg][:, :], wgu[sg_][:, kc * 128:kc * 128 + 128], xn[:, kc, t * 512:t * 512 + 512], kc == 0, kc == 15,
                       [("wgu", sg_), ("xn", kc)], [PSK(bg)])
                for kc in range(16):
                    mm(ps[bu][:, :], wgu[su_][:, kc * 128:kc * 128 + 128], xn[:, kc, t * 512:t * 512 + 512], kc == 0, kc == 15,
                       [("wgu", su_), ("xn", kc)], [PSK(bu)])
                st = gu_i % 2
                act(sgt[st][:, :], ps[bg][:, :], AF.Silu, [PSK(bg)], [("sgt", st)])
                tt("dve", hid[hs][:, fc % 4, t * 512:t * 512 + 512], sgt[st][:, :], ps[bu][:, :], ALU.mult,
                   [("sgt", st), PSK(bu)], [("hid", hs, fc % 4)])
            if fc % 4 == 3 and g >= 1:
                gp = g - 1
                down_group(wd_, gp * 4, lambda fl, t, gp=gp: hid[gp % 3][:, fl, t * 512:t * 512 + 512],
                           lambda fl, gp=gp: [("hid", gp % 3, fl)], gth)
        gp = NFC // 4 - 1
        down_group(wd_, gp * 4, lambda fl, t, gp=gp: hid[gp % 3][:, fl, t * 512:t * 512 + 512],
                   lambda fl, gp=gp: [("hid", gp % 3, fl)], gth)

    def finish():
        dma("sp", outT.rearrange("(k p) t -> p k t", p=128), h[:, :, :], HALL, ["out"], "st_out")
        P.add("sp", lambda e: e.nop(), ["out"], [])
        P.add("sp", lambda e: e.nop(), [], [], extra_deps=list(P.last_dma.values()))
        with es:
            with nc.Block() as block:
                P.emit(nc, block, engsem, dmasem)
        return nc

    if not LITE:
        norm_mod(0, 0, 1)
        ffn(w1g, w1u, w1d, 2, side=adaB)
        ada_transposes(mod_gB, "mod_gB", 24, 48, 7)
    if DEBUG:
        dma("sp", dbg["h1"].rearrange("(k p) t -> p k t", p=128), h[:, :, :], HALL, [], "dbg")

    if STAGE == 1:
        return finish()
    norm_mod(1, 3, 4, blockmajor=True)
    if DEBUG:
        dma("sp", dbg["nrm"].rearrange("(k p) (b t) -> p b k t", p=128, b=4), xnB, [("xn", kc) for kc in range(16)], [], "dbg")
    for b4 in range(4):
        dsem("cc_nrm%d" % b4)
        dma("sp", nrm_b.bitcast(BF16)[b4], xnB[:, b4].rearrange("p k t -> p (k t)"),
            [("xn", kc) for kc in range(16)], [("nrm_b", b4)], "st_nrm")

    def nrm_gather(b4):
        P.add("pool", lambda e, b4=b4: e.collective_compute("AllGather", ALU.bypass, replica_groups=[[0, 1, 2, 3], [4, 5, 6, 7]], dma_qos=CC_QOS,
                                                            ins=[nrm_b[b4].opt()], outs=[nrm_g[b4].opt()]),
              [("nrm_b", b4)], [("nrm_g", b4)], dma="cc_nrm%d" % b4, inc=1)
    nrm_gather(0)

    if STAGE == 2:
        return finish()
    nops = {
        "pe": lambda e: e.matmul(ps[7][0:1, 0:1], sels[:, :], sels[:, :], start=True, stop=True),
        "act": lambda e: e.activation(scr[:, 0:1], scr[:, 0:1], AF.Identity),
        "dve": lambda e: e.memset(scr[:, 1:2], 0.0),
        "pool": lambda e: e.memset(scr[:, 2:3], 0.0),
        "sp": lambda e: e.nop(),
    }
    memset("dve", scr[:, :], 0.0, ["scr"])
    P.barrier(nops)

    try:
        castdma(dmask[:, :], dmaskd, (), ["dmask"], "ld_dm")
        castdma(perm[:, :], permd, (), ["perm"], "ld_dm")
        for pc in range(4):
            castdma(ropeC[:, pc * 1024:pc * 1024 + 1024], ropeCd[:, pc * 1024:pc * 1024 + 1024], (), ["ropeT"], "ld_dm")
            castdma(ropeS[:, pc * 1024:pc * 1024 + 1024], ropeSd[:, pc * 1024:pc * 1024 + 1024], (), ["ropeT"], "ld_dm")
        memset("pool", kD[:, 0:64], 0.0, ["kDpad"])
        memset("pool", kD[:, 64 + S:128 + S], 0.0, ["kDpad"])
        memset("pool", kTp[:, 0:PADN], 0.0, ["kTpad"])
        memset("pool", kTp[:, PADN + S:PADN + S + PADN], 0.0, ["kTpad"])
        memset("pool", vD[:, 0, :], 0.0, ["vDpad"])
        memset("pool", vD[:, 32, :], 0.0, ["vDpad"])
        nr_i = [0]; sq2_i = [0]; pt_i = [0]; po_i = [0]; rp_i = [0]
        ckpt(1)

        for hd_ in range(4):
            dsem("cc_o%d" % hd_)
        pending_fin = []
        for i3 in range(3):
            castdma(whs[i3][:, :], wh[0, i3], (), [("whs", i3)], ("whs", i3))
        castdma(Bt[:, :], btd[0], (), ["Bt"], "ld_bt")
        for b4 in range(1, 4):
            nrm_gather(b4)
        for hd in range(4):
            is_na = hd < 2
            if hd == 1:
                castdma(Bt[:, :], btd[hd], (), ["Bt"], "ld_bt")
            gq = qknsc[:, 0:1] if is_na else qknsc[:, 2:3]
            gk = qknsc[:, 1:2] if is_na else qknsc[:, 3:4]
            for tbi in range(16):
                b4 = tbi // 4; r = tbi % 4
                tb = r * 4 + b4
                ns = nr_i[0] % 2; nr_i[0] += 1
                dma("sp", nrmr[ns][:, :, :].rearrange("p k t -> p (k t)"), nrm_g.bitcast(BF16)[b4, r * 128:(r + 1) * 128, :],
                    [("nrm_g", b4)], [("nrmr", ns)], ("nrmr", ns))
                bq = tbi % 2
                bv = 2 + tbi % 2
                for kc in range(16):
                    mm(ps[bq][:, 0:256], whs[0][:, kc * 128:kc * 128 + 128], nrmr[ns][:, kc, :], kc == 0, kc == 15,
                       [("whs", 0), ("nrmr", ns)], [PSK(bq)])
                for kc in range(16):
                    mm(ps[bq][:, 256:512], whs[1][:, kc * 128:kc * 128 + 128], nrmr[ns][:, kc, :], kc == 0, kc == 15,
                       [("whs", 1), ("nrmr", ns)], [PSK(bq)])
                for tc in range(2):
                    for kc in range(16):
                        mm(ps[bv][:, tc * 128:tc * 128 + 128], nrmr[ns][:, kc, tc * 128:tc * 128 + 128],
                           whs[2][:, kc * 128:kc * 128 + 128], kc == 0, kc == 15, [("whs", 2), ("nrmr", ns)], [PSK(bv)])
                s2 = sq2_i[0] % 2; sq2_i[0] += 1
                act(sqt[s2][:, :], ps[bq][:, :], AF.Square, [PSK(bq)], [("sqt", s2)])
                mm(ps[6][:, :], ones[:, :], sqt[s2][:, :], True, True, ["ones", ("sqt", s2)], [PSK(6)])
                act(rstd[:, 0:512], ps[6][:, :], AF.Ln, [PSK(6)], ["rstd"], bias=EPS, scale=1.0 / 128)
                act(rstd[:, 0:512], rstd[:, 0:512], AF.Exp, ["rstd"], ["rstd"], scale=-0.5)
                tok0 = tb * 256
                stt(qT[:, tok0:tok0 + 256], ps[bq][:, 0:256], gq, rstd[:, 0:256], ALU.mult, ALU.mult,
                    [PSK(bq), "qknsc", "rstd"], ["qT"])
                stt(kTp[:, PADN + tok0:PADN + tok0 + 256], ps[bq][:, 256:512], gk, rstd[:, 256:512], ALU.mult, ALU.mult,
                    [PSK(bq), "qknsc", "rstd"], ["kT"])
                act(vS[:, tb * 2:tb * 2 + 2, :], ps[bv][:, 0:256].rearrange("p (c d) -> p c d", c=2), AF.Identity, [PSK(bv)], ["vS"])
                if pending_fin and tbi % 2 == 1:
                    pending_fin.pop(0)()
            if hd < 3:
                for i3 in range(3):
                    castdma(whs[i3][:, :], wh[hd + 1, i3], (), [("whs", i3)], ("whs", i3))
            ckpt(2 if hd == 0 else (5 if hd == 2 else -1))
            if not is_na:
                for which, buf, off, key in ((0, qT, 0, "qT"), (1, kTp, PADN, "kT")):
                    for pc in range(8):
                        cc0 = pc * 512
                        x0 = buf[0:32, off + cc0:off + cc0 + 512]
                        rb = 6 + pc % 2
                        mm(ps[rb][0:32, :], perm[:, :], x0, True, True, ["perm", key], [PSK(rb)])
                        tt("dve", rtmp[0][:, :], x0, ropeC[:, cc0:cc0 + 512], ALU.mult, [key, "ropeT"], [("rtmp", 0)])
                        tt("dve", rtmp[1][:, :], ps[rb][0:32, :], ropeS[:, cc0:cc0 + 512], ALU.mult, [PSK(rb), "ropeT"], [("rtmp", 1)])
                        tt("dve", x0, rtmp[0][:, :], rtmp[1][:, :], ALU.add, [("rtmp", 0), ("rtmp", 1)], [key])
            if hd == 2:
                ckpt(6)
            if DEBUG:
                dma("sp", dbg["qT"][hd], qT[:, :], ["qT"], [], "dbg")
                dma("sp", dbg["kT"][hd], kTp[:, PADN:PADN + S], ["kT"], [], "dbg")

            def tile_S(tl):
                if tl.get("pre_S"):
                    tl["pre_S"]()
                chunks = tl["chunks"]; qap = tl["q"]
                nch = len(chunks)
                sb = (pt_i[0] % 2) * 2
                pti = pt_i[0] % 2; pt_i[0] += 1
                for ci, kap in enumerate(chunks):
                    b = sb + ci // 4
                    col = (ci % 4) * 128
                    mm(ps[b][:, col:col + 128], kap, qap, True, False, ["kT", "kTpad", "kDpad", "qT"] + tl.get("kkeys", []), [PSK(b)])
                    mm(ps[b][:, col:col + 128], ident[:, :], tl["mask"](ci), False, True, ["ident"] + tl["mkeys"], [PSK(b)])
                n1 = min(nch, 4) * 128
                act(PT[pti][:, 0:n1], ps[sb][:, 0:n1], AF.Exp, [PSK(sb)], [("PT", pti)])
                if nch > 4:
                    n2 = (nch - 4) * 128
                    act(PT[pti][:, 512:512 + n2], ps[sb + 1][:, 0:n2], AF.Exp, [PSK(sb + 1)], [("PT", pti)])
                return pti, nch

            def tile_PV(tl, st):
                if tl.get("pre_PV"):
                    tl["pre_PV"]()
                pti, nch = st
                bo = 4 + po_i[0] % 2; po_i[0] += 1
                for ci in range(nch):
                    mm(ps[bo][:, 0:128], tl["v"](ci), PT[pti][:, ci * 128:ci * 128 + 128], ci == 0, ci == nch - 1,
                       tl["vkeys"] + [("PT", pti)], [PSK(bo)])
                for ci in range(nch):
                    mm(ps[bo][:, 128:256], ones[:, :], PT[pti][:, ci * 128:ci * 128 + 128], ci == 0, ci == nch - 1,
                       ["ones", ("PT", pti)], [PSK(bo)])
                src = ps[bo][:, 0:256].rearrange("p (w q) -> p w q", w=2)
                if tl["first"]:
                    cp("dve", tl["acc"], src, [PSK(bo)], ["acc_all"])
                else:
                    tt("dve", tl["acc"], src, tl["acc"], ALU.add, [PSK(bo)], ["acc_all"])

            tiles = []
            if is_na:
                for n in range(32):
                    lst = NA_TILES[n]
                    tiles.append(dict(
                        chunks=[kTp[:, PADN + c * 128:PADN + c * 128 + 128] for c, _ in lst],
                        q=qT[:, n * 128:n * 128 + 128],
                        v=(lambda ci, lst=lst: vS[:, lst[ci][0], :]), vkeys=["vS"],
                        mask=(lambda ci, lst=lst: Bt[:, lst[ci][1] * 128:lst[ci][1] * 128 + 128]), mkeys=["Bt"],
                        acc=acc[:, :, n * 128:n * 128 + 128], first=True))
            else:
                vdv = vdram.rearrange("(c p) d -> p c d", p=128)
                for q4 in range(4):
                    dma("sp", vdv[:, q4 * 8:q4 * 8 + 8, :], vS[:, q4 * 8:q4 * 8 + 8, :], ["vS"], ["vdram"], "st_v")
                for Dd in (1, 4, 16):
                    L = S // Dd

                    def ld_vD(Dd=Dd):
                        na_ = (S // Dd) // 128
                        Vv = vdram.rearrange("(a i r) d -> i r a d", i=128, r=Dd)
                        for g in range(4):
                            if na_ >= 8:
                                rs = slice((8 * g) // na_, (8 * g) // na_ + 1); as_ = slice((8 * g) % na_, (8 * g) % na_ + 8)
                            else:
                                rs = slice((8 * g) // na_, (8 * g) // na_ + 8 // na_); as_ = slice(0, na_)
                            keys = [("vD", c) for c in range(8 * g, 8 * g + 9)]
                            na_u = as_.stop - as_.start
                            for ri, r_ in enumerate(range(rs.start, rs.stop)):
                                c0 = 8 * g + ri * na_u
                                dma("sp", vD[64:128, c0:c0 + na_u, :], Vv[0:64, r_, as_], ["vdram"], keys, "ld_vD")
                                dma("sp", vD[0:64, c0 + 1:c0 + 1 + na_u, :], Vv[64:128, r_, as_], ["vdram"], keys, "ld_vD")

                    def mk_kD(Dd=Dd):
                        srcv = kTp[:, PADN:PADN + S].rearrange("p (u r) -> p r u", r=Dd)
                        for g in range(4):
                            nr = Dd // 4
                            cp("pool", kD[:, 64 + 1024 * g:64 + 1024 * g + 1024].rearrange("p (r u) -> p r u", r=nr),
                               srcv[:, g * nr:(g + 1) * nr, :], ["kT"], [("kD", c) for c in range(8 * g, 8 * g + 9)])

                    for n in range(32):
                        m0 = 128 * n
                        rho = m0 // L; u0 = m0 % L
                        q0 = rho + Dd * u0
                        if Dd == 1:
                            chunks = [kTp[:, PADN + m0 - 64:PADN + m0 + 64], kTp[:, PADN + m0 + 64:PADN + m0 + 192]]
                        else:
                            chunks = [kD[:, m0:m0 + 128], kD[:, m0 + 128:m0 + 256]]
                        mt = [1 if u0 == 0 else 0, 3 if u0 + 128 == L else 2]
                        tiles.append(dict(
                            chunks=chunks, q=qT[:, q0:q0 + 127 * Dd + 1:Dd],
                            kkeys=([("kD", n), ("kD", n + 1)] if Dd > 1 else []),
                            v=(lambda ci, n=n: vD[:, n + ci, :]), vkeys=[("vD", n), ("vD", n + 1), "vDpad"],
                            mask=(lambda ci, mt=mt: dmask[:, mt[ci] * 128:mt[ci] * 128 + 128]), mkeys=["dmask"],
                            acc=acc[:, :, q0:q0 + 127 * Dd + 1:Dd], first=(Dd == 1),
                            pre_S=(mk_kD if (n == 0 and Dd > 1) else None), pre_PV=(ld_vD if n == 0 else None)))
            prev = None
            for tl in tiles:
                st = tile_S(tl)
                if prev is not None:
                    tile_PV(*prev)
                prev = (tl, st)
            tile_PV(*prev)
            def make_fin(hd=hd):
                fns = []
                for pc in range(4):
                    def piece(pc=pc):
                        sl_ = slice(pc * 1024, pc * 1024 + 1024)
                        act(acc[:, 1, sl_], acc[:, 1, sl_], AF.Ln, ["acc_all"], ["acc_all"])
                        act(acc[:, 1, sl_], acc[:, 1, sl_], AF.Exp, ["acc_all"], ["acc_all"], scale=-1.0)
                        tt("dve", oT[:, sl_], acc[:, 0, sl_], acc[:, 1, sl_], ALU.mult, ["acc_all"], ["oT", "acc_all"])
                    fns.append(piece)

                def store():
                    for tq in range(4):
                        dma("sp", o_b.bitcast(BF16)[hd, tq * 128:tq * 128 + 128, :], oT[:, tq * 1024:tq * 1024 + 1024],
                            ["oT"], [("o_b", hd)], "st_o")
                    P.add("pool", lambda e: e.collective_compute("AllGather", ALU.bypass, replica_groups=[[0, 1, 2, 3], [4, 5, 6, 7]], dma_qos=CC_QOS,
                                                                 ins=[o_b[hd].opt()], outs=[o_g[hd * 2048:(hd + 1) * 2048, :].opt()]),
                          [("o_b", hd)], [("o_g", hd)], dma="cc_o%d" % hd, inc=1)
                    if DEBUG:
                        dma("sp", dbg["oT"][hd], oT[:, :], ["oT"], [], "dbg")
                fns.append(store)
                return fns
            pending_fin.extend(make_fin())
            if hd == 3:
                while pending_fin:
                    pending_fin.pop(0)()
            ckpt({0: 3, 1: 4, 2: 10, 3: 11}[hd])

    except _Stop:
        return finish()
    P.barrier(nops)

    if STAGE == 3:
        return finish()
    def ld_o(e):
        ogb = o_g.bitcast(BF16).rearrange("(h r j q) t -> h j q r t", h=4, j=4, r=4)
        pid = e.partition_id()
        j = pid % 4
        for g in range(2):
            for l in range(2):
                src = ogb[2 * g + l, bass.ds(j, 1)].rearrange("o q r t -> (o q) r t")
                e.dma_start(out=xn[:, g * 8 + l:g * 8 + 8:2, :], in_=src).then_inc(dmasem["ld_o"], 16)
        return None
    dsem("ld_o")
    P.add("pool", ld_o, [("o_g", i) for i in range(4)], [("xn", kc) for kc in range(16)], dma="ld_o", inc=64)
    for grp in range(2):
        for k8 in range(8):
            kc = grp * 8 + k8
            s = sq_i[0] % 2; sq_i[0] += 1
            tt("pool", sqr[s][:, :], xn[:, kc, :], xn[:, kc, :], ALU.mult, [("xn", kc)], [("sqr", s)])
            for t in range(2):
                mm(ps[2 + t][:, :], ones[:, :], sqr[s][:, t * 512:t * 512 + 512], k8 == 0, k8 == 7,
                   ["ones", ("sqr", s)], [PSK(2 + t)])
        for t in range(2):
            act(rstd[:, t * 512:t * 512 + 512], ps[2 + t][:, :], AF.Ln, [PSK(2 + t)], ["rstd"], bias=EPS, scale=1.0 / 1024)
        act(rstd[:, :], rstd[:, :], AF.Exp, ["rstd"], ["rstd"], scale=-0.5)
        for k8 in range(8):
            kc = grp * 8 + k8
            stt(xn[:, kc, :], xn[:, kc, :], gouts[:, kc:kc + 1], rstd[:, :], ALU.mult, ALU.mult,
                [("xn", kc), "gouts", "rstd"], [("xn", kc)])
    cp("dve", gth[:, :], modT[:, 5 * 16:5 * 16 + 16], ["modT"], ["gth"])
    for g in range(4):
        down_group(wo, g * 4, lambda fl, t, g=g: xn[:, g * 4 + fl, t * 512:t * 512 + 512],
                   lambda fl, g=g: [("xn", g * 4 + fl)], gth)
    if DEBUG:
        dma("sp", dbg["h2"].rearrange("(k p) t -> p k t", p=128), h[:, :, :], HALL, [], "dbg")

    if not LITE:
        norm_mod(2, 6, 7)
        ffn(w2g, w2u, w2d, 8)

    return finish()


def _tile_w(W):
    K, N = W.shape
    return np.ascontiguousarray(W.reshape(K // 128, 128, N // 128, 128).transpose(2, 1, 0, 3)).reshape(N // 128, 128, (K // 128) * 128)


def _vec16(v):
    return np.ascontiguousarray(v.reshape(-1, 128).T)


_NC_CACHE = {}


def kernel(x, c, w_ada, b_ada, g_ffn1, w1_gate, w1_up, w1_down, g_mix, w_qkv, qn_na, kn_na, qn_dil, kn_dil,
           rpb_na, g_out_na, g_out_dil, w_o, g_ffn2, w2_gate, w2_up, w2_down):
    f = lambda a: np.asarray(a, dtype=np.float32)
    x = f(x); c = f(c)
    w_ada = f(w_ada)[0]; b_ada = f(b_ada)[0]
    if not LITE:
        w1g = _tile_w(f(w1_gate)[0]); w1u = _tile_w(f(w1_up)[0]); w1d = np.ascontiguousarray(f(w1_down)[0]).reshape(NFC, 128, 2048)
        w2g = _tile_w(f(w2_gate)[0]); w2u = _tile_w(f(w2_up)[0]); w2d = np.ascontiguousarray(f(w2_down)[0]).reshape(NFC, 128, 2048)
    wqkv = f(w_qkv)[0]
    wo = np.ascontiguousarray(f(w_o)[0]).reshape(16, 128, 2048)
    gv = np.concatenate([_vec16(f(g_ffn1)[0]), _vec16(f(g_mix)[0]), _vec16(f(g_ffn2)[0])], axis=1)
    gout = np.concatenate([_vec16(f(g_out_na)[0]), _vec16(f(g_out_dil)[0])], axis=1)
    qkn = np.stack([f(qn_na)[0], f(kn_na)[0], f(qn_dil)[0], f(kn_dil)[0]], axis=1)
    rpb = f(rpb_na)[0]
    cT = np.ascontiguousarray(c.T.reshape(16, 128, 2).transpose(1, 0, 2)).reshape(128, 32)
    ropeC, ropeS = rope_tabs()
    badaT_h = np.ascontiguousarray(b_ada.reshape(144, 128).T)
    dmask = dil_masks()
    ident = np.eye(128, dtype=np.float32)
    perm32 = np.zeros((32, 32), np.float32)
    perm32[(np.arange(32) + 16) % 32, np.arange(32)] = 1.0
    in_maps = []
    for core in range(8):
        b = core // 4; j = core % 4
        heads = [2 * j, 2 * j + 1, 8 + 2 * j, 9 + 2 * j]
        whl = []
        for hh in heads:
            blk = []
            for part in range(3):
                col0 = part * D + hh * 128
                blk.append(_tile_w(wqkv[:, col0:col0 + 128])[0])
            whl.append(np.stack(blk))
        m = {
            "xT": np.ascontiguousarray(x[b, j * NT:(j + 1) * NT, :].T),
            "cT": cT,
            "wada": _tile_w(np.concatenate([w_ada[:, 12 * j * 128:(12 * j + 12) * 128],
                                            w_ada[:, (48 + 24 * j) * 128:(48 + 24 * j + 24) * 128]], axis=1)),
            "bada": badaT_h,
            "sel": np.array([[1.0 - b], [float(b)]], np.float32),
            "wh": np.stack(whl), "wo": wo, "gv": gv, "gout": gout, "qkn": np.ascontiguousarray(qkn),
            "bt": np.stack([build_bias_tiles(rpb[2 * j]), build_bias_tiles(rpb[2 * j + 1])]),
            "ropeC": ropeC, "ropeS": ropeS, "dmask": dmask, "ident": ident, "perm": perm32,
        }
        if not LITE:
            m.update({"w1g": w1g, "w1u": w1u, "w1d": w1d, "w2g": w2g, "w2u": w2u, "w2d": w2d})
        in_maps.append(m)
    if "nc" not in _NC_CACHE:
        _NC_CACHE["nc"] = build_nc()
    nc = _NC_CACHE["nc"]
    res = run_bass_kernel_spmd(nc, in_maps, core_ids=list(range(8)))
    out = np.empty((2, S, D), np.float32)
    for core in range(8):
        b = core // 4; j = core % 4
        out[b, j * NT:(j + 1) * NT, :] = res.results[core]["outT"].T
    if DEBUG:
        kernel.last = res
    return out
```

```python
import numpy as np
from contextlib import ExitStack
import concourse.bass as bass
import concourse.mybir as mybir
from concourse.bass_utils import run_bass_kernel_spmd

F32 = mybir.dt.float32
BF16 = mybir.dt.bfloat16
AF = mybir.ActivationFunctionType
ALU = mybir.AluOpType

D = 2048
NT = 1024
S = 4096
DFF = 5632
NFC = 44
EPS = 1e-6
SCALE = 128.0 ** -0.5
NEGM = -30000.0
PADN = 64
DEBUG = False
STAGE = 9
ATTSTOP = 0
CC_QOS = "P2"
NRM_NP = 4
LITE = False


class Op:
    __slots__ = ("eng", "fn", "deps", "is_dma", "semkey", "inc", "cum", "targets", "signal", "seq", "idx")


class _Stop(Exception):
    pass


def ckpt(n):
    if ATTSTOP == n:
        raise _Stop()


class Prog:
    ENGS = ("pe", "act", "dve", "pool", "sp")

    def __init__(self):
        self.ops = {e: [] for e in self.ENGS}
        self.lastw = {}
        self.readers = {}
        self.dma_count = {}
        self.last_dma = {}
        self.n = 0

    def add(self, eng, fn, reads=(), writes=(), dma=None, inc=16, extra_deps=()):
        op = Op()
        op.eng = eng; op.fn = fn; op.is_dma = dma is not None; op.semkey = dma; op.inc = inc
        op.signal = False; op.seq = 0; op.idx = self.n; self.n += 1
        deps = []
        for k in reads:
            w = self.lastw.get(k)
            if w is not None:
                deps.append((w, "raw"))
        for k in writes:
            w = self.lastw.get(k)
            if w is not None:
                deps.append((w, "waw"))
            for r in self.readers.get(k, ()):
                deps.append((r, "war"))
        for d in extra_deps:
            deps.append((d, "raw"))
        for k in reads:
            self.readers.setdefault(k, []).append(op)
        for k in writes:
            self.lastw[k] = op
            self.readers[k] = []
        op.deps = deps
        op.targets = {d.semkey: self.dma_count[d.semkey] for d, _ in deps if d.is_dma}
        if op.is_dma:
            self.dma_count[dma] = self.dma_count.get(dma, 0) + inc
            op.cum = self.dma_count[dma]
            self.last_dma[dma] = op
        self.ops[eng].append(op)
        return op

    def barrier(self, mk_nop):
        firsts = []
        for e in ("pe", "act", "dve", "pool"):
            firsts.append(self.add(e, mk_nop[e], writes=[("bar1", e, self.n)]))
        dmas = [op for key, op in self.last_dma.items() if not str(key).startswith("cc_")]
        for e in self.ENGS:
            self.add(e, mk_nop[e], writes=[("bar2", e, self.n)], extra_deps=firsts + dmas)

    def needs_wait(self, op, d, kind):
        if d.is_dma:
            return True
        if d.eng != op.eng:
            return True
        if op.eng == "pe":
            return False
        if op.is_dma:
            return True
        return kind == "raw"

    def finalize(self):
        for e in self.ENGS:
            for op in self.ops[e]:
                latest = {}
                for d, kind in op.deps:
                    if (not d.is_dma) and self.needs_wait(op, d, kind):
                        if d.eng not in latest or d.idx > latest[d.eng].idx:
                            latest[d.eng] = d
                for d in latest.values():
                    d.signal = True
                op.deps = [(d, k) for d, k in op.deps if d.is_dma or latest.get(d.eng) is d]
        for e in self.ENGS:
            s = 0
            for op in self.ops[e]:
                if op.signal and not op.is_dma:
                    s += 1
                    op.seq = s

    def emit(self, nc, block, engsem, dmasem):
        self.finalize()
        decos = {"pe": block.tensor, "act": block.scalar, "dve": block.vector, "pool": block.gpsimd, "sp": block.sync}
        for e in self.ENGS:
            ops = self.ops[e]

            def body(eng, ops=ops, e=e):
                waited = {}
                for op in ops:
                    need = {}
                    for d, kind in op.deps:
                        if not self.needs_wait(op, d, kind):
                            continue
                        if d.is_dma:
                            key = ("d", d.semkey); val = op.targets[d.semkey]; sem = dmasem[d.semkey]
                        else:
                            key = ("e", d.eng); val = d.seq; sem = engsem[d.eng]
                        if val > need.get(key, (0, None))[0]:
                            need[key] = (val, sem)
                    for key, (val, sem) in need.items():
                        if waited.get(key, 0) >= val:
                            continue
                        eng.wait_ge(sem, val)
                        waited[key] = val
                    inst = op.fn(eng)
                    if inst is None:
                        continue
                    if op.is_dma:
                        inst.then_inc(dmasem[op.semkey], op.inc)
                    elif op.signal:
                        inst.then_inc(engsem[e], 1)
            decos[e](body)


def na_tile_tables():
    types = {}
    tiles = []
    for n in range(32):
        rows = []
        for rq in (2 * n, 2 * n + 1):
            rs = min(max(rq - 4, 0), 56)
            rows.append((rq, rs))
        lo = min(r[1] for r in rows) // 2
        hi = (max(r[1] for r in rows) + 7) // 2
        lst = []
        for c in range(lo, hi + 1):
            key = []
            for rkl in range(2):
                for rql in range(2):
                    rk = 2 * c + rkl
                    rq, rs = rows[rql]
                    if rs <= rk < rs + 8:
                        key.append(rk - rq + 7)
                    else:
                        key.append(-1)
            key = tuple(key)
            if all(k < 0 for k in key):
                continue
            if key not in types:
                types[key] = len(types)
            lst.append((c, types[key]))
        tiles.append(lst)
    return tiles, types


NA_TILES, NA_TYPES = na_tile_tables()
NTYPES = len(NA_TYPES)


def build_bias_tiles(rpb_h):
    out = np.full((128, NTYPES, 128), NEGM, np.float32)
    ck = np.arange(64)[:, None]
    cq = np.arange(64)[None, :]
    cs = np.clip(cq - 8, 0, 48)
    cvalid = (ck >= cs) & (ck < cs + 16)
    coff = np.clip(ck - cq + 15, 0, 30)
    for key, t in NA_TYPES.items():
        i = 0
        for rkl in range(2):
            for rql in range(2):
                a = key[i]; i += 1
                if a < 0:
                    continue
                blk = np.where(cvalid, rpb_h[a][coff], np.float32(NEGM))
                out[64 * rkl:64 * rkl + 64, t, 64 * rql:64 * rql + 64] = blk
    return out.reshape(128, NTYPES * 128)


def dil_masks():
    i = np.arange(128)[:, None]
    j = np.arange(128)[None, :]
    A = np.where(j <= i, 0.0, NEGM)
    B = np.where(j >= i, 0.0, NEGM)
    A1 = A.copy(); A1[:64, :] = NEGM
    B1 = B.copy(); B1[64:, :] = NEGM
    return np.concatenate([A, A1, B, B1], axis=1).astype(np.float32)


def rope_tabs():
    pos = np.arange(S, dtype=np.float32)
    inv = np.power(np.float32(500000.0), -np.arange(0, 32, 2, dtype=np.float32) / np.float32(32)).astype(np.float32)
    ang = (pos[None, :] * inv[:, None]).astype(np.float32)
    c = np.cos(ang).astype(np.float32); s = np.sin(ang).astype(np.float32)
    C = np.concatenate([c, c], axis=0)
    Sg = np.concatenate([-s, s], axis=0)
    return C, Sg


def build_nc():
    nc = bass.Bass("TRN2", target_bir_lowering=False)
    P = Prog()

    def din(name, shape, dt=F32):
        return nc.dram_tensor(name, list(shape), dt, kind="ExternalInput").ap()

    xT = din("xT", [D, NT]); cT = din("cT", [128, 32]); wada = din("wada", [36, 128, 2048]); bada = din("bada", [128, 144])
    sel = din("sel", [2, 1])
    if not LITE:
        w1g = din("w1g", [NFC, 128, 2048]); w1u = din("w1u", [NFC, 128, 2048]); w1d = din("w1d", [NFC, 128, 2048])
        w2g = din("w2g", [NFC, 128, 2048]); w2u = din("w2u", [NFC, 128, 2048]); w2d = din("w2d", [NFC, 128, 2048])
    wh = din("wh", [4, 3, 128, 2048]); wo = din("wo", [16, 128, 2048])
    gv = din("gv", [128, 48]); goutd = din("gout", [128, 16]); qknd = din("qkn", [128, 4])
    btd = din("bt", [2, 128, NTYPES * 128]); ropeCd = din("ropeC", [32, S]); ropeSd = din("ropeS", [32, S])
    dmaskd = din("dmask", [128, 512]); identd = din("ident", [128, 128]); permd = din("perm", [32, 32])
    outT = nc.dram_tensor("outT", [D, NT], F32, kind="ExternalOutput").ap()
    dbg = {}
    if DEBUG:
        dbg["h1"] = nc.dram_tensor("dbg_h1", [D, NT], F32, kind="ExternalOutput").ap()
        dbg["modT"] = nc.dram_tensor("dbg_modT", [128, 144], F32, kind="ExternalOutput").ap()
        dbg["oT"] = nc.dram_tensor("dbg_oT", [4, 128, S], BF16, kind="ExternalOutput").ap()
        dbg["nrm"] = nc.dram_tensor("dbg_nrm", [D, NT], BF16, kind="ExternalOutput").ap()
        dbg["qT"] = nc.dram_tensor("dbg_qT", [4, 128, S], BF16, kind="ExternalOutput").ap()
        dbg["kT"] = nc.dram_tensor("dbg_kT", [4, 128, S], BF16, kind="ExternalOutput").ap()
        dbg["h2"] = nc.dram_tensor("dbg_h2", [D, NT], F32, kind="ExternalOutput").ap()

    mod_bA = nc.dram_tensor("mod_bA", [2, 1536], F32).ap()
    mod_gA = nc.dram_tensor("mod_gA", [8, 1536], F32).ap()
    mod_bB = nc.dram_tensor("mod_bB", [2, 3072], F32).ap()
    mod_gB = nc.dram_tensor("mod_gB", [8, 3072], F32).ap()
    nrm_b = nc.dram_tensor("nrm_b", [4, 128, 2048], F32).ap()
    nrm_g = nc.dram_tensor("nrm_g", [4, 512, 2048], F32).ap()
    vdram = nc.dram_tensor("vdram", [S, 128], BF16).ap()
    o_b = nc.dram_tensor("o_b", [4, 512, 512], F32).ap()
    o_g = nc.dram_tensor("o_g", [16 * 512, 512], F32).ap()

    es = ExitStack()
    ARENA = 212480
    es.enter_context(nc.sbuf_tensor("arena", [128, ARENA + 64], mybir.dt.uint8))
    base0 = (nc.sbuf_base - (ARENA + 64) + 31) // 32 * 32
    cur = [base0]

    def salloc(name, shape, dt, at=None):
        nbytes = int(np.prod(shape[1:])) * (4 if dt == F32 else 2)
        nbytes = (nbytes + 31) // 32 * 32
        if at is None:
            off = cur[0]; cur[0] += nbytes
        else:
            off = at
        assert off + nbytes <= base0 + ARENA, (name, off + nbytes - base0, ARENA)
        return nc.alloc_sbuf_tensor_at(name, list(shape), dt, offset=off), off + nbytes

    h, _ = salloc("h", [128, 16, NT], F32)
    modT, _ = salloc("modT", [128, 144], F32)
    gvs, _ = salloc("gvs", [128, 48], F32)
    gouts, _ = salloc("gouts", [128, 16], F32)
    qkns, _ = salloc("qkns", [128, 4], F32)
    qknsc, _ = salloc("qknsc", [128, 4], F32)
    Avec, _ = salloc("Avec", [128, 16], F32)
    tmpA, _ = salloc("tmpA", [128, 16], F32)
    gth, _ = salloc("gth", [128, 16], F32)
    ident, _ = salloc("ident", [128, 128], BF16)
    ones, _ = salloc("ones", [128, 128], BF16)
    rstd, _ = salloc("rstd", [128, NT], F32)
    sels, _ = salloc("sels", [2, 1], F32)
    scr, _ = salloc("scr", [128, 8], F32)
    R0 = cur[0]
    xn, e1 = salloc("xn", [128, 16, NT], BF16)
    wgu = []
    for i in range(6):
        t, _ = salloc(f"wgu{i}", [128, 2048], BF16); wgu.append(t)
    wdr = []
    for i in range(6):
        t, _ = salloc(f"wd{i}", [128, 2048], BF16); wdr.append(t)
    hid = []
    hid_off = cur[0]
    for i in range(3):
        t, _ = salloc(f"hid{i}", [128, 4, NT], BF16); hid.append(t)
    sgt = []
    for i in range(2):
        t, _ = salloc(f"sgt{i}", [128, 512], F32); sgt.append(t)
    sqr = []
    for i in range(2):
        t, _ = salloc(f"sqr{i}", [128, NT], BF16); sqr.append(t)
    tmpf = []
    for i in range(2):
        t, _ = salloc(f"tmpf{i}", [128, NT], F32); tmpf.append(t)
    cTf, _ = salloc("cTf", [128, 32], F32)
    csb, _ = salloc("csb", [128, 32], BF16)
    badaT, _ = salloc("badaT", [128, 144], F32)
    stg = []
    for i in range(2):
        t, _ = salloc(f"stg{i}", [2, 512], F32); stg.append(t)
    stgr = []
    for i in range(2):
        t, _ = salloc(f"stgr{i}", [2, 512], F32); stgr.append(t)
    ffn_end = cur[0]
    cur[0] = R0
    nrmr = []
    for i in range(2):
        t, _ = salloc(f"nrmr{i}", [128, 16, 256], BF16); nrmr.append(t)
    whs = []
    for i in range(3):
        t, _ = salloc(f"whs{i}", [128, 2048], BF16); whs.append(t)
    qT, _ = salloc("qT", [128, S], BF16)
    kTp, _ = salloc("kTp", [128, S + 2 * PADN], BF16)
    vS, _ = salloc("vS", [128, 32, 128], BF16)
    vD, _ = salloc("vD", [128, 33, 128], BF16)
    acc, _ = salloc("acc", [128, 2, S], F32)
    PT = []
    for i in range(2):
        t, _ = salloc(f"PT{i}", [128, 640], BF16); PT.append(t)
    Bt, _ = salloc("Bt", [128, NTYPES * 128], BF16)
    oT, _ = salloc("oT", [128, S], BF16)
    ropeC, _ = salloc("ropeCt", [32, S], BF16)
    ropeS, _ = salloc("ropeSt", [32, S], BF16)
    kD, _ = salloc("kD", [128, S + 128], BF16)
    perm, _ = salloc("perm", [32, 32], BF16)
    rtmp = []
    for i in range(2):
        t, _ = salloc(f"rtmp{i}", [32, 512], F32); rtmp.append(t)
    sqt = []
    for i in range(2):
        t, _ = salloc(f"sqt{i}", [128, 512], BF16); sqt.append(t)
    dmask, _ = salloc("dmask", [128, 512], BF16)
    att_end = cur[0]
    assert max(att_end, ffn_end) <= base0 + ARENA

    ps = [es.enter_context(nc.psum_tensor(f"ps{i}", [128, 512], F32)) for i in range(8)]
    psS = []
    engsem = {e: es.enter_context(nc.semaphore(f"sem_{e}")) for e in ("pe", "act", "dve", "pool")}
    dmasem = {}

    def dsem(key):
        if key not in dmasem:
            dmasem[key] = es.enter_context(nc.semaphore("d_" + str(key)))
        return key

    PSK = lambda b: ("ps", b)
    HK = lambda kc: [("h", kc, 0), ("h", kc, 1)]
    HALL = [("h", kc, t) for kc in range(16) for t in range(2)]

    def dma(eng, out, in_, reads, writes, key, **kw):
        dsem(key)
        return P.add(eng, lambda e: e.dma_start(out=out, in_=in_, **kw), reads, writes, dma=key)

    def castdma(out, in_, reads, writes, key):
        return dma("pool", out, in_, reads, writes, key, max_dma_last_dim=8192)

    def mm(out, lhsT, rhs, start, stop, reads, writes, **kw):
        return P.add("pe", lambda e: e.matmul(out, lhsT, rhs, start=start, stop=stop, **kw), reads, writes)

    def act(out, in_, func, reads, writes, bias=None, scale=None):
        kw = {}
        if bias is not None:
            kw["bias"] = bias
        if scale is not None:
            kw["scale"] = scale
        return P.add("act", lambda e: e.activation(out, in_, func, **kw), reads, writes)

    def tt(eng, out, in0, in1, op, reads, writes):
        return P.add(eng, lambda e: e.tensor_tensor(out, in0, in1, op), reads, writes)

    def stt(out, in0, scalar, in1, op0, op1, reads, writes):
        return P.add("dve", lambda e: e.scalar_tensor_tensor(out, in0, scalar, in1, op0, op1), reads, writes)

    def ts(eng, out, in0, s1, s2, op0, op1, reads, writes):
        return P.add(eng, lambda e: e.tensor_scalar(out, in0, s1, s2, op0, op1), reads, writes)

    def cp(eng, out, in_, reads, writes):
        return P.add(eng, lambda e: e.tensor_copy(out, in_), reads, writes)

    def memset(eng, ap, val, writes):
        return P.add(eng, lambda e: e.memset(ap, val), (), writes)

    dma("sp", h[:, :, :], xT.rearrange("(k p) t -> p k t", p=128), (), HALL, "ld_h")
    dma("sp", cTf[:, :], cT, (), ["cTf"], "ld_c")
    dma("sp", badaT[:, :], bada, (), ["badaT"], "ld_c")
    dma("sp", sels[:, :], sel, (), ["sels"], "ld_c")
    dma("sp", gvs[:, :], gv, (), ["gvs"], "ld_c")
    dma("sp", gouts[:, :], goutd, (), ["gouts"], "ld_c")
    dma("sp", qkns[:, :], qknd, (), ["qkns"], "ld_c")
    castdma(ident[:, :], identd, (), ["ident"], "ld_id")
    memset("dve", ones[:, :], 1.0, ["ones"])

    act(csb[:, :], cTf[:, :], AF.Silu, ["cTf"], ["csb"])
    wgu_i = [0]

    def next_wgu():
        i = wgu_i[0] % 6; wgu_i[0] += 1
        return i

    stg_i = [0]; stgr_i = [0]

    def ada_chunk(c, c0, bank, dst, dkey):
        sl = next_wgu()
        castdma(wgu[sl][:, :], wada[c], (), [("wgu", sl)], ("wgu", sl))
        lc = c - c0
        for kc in range(16):
            mm(ps[bank][0:2, (lc % 4) * 128:(lc % 4) * 128 + 128], csb[:, kc * 2:kc * 2 + 2], wgu[sl][:, kc * 128:kc * 128 + 128],
               kc == 0, kc == 15, ["csb", ("wgu", sl)], [PSK(bank)])
        if lc % 4 == 3:
            s_ = stg_i[0] % 2; stg_i[0] += 1
            cp("dve", stg[s_][:, :], ps[bank][0:2, 0:512], [PSK(bank)], [("stg", s_)])
            dma("sp", dst[:, (lc // 4) * 512:(lc // 4) * 512 + 512], stg[s_][:, :], [("stg", s_)], [dkey], ("stgd", s_))

    def ada_gather(src, dst, skey, gkey, sem):
        dsem(sem)
        P.add("pool", lambda e: e.collective_compute("AllGather", ALU.bypass, replica_groups=[[0, 1, 2, 3], [4, 5, 6, 7]], dma_qos=CC_QOS,
                                                     ins=[src.opt()], outs=[dst.opt()]),
              [skey], [gkey], dma=sem, inc=1)

    def ada_transposes(gathered, gkey, nchunk, gi_base, bank):
        for r in range(4):
            for grp in range(nchunk // 4):
                s_ = stgr_i[0] % 2; stgr_i[0] += 1
                dma("sp", stgr[s_][:, :], gathered[2 * r:2 * r + 2, grp * 512:grp * 512 + 512], [gkey], [("stgr", s_)], ("stgrd", s_))
                for k in range(4):
                    gi = gi_base + r * nchunk + grp * 4 + k
                    mm(ps[bank][:, gi:gi + 1], stgr[s_][:, k * 128:k * 128 + 128], sels[:, :], True, True,
                       [("stgr", s_), "sels"], [PSK(bank)])
        lo, hi = gi_base, gi_base + 4 * nchunk
        tt("dve", modT[:, lo:hi], ps[bank][:, lo:hi], badaT[:, lo:hi], ALU.add, [PSK(bank), "badaT"], ["modT"])

    for c in range(12):
        ada_chunk(c, 0, 0, mod_bA, "mod_bA")
    ada_gather(mod_bA, mod_gA, "mod_bA", "mod_gA", "cc_modA")
    ada_transposes(mod_gA, "mod_gA", 12, 0, 1)
    adaB = {fc: (lambda c=12 + fc: ada_chunk(c, 12, 7, mod_bB, "mod_bB")) for fc in range(24)}
    adaB[24] = lambda: ada_gather(mod_bB, mod_gB, "mod_bB", "mod_gB", "cc_modB")
    if LITE:
        for fc in range(25):
            adaB[fc]()
        ada_transposes(mod_gB, "mod_gB", 24, 48, 7)
    if DEBUG:
        dma("sp", dbg["modT"], modT[:, :], ["modT"], [], "dbg")
    cp("dve", qknsc[:, :], qkns[:, :], ["qkns"], ["qknsc"])
    ts("dve", qknsc[:, 0:1], qkns[:, 0:1], SCALE, None, ALU.mult, ALU.bypass, ["qkns", "qknsc"], ["qknsc"])
    ts("dve", qknsc[:, 2:3], qkns[:, 2:3], SCALE, None, ALU.mult, ALU.bypass, ["qkns", "qknsc"], ["qknsc"])

    MS = lambda m, kc: modT[:, m * 16 + kc:m * 16 + kc + 1]

    sq_i = [0]; tf_i = [0]

    xnB = xn[:, :, :].rearrange("p k t -> p (k t)").rearrange("p (b k t) -> p b k t", b=4, k=16)

    def norm_mod(gidx, m_sh, m_sc, blockmajor=False):
        ts("dve", tmpA[:, :], modT[:, m_sc * 16:m_sc * 16 + 16], 1.0, None, ALU.add, ALU.bypass, ["modT"], ["tmpA"])
        tt("dve", Avec[:, :], tmpA[:, :], gvs[:, gidx * 16:gidx * 16 + 16], ALU.mult, ["tmpA", "gvs"], ["Avec"])
        for kc in range(16):
            s = sq_i[0] % 2; sq_i[0] += 1
            if kc % 2 == 0:
                act(sqr[s][:, :], h[:, kc, :], AF.Square, HK(kc), [("sqr", s)])
            else:
                tt("dve", sqr[s][:, :], h[:, kc, :], h[:, kc, :], ALU.mult, HK(kc), [("sqr", s)])
            for t in range(2):
                mm(ps[2 + t][:, :], ones[:, :], sqr[s][:, t * 512:t * 512 + 512], kc == 0, kc == 15,
                   ["ones", ("sqr", s)], [PSK(2 + t)])
        for t in range(2):
            act(rstd[:, t * 512:t * 512 + 512], ps[2 + t][:, :], AF.Ln, [PSK(2 + t)], ["rstd"], bias=EPS, scale=1.0 / D)
        act(rstd[:, :], rstd[:, :], AF.Exp, ["rstd"], ["rstd"], scale=-0.5)
        for kc in range(16):
            s = tf_i[0] % 2; tf_i[0] += 1
            stt(tmpf[s][:, :], h[:, kc, :], Avec[:, kc:kc + 1], rstd[:, :], ALU.mult, ALU.mult,
                HK(kc) + ["Avec", "rstd"], [("tmpf", s)])
            if blockmajor:
                act(xnB[:, :, kc, :], tmpf[s][:, :].rearrange("p (b t) -> p b t", b=4), AF.Identity,
                    [("tmpf", s), "modT"], [("xn", kc)], bias=MS(m_sh, kc), scale=1.0)
            else:
                act(xn[:, kc, :], tmpf[s][:, :], AF.Identity, [("tmpf", s), "modT"], [("xn", kc)], bias=MS(m_sh, kc), scale=1.0)

    wd_i = [0]
    pd_i = [0]

    def down_group(wsrc, fc0, rhs_fn, rhs_keys, scal):
        slots = []
        for fl in range(4):
            sl = wd_i[0] % 6; wd_i[0] += 1
            castdma(wdr[sl][:, :], wsrc[fc0 + fl], (), [("wd", sl)], ("wd", sl))
            slots.append(sl)
        for dc in range(16):
            for t in range(2):
                b = 4 + pd_i[0] % 4; pd_i[0] += 1
                for fl in range(4):
                    mm(ps[b][:, :], wdr[slots[fl]][:, dc * 128:dc * 128 + 128], rhs_fn(fl, t), fl == 0, fl == 3,
                       [("wd", slots[fl])] + rhs_keys(fl), [PSK(b)])
                stt(h[:, dc, t * 512:t * 512 + 512], ps[b][:, :], scal[:, dc:dc + 1], h[:, dc, t * 512:t * 512 + 512],
                    ALU.mult, ALU.add, [PSK(b), "gth", ("h", dc, t)], [("h", dc, t)])

    def ffn(wg, wu, wd_, m_gt, side=None):
        ts("dve", gth[:, :], modT[:, m_gt * 16:m_gt * 16 + 16], 0.5, None, ALU.mult, ALU.bypass, ["modT"], ["gth"])
        gu_i = 0
        for fc in range(NFC):
            if side and fc in side:
                side[fc]()
            g = fc // 4
            hs = g % 3
            sg_ = next_wgu(); su_ = next_wgu()
            castdma(wgu[sg_][:, :], wg[fc], (), [("wgu", sg_)], ("wgu", sg_))
            castdma(wgu[su_][:, :], wu[fc], (), [("wgu", su_)], ("wgu", su_))
            for t in range(2):
                bg = (gu_i % 2) * 2; bu = bg + 1; gu_i += 1
                for kc in range(16):
                    mm(ps[bg][:, :], wgu[sg_][:, kc * 128:kc * 128 + 128], xn[:, kc, t * 512:t * 512 + 512], kc == 0, kc == 15,
                       [("wgu", sg_), ("xn", kc)], [PSK(bg)])
                for kc in range(16):
                    mm(ps[bu][:, :], wgu[su_][:, kc * 128:kc * 128 + 128], xn[:, kc, t * 512:t * 512 + 512], kc == 0, kc == 15,
                       [("wgu", su_), ("xn", kc)], [PSK(bu)])
                st = gu_i % 2
                act(sgt[st][:, :], ps[bg][:, :], AF.Silu, [PSK(bg)], [("sgt", st)])
                tt("dve", hid[hs][:, fc % 4, t * 512:t * 512 + 512], sgt[st][:, :], ps[bu][:, :], ALU.mult,
                   [("sgt", st), PSK(bu)], [("hid", hs, fc % 4)])
            if fc % 4 == 3 and g >= 1:
                gp = g - 1
                down_group(wd_, gp * 4, lambda fl, t, gp=gp: hid[gp % 3][:, fl, t * 512:t * 512 + 512],
                           lambda fl, gp=gp: [("hid", gp % 3, fl)], gth)
        gp = NFC // 4 - 1
        down_group(wd_, gp * 4, lambda fl, t, gp=gp: hid[gp % 3][:, fl, t * 512:t * 512 + 512],
                   lambda fl, gp=gp: [("hid", gp % 3, fl)], gth)

    def finish():
        dma("sp", outT.rearrange("(k p) t -> p k t", p=128), h[:, :, :], HALL, ["out"], "st_out")
        P.add("sp", lambda e: e.nop(), ["out"], [])
        P.add("sp", lambda e: e.nop(), [], [], extra_deps=list(P.last_dma.values()))
        with es:
            with nc.Block() as block:
                P.emit(nc, block, engsem, dmasem)
        return nc

    if not LITE:
        norm_mod(0, 0, 1)
        ffn(w1g, w1u, w1d, 2, side=adaB)
        ada_transposes(mod_gB, "mod_gB", 24, 48, 7)
    if DEBUG:
        dma("sp", dbg["h1"].rearrange("(k p) t -> p k t", p=128), h[:, :, :], HALL, [], "dbg")

    if STAGE == 1:
        return finish()
    norm_mod(1, 3, 4, blockmajor=True)
    if DEBUG:
        dma("sp", dbg["nrm"].rearrange("(k p) (b t) -> p b k t", p=128, b=4), xnB, [("xn", kc) for kc in range(16)], [], "dbg")
    for b4 in range(4):
        dsem("cc_nrm%d" % b4)
        dma("sp", nrm_b.bitcast(BF16)[b4], xnB[:, b4].rearrange("p k t -> p (k t)"),
            [("xn", kc) for kc in range(16)], [("nrm_b", b4)], "st_nrm")

    def nrm_gather(b4):
        P.add("pool", lambda e, b4=b4: e.collective_compute("AllGather", ALU.bypass, replica_groups=[[0, 1, 2, 3], [4, 5, 6, 7]], dma_qos=CC_QOS,
                                                            ins=[nrm_b[b4].opt()], outs=[nrm_g[b4].opt()]),
              [("nrm_b", b4)], [("nrm_g", b4)], dma="cc_nrm%d" % b4, inc=1)
    nrm_gather(0)

    if STAGE == 2:
        return finish()
    nops = {
        "pe": lambda e: e.matmul(ps[7][0:1, 0:1], sels[:, :], sels[:, :], start=True, stop=True),
        "act": lambda e: e.activation(scr[:, 0:1], scr[:, 0:1], AF.Identity),
        "dve": lambda e: e.memset(scr[:, 1:2], 0.0),
        "pool": lambda e: e.memset(scr[:, 2:3], 0.0),
        "sp": lambda e: e.nop(),
    }
    memset("dve", scr[:, :], 0.0, ["scr"])
    P.barrier(nops)

    try:
        castdma(dmask[:, :], dmaskd, (), ["dmask"], "ld_dm")
        castdma(perm[:, :], permd, (), ["perm"], "ld_dm")
        for pc in range(4):
            castdma(ropeC[:, pc * 1024:pc * 1024 + 1024], ropeCd[:, pc * 1024:pc * 1024 + 1024], (), ["ropeT"], "ld_dm")
            castdma(ropeS[:, pc * 1024:pc * 1024 + 1024], ropeSd[:, pc * 1024:pc * 1024 + 1024], (), ["ropeT"], "ld_dm")
        memset("pool", kD[:, 0:64], 0.0, ["kDpad"])
        memset("pool", kD[:, 64 + S:128 + S], 0.0, ["kDpad"])
        memset("pool", kTp[:, 0:PADN], 0.0, ["kTpad"])
        memset("pool", kTp[:, PADN + S:PADN + S + PADN], 0.0, ["kTpad"])
        memset("pool", vD[:, 0, :], 0.0, ["vDpad"])
        memset("pool", vD[:, 32, :], 0.0, ["vDpad"])
        nr_i = [0]; sq2_i = [0]; pt_i = [0]; po_i = [0]; rp_i = [0]
        ckpt(1)

        for hd_ in range(4):
            dsem("cc_o%d" % hd_)
        pending_fin = []
        for i3 in range(3):
            castdma(whs[i3][:, :], wh[0, i3], (), [("whs", i3)], ("whs", i3))
        castdma(Bt[:, :], btd[0], (), ["Bt"], "ld_bt")
        for b4 in range(1, 4):
            nrm_gather(b4)
        for hd in range(4):
            is_na = hd < 2
            if hd == 1:
                castdma(Bt[:, :], btd[hd], (), ["Bt"], "ld_bt")
            gq = qknsc[:, 0:1] if is_na else qknsc[:, 2:3]
            gk = qknsc[:, 1:2] if is_na else qknsc[:, 3:4]
            for tbi in range(16):
                b4 = tbi // 4; r = tbi % 4
                tb = r * 4 + b4
                ns = nr_i[0] % 2; nr_i[0] += 1
                dma("sp", nrmr[ns][:, :, :].rearrange("p k t -> p (k t)"), nrm_g.bitcast(BF16)[b4, r * 128:(r + 1) * 128, :],
                    [("nrm_g", b4)], [("nrmr", ns)], ("nrmr", ns))
                bq = tbi % 2
                bv = 2 + tbi % 2
                for kc in range(16):
                    mm(ps[bq][:, 0:256], whs[0][:, kc * 128:kc * 128 + 128], nrmr[ns][:, kc, :], kc == 0, kc == 15,
                       [("whs", 0), ("nrmr", ns)], [PSK(bq)])
                for kc in range(16):
                    mm(ps[bq][:, 256:512], whs[1][:, kc * 128:kc * 128 + 128], nrmr[ns][:, kc, :], kc == 0, kc == 15,
                       [("whs", 1), ("nrmr", ns)], [PSK(bq)])
                for tc in range(2):
                    for kc in range(16):
                        mm(ps[bv][:, tc * 128:tc * 128 + 128], nrmr[ns][:, kc, tc * 128:tc * 128 + 128],
                           whs[2][:, kc * 128:kc * 128 + 128], kc == 0, kc == 15, [("whs", 2), ("nrmr", ns)], [PSK(bv)])
                s2 = sq2_i[0] % 2; sq2_i[0] += 1
                act(sqt[s2][:, :], ps[bq][:, :], AF.Square, [PSK(bq)], [("sqt", s2)])
                mm(ps[6][:, :], ones[:, :], sqt[s2][:, :], True, True, ["ones", ("sqt", s2)], [PSK(6)])
                act(rstd[:, 0:512], ps[6][:, :], AF.Ln, [PSK(6)], ["rstd"], bias=EPS, scale=1.0 / 128)
                act(rstd[:, 0:512], rstd[:, 0:512], AF.Exp, ["rstd"], ["rstd"], scale=-0.5)
                tok0 = tb * 256
                stt(qT[:, tok0:tok0 + 256], ps[bq][:, 0:256], gq, rstd[:, 0:256], ALU.mult, ALU.mult,
                    [PSK(bq), "qknsc", "rstd"], ["qT"])
                stt(kTp[:, PADN + tok0:PADN + tok0 + 256], ps[bq][:, 256:512], gk, rstd[:, 256:512], ALU.mult, ALU.mult,
                    [PSK(bq), "qknsc", "rstd"], ["kT"])
                act(vS[:, tb * 2:tb * 2 + 2, :], ps[bv][:, 0:256].rearrange("p (c d) -> p c d", c=2), AF.Identity, [PSK(bv)], ["vS"])
                if pending_fin and tbi % 2 == 1:
                    pending_fin.pop(0)()
            if hd < 3:
                for i3 in range(3):
                    castdma(whs[i3][:, :], wh[hd + 1, i3], (), [("whs", i3)], ("whs", i3))
            ckpt(2 if hd == 0 else (5 if hd == 2 else -1))
            if not is_na:
                for which, buf, off, key in ((0, qT, 0, "qT"), (1, kTp, PADN, "kT")):
                    for pc in range(8):
                        cc0 = pc * 512
                        x0 = buf[0:32, off + cc0:off + cc0 + 512]
                        rb = 6 + pc % 2
                        mm(ps[rb][0:32, :], perm[:, :], x0, True, True, ["perm", key], [PSK(rb)])
                        tt("dve", rtmp[0][:, :], x0, ropeC[:, cc0:cc0 + 512], ALU.mult, [key, "ropeT"], [("rtmp", 0)])
                        tt("dve", rtmp[1][:, :], ps[rb][0:32, :], ropeS[:, cc0:cc0 + 512], ALU.mult, [PSK(rb), "ropeT"], [("rtmp", 1)])
                        tt("dve", x0, rtmp[0][:, :], rtmp[1][:, :], ALU.add, [("rtmp", 0), ("rtmp", 1)], [key])
            if hd == 2:
                ckpt(6)
            if DEBUG:
                dma("sp", dbg["qT"][hd], qT[:, :], ["qT"], [], "dbg")
                dma("sp", dbg["kT"][hd], kTp[:, PADN:PADN + S], ["kT"], [], "dbg")

            def tile_S(tl):
                if tl.get("pre_S"):
                    tl["pre_S"]()
                chunks = tl["chunks"]; qap = tl["q"]
                nch = len(chunks)
                sb = (pt_i[0] % 2) * 2
                pti = pt_i[0] % 2; pt_i[0] += 1
                for ci, kap in enumerate(chunks):
                    b = sb + ci // 4
                    col = (ci % 4) * 128
                    mm(ps[b][:, col:col + 128], kap, qap, True, False, ["kT", "kTpad", "kDpad", "qT"] + tl.get("kkeys", []), [PSK(b)])
                    mm(ps[b][:, col:col + 128], ident[:, :], tl["mask"](ci), False, True, ["ident"] + tl["mkeys"], [PSK(b)])
                n1 = min(nch, 4) * 128
                act(PT[pti][:, 0:n1], ps[sb][:, 0:n1], AF.Exp, [PSK(sb)], [("PT", pti)])
                if nch > 4:
                    n2 = (nch - 4) * 128
                    act(PT[pti][:, 512:512 + n2], ps[sb + 1][:, 0:n2], AF.Exp, [PSK(sb + 1)], [("PT", pti)])
                return pti, nch

            def tile_PV(tl, st):
                if tl.get("pre_PV"):
                    tl["pre_PV"]()
                pti, nch = st
                bo = 4 + po_i[0] % 2; po_i[0] += 1
                for ci in range(nch):
                    mm(ps[bo][:, 0:128], tl["v"](ci), PT[pti][:, ci * 128:ci * 128 + 128], ci == 0, ci == nch - 1,
                       tl["vkeys"] + [("PT", pti)], [PSK(bo)])
                for ci in range(nch):
                    mm(ps[bo][:, 128:256], ones[:, :], PT[pti][:, ci * 128:ci * 128 + 128], ci == 0, ci == nch - 1,
                       ["ones", ("PT", pti)], [PSK(bo)])
                src = ps[bo][:, 0:256].rearrange("p (w q) -> p w q", w=2)
                if tl["first"]:
                    cp("dve", tl["acc"], src, [PSK(bo)], ["acc_all"])
                else:
                    tt("dve", tl["acc"], src, tl["acc"], ALU.add, [PSK(bo)], ["acc_all"])

            tiles = []
            if is_na:
                for n in range(32):
                    lst = NA_TILES[n]
                    tiles.append(dict(
                        chunks=[kTp[:, PADN + c * 128:PADN + c * 128 + 128] for c, _ in lst],
                        q=qT[:, n * 128:n * 128 + 128],
                        v=(lambda ci, lst=lst: vS[:, lst[ci][0], :]), vkeys=["vS"],
                        mask=(lambda ci, lst=lst: Bt[:, lst[ci][1] * 128:lst[ci][1] * 128 + 128]), mkeys=["Bt"],
                        acc=acc[:, :, n * 128:n * 128 + 128], first=True))
            else:
                vdv = vdram.rearrange("(c p) d -> p c d", p=128)
                for q4 in range(4):
                    dma("sp", vdv[:, q4 * 8:q4 * 8 + 8, :], vS[:, q4 * 8:q4 * 8 + 8, :], ["vS"], ["vdram"], "st_v")
                for Dd in (1, 4, 16):
                    L = S // Dd

                    def ld_vD(Dd=Dd):
                        na_ = (S // Dd) // 128
                        Vv = vdram.rearrange("(a i r) d -> i r a d", i=128, r=Dd)
                        for g in range(4):
                            if na_ >= 8:
                                rs = slice((8 * g) // na_, (8 * g) // na_ + 1); as_ = slice((8 * g) % na_, (8 * g) % na_ + 8)
                            else:
                                rs = slice((8 * g) // na_, (8 * g) // na_ + 8 // na_); as_ = slice(0, na_)
                            keys = [("vD", c) for c in range(8 * g, 8 * g + 9)]
                            na_u = as_.stop - as_.start
                            for ri, r_ in enumerate(range(rs.start, rs.stop)):
                                c0 = 8 * g + ri * na_u
                                dma("sp", vD[64:128, c0:c0 + na_u, :], Vv[0:64, r_, as_], ["vdram"], keys, "ld_vD")
                                dma("sp", vD[0:64, c0 + 1:c0 + 1 + na_u, :], Vv[64:128, r_, as_], ["vdram"], keys, "ld_vD")

                    def mk_kD(Dd=Dd):
                        srcv = kTp[:, PADN:PADN + S].rearrange("p (u r) -> p r u", r=Dd)
                        for g in range(4):
                            nr = Dd // 4
                            cp("pool", kD[:, 64 + 1024 * g:64 + 1024 * g + 1024].rearrange("p (r u) -> p r u", r=nr),
                               srcv[:, g * nr:(g + 1) * nr, :], ["kT"], [("kD", c) for c in range(8 * g, 8 * g + 9)])

                    for n in range(32):
                        m0 = 128 * n
                        rho = m0 // L; u0 = m0 % L
                        q0 = rho + Dd * u0
                        if Dd == 1:
                            chunks = [kTp[:, PADN + m0 - 64:PADN + m0 + 64], kTp[:, PADN + m0 + 64:PADN + m0 + 192]]
                        else:
                            chunks = [kD[:, m0:m0 + 128], kD[:, m0 + 128:m0 + 256]]
                        mt = [1 if u0 == 0 else 0, 3 if u0 + 128 == L else 2]
                        tiles.append(dict(
                            chunks=chunks, q=qT[:, q0:q0 + 127 * Dd + 1:Dd],
                            kkeys=([("kD", n), ("kD", n + 1)] if Dd > 1 else []),
                            v=(lambda ci, n=n: vD[:, n + ci, :]), vkeys=[("vD", n), ("vD", n + 1), "vDpad"],
                            mask=(lambda ci, mt=mt: dmask[:, mt[ci] * 128:mt[ci] * 128 + 128]), mkeys=["dmask"],
                            acc=acc[:, :, q0:q0 + 127 * Dd + 1:Dd], first=(Dd == 1),
                            pre_S=(mk_kD if (n == 0 and Dd > 1) else None), pre_PV=(ld_vD if n == 0 else None)))
            prev = None
            for tl in tiles:
                st = tile_S(tl)
                if prev is not None:
                    tile_PV(*prev)
                prev = (tl, st)
            tile_PV(*prev)
            def make_fin(hd=hd):
                fns = []
                for pc in range(4):
                    def piece(pc=pc):
                        sl_ = slice(pc * 1024, pc * 1024 + 1024)
                        act(acc[:, 1, sl_], acc[:, 1, sl_], AF.Ln, ["acc_all"], ["acc_all"])
                        act(acc[:, 1, sl_], acc[:, 1, sl_], AF.Exp, ["acc_all"], ["acc_all"], scale=-1.0)
                        tt("dve", oT[:, sl_], acc[:, 0, sl_], acc[:, 1, sl_], ALU.mult, ["acc_all"], ["oT", "acc_all"])
                    fns.append(piece)

                def store():
                    for tq in range(4):
                        dma("sp", o_b.bitcast(BF16)[hd, tq * 128:tq * 128 + 128, :], oT[:, tq * 1024:tq * 1024 + 1024],
                            ["oT"], [("o_b", hd)], "st_o")
                    P.add("pool", lambda e: e.collective_compute("AllGather", ALU.bypass, replica_groups=[[0, 1, 2, 3], [4, 5, 6, 7]], dma_qos=CC_QOS,
                                                                 ins=[o_b[hd].opt()], outs=[o_g[hd * 2048:(hd + 1) * 2048, :].opt()]),
                          [("o_b", hd)], [("o_g", hd)], dma="cc_o%d" % hd, inc=1)
                    if DEBUG:
                        dma("sp", dbg["oT"][hd], oT[:, :], ["oT"], [], "dbg")
                fns.append(store)
                return fns
            pending_fin.extend(make_fin())
            if hd == 3:
                while pending_fin:
                    pending_fin.pop(0)()
            ckpt({0: 3, 1: 4, 2: 10, 3: 11}[hd])

    except _Stop:
        return finish()
    P.barrier(nops)

    if STAGE == 3:
        return finish()
    def ld_o(e):
        ogb = o_g.bitcast(BF16).rearrange("(h r j q) t -> h j q r t", h=4, j=4, r=4)
        pid = e.partition_id()
        j = pid % 4
        for g in range(2):
            for l in range(2):
                src = ogb[2 * g + l, bass.ds(j, 1)].rearrange("o q r t -> (o q) r t")
                e.dma_start(out=xn[:, g * 8 + l:g * 8 + 8:2, :], in_=src).then_inc(dmasem["ld_o"], 16)
        return None
    dsem("ld_o")
    P.add("pool", ld_o, [("o_g", i) for i in range(4)], [("xn", kc) for kc in range(16)], dma="ld_o", inc=64)
    for grp in range(2):
        for k8 in range(8):
            kc = grp * 8 + k8
            s = sq_i[0] % 2; sq_i[0] += 1
            if kc % 2 == 0:
                act(sqr[s][:, :], xn[:, kc, :], AF.Square, [("xn", kc)], [("sqr", s)])
            else:
                tt("dve", sqr[s][:, :], xn[:, kc, :], xn[:, kc, :], ALU.mult, [("xn", kc)], [("sqr", s)])
            for t in range(2):
                mm(ps[2 + t][:, :], ones[:, :], sqr[s][:, t * 512:t * 512 + 512], k8 == 0, k8 == 7,
                   ["ones", ("sqr", s)], [PSK(2 + t)])
        for t in range(2):
            act(rstd[:, t * 512:t * 512 + 512], ps[2 + t][:, :], AF.Ln, [PSK(2 + t)], ["rstd"], bias=EPS, scale=1.0 / 1024)
        act(rstd[:, :], rstd[:, :], AF.Exp, ["rstd"], ["rstd"], scale=-0.5)
        for k8 in range(8):
            kc = grp * 8 + k8
            stt(xn[:, kc, :], xn[:, kc, :], gouts[:, kc:kc + 1], rstd[:, :], ALU.mult, ALU.mult,
                [("xn", kc), "gouts", "rstd"], [("xn", kc)])
    cp("dve", gth[:, :], modT[:, 5 * 16:5 * 16 + 16], ["modT"], ["gth"])
    for g in range(4):
        down_group(wo, g * 4, lambda fl, t, g=g: xn[:, g * 4 + fl, t * 512:t * 512 + 512],
                   lambda fl, g=g: [("xn", g * 4 + fl)], gth)
    if DEBUG:
        dma("sp", dbg["h2"].rearrange("(k p) t -> p k t", p=128), h[:, :, :], HALL, [], "dbg")

    if not LITE:
        norm_mod(2, 6, 7)
        ffn(w2g, w2u, w2d, 8)

    return finish()


def _tile_w(W):
    K, N = W.shape
    return np.ascontiguousarray(W.reshape(K // 128, 128, N // 128, 128).transpose(2, 1, 0, 3)).reshape(N // 128, 128, (K // 128) * 128)


def _vec16(v):
    return np.ascontiguousarray(v.reshape(-1, 128).T)


_NC_CACHE = {}


def kernel(x, c, w_ada, b_ada, g_ffn1, w1_gate, w1_up, w1_down, g_mix, w_qkv, qn_na, kn_na, qn_dil, kn_dil,
           rpb_na, g_out_na, g_out_dil, w_o, g_ffn2, w2_gate, w2_up, w2_down):
    f = lambda a: np.asarray(a, dtype=np.float32)
    x = f(x); c = f(c)
    w_ada = f(w_ada)[0]; b_ada = f(b_ada)[0]
    if not LITE:
        w1g = _tile_w(f(w1_gate)[0]); w1u = _tile_w(f(w1_up)[0]); w1d = np.ascontiguousarray(f(w1_down)[0]).reshape(NFC, 128, 2048)
        w2g = _tile_w(f(w2_gate)[0]); w2u = _tile_w(f(w2_up)[0]); w2d = np.ascontiguousarray(f(w2_down)[0]).reshape(NFC, 128, 2048)
    wqkv = f(w_qkv)[0]
    wo = np.ascontiguousarray(f(w_o)[0]).reshape(16, 128, 2048)
    gv = np.concatenate([_vec16(f(g_ffn1)[0]), _vec16(f(g_mix)[0]), _vec16(f(g_ffn2)[0])], axis=1)
    gout = np.concatenate([_vec16(f(g_out_na)[0]), _vec16(f(g_out_dil)[0])], axis=1)
    qkn = np.stack([f(qn_na)[0], f(kn_na)[0], f(qn_dil)[0], f(kn_dil)[0]], axis=1)
    rpb = f(rpb_na)[0]
    cT = np.ascontiguousarray(c.T.reshape(16, 128, 2).transpose(1, 0, 2)).reshape(128, 32)
    ropeC, ropeS = rope_tabs()
    badaT_h = np.ascontiguousarray(b_ada.reshape(144, 128).T)
    dmask = dil_masks()
    ident = np.eye(128, dtype=np.float32)
    perm32 = np.zeros((32, 32), np.float32)
    perm32[(np.arange(32) + 16) % 32, np.arange(32)] = 1.0
    in_maps = []
    for core in range(8):
        b = core // 4; j = core % 4
        heads = [2 * j, 2 * j + 1, 8 + 2 * j, 9 + 2 * j]
        whl = []
        for hh in heads:
            blk = []
            for part in range(3):
                col0 = part * D + hh * 128
                blk.append(_tile_w(wqkv[:, col0:col0 + 128])[0])
            whl.append(np.stack(blk))
        m = {
            "xT": np.ascontiguousarray(x[b, j * NT:(j + 1) * NT, :].T),
            "cT": cT,
            "wada": _tile_w(np.concatenate([w_ada[:, 12 * j * 128:(12 * j + 12) * 128],
                                            w_ada[:, (48 + 24 * j) * 128:(48 + 24 * j + 24) * 128]], axis=1)),
            "bada": badaT_h,
            "sel": np.array([[1.0 - b], [float(b)]], np.float32),
            "wh": np.stack(whl), "wo": wo, "gv": gv, "gout": gout, "qkn": np.ascontiguousarray(qkn),
            "bt": np.stack([build_bias_tiles(rpb[2 * j]), build_bias_tiles(rpb[2 * j + 1])]),
            "ropeC": ropeC, "ropeS": ropeS, "dmask": dmask, "ident": ident, "perm": perm32,
        }
        if not LITE:
            m.update({"w1g": w1g, "w1u": w1u, "w1d": w1d, "w2g": w2g, "w2u": w2u, "w2d": w2d})
        in_maps.append(m)
    if "nc" not in _NC_CACHE:
        _NC_CACHE["nc"] = build_nc()
    nc = _NC_CACHE["nc"]
    res = run_bass_kernel_spmd(nc, in_maps, core_ids=list(range(8)))
    out = np.empty((2, S, D), np.float32)
    for core in range(8):
        b = core // 4; j = core % 4
        out[b, j * NT:(j + 1) * NT, :] = res.results[core]["outT"].T
    if DEBUG:
        kernel.last = res
    return out
```

```python
import numpy as np
from contextlib import ExitStack
import concourse.bass as bass
import concourse.mybir as mybir
from concourse.bass_utils import run_bass_kernel_spmd

F32 = mybir.dt.float32
BF16 = mybir.dt.bfloat16
AF = mybir.ActivationFunctionType
ALU = mybir.AluOpType

D = 2048
NT = 1024
S = 4096
DFF = 5632
NFC = 44
EPS = 1e-6
SCALE = 128.0 ** -0.5
NEGM = -30000.0
PADN = 64
DEBUG = False
STAGE = 9
ATTSTOP = 0
CC_QOS = "P2"
NRM_NP = 4
LITE = False


class Op:
    __slots__ = ("eng", "fn", "deps", "is_dma", "semkey", "inc", "cum", "targets", "signal", "seq", "idx")


class _Stop(Exception):
    pass


def ckpt(n):
    if ATTSTOP == n:
        raise _Stop()


class Prog:
    ENGS = ("pe", "act", "dve", "pool", "sp")

    def __init__(self):
        self.ops = {e: [] for e in self.ENGS}
        self.lastw = {}
        self.readers = {}
        self.dma_count = {}
        self.last_dma = {}
        self.n = 0

    def add(self, eng, fn, reads=(), writes=(), dma=None, inc=16, extra_deps=()):
        op = Op()
        op.eng = eng; op.fn = fn; op.is_dma = dma is not None; op.semkey = dma; op.inc = inc
        op.signal = False; op.seq = 0; op.idx = self.n; self.n += 1
        deps = []
        for k in reads:
            w = self.lastw.get(k)
            if w is not None:
                deps.append((w, "raw"))
        for k in writes:
            w = self.lastw.get(k)
            if w is not None:
                deps.append((w, "waw"))
            for r in self.readers.get(k, ()):
                deps.append((r, "war"))
        for d in extra_deps:
            deps.append((d, "raw"))
        for k in reads:
            self.readers.setdefault(k, []).append(op)
        for k in writes:
            self.lastw[k] = op
            self.readers[k] = []
        op.deps = deps
        op.targets = {d.semkey: self.dma_count[d.semkey] for d, _ in deps if d.is_dma}
        if op.is_dma:
            self.dma_count[dma] = self.dma_count.get(dma, 0) + inc
            op.cum = self.dma_count[dma]
            self.last_dma[dma] = op
        self.ops[eng].append(op)
        return op

    def barrier(self, mk_nop):
        firsts = []
        for e in ("pe", "act", "dve", "pool"):
            firsts.append(self.add(e, mk_nop[e], writes=[("bar1", e, self.n)]))
        dmas = [op for key, op in self.last_dma.items() if not str(key).startswith("cc_")]
        for e in self.ENGS:
            self.add(e, mk_nop[e], writes=[("bar2", e, self.n)], extra_deps=firsts + dmas)

    def needs_wait(self, op, d, kind):
        if d.is_dma:
            return True
        if d.eng != op.eng:
            return True
        if op.eng == "pe":
            return False
        if op.is_dma:
            return True
        return kind == "raw"

    def finalize(self):
        for e in self.ENGS:
            for op in self.ops[e]:
                latest = {}
                for d, kind in op.deps:
                    if (not d.is_dma) and self.needs_wait(op, d, kind):
                        if d.eng not in latest or d.idx > latest[d.eng].idx:
                            latest[d.eng] = d
                for d in latest.values():
                    d.signal = True
                op.deps = [(d, k) for d, k in op.deps if d.is_dma or latest.get(d.eng) is d]
        for e in self.ENGS:
            s = 0
            for op in self.ops[e]:
                if op.signal and not op.is_dma:
                    s += 1
                    op.seq = s

    def emit(self, nc, block, engsem, dmasem):
        self.finalize()
        decos = {"pe": block.tensor, "act": block.scalar, "dve": block.vector, "pool": block.gpsimd, "sp": block.sync}
        for e in self.ENGS:
            ops = self.ops[e]

            def body(eng, ops=ops, e=e):
                waited = {}
                for op in ops:
                    need = {}
                    for d, kind in op.deps:
                        if not self.needs_wait(op, d, kind):
                            continue
                        if d.is_dma:
                            key = ("d", d.semkey); val = op.targets[d.semkey]; sem = dmasem[d.semkey]
                        else:
                            key = ("e", d.eng); val = d.seq; sem = engsem[d.eng]
                        if val > need.get(key, (0, None))[0]:
                            need[key] = (val, sem)
                    for key, (val, sem) in need.items():
                        if waited.get(key, 0) >= val:
                            continue
                        eng.wait_ge(sem, val)
                        waited[key] = val
                    inst = op.fn(eng)
                    if inst is None:
                        continue
                    if op.is_dma:
                        inst.then_inc(dmasem[op.semkey], op.inc)
                    elif op.signal:
                        inst.then_inc(engsem[e], 1)
            decos[e](body)


def na_tile_tables():
    types = {}
    tiles = []
    for n in range(32):
        rows = []
        for rq in (2 * n, 2 * n + 1):
            rs = min(max(rq - 4, 0), 56)
            rows.append((rq, rs))
        lo = min(r[1] for r in rows) // 2
        hi = (max(r[1] for r in rows) + 7) // 2
        lst = []
        for c in range(lo, hi + 1):
            key = []
            for rkl in range(2):
                for rql in range(2):
                    rk = 2 * c + rkl
                    rq, rs = rows[rql]
                    if rs <= rk < rs + 8:
                        key.append(rk - rq + 7)
                    else:
                        key.append(-1)
            key = tuple(key)
            if all(k < 0 for k in key):
                continue
            if key not in types:
                types[key] = len(types)
            lst.append((c, types[key]))
        tiles.append(lst)
    return tiles, types


NA_TILES, NA_TYPES = na_tile_tables()
NTYPES = len(NA_TYPES)


def build_bias_tiles(rpb_h):
    out = np.full((128, NTYPES, 128), NEGM, np.float32)
    ck = np.arange(64)[:, None]
    cq = np.arange(64)[None, :]
    cs = np.clip(cq - 8, 0, 48)
    cvalid = (ck >= cs) & (ck < cs + 16)
    coff = np.clip(ck - cq + 15, 0, 30)
    for key, t in NA_TYPES.items():
        i = 0
        for rkl in range(2):
            for rql in range(2):
                a = key[i]; i += 1
                if a < 0:
                    continue
                blk = np.where(cvalid, rpb_h[a][coff], np.float32(NEGM))
                out[64 * rkl:64 * rkl + 64, t, 64 * rql:64 * rql + 64] = blk
    return out.reshape(128, NTYPES * 128)


def dil_masks():
    i = np.arange(128)[:, None]
    j = np.arange(128)[None, :]
    A = np.where(j <= i, 0.0, NEGM)
    B = np.where(j >= i, 0.0, NEGM)
    A1 = A.copy(); A1[:64, :] = NEGM
    B1 = B.copy(); B1[64:, :] = NEGM
    return np.concatenate([A, A1, B, B1], axis=1).astype(np.float32)


def rope_tabs():
    pos = np.arange(S, dtype=np.float32)
    inv = np.power(np.float32(500000.0), -np.arange(0, 32, 2, dtype=np.float32) / np.float32(32)).astype(np.float32)
    ang = (pos[None, :] * inv[:, None]).astype(np.float32)
    c = np.cos(ang).astype(np.float32); s = np.sin(ang).astype(np.float32)
    C = np.concatenate([c, c], axis=0)
    Sg = np.concatenate([-s, s], axis=0)
    return C, Sg


def build_nc():
    nc = bass.Bass("TRN2", target_bir_lowering=False)
    P = Prog()

    def din(name, shape, dt=F32):
        return nc.dram_tensor(name, list(shape), dt, kind="ExternalInput").ap()

    xT = din("xT", [D, NT]); cT = din("cT", [128, 32]); wada = din("wada", [36, 128, 2048]); bada = din("bada", [128, 144])
    sel = din("sel", [2, 1])
    if not LITE:
        w1g = din("w1g", [NFC, 128, 2048]); w1u = din("w1u", [NFC, 128, 2048]); w1d = din("w1d", [NFC, 128, 2048])
        w2g = din("w2g", [NFC, 128, 2048]); w2u = din("w2u", [NFC, 128, 2048]); w2d = din("w2d", [NFC, 128, 2048])
    wh = din("wh", [4, 3, 128, 2048]); wo = din("wo", [16, 128, 2048])
    gv = din("gv", [128, 48]); goutd = din("gout", [128, 16]); qknd = din("qkn", [128, 4])
    btd = din("bt", [2, 128, NTYPES * 128]); ropeCd = din("ropeC", [32, S]); ropeSd = din("ropeS", [32, S])
    dmaskd = din("dmask", [128, 512]); identd = din("ident", [128, 128]); permd = din("perm", [32, 32])
    outT = nc.dram_tensor("outT", [D, NT], F32, kind="ExternalOutput").ap()
    dbg = {}
    if DEBUG:
        dbg["h1"] = nc.dram_tensor("dbg_h1", [D, NT], F32, kind="ExternalOutput").ap()
        dbg["modT"] = nc.dram_tensor("dbg_modT", [128, 144], F32, kind="ExternalOutput").ap()
        dbg["oT"] = nc.dram_tensor("dbg_oT", [4, 128, S], BF16, kind="ExternalOutput").ap()
        dbg["nrm"] = nc.dram_tensor("dbg_nrm", [D, NT], BF16, kind="ExternalOutput").ap()
        dbg["qT"] = nc.dram_tensor("dbg_qT", [4, 128, S], BF16, kind="ExternalOutput").ap()
        dbg["kT"] = nc.dram_tensor("dbg_kT", [4, 128, S], BF16, kind="ExternalOutput").ap()
        dbg["h2"] = nc.dram_tensor("dbg_h2", [D, NT], F32, kind="ExternalOutput").ap()

    mod_bA = nc.dram_tensor("mod_bA", [2, 1536], F32).ap()
    mod_gA = nc.dram_tensor("mod_gA", [8, 1536], F32).ap()
    mod_bB = nc.dram_tensor("mod_bB", [2, 3072], F32).ap()
    mod_gB = nc.dram_tensor("mod_gB", [8, 3072], F32).ap()
    nrm_b = nc.dram_tensor("nrm_b", [4, 128, 2048], F32).ap()
    nrm_g = nc.dram_tensor("nrm_g", [4, 512, 2048], F32).ap()
    vdram = nc.dram_tensor("vdram", [S, 128], BF16).ap()
    o_b = nc.dram_tensor("o_b", [4, 512, 512], F32).ap()
    o_g = nc.dram_tensor("o_g", [16 * 512, 512], F32).ap()

    es = ExitStack()
    ARENA = 212480
    es.enter_context(nc.sbuf_tensor("arena", [128, ARENA + 64], mybir.dt.uint8))
    base0 = (nc.sbuf_base - (ARENA + 64) + 31) // 32 * 32
    cur = [base0]

    def salloc(name, shape, dt, at=None):
        nbytes = int(np.prod(shape[1:])) * (4 if dt == F32 else 2)
        nbytes = (nbytes + 31) // 32 * 32
        if at is None:
            off = cur[0]; cur[0] += nbytes
        else:
            off = at
        assert off + nbytes <= base0 + ARENA, (name, off + nbytes - base0, ARENA)
        return nc.alloc_sbuf_tensor_at(name, list(shape), dt, offset=off), off + nbytes

    h, _ = salloc("h", [128, 16, NT], F32)
    modT, _ = salloc("modT", [128, 144], F32)
    gvs, _ = salloc("gvs", [128, 48], F32)
    gouts, _ = salloc("gouts", [128, 16], F32)
    qkns, _ = salloc("qkns", [128, 4], F32)
    qknsc, _ = salloc("qknsc", [128, 4], F32)
    Avec, _ = salloc("Avec", [128, 16], F32)
    tmpA, _ = salloc("tmpA", [128, 16], F32)
    gth, _ = salloc("gth", [128, 16], F32)
    ident, _ = salloc("ident", [128, 128], BF16)
    ones, _ = salloc("ones", [128, 128], BF16)
    rstd, _ = salloc("rstd", [128, NT], F32)
    sels, _ = salloc("sels", [2, 1], F32)
    scr, _ = salloc("scr", [128, 8], F32)
    R0 = cur[0]
    xn, e1 = salloc("xn", [128, 16, NT], BF16)
    wgu = []
    for i in range(6):
        t, _ = salloc(f"wgu{i}", [128, 2048], BF16); wgu.append(t)
    wdr = []
    for i in range(6):
        t, _ = salloc(f"wd{i}", [128, 2048], BF16); wdr.append(t)
    hid = []
    hid_off = cur[0]
    for i in range(3):
        t, _ = salloc(f"hid{i}", [128, 4, NT], BF16); hid.append(t)
    sgt = []
    for i in range(2):
        t, _ = salloc(f"sgt{i}", [128, 512], F32); sgt.append(t)
    sqr = []
    for i in range(2):
        t, _ = salloc(f"sqr{i}", [128, NT], BF16); sqr.append(t)
    tmpf = []
    for i in range(2):
        t, _ = salloc(f"tmpf{i}", [128, NT], F32); tmpf.append(t)
    cTf, _ = salloc("cTf", [128, 32], F32)
    csb, _ = salloc("csb", [128, 32], BF16)
    badaT, _ = salloc("badaT", [128, 144], F32)
    stg = []
    for i in range(2):
        t, _ = salloc(f"stg{i}", [2, 512], F32); stg.append(t)
    stgr = []
    for i in range(2):
        t, _ = salloc(f"stgr{i}", [2, 512], F32); stgr.append(t)
    ffn_end = cur[0]
    cur[0] = R0
    nrmr = []
    for i in range(2):
        t, _ = salloc(f"nrmr{i}", [128, 16, 256], BF16); nrmr.append(t)
    whs = []
    for i in range(3):
        t, _ = salloc(f"whs{i}", [128, 2048], BF16); whs.append(t)
    qT, _ = salloc("qT", [128, S], BF16)
    kTp, _ = salloc("kTp", [128, S + 2 * PADN], BF16)
    vS, _ = salloc("vS", [128, 32, 128], BF16)
    vD, _ = salloc("vD", [128, 33, 128], BF16)
    acc, _ = salloc("acc", [128, 2, S], F32)
    PT = []
    for i in range(2):
        t, _ = salloc(f"PT{i}", [128, 640], BF16); PT.append(t)
    Bt, _ = salloc("Bt", [128, NTYPES * 128], BF16)
    oT, _ = salloc("oT", [128, S], BF16)
    ropeC, _ = salloc("ropeCt", [32, S], BF16)
    ropeS, _ = salloc("ropeSt", [32, S], BF16)
    kD, _ = salloc("kD", [128, S + 128], BF16)
    perm, _ = salloc("perm", [32, 32], BF16)
    rtmp = []
    for i in range(2):
        t, _ = salloc(f"rtmp{i}", [32, 512], F32); rtmp.append(t)
    sqt = []
    for i in range(2):
        t, _ = salloc(f"sqt{i}", [128, 512], BF16); sqt.append(t)
    dmask, _ = salloc("dmask", [128, 512], BF16)
    att_end = cur[0]
    assert max(att_end, ffn_end) <= base0 + ARENA

    ps = [es.enter_context(nc.psum_tensor(f"ps{i}", [128, 512], F32)) for i in range(8)]
    psS = []
    engsem = {e: es.enter_context(nc.semaphore(f"sem_{e}")) for e in ("pe", "act", "dve", "pool")}
    dmasem = {}

    def dsem(key):
        if key not in dmasem:
            dmasem[key] = es.enter_context(nc.semaphore("d_" + str(key)))
        return key

    PSK = lambda b: ("ps", b)
    HK = lambda kc: [("h", kc, 0), ("h", kc, 1)]
    HALL = [("h", kc, t) for kc in range(16) for t in range(2)]

    def dma(eng, out, in_, reads, writes, key, **kw):
        dsem(key)
        return P.add(eng, lambda e: e.dma_start(out=out, in_=in_, **kw), reads, writes, dma=key)

    def castdma(out, in_, reads, writes, key):
        return dma("pool", out, in_, reads, writes, key, max_dma_last_dim=8192)

    def mm(out, lhsT, rhs, start, stop, reads, writes, **kw):
        return P.add("pe", lambda e: e.matmul(out, lhsT, rhs, start=start, stop=stop, **kw), reads, writes)

    def act(out, in_, func, reads, writes, bias=None, scale=None):
        kw = {}
        if bias is not None:
            kw["bias"] = bias
        if scale is not None:
            kw["scale"] = scale
        return P.add("act", lambda e: e.activation(out, in_, func, **kw), reads, writes)

    def tt(eng, out, in0, in1, op, reads, writes):
        return P.add(eng, lambda e: e.tensor_tensor(out, in0, in1, op), reads, writes)

    def stt(out, in0, scalar, in1, op0, op1, reads, writes):
        return P.add("dve", lambda e: e.scalar_tensor_tensor(out, in0, scalar, in1, op0, op1), reads, writes)

    def ts(eng, out, in0, s1, s2, op0, op1, reads, writes):
        return P.add(eng, lambda e: e.tensor_scalar(out, in0, s1, s2, op0, op1), reads, writes)

    def cp(eng, out, in_, reads, writes):
        return P.add(eng, lambda e: e.tensor_copy(out, in_), reads, writes)

    def memset(eng, ap, val, writes):
        return P.add(eng, lambda e: e.memset(ap, val), (), writes)

    dma("sp", h[:, :, :], xT.rearrange("(k p) t -> p k t", p=128), (), HALL, "ld_h")
    dma("sp", cTf[:, :], cT, (), ["cTf"], "ld_c")
    dma("sp", badaT[:, :], bada, (), ["badaT"], "ld_c")
    dma("sp", sels[:, :], sel, (), ["sels"], "ld_c")
    dma("sp", gvs[:, :], gv, (), ["gvs"], "ld_c")
    dma("sp", gouts[:, :], goutd, (), ["gouts"], "ld_c")
    dma("sp", qkns[:, :], qknd, (), ["qkns"], "ld_c")
    castdma(ident[:, :], identd, (), ["ident"], "ld_id")
    memset("dve", ones[:, :], 1.0, ["ones"])

    act(csb[:, :], cTf[:, :], AF.Silu, ["cTf"], ["csb"])
    wgu_i = [0]

    def next_wgu():
        i = wgu_i[0] % 6; wgu_i[0] += 1
        return i

    stg_i = [0]; stgr_i = [0]

    def ada_chunk(c, c0, bank, dst, dkey):
        sl = next_wgu()
        castdma(wgu[sl][:, :], wada[c], (), [("wgu", sl)], ("wgu", sl))
        lc = c - c0
        for kc in range(16):
            mm(ps[bank][0:2, (lc % 4) * 128:(lc % 4) * 128 + 128], csb[:, kc * 2:kc * 2 + 2], wgu[sl][:, kc * 128:kc * 128 + 128],
               kc == 0, kc == 15, ["csb", ("wgu", sl)], [PSK(bank)])
        if lc % 4 == 3:
            s_ = stg_i[0] % 2; stg_i[0] += 1
            cp("dve", stg[s_][:, :], ps[bank][0:2, 0:512], [PSK(bank)], [("stg", s_)])
            dma("sp", dst[:, (lc // 4) * 512:(lc // 4) * 512 + 512], stg[s_][:, :], [("stg", s_)], [dkey], ("stgd", s_))

    def ada_gather(src, dst, skey, gkey, sem):
        dsem(sem)
        P.add("pool", lambda e: e.collective_compute("AllGather", ALU.bypass, replica_groups=[[0, 1, 2, 3], [4, 5, 6, 7]], dma_qos=CC_QOS,
                                                     ins=[src.opt()], outs=[dst.opt()]),
              [skey], [gkey], dma=sem, inc=1)

    def ada_transposes(gathered, gkey, nchunk, gi_base, bank):
        for r in range(4):
            for grp in range(nchunk // 4):
                s_ = stgr_i[0] % 2; stgr_i[0] += 1
                dma("sp", stgr[s_][:, :], gathered[2 * r:2 * r + 2, grp * 512:grp * 512 + 512], [gkey], [("stgr", s_)], ("stgrd", s_))
                for k in range(4):
                    gi = gi_base + r * nchunk + grp * 4 + k
                    mm(ps[bank][:, gi:gi + 1], stgr[s_][:, k * 128:k * 128 + 128], sels[:, :], True, True,
                       [("stgr", s_), "sels"], [PSK(bank)])
        lo, hi = gi_base, gi_base + 4 * nchunk
        tt("dve", modT[:, lo:hi], ps[bank][:, lo:hi], badaT[:, lo:hi], ALU.add, [PSK(bank), "badaT"], ["modT"])

    for c in range(12):
        ada_chunk(c, 0, 0, mod_bA, "mod_bA")
    ada_gather(mod_bA, mod_gA, "mod_bA", "mod_gA", "cc_modA")
    ada_transposes(mod_gA, "mod_gA", 12, 0, 1)
    adaB = {fc: (lambda c=12 + fc: ada_chunk(c, 12, 7, mod_bB, "mod_bB")) for fc in range(24)}
    adaB[24] = lambda: ada_gather(mod_bB, mod_gB, "mod_bB", "mod_gB", "cc_modB")
    if LITE:
        for fc in range(25):
            adaB[fc]()
        ada_transposes(mod_gB, "mod_gB", 24, 48, 7)
    if DEBUG:
        dma("sp", dbg["modT"], modT[:, :], ["modT"], [], "dbg")
    cp("dve", qknsc[:, :], qkns[:, :], ["qkns"], ["qknsc"])
    ts("dve", qknsc[:, 0:1], qkns[:, 0:1], SCALE, None, ALU.mult, ALU.bypass, ["qkns", "qknsc"], ["qknsc"])
    ts("dve", qknsc[:, 2:3], qkns[:, 2:3], SCALE, None, ALU.mult, ALU.bypass, ["qkns", "qknsc"], ["qknsc"])

    MS = lambda m, kc: modT[:, m * 16 + kc:m * 16 + kc + 1]

    sq_i = [0]; tf_i = [0]

    xnB = xn[:, :, :].rearrange("p k t -> p (k t)").rearrange("p (b k t) -> p b k t", b=4, k=16)

    def norm_mod(gidx, m_sh, m_sc, blockmajor=False):
        ts("dve", tmpA[:, :], modT[:, m_sc * 16:m_sc * 16 + 16], 1.0, None, ALU.add, ALU.bypass, ["modT"], ["tmpA"])
        tt("dve", Avec[:, :], tmpA[:, :], gvs[:, gidx * 16:gidx * 16 + 16], ALU.mult, ["tmpA", "gvs"], ["Avec"])
        for kc in range(16):
            s = sq_i[0] % 2; sq_i[0] += 1
            if kc % 2 == 0:
                act(sqr[s][:, :], h[:, kc, :], AF.Square, HK(kc), [("sqr", s)])
            else:
                tt("dve", sqr[s][:, :], h[:, kc, :], h[:, kc, :], ALU.mult, HK(kc), [("sqr", s)])
            for t in range(2):
                mm(ps[2 + t][:, :], ones[:, :], sqr[s][:, t * 512:t * 512 + 512], kc == 0, kc == 15,
                   ["ones", ("sqr", s)], [PSK(2 + t)])
        for t in range(2):
            act(rstd[:, t * 512:t * 512 + 512], ps[2 + t][:, :], AF.Ln, [PSK(2 + t)], ["rstd"], bias=EPS, scale=1.0 / D)
        act(rstd[:, :], rstd[:, :], AF.Exp, ["rstd"], ["rstd"], scale=-0.5)
        for kc in range(16):
            s = tf_i[0] % 2; tf_i[0] += 1
            stt(tmpf[s][:, :], h[:, kc, :], Avec[:, kc:kc + 1], rstd[:, :], ALU.mult, ALU.mult,
                HK(kc) + ["Avec", "rstd"], [("tmpf", s)])
            if blockmajor:
                act(xnB[:, :, kc, :], tmpf[s][:, :].rearrange("p (b t) -> p b t", b=4), AF.Identity,
                    [("tmpf", s), "modT"], [("xn", kc)], bias=MS(m_sh, kc), scale=1.0)
            else:
                act(xn[:, kc, :], tmpf[s][:, :], AF.Identity, [("tmpf", s), "modT"], [("xn", kc)], bias=MS(m_sh, kc), scale=1.0)

    wd_i = [0]
    pd_i = [0]

    def down_group(wsrc, fc0, rhs_fn, rhs_keys, scal):
        slots = []
        for fl in range(4):
            sl = wd_i[0] % 6; wd_i[0] += 1
            castdma(wdr[sl][:, :], wsrc[fc0 + fl], (), [("wd", sl)], ("wd", sl))
            slots.append(sl)
        for dc in range(16):
            for t in range(2):
                b = 4 + pd_i[0] % 4; pd_i[0] += 1
                for fl in range(4):
                    mm(ps[b][:, :], wdr[slots[fl]][:, dc * 128:dc * 128 + 128], rhs_fn(fl, t), fl == 0, fl == 3,
                       [("wd", slots[fl])] + rhs_keys(fl), [PSK(b)])
                stt(h[:, dc, t * 512:t * 512 + 512], ps[b][:, :], scal[:, dc:dc + 1], h[:, dc, t * 512:t * 512 + 512],
                    ALU.mult, ALU.add, [PSK(b), "gth", ("h", dc, t)], [("h", dc, t)])

    def ffn(wg, wu, wd_, m_gt, side=None):
        ts("dve", gth[:, :], modT[:, m_gt * 16:m_gt * 16 + 16], 0.5, None, ALU.mult, ALU.bypass, ["modT"], ["gth"])
        gu_i = 0
        for fc in range(NFC):
            if side and fc in side:
                side[fc]()
            g = fc // 4
            hs = g % 3
            sg_ = next_wgu(); su_ = next_wgu()
            castdma(wgu[sg_][:, :], wg[fc], (), [("wgu", sg_)], ("wgu", sg_))
            castdma(wgu[su_][:, :], wu[fc], (), [("wgu", su_)], ("wgu", su_))
            for t in range(2):
                bg = (gu_i % 2) * 2; bu = bg + 1; gu_i += 1
                for kc in range(16):
                    mm(ps[bg][:, :], wgu[sg_][:, kc * 128:kc * 128 + 128], xn[:, kc, t * 512:t * 512 + 512], kc == 0, kc == 15,
                       [("wgu", sg_), ("xn", kc)], [PSK(bg)])
                for kc in range(16):
                    mm(ps[bu][:, :], wgu[su_][:, kc * 128:kc * 128 + 128], xn[:, kc, t * 512:t * 512 + 512], kc == 0, kc == 15,
                       [("wgu", su_), ("xn", kc)], [PSK(bu)])
                st = gu_i % 2
                act(sgt[st][:, :], ps[bg][:, :], AF.Silu, [PSK(bg)], [("sgt", st)])
                tt("dve", hid[hs][:, fc % 4, t * 512:t * 512 + 512], sgt[st][:, :], ps[bu][:, :], ALU.mult,
                   [("sgt", st), PSK(bu)], [("hid", hs, fc % 4)])
            if fc % 4 == 3 and g >= 1:
                gp = g - 1
                down_group(wd_, gp * 4, lambda fl, t, gp=gp: hid[gp % 3][:, fl, t * 512:t * 512 + 512],
                           lambda fl, gp=gp: [("hid", gp % 3, fl)], gth)
        gp = NFC // 4 - 1
        down_group(wd_, gp * 4, lambda fl, t, gp=gp: hid[gp % 3][:, fl, t * 512:t * 512 + 512],
                   lambda fl, gp=gp: [("hid", gp % 3, fl)], gth)

    def finish():
        dma("sp", outT.rearrange("(k p) t -> p k t", p=128), h[:, :, :], HALL, ["out"], "st_out")
        P.add("sp", lambda e: e.nop(), ["out"], [])
        P.add("sp", lambda e: e.nop(), [], [], extra_deps=list(P.last_dma.values()))
        with es:
            with nc.Block() as block:
                P.emit(nc, block, engsem, dmasem)
        return nc

    if not LITE:
        norm_mod(0, 0, 1)
        ffn(w1g, w1u, w1d, 2, side=adaB)
        ada_transposes(mod_gB, "mod_gB", 24, 48, 7)
    if DEBUG:
        dma("sp", dbg["h1"].rearrange("(k p) t -> p k t", p=128), h[:, :, :], HALL, [], "dbg")

    if STAGE == 1:
        return finish()
    norm_mod(1, 3, 4, blockmajor=True)
    if DEBUG:
        dma("sp", dbg["nrm"].rearrange("(k p) (b t) -> p b k t", p=128, b=4), xnB, [("xn", kc) for kc in range(16)], [], "dbg")
    for b4 in range(4):
        dsem("cc_nrm%d" % b4)
        dma("sp", nrm_b.bitcast(BF16)[b4], xnB[:, b4].rearrange("p k t -> p (k t)"),
            [("xn", kc) for kc in range(16)], [("nrm_b", b4)], "st_nrm")

    def nrm_gather(b4):
        P.add("pool", lambda e, b4=b4: e.collective_compute("AllGather", ALU.bypass, replica_groups=[[0, 1, 2, 3], [4, 5, 6, 7]], dma_qos=CC_QOS,
                                                            ins=[nrm_b[b4].opt()], outs=[nrm_g[b4].opt()]),
              [("nrm_b", b4)], [("nrm_g", b4)], dma="cc_nrm%d" % b4, inc=1)
    nrm_gather(0)

    if STAGE == 2:
        return finish()
    nops = {
        "pe": lambda e: e.matmul(ps[7][0:1, 0:1], sels[:, :], sels[:, :], start=True, stop=True),
        "act": lambda e: e.activation(scr[:, 0:1], scr[:, 0:1], AF.Identity),
        "dve": lambda e: e.memset(scr[:, 1:2], 0.0),
        "pool": lambda e: e.memset(scr[:, 2:3], 0.0),
        "sp": lambda e: e.nop(),
    }
    memset("dve", scr[:, :], 0.0, ["scr"])
    P.barrier(nops)

    try:
        castdma(dmask[:, :], dmaskd, (), ["dmask"], "ld_dm")
        castdma(perm[:, :], permd, (), ["perm"], "ld_dm")
        for pc in range(4):
            castdma(ropeC[:, pc * 1024:pc * 1024 + 1024], ropeCd[:, pc * 1024:pc * 1024 + 1024], (), ["ropeT"], "ld_dm")
            castdma(ropeS[:, pc * 1024:pc * 1024 + 1024], ropeSd[:, pc * 1024:pc * 1024 + 1024], (), ["ropeT"], "ld_dm")
        memset("pool", kD[:, 0:64], 0.0, ["kDpad"])
        memset("pool", kD[:, 64 + S:128 + S], 0.0, ["kDpad"])
        memset("pool", kTp[:, 0:PADN], 0.0, ["kTpad"])
        memset("pool", kTp[:, PADN + S:PADN + S + PADN], 0.0, ["kTpad"])
        memset("pool", vD[0:64, 0, :], 0.0, ["vDpad"])
        memset("pool", vD[64:128, 32, :], 0.0, ["vDpad"])
        nr_i = [0]; sq2_i = [0]; pt_i = [0]; po_i = [0]; rp_i = [0]
        ckpt(1)

        for hd_ in range(4):
            dsem("cc_o%d" % hd_)
        pending_fin = []
        for i3 in range(3):
            castdma(whs[i3][:, :], wh[0, i3], (), [("whs", i3)], ("whs", i3))
        castdma(Bt[:, :], btd[0], (), ["Bt"], "ld_bt")
        for b4 in range(1, 4):
            nrm_gather(b4)
        for hd in range(4):
            is_na = hd < 2
            if hd == 1:
                castdma(Bt[:, :], btd[hd], (), ["Bt"], "ld_bt")
            gq = qknsc[:, 0:1] if is_na else qknsc[:, 2:3]
            gk = qknsc[:, 1:2] if is_na else qknsc[:, 3:4]
            for tbi in range(16):
                b4 = tbi // 4; r = tbi % 4
                tb = r * 4 + b4
                ns = nr_i[0] % 2; nr_i[0] += 1
                dma("sp", nrmr[ns][:, :, :].rearrange("p k t -> p (k t)"), nrm_g.bitcast(BF16)[b4, r * 128:(r + 1) * 128, :],
                    [("nrm_g", b4)], [("nrmr", ns)], ("nrmr", ns))
                bq = tbi % 2
                bv = 2 + tbi % 2
                for kc in range(16):
                    mm(ps[bq][:, 0:256], whs[0][:, kc * 128:kc * 128 + 128], nrmr[ns][:, kc, :], kc == 0, kc == 15,
                       [("whs", 0), ("nrmr", ns)], [PSK(bq)])
                for kc in range(16):
                    mm(ps[bq][:, 256:512], whs[1][:, kc * 128:kc * 128 + 128], nrmr[ns][:, kc, :], kc == 0, kc == 15,
                       [("whs", 1), ("nrmr", ns)], [PSK(bq)])
                for tc in range(2):
                    for kc in range(16):
                        mm(ps[bv][:, tc * 128:tc * 128 + 128], nrmr[ns][:, kc, tc * 128:tc * 128 + 128],
                           whs[2][:, kc * 128:kc * 128 + 128], kc == 0, kc == 15, [("whs", 2), ("nrmr", ns)], [PSK(bv)])
                s2 = sq2_i[0] % 2; sq2_i[0] += 1
                act(sqt[s2][:, :], ps[bq][:, :], AF.Square, [PSK(bq)], [("sqt", s2)])
                mm(ps[6][:, :], ones[:, :], sqt[s2][:, :], True, True, ["ones", ("sqt", s2)], [PSK(6)])
                act(rstd[:, 0:512], ps[6][:, :], AF.Ln, [PSK(6)], ["rstd"], bias=EPS, scale=1.0 / 128)
                act(rstd[:, 0:512], rstd[:, 0:512], AF.Exp, ["rstd"], ["rstd"], scale=-0.5)
                tok0 = tb * 256
                stt(qT[:, tok0:tok0 + 256], ps[bq][:, 0:256], gq, rstd[:, 0:256], ALU.mult, ALU.mult,
                    [PSK(bq), "qknsc", "rstd"], ["qT"])
                stt(kTp[:, PADN + tok0:PADN + tok0 + 256], ps[bq][:, 256:512], gk, rstd[:, 256:512], ALU.mult, ALU.mult,
                    [PSK(bq), "qknsc", "rstd"], ["kT"])
                act(vS[:, tb * 2:tb * 2 + 2, :], ps[bv][:, 0:256].rearrange("p (c d) -> p c d", c=2), AF.Identity, [PSK(bv)], ["vS"])
                if pending_fin and tbi % 2 == 1:
                    pending_fin.pop(0)()
            if hd < 3:
                for i3 in range(3):
                    castdma(whs[i3][:, :], wh[hd + 1, i3], (), [("whs", i3)], ("whs", i3))
            ckpt(2 if hd == 0 else (5 if hd == 2 else -1))
            if not is_na:
                for which, buf, off, key in ((0, qT, 0, "qT"), (1, kTp, PADN, "kT")):
                    for pc in range(8):
                        cc0 = pc * 512
                        x0 = buf[0:32, off + cc0:off + cc0 + 512]
                        rb = 6 + pc % 2
                        mm(ps[rb][0:32, :], perm[:, :], x0, True, True, ["perm", key], [PSK(rb)])
                        tt("dve", rtmp[0][:, :], x0, ropeC[:, cc0:cc0 + 512], ALU.mult, [key, "ropeT"], [("rtmp", 0)])
                        tt("dve", rtmp[1][:, :], ps[rb][0:32, :], ropeS[:, cc0:cc0 + 512], ALU.mult, [PSK(rb), "ropeT"], [("rtmp", 1)])
                        tt("dve", x0, rtmp[0][:, :], rtmp[1][:, :], ALU.add, [("rtmp", 0), ("rtmp", 1)], [key])
            if hd == 2:
                ckpt(6)
            if DEBUG:
                dma("sp", dbg["qT"][hd], qT[:, :], ["qT"], [], "dbg")
                dma("sp", dbg["kT"][hd], kTp[:, PADN:PADN + S], ["kT"], [], "dbg")

            def tile_S(tl):
                if tl.get("pre_S"):
                    tl["pre_S"]()
                chunks = tl["chunks"]; qap = tl["q"]
                nch = len(chunks)
                sb = (pt_i[0] % 2) * 2
                pti = pt_i[0] % 2; pt_i[0] += 1
                for ci, kap in enumerate(chunks):
                    b = sb + ci // 4
                    col = (ci % 4) * 128
                    mm(ps[b][:, col:col + 128], kap, qap, True, False, ["kT", "kTpad", "kDpad", "qT"] + tl.get("kkeys", []), [PSK(b)])
                    mm(ps[b][:, col:col + 128], ident[:, :], tl["mask"](ci), False, True, ["ident"] + tl["mkeys"], [PSK(b)])
                n1 = min(nch, 4) * 128
                act(PT[pti][:, 0:n1], ps[sb][:, 0:n1], AF.Exp, [PSK(sb)], [("PT", pti)])
                if nch > 4:
                    n2 = (nch - 4) * 128
                    act(PT[pti][:, 512:512 + n2], ps[sb + 1][:, 0:n2], AF.Exp, [PSK(sb + 1)], [("PT", pti)])
                return pti, nch

            def tile_PV(tl, st):
                if tl.get("pre_PV"):
                    tl["pre_PV"]()
                pti, nch = st
                bo = 4 + po_i[0] % 2; po_i[0] += 1
                for ci in range(nch):
                    mm(ps[bo][:, 0:128], tl["v"](ci), PT[pti][:, ci * 128:ci * 128 + 128], ci == 0, ci == nch - 1,
                       tl["vkeys"] + [("PT", pti)], [PSK(bo)])
                for ci in range(nch):
                    mm(ps[bo][:, 128:256], ones[:, :], PT[pti][:, ci * 128:ci * 128 + 128], ci == 0, ci == nch - 1,
                       ["ones", ("PT", pti)], [PSK(bo)])
                src = ps[bo][:, 0:256].rearrange("p (w q) -> p w q", w=2)
                if tl["first"]:
                    cp("dve", tl["acc"], src, [PSK(bo)], ["acc_all"])
                else:
                    tt("dve", tl["acc"], src, tl["acc"], ALU.add, [PSK(bo)], ["acc_all"])

            tiles = []
            if is_na:
                for n in range(32):
                    lst = NA_TILES[n]
                    tiles.append(dict(
                        chunks=[kTp[:, PADN + c * 128:PADN + c * 128 + 128] for c, _ in lst],
                        q=qT[:, n * 128:n * 128 + 128],
                        v=(lambda ci, lst=lst: vS[:, lst[ci][0], :]), vkeys=["vS"],
                        mask=(lambda ci, lst=lst: Bt[:, lst[ci][1] * 128:lst[ci][1] * 128 + 128]), mkeys=["Bt"],
                        acc=acc[:, :, n * 128:n * 128 + 128], first=True))
            else:
                vdv = vdram.rearrange("(c p) d -> p c d", p=128)
                for q4 in range(4):
                    dma("sp", vdv[:, q4 * 8:q4 * 8 + 8, :], vS[:, q4 * 8:q4 * 8 + 8, :], ["vS"], [("vdram", q4)], "st_v")
                for Dd in (1, 4, 16):
                    L = S // Dd

                    def ld_vD(Dd=Dd):
                        na_ = (S // Dd) // 128
                        Vv = vdram.rearrange("(a i r) d -> i r a d", i=128, r=Dd)
                        for g in range(4):
                            if na_ >= 8:
                                rs = slice((8 * g) // na_, (8 * g) // na_ + 1); as_ = slice((8 * g) % na_, (8 * g) % na_ + 8)
                            else:
                                rs = slice((8 * g) // na_, (8 * g) // na_ + 8 // na_); as_ = slice(0, na_)
                            na_u = as_.stop - as_.start
                            vdk = [("vdram", q4) for q4 in range(4)]
                            for ri, r_ in enumerate(range(rs.start, rs.stop)):
                                c0 = 8 * g + ri * na_u
                                dma("sp", vD[64:128, c0:c0 + na_u, :], Vv[0:64, r_, as_], vdk,
                                    [("vDh", c) for c in range(c0, c0 + na_u)], "ld_vD")
                                dma("sp", vD[0:64, c0 + 1:c0 + 1 + na_u, :], Vv[64:128, r_, as_], vdk,
                                    [("vDl", c) for c in range(c0 + 1, c0 + 1 + na_u)], "ld_vD")

                    def mk_kD(Dd=Dd):
                        srcv = kTp[:, PADN:PADN + S].rearrange("p (u r) -> p r u", r=Dd)
                        for g in range(4):
                            nr = Dd // 4
                            cp("pool", kD[:, 64 + 1024 * g:64 + 1024 * g + 1024].rearrange("p (r u) -> p r u", r=nr),
                               srcv[:, g * nr:(g + 1) * nr, :], ["kT"], [("kD", c) for c in range(8 * g, 8 * g + 9)])

                    for n in range(32):
                        m0 = 128 * n
                        rho = m0 // L; u0 = m0 % L
                        q0 = rho + Dd * u0
                        if Dd == 1:
                            chunks = [kTp[:, PADN + m0 - 64:PADN + m0 + 64], kTp[:, PADN + m0 + 64:PADN + m0 + 192]]
                        else:
                            chunks = [kD[:, m0:m0 + 128], kD[:, m0 + 128:m0 + 256]]
                        mt = [1 if u0 == 0 else 0, 3 if u0 + 128 == L else 2]
                        tiles.append(dict(
                            chunks=chunks, q=qT[:, q0:q0 + 127 * Dd + 1:Dd],
                            kkeys=([("kD", n), ("kD", n + 1)] if Dd > 1 else []),
                            v=(lambda ci, n=n: vD[:, n + ci, :]), vkeys=[("vDh", n), ("vDl", n), ("vDh", n + 1), ("vDl", n + 1), "vDpad"],
                            mask=(lambda ci, mt=mt: dmask[:, mt[ci] * 128:mt[ci] * 128 + 128]), mkeys=["dmask"],
                            acc=acc[:, :, q0:q0 + 127 * Dd + 1:Dd], first=(Dd == 1),
                            pre_S=(mk_kD if (n == 0 and Dd > 1) else None), pre_PV=(ld_vD if n == 0 else None)))
            prev = None
            for tl in tiles:
                st = tile_S(tl)
                if prev is not None:
                    tile_PV(*prev)
                prev = (tl, st)
            tile_PV(*prev)
            def make_fin(hd=hd):
                fns = []
                for pc in range(4):
                    def piece(pc=pc):
                        sl_ = slice(pc * 1024, pc * 1024 + 1024)
                        act(acc[:, 1, sl_], acc[:, 1, sl_], AF.Ln, ["acc_all"], ["acc_all"])
                        act(acc[:, 1, sl_], acc[:, 1, sl_], AF.Exp, ["acc_all"], ["acc_all"], scale=-1.0)
                        tt("dve", oT[:, sl_], acc[:, 0, sl_], acc[:, 1, sl_], ALU.mult, ["acc_all"], ["oT", "acc_all"])
                    fns.append(piece)

                def store():
                    for tq in range(4):
                        dma("sp", o_b.bitcast(BF16)[hd, tq * 128:tq * 128 + 128, :], oT[:, tq * 1024:tq * 1024 + 1024],
                            ["oT"], [("o_b", hd)], "st_o")
                    P.add("pool", lambda e: e.collective_compute("AllGather", ALU.bypass, replica_groups=[[0, 1, 2, 3], [4, 5, 6, 7]], dma_qos=CC_QOS,
                                                                 ins=[o_b[hd].opt()], outs=[o_g[hd * 2048:(hd + 1) * 2048, :].opt()]),
                          [("o_b", hd)], [("o_g", hd)], dma="cc_o%d" % hd, inc=1)
                    if DEBUG:
                        dma("sp", dbg["oT"][hd], oT[:, :], ["oT"], [], "dbg")
                fns.append(store)
                return fns
            pending_fin.extend(make_fin())
            if hd == 3:
                while pending_fin:
                    pending_fin.pop(0)()
            ckpt({0: 3, 1: 4, 2: 10, 3: 11}[hd])

    except _Stop:
        return finish()
    P.barrier(nops)

    if STAGE == 3:
        return finish()
    def ld_o(e):
        ogb = o_g.bitcast(BF16).rearrange("(h r j q) t -> h j q r t", h=4, j=4, r=4)
        pid = e.partition_id()
        j = pid % 4
        for g in range(2):
            for l in range(2):
                src = ogb[2 * g + l, bass.ds(j, 1)].rearrange("o q r t -> (o q) r t")
                e.dma_start(out=xn[:, g * 8 + l:g * 8 + 8:2, :], in_=src).then_inc(dmasem["ld_o"], 16)
        return None
    dsem("ld_o")
    P.add("pool", ld_o, [("o_g", i) for i in range(4)], [("xn", kc) for kc in range(16)], dma="ld_o", inc=64)
    for grp in range(2):
        for k8 in range(8):
            kc = grp * 8 + k8
            s = sq_i[0] % 2; sq_i[0] += 1
            if kc % 2 == 0:
                act(sqr[s][:, :], xn[:, kc, :], AF.Square, [("xn", kc)], [("sqr", s)])
            else:
                tt("dve", sqr[s][:, :], xn[:, kc, :], xn[:, kc, :], ALU.mult, [("xn", kc)], [("sqr", s)])
            for t in range(2):
                mm(ps[2 + t][:, :], ones[:, :], sqr[s][:, t * 512:t * 512 + 512], k8 == 0, k8 == 7,
                   ["ones", ("sqr", s)], [PSK(2 + t)])
        for t in range(2):
            act(rstd[:, t * 512:t * 512 + 512], ps[2 + t][:, :], AF.Ln, [PSK(2 + t)], ["rstd"], bias=EPS, scale=1.0 / 1024)
        act(rstd[:, :], rstd[:, :], AF.Exp, ["rstd"], ["rstd"], scale=-0.5)
        for k8 in range(8):
            kc = grp * 8 + k8
            stt(xn[:, kc, :], xn[:, kc, :], gouts[:, kc:kc + 1], rstd[:, :], ALU.mult, ALU.mult,
                [("xn", kc), "gouts", "rstd"], [("xn", kc)])
    cp("dve", gth[:, :], modT[:, 5 * 16:5 * 16 + 16], ["modT"], ["gth"])
    for g in range(4):
        down_group(wo, g * 4, lambda fl, t, g=g: xn[:, g * 4 + fl, t * 512:t * 512 + 512],
                   lambda fl, g=g: [("xn", g * 4 + fl)], gth)
    if DEBUG:
        dma("sp", dbg["h2"].rearrange("(k p) t -> p k t", p=128), h[:, :, :], HALL, [], "dbg")

    if not LITE:
        norm_mod(2, 6, 7)
        ffn(w2g, w2u, w2d, 8)

    return finish()


def _tile_w(W):
    K, N = W.shape
    return np.ascontiguousarray(W.reshape(K // 128, 128, N // 128, 128).transpose(2, 1, 0, 3)).reshape(N // 128, 128, (K // 128) * 128)


def _vec16(v):
    return np.ascontiguousarray(v.reshape(-1, 128).T)


_NC_CACHE = {}


def kernel(x, c, w_ada, b_ada, g_ffn1, w1_gate, w1_up, w1_down, g_mix, w_qkv, qn_na, kn_na, qn_dil, kn_dil,
           rpb_na, g_out_na, g_out_dil, w_o, g_ffn2, w2_gate, w2_up, w2_down):
    f = lambda a: np.asarray(a, dtype=np.float32)
    x = f(x); c = f(c)
    w_ada = f(w_ada)[0]; b_ada = f(b_ada)[0]
    if not LITE:
        w1g = _tile_w(f(w1_gate)[0]); w1u = _tile_w(f(w1_up)[0]); w1d = np.ascontiguousarray(f(w1_down)[0]).reshape(NFC, 128, 2048)
        w2g = _tile_w(f(w2_gate)[0]); w2u = _tile_w(f(w2_up)[0]); w2d = np.ascontiguousarray(f(w2_down)[0]).reshape(NFC, 128, 2048)
    wqkv = f(w_qkv)[0]
    wo = np.ascontiguousarray(f(w_o)[0]).reshape(16, 128, 2048)
    gv = np.concatenate([_vec16(f(g_ffn1)[0]), _vec16(f(g_mix)[0]), _vec16(f(g_ffn2)[0])], axis=1)
    gout = np.concatenate([_vec16(f(g_out_na)[0]), _vec16(f(g_out_dil)[0])], axis=1)
    qkn = np.stack([f(qn_na)[0], f(kn_na)[0], f(qn_dil)[0], f(kn_dil)[0]], axis=1)
    rpb = f(rpb_na)[0]
    cT = np.ascontiguousarray(c.T.reshape(16, 128, 2).transpose(1, 0, 2)).reshape(128, 32)
    ropeC, ropeS = rope_tabs()
    badaT_h = np.ascontiguousarray(b_ada.reshape(144, 128).T)
    dmask = dil_masks()
    ident = np.eye(128, dtype=np.float32)
    perm32 = np.zeros((32, 32), np.float32)
    perm32[(np.arange(32) + 16) % 32, np.arange(32)] = 1.0
    in_maps = []
    for core in range(8):
        b = core // 4; j = core % 4
        heads = [2 * j, 2 * j + 1, 8 + 2 * j, 9 + 2 * j]
        whl = []
        for hh in heads:
            blk = []
            for part in range(3):
                col0 = part * D + hh * 128
                blk.append(_tile_w(wqkv[:, col0:col0 + 128])[0])
            whl.append(np.stack(blk))
        m = {
            "xT": np.ascontiguousarray(x[b, j * NT:(j + 1) * NT, :].T),
            "cT": cT,
            "wada": _tile_w(np.concatenate([w_ada[:, 12 * j * 128:(12 * j + 12) * 128],
                                            w_ada[:, (48 + 24 * j) * 128:(48 + 24 * j + 24) * 128]], axis=1)),
            "bada": badaT_h,
            "sel": np.array([[1.0 - b], [float(b)]], np.float32),
            "wh": np.stack(whl), "wo": wo, "gv": gv, "gout": gout, "qkn": np.ascontiguousarray(qkn),
            "bt": np.stack([build_bias_tiles(rpb[2 * j]), build_bias_tiles(rpb[2 * j + 1])]),
            "ropeC": ropeC, "ropeS": ropeS, "dmask": dmask, "ident": ident, "perm": perm32,
        }
        if not LITE:
            m.update({"w1g": w1g, "w1u": w1u, "w1d": w1d, "w2g": w2g, "w2u": w2u, "w2d": w2d})
        in_maps.append(m)
    if "nc" not in _NC_CACHE:
        _NC_CACHE["nc"] = build_nc()
    nc = _NC_CACHE["nc"]
    res = run_bass_kernel_spmd(nc, in_maps, core_ids=list(range(8)))
    out = np.empty((2, S, D), np.float32)
    for core in range(8):
        b = core // 4; j = core % 4
        out[b, j * NT:(j + 1) * NT, :] = res.results[core]["outT"].T
    if DEBUG:
        kernel.last = res
    return out
```

```python
import numpy as np
from contextlib import ExitStack
import concourse.bass as bass
import concourse.mybir as mybir
from concourse.bass_utils import run_bass_kernel_spmd

F32 = mybir.dt.float32
BF16 = mybir.dt.bfloat16
AF = mybir.ActivationFunctionType
ALU = mybir.AluOpType

D = 2048
NT = 1024
S = 4096
DFF = 5632
NFC = 44
EPS = 1e-6
SCALE = 128.0 ** -0.5
NEGM = -30000.0
PADN = 64
DEBUG = False
STAGE = 9
ATTSTOP = 0
CC_QOS = "P2"
NRM_NP = 4
LITE = False


class Op:
    __slots__ = ("eng", "fn", "deps", "is_dma", "semkey", "inc", "cum", "targets", "signal", "seq", "idx")


class _Stop(Exception):
    pass


def ckpt(n):
    if ATTSTOP == n:
        raise _Stop()


class Prog:
    ENGS = ("pe", "act", "dve", "pool", "sp")

    def __init__(self):
        self.ops = {e: [] for e in self.ENGS}
        self.lastw = {}
        self.readers = {}
        self.dma_count = {}
        self.last_dma = {}
        self.n = 0

    def add(self, eng, fn, reads=(), writes=(), dma=None, inc=16, extra_deps=()):
        op = Op()
        op.eng = eng; op.fn = fn; op.is_dma = dma is not None; op.semkey = dma; op.inc = inc
        op.signal = False; op.seq = 0; op.idx = self.n; self.n += 1
        deps = []
        for k in reads:
            w = self.lastw.get(k)
            if w is not None:
                deps.append((w, "raw"))
        for k in writes:
            rds = self.readers.get(k, ())
            w = self.lastw.get(k)
            if w is not None and not rds:
                deps.append((w, "waw"))
            for r in rds:
                deps.append((r, "war"))
        for d in extra_deps:
            deps.append((d, "raw"))
        for k in reads:
            self.readers.setdefault(k, []).append(op)
        for k in writes:
            self.lastw[k] = op
            self.readers[k] = []
        op.deps = deps
        op.targets = {d.semkey: self.dma_count[d.semkey] for d, _ in deps if d.is_dma}
        if op.is_dma:
            self.dma_count[dma] = self.dma_count.get(dma, 0) + inc
            op.cum = self.dma_count[dma]
            self.last_dma[dma] = op
        self.ops[eng].append(op)
        return op

    def barrier(self, mk_nop):
        firsts = []
        for e in ("pe", "act", "dve", "pool"):
            firsts.append(self.add(e, mk_nop[e], writes=[("bar1", e, self.n)]))
        dmas = [op for key, op in self.last_dma.items() if not str(key).startswith("cc_")]
        for e in self.ENGS:
            self.add(e, mk_nop[e], writes=[("bar2", e, self.n)], extra_deps=firsts + dmas)

    def needs_wait(self, op, d, kind):
        if d.is_dma:
            return True
        if d.eng != op.eng:
            return True
        if op.eng == "pe":
            return False
        if op.is_dma:
            return True
        return kind == "raw"

    def finalize(self):
        for e in self.ENGS:
            for op in self.ops[e]:
                latest = {}
                for d, kind in op.deps:
                    if (not d.is_dma) and self.needs_wait(op, d, kind):
                        if d.eng not in latest or d.idx > latest[d.eng].idx:
                            latest[d.eng] = d
                for d in latest.values():
                    d.signal = True
                op.deps = [(d, k) for d, k in op.deps if d.is_dma or latest.get(d.eng) is d]
        for e in self.ENGS:
            s = 0
            for op in self.ops[e]:
                if op.signal and not op.is_dma:
                    s += 1
                    op.seq = s

    def emit(self, nc, block, engsem, dmasem):
        self.finalize()
        decos = {"pe": block.tensor, "act": block.scalar, "dve": block.vector, "pool": block.gpsimd, "sp": block.sync}
        for e in self.ENGS:
            ops = self.ops[e]

            def body(eng, ops=ops, e=e):
                waited = {}
                for op in ops:
                    need = {}
                    for d, kind in op.deps:
                        if not self.needs_wait(op, d, kind):
                            continue
                        if d.is_dma:
                            key = ("d", d.semkey); val = op.targets[d.semkey]; sem = dmasem[d.semkey]
                        else:
                            key = ("e", d.eng); val = d.seq; sem = engsem[d.eng]
                        if val > need.get(key, (0, None))[0]:
                            need[key] = (val, sem)
                    for key, (val, sem) in need.items():
                        if waited.get(key, 0) >= val:
                            continue
                        eng.wait_ge(sem, val)
                        waited[key] = val
                    inst = op.fn(eng)
                    if inst is None:
                        continue
                    if op.is_dma:
                        inst.then_inc(dmasem[op.semkey], op.inc)
                    elif op.signal:
                        inst.then_inc(engsem[e], 1)
            decos[e](body)


def na_tile_tables():
    types = {}
    tiles = []
    for n in range(32):
        rows = []
        for rq in (2 * n, 2 * n + 1):
            rs = min(max(rq - 4, 0), 56)
            rows.append((rq, rs))
        lo = min(r[1] for r in rows) // 2
        hi = (max(r[1] for r in rows) + 7) // 2
        lst = []
        for c in range(lo, hi + 1):
            key = []
            for rkl in range(2):
                for rql in range(2):
                    rk = 2 * c + rkl
                    rq, rs = rows[rql]
                    if rs <= rk < rs + 8:
                        key.append(rk - rq + 7)
                    else:
                        key.append(-1)
            key = tuple(key)
            if all(k < 0 for k in key):
                continue
            if key not in types:
                types[key] = len(types)
            lst.append((c, types[key]))
        tiles.append(lst)
    return tiles, types


NA_TILES, NA_TYPES = na_tile_tables()
NTYPES = len(NA_TYPES)


def build_bias_tiles(rpb_h):
    out = np.full((128, NTYPES, 128), NEGM, np.float32)
    ck = np.arange(64)[:, None]
    cq = np.arange(64)[None, :]
    cs = np.clip(cq - 8, 0, 48)
    cvalid = (ck >= cs) & (ck < cs + 16)
    coff = np.clip(ck - cq + 15, 0, 30)
    for key, t in NA_TYPES.items():
        i = 0
        for rkl in range(2):
            for rql in range(2):
                a = key[i]; i += 1
                if a < 0:
                    continue
                blk = np.where(cvalid, rpb_h[a][coff], np.float32(NEGM))
                out[64 * rkl:64 * rkl + 64, t, 64 * rql:64 * rql + 64] = blk
    return out.reshape(128, NTYPES * 128)


def dil_masks():
    i = np.arange(128)[:, None]
    j = np.arange(128)[None, :]
    A = np.where(j <= i, 0.0, NEGM)
    B = np.where(j >= i, 0.0, NEGM)
    A1 = A.copy(); A1[:64, :] = NEGM
    B1 = B.copy(); B1[64:, :] = NEGM
    return np.concatenate([A, A1, B, B1], axis=1).astype(np.float32)


def rope_tabs():
    pos = np.arange(S, dtype=np.float32)
    inv = np.power(np.float32(500000.0), -np.arange(0, 32, 2, dtype=np.float32) / np.float32(32)).astype(np.float32)
    ang = (pos[None, :] * inv[:, None]).astype(np.float32)
    c = np.cos(ang).astype(np.float32); s = np.sin(ang).astype(np.float32)
    C = np.concatenate([c, c], axis=0)
    Sg = np.concatenate([-s, s], axis=0)
    return C, Sg


def build_nc():
    nc = bass.Bass("TRN2", target_bir_lowering=False)
    P = Prog()

    def din(name, shape, dt=F32):
        return nc.dram_tensor(name, list(shape), dt, kind="ExternalInput").ap()

    xT = din("xT", [D, NT]); cT = din("cT", [128, 32]); wada = din("wada", [36, 128, 2048]); bada = din("bada", [128, 144])
    sel = din("sel", [2, 1])
    if not LITE:
        w1g = din("w1g", [NFC, 128, 2048]); w1u = din("w1u", [NFC, 128, 2048]); w1d = din("w1d", [NFC, 128, 2048])
        w2g = din("w2g", [NFC, 128, 2048]); w2u = din("w2u", [NFC, 128, 2048]); w2d = din("w2d", [NFC, 128, 2048])
    wh = din("wh", [4, 3, 128, 2048]); wo = din("wo", [16, 128, 2048])
    gv = din("gv", [128, 48]); goutd = din("gout", [128, 16]); qknd = din("qkn", [128, 4])
    btd = din("bt", [2, 128, NTYPES * 128]); ropeCd = din("ropeC", [32, S]); ropeSd = din("ropeS", [32, S])
    dmaskd = din("dmask", [128, 512]); identd = din("ident", [128, 128]); permd = din("perm", [32, 32])
    outT = nc.dram_tensor("outT", [D, NT], F32, kind="ExternalOutput").ap()
    dbg = {}
    if DEBUG:
        dbg["h1"] = nc.dram_tensor("dbg_h1", [D, NT], F32, kind="ExternalOutput").ap()
        dbg["modT"] = nc.dram_tensor("dbg_modT", [128, 144], F32, kind="ExternalOutput").ap()
        dbg["oT"] = nc.dram_tensor("dbg_oT", [4, 128, S], BF16, kind="ExternalOutput").ap()
        dbg["nrm"] = nc.dram_tensor("dbg_nrm", [D, NT], BF16, kind="ExternalOutput").ap()
        dbg["qT"] = nc.dram_tensor("dbg_qT", [4, 128, S], BF16, kind="ExternalOutput").ap()
        dbg["kT"] = nc.dram_tensor("dbg_kT", [4, 128, S], BF16, kind="ExternalOutput").ap()
        dbg["h2"] = nc.dram_tensor("dbg_h2", [D, NT], F32, kind="ExternalOutput").ap()

    mod_bA = nc.dram_tensor("mod_bA", [2, 1536], F32).ap()
    mod_gA = nc.dram_tensor("mod_gA", [8, 1536], F32).ap()
    mod_bB = nc.dram_tensor("mod_bB", [2, 3072], F32).ap()
    mod_gB = nc.dram_tensor("mod_gB", [8, 3072], F32).ap()
    nrm_b = nc.dram_tensor("nrm_b", [4, 128, 2048], F32).ap()
    nrm_g = nc.dram_tensor("nrm_g", [4, 512, 2048], F32).ap()
    vdram = nc.dram_tensor("vdram", [S, 128], BF16).ap()
    o_b = nc.dram_tensor("o_b", [4, 512, 512], F32).ap()
    o_g = nc.dram_tensor("o_g", [16 * 512, 512], F32).ap()

    es = ExitStack()
    ARENA = 212480
    es.enter_context(nc.sbuf_tensor("arena", [128, ARENA + 64], mybir.dt.uint8))
    base0 = (nc.sbuf_base - (ARENA + 64) + 31) // 32 * 32
    cur = [base0]

    def salloc(name, shape, dt, at=None):
        nbytes = int(np.prod(shape[1:])) * (4 if dt == F32 else 2)
        nbytes = (nbytes + 31) // 32 * 32
        if at is None:
            off = cur[0]; cur[0] += nbytes
        else:
            off = at
        assert off + nbytes <= base0 + ARENA, (name, off + nbytes - base0, ARENA)
        return nc.alloc_sbuf_tensor_at(name, list(shape), dt, offset=off), off + nbytes

    h, _ = salloc("h", [128, 16, NT], F32)
    modT, _ = salloc("modT", [128, 144], F32)
    gvs, _ = salloc("gvs", [128, 48], F32)
    gouts, _ = salloc("gouts", [128, 16], F32)
    qkns, _ = salloc("qkns", [128, 4], F32)
    qknsc, _ = salloc("qknsc", [128, 4], F32)
    Avec, _ = salloc("Avec", [128, 16], F32)
    tmpA, _ = salloc("tmpA", [128, 16], F32)
    gth, _ = salloc("gth", [128, 16], F32)
    ident, _ = salloc("ident", [128, 128], BF16)
    ones, _ = salloc("ones", [128, 128], BF16)
    rstd, _ = salloc("rstd", [128, NT], F32)
    sels, _ = salloc("sels", [2, 1], F32)
    scr, _ = salloc("scr", [128, 8], F32)
    R0 = cur[0]
    xn, e1 = salloc("xn", [128, 16, NT], BF16)
    wgu = []
    for i in range(6):
        t, _ = salloc(f"wgu{i}", [128, 2048], BF16); wgu.append(t)
    wdr = []
    for i in range(6):
        t, _ = salloc(f"wd{i}", [128, 2048], BF16); wdr.append(t)
    hid = []
    hid_off = cur[0]
    for i in range(3):
        t, _ = salloc(f"hid{i}", [128, 4, NT], BF16); hid.append(t)
    sgt = []
    for i in range(2):
        t, _ = salloc(f"sgt{i}", [128, 512], F32); sgt.append(t)
    sqr = []
    for i in range(2):
        t, _ = salloc(f"sqr{i}", [128, NT], BF16); sqr.append(t)
    tmpf = []
    for i in range(2):
        t, _ = salloc(f"tmpf{i}", [128, NT], F32); tmpf.append(t)
    cTf, _ = salloc("cTf", [128, 32], F32)
    csb, _ = salloc("csb", [128, 32], BF16)
    badaT, _ = salloc("badaT", [128, 144], F32)
    stg = []
    for i in range(2):
        t, _ = salloc(f"stg{i}", [2, 512], F32); stg.append(t)
    stgr = []
    for i in range(2):
        t, _ = salloc(f"stgr{i}", [2, 512], F32); stgr.append(t)
    ffn_end = cur[0]
    cur[0] = R0
    nrmr = []
    for i in range(2):
        t, _ = salloc(f"nrmr{i}", [128, 16, 256], BF16); nrmr.append(t)
    whs = []
    for i in range(3):
        t, _ = salloc(f"whs{i}", [128, 2048], BF16); whs.append(t)
    qT, _ = salloc("qT", [128, S], BF16)
    kTp, _ = salloc("kTp", [128, S + 2 * PADN], BF16)
    vS, _ = salloc("vS", [128, 32, 128], BF16)
    vD, _ = salloc("vD", [128, 33, 128], BF16)
    acc, _ = salloc("acc", [128, 2, S], F32)
    PT = []
    for i in range(2):
        t, _ = salloc(f"PT{i}", [128, 640], BF16); PT.append(t)
    Bt, _ = salloc("Bt", [128, NTYPES * 128], BF16)
    oT, _ = salloc("oT", [128, S], BF16)
    ropeC, _ = salloc("ropeCt", [32, S], BF16)
    ropeS, _ = salloc("ropeSt", [32, S], BF16)
    kD, _ = salloc("kD", [128, S + 128], BF16)
    perm, _ = salloc("perm", [32, 32], BF16)
    rtmp = []
    for i in range(2):
        t, _ = salloc(f"rtmp{i}", [32, 512], F32); rtmp.append(t)
    sqt = []
    for i in range(2):
        t, _ = salloc(f"sqt{i}", [128, 512], BF16); sqt.append(t)
    dmask, _ = salloc("dmask", [128, 512], BF16)
    att_end = cur[0]
    assert max(att_end, ffn_end) <= base0 + ARENA

    ps = [es.enter_context(nc.psum_tensor(f"ps{i}", [128, 512], F32)) for i in range(8)]
    psS = []
    engsem = {e: es.enter_context(nc.semaphore(f"sem_{e}")) for e in ("pe", "act", "dve", "pool")}
    dmasem = {}

    def dsem(key):
        if key not in dmasem:
            dmasem[key] = es.enter_context(nc.semaphore("d_" + str(key)))
        return key

    PSK = lambda b: ("ps", b)
    HK = lambda kc: [("h", kc, 0), ("h", kc, 1)]
    HALL = [("h", kc, t) for kc in range(16) for t in range(2)]

    def dma(eng, out, in_, reads, writes, key, **kw):
        dsem(key)
        return P.add(eng, lambda e: e.dma_start(out=out, in_=in_, **kw), reads, writes, dma=key)

    def castdma(out, in_, reads, writes, key):
        return dma("pool", out, in_, reads, writes, key, max_dma_last_dim=8192)

    def mm(out, lhsT, rhs, start, stop, reads, writes, **kw):
        return P.add("pe", lambda e: e.matmul(out, lhsT, rhs, start=start, stop=stop, **kw), reads, writes)

    def act(out, in_, func, reads, writes, bias=None, scale=None):
        kw = {}
        if bias is not None:
            kw["bias"] = bias
        if scale is not None:
            kw["scale"] = scale
        return P.add("act", lambda e: e.activation(out, in_, func, **kw), reads, writes)

    def tt(eng, out, in0, in1, op, reads, writes):
        return P.add(eng, lambda e: e.tensor_tensor(out, in0, in1, op), reads, writes)

    def stt(out, in0, scalar, in1, op0, op1, reads, writes):
        return P.add("dve", lambda e: e.scalar_tensor_tensor(out, in0, scalar, in1, op0, op1), reads, writes)

    def ts(eng, out, in0, s1, s2, op0, op1, reads, writes):
        return P.add(eng, lambda e: e.tensor_scalar(out, in0, s1, s2, op0, op1), reads, writes)

    def cp(eng, out, in_, reads, writes):
        return P.add(eng, lambda e: e.tensor_copy(out, in_), reads, writes)

    def memset(eng, ap, val, writes):
        return P.add(eng, lambda e: e.memset(ap, val), (), writes)

    dma("sp", h[:, :, :], xT.rearrange("(k p) t -> p k t", p=128), (), HALL, "ld_h")
    dma("sp", cTf[:, :], cT, (), ["cTf"], "ld_c")
    dma("sp", badaT[:, :], bada, (), ["badaT"], "ld_c")
    dma("sp", sels[:, :], sel, (), ["sels"], "ld_c")
    dma("sp", gvs[:, :], gv, (), ["gvs"], "ld_c")
    dma("sp", gouts[:, :], goutd, (), ["gouts"], "ld_c")
    dma("sp", qkns[:, :], qknd, (), ["qkns"], "ld_c")
    castdma(ident[:, :], identd, (), ["ident"], "ld_id")
    memset("dve", ones[:, :], 1.0, ["ones"])

    act(csb[:, :], cTf[:, :], AF.Silu, ["cTf"], ["csb"])
    wgu_i = [0]

    def next_wgu():
        i = wgu_i[0] % 6; wgu_i[0] += 1
        return i

    stg_i = [0]; stgr_i = [0]

    def ada_chunk(c, c0, bank, dst, dkey):
        sl = next_wgu()
        castdma(wgu[sl][:, :], wada[c], (), [("wgu", sl)], ("wgu", sl))
        lc = c - c0
        for kc in range(16):
            mm(ps[bank][0:2, (lc % 4) * 128:(lc % 4) * 128 + 128], csb[:, kc * 2:kc * 2 + 2], wgu[sl][:, kc * 128:kc * 128 + 128],
               kc == 0, kc == 15, ["csb", ("wgu", sl)], [PSK(bank)])
        if lc % 4 == 3:
            s_ = stg_i[0] % 2; stg_i[0] += 1
            cp("dve", stg[s_][:, :], ps[bank][0:2, 0:512], [PSK(bank)], [("stg", s_)])
            dma("sp", dst[:, (lc // 4) * 512:(lc // 4) * 512 + 512], stg[s_][:, :], [("stg", s_)], [dkey], ("stgd", s_))

    def ada_gather(src, dst, skey, gkey, sem):
        dsem(sem)
        P.add("pool", lambda e: e.collective_compute("AllGather", ALU.bypass, replica_groups=[[0, 1, 2, 3], [4, 5, 6, 7]], dma_qos=CC_QOS,
                                                     ins=[src.opt()], outs=[dst.opt()]),
              [skey], [gkey], dma=sem, inc=1)

    def ada_transposes(gathered, gkey, nchunk, gi_base, bank):
        for r in range(4):
            for grp in range(nchunk // 4):
                s_ = stgr_i[0] % 2; stgr_i[0] += 1
                dma("sp", stgr[s_][:, :], gathered[2 * r:2 * r + 2, grp * 512:grp * 512 + 512], [gkey], [("stgr", s_)], ("stgrd", s_))
                for k in range(4):
                    gi = gi_base + r * nchunk + grp * 4 + k
                    mm(ps[bank][:, gi:gi + 1], stgr[s_][:, k * 128:k * 128 + 128], sels[:, :], True, True,
                       [("stgr", s_), "sels"], [PSK(bank)])
        lo, hi = gi_base, gi_base + 4 * nchunk
        tt("dve", modT[:, lo:hi], ps[bank][:, lo:hi], badaT[:, lo:hi], ALU.add, [PSK(bank), "badaT"], ["modT"])

    for c in range(12):
        ada_chunk(c, 0, 0, mod_bA, "mod_bA")
    ada_gather(mod_bA, mod_gA, "mod_bA", "mod_gA", "cc_modA")
    ada_transposes(mod_gA, "mod_gA", 12, 0, 1)
    adaB = {fc: (lambda c=12 + fc: ada_chunk(c, 12, 7, mod_bB, "mod_bB")) for fc in range(24)}
    adaB[24] = lambda: ada_gather(mod_bB, mod_gB, "mod_bB", "mod_gB", "cc_modB")
    if LITE:
        for fc in range(25):
            adaB[fc]()
        ada_transposes(mod_gB, "mod_gB", 24, 48, 7)
    if DEBUG:
        dma("sp", dbg["modT"], modT[:, :], ["modT"], [], "dbg")
    cp("dve", qknsc[:, :], qkns[:, :], ["qkns"], ["qknsc"])
    ts("dve", qknsc[:, 0:1], qkns[:, 0:1], SCALE, None, ALU.mult, ALU.bypass, ["qkns", "qknsc"], ["qknsc"])
    ts("dve", qknsc[:, 2:3], qkns[:, 2:3], SCALE, None, ALU.mult, ALU.bypass, ["qkns", "qknsc"], ["qknsc"])

    MS = lambda m, kc: modT[:, m * 16 + kc:m * 16 + kc + 1]

    sq_i = [0]; tf_i = [0]

    xnB = xn[:, :, :].rearrange("p k t -> p (k t)").rearrange("p (b k t) -> p b k t", b=4, k=16)

    def norm_mod(gidx, m_sh, m_sc, blockmajor=False):
        ts("dve", tmpA[:, :], modT[:, m_sc * 16:m_sc * 16 + 16], 1.0, None, ALU.add, ALU.bypass, ["modT"], ["tmpA"])
        tt("dve", Avec[:, :], tmpA[:, :], gvs[:, gidx * 16:gidx * 16 + 16], ALU.mult, ["tmpA", "gvs"], ["Avec"])
        for kc in range(16):
            s = sq_i[0] % 2; sq_i[0] += 1
            if kc % 2 == 0:
                act(sqr[s][:, :], h[:, kc, :], AF.Square, HK(kc), [("sqr", s)])
            else:
                tt("dve", sqr[s][:, :], h[:, kc, :], h[:, kc, :], ALU.mult, HK(kc), [("sqr", s)])
            for t in range(2):
                mm(ps[2 + t][:, :], ones[:, :], sqr[s][:, t * 512:t * 512 + 512], kc == 0, kc == 15,
                   ["ones", ("sqr", s)], [PSK(2 + t)])
        for t in range(2):
            act(rstd[:, t * 512:t * 512 + 512], ps[2 + t][:, :], AF.Ln, [PSK(2 + t)], ["rstd"], bias=EPS, scale=1.0 / D)
        act(rstd[:, :], rstd[:, :], AF.Exp, ["rstd"], ["rstd"], scale=-0.5)
        for kc in range(16):
            s = tf_i[0] % 2; tf_i[0] += 1
            stt(tmpf[s][:, :], h[:, kc, :], Avec[:, kc:kc + 1], rstd[:, :], ALU.mult, ALU.mult,
                HK(kc) + ["Avec", "rstd"], [("tmpf", s)])
            if blockmajor:
                act(xnB[:, :, kc, :], tmpf[s][:, :].rearrange("p (b t) -> p b t", b=4), AF.Identity,
                    [("tmpf", s), "modT"], [("xn", kc)], bias=MS(m_sh, kc), scale=1.0)
            else:
                act(xn[:, kc, :], tmpf[s][:, :], AF.Identity, [("tmpf", s), "modT"], [("xn", kc)], bias=MS(m_sh, kc), scale=1.0)

    wd_i = [0]
    pd_i = [0]

    def down_group(wsrc, fc0, rhs_fn, rhs_keys, scal):
        slots = []
        for fl in range(4):
            sl = wd_i[0] % 6; wd_i[0] += 1
            castdma(wdr[sl][:, :], wsrc[fc0 + fl], (), [("wd", sl)], ("wd", sl))
            slots.append(sl)
        for dc in range(16):
            for t in range(2):
                b = 4 + pd_i[0] % 4; pd_i[0] += 1
                for fl in range(4):
                    mm(ps[b][:, :], wdr[slots[fl]][:, dc * 128:dc * 128 + 128], rhs_fn(fl, t), fl == 0, fl == 3,
                       [("wd", slots[fl])] + rhs_keys(fl), [PSK(b)])
                stt(h[:, dc, t * 512:t * 512 + 512], ps[b][:, :], scal[:, dc:dc + 1], h[:, dc, t * 512:t * 512 + 512],
                    ALU.mult, ALU.add, [PSK(b), "gth", ("h", dc, t)], [("h", dc, t)])

    def ffn(wg, wu, wd_, m_gt, side=None):
        ts("dve", gth[:, :], modT[:, m_gt * 16:m_gt * 16 + 16], 0.5, None, ALU.mult, ALU.bypass, ["modT"], ["gth"])
        gu_i = 0
        for fc in range(NFC):
            if side and fc in side:
                side[fc]()
            g = fc // 4
            hs = g % 3
            sg_ = next_wgu(); su_ = next_wgu()
            castdma(wgu[sg_][:, :], wg[fc], (), [("wgu", sg_)], ("wgu", sg_))
            castdma(wgu[su_][:, :], wu[fc], (), [("wgu", su_)], ("wgu", su_))
            for t in range(2):
                bg = (gu_i % 2) * 2; bu = bg + 1; gu_i += 1
                for kc in range(16):
                    mm(ps[bg][:, :], wgu[sg_][:, kc * 128:kc * 128 + 128], xn[:, kc, t * 512:t * 512 + 512], kc == 0, kc == 15,
                       [("wgu", sg_), ("xn", kc)], [PSK(bg)])
                for kc in range(16):
                    mm(ps[bu][:, :], wgu[su_][:, kc * 128:kc * 128 + 128], xn[:, kc, t * 512:t * 512 + 512], kc == 0, kc == 15,
                       [("wgu", su_), ("xn", kc)], [PSK(bu)])
                st = gu_i % 2
                act(sgt[st][:, :], ps[bg][:, :], AF.Silu, [PSK(bg)], [("sgt", st)])
                tt("dve", hid[hs][:, fc % 4, t * 512:t * 512 + 512], sgt[st][:, :], ps[bu][:, :], ALU.mult,
                   [("sgt", st), PSK(bu)], [("hid", hs, fc % 4)])
            if fc % 4 == 3 and g >= 1:
                gp = g - 1
                down_group(wd_, gp * 4, lambda fl, t, gp=gp: hid[gp % 3][:, fl, t * 512:t * 512 + 512],
                           lambda fl, gp=gp: [("hid", gp % 3, fl)], gth)
        gp = NFC // 4 - 1
        down_group(wd_, gp * 4, lambda fl, t, gp=gp: hid[gp % 3][:, fl, t * 512:t * 512 + 512],
                   lambda fl, gp=gp: [("hid", gp % 3, fl)], gth)

    def finish():
        dma("sp", outT.rearrange("(k p) t -> p k t", p=128), h[:, :, :], HALL, ["out"], "st_out")
        P.add("sp", lambda e: e.nop(), ["out"], [])
        P.add("sp", lambda e: e.nop(), [], [], extra_deps=list(P.last_dma.values()))
        with es:
            with nc.Block() as block:
                P.emit(nc, block, engsem, dmasem)
        return nc

    if not LITE:
        norm_mod(0, 0, 1)
        ffn(w1g, w1u, w1d, 2, side=adaB)
        ada_transposes(mod_gB, "mod_gB", 24, 48, 7)
    if DEBUG:
        dma("sp", dbg["h1"].rearrange("(k p) t -> p k t", p=128), h[:, :, :], HALL, [], "dbg")

    if STAGE == 1:
        return finish()
    norm_mod(1, 3, 4, blockmajor=True)
    if DEBUG:
        dma("sp", dbg["nrm"].rearrange("(k p) (b t) -> p b k t", p=128, b=4), xnB, [("xn", kc) for kc in range(16)], [], "dbg")
    for b4 in range(4):
        dsem("cc_nrm%d" % b4)
        dma("sp", nrm_b.bitcast(BF16)[b4], xnB[:, b4].rearrange("p k t -> p (k t)"),
            [("xn", kc) for kc in range(16)], [("nrm_b", b4)], "st_nrm")

    def nrm_gather(b4):
        P.add("pool", lambda e, b4=b4: e.collective_compute("AllGather", ALU.bypass, replica_groups=[[0, 1, 2, 3], [4, 5, 6, 7]], dma_qos=CC_QOS,
                                                            ins=[nrm_b[b4].opt()], outs=[nrm_g[b4].opt()]),
              [("nrm_b", b4)], [("nrm_g", b4)], dma="cc_nrm%d" % b4, inc=1)
    nrm_gather(0)

    if STAGE == 2:
        return finish()
    nops = {
        "pe": lambda e: e.matmul(ps[7][0:1, 0:1], sels[:, :], sels[:, :], start=True, stop=True),
        "act": lambda e: e.activation(scr[:, 0:1], scr[:, 0:1], AF.Identity),
        "dve": lambda e: e.memset(scr[:, 1:2], 0.0),
        "pool": lambda e: e.memset(scr[:, 2:3], 0.0),
        "sp": lambda e: e.nop(),
    }
    memset("dve", scr[:, :], 0.0, ["scr"])
    P.barrier(nops)

    try:
        castdma(dmask[:, :], dmaskd, (), ["dmask"], "ld_dm")
        castdma(perm[:, :], permd, (), ["perm"], "ld_dm")
        for pc in range(4):
            castdma(ropeC[:, pc * 1024:pc * 1024 + 1024], ropeCd[:, pc * 1024:pc * 1024 + 1024], (), ["ropeT"], "ld_dm")
            castdma(ropeS[:, pc * 1024:pc * 1024 + 1024], ropeSd[:, pc * 1024:pc * 1024 + 1024], (), ["ropeT"], "ld_dm")
        memset("pool", kD[:, 0:64], 0.0, ["kDpad"])
        memset("pool", kD[:, 64 + S:128 + S], 0.0, ["kDpad"])
        memset("pool", kTp[:, 0:PADN], 0.0, ["kTpad"])
        memset("pool", kTp[:, PADN + S:PADN + S + PADN], 0.0, ["kTpad"])
        memset("pool", vD[0:64, 0, :], 0.0, ["vDpad"])
        memset("pool", vD[64:128, 32, :], 0.0, ["vDpad"])
        nr_i = [0]; sq2_i = [0]; pt_i = [0]; po_i = [0]; rp_i = [0]
        ckpt(1)

        for hd_ in range(4):
            dsem("cc_o%d" % hd_)
        pending_fin = []
        for i3 in range(3):
            castdma(whs[i3][:, :], wh[0, i3], (), [("whs", i3)], ("whs", i3))
        castdma(Bt[:, :], btd[0], (), ["Bt"], "ld_bt")
        for b4 in range(1, 4):
            nrm_gather(b4)
        for hd in range(4):
            is_na = hd < 2
            if hd == 1:
                castdma(Bt[:, :], btd[hd], (), ["Bt"], "ld_bt")
            gq = qknsc[:, 0:1] if is_na else qknsc[:, 2:3]
            gk = qknsc[:, 1:2] if is_na else qknsc[:, 3:4]
            for tbi in range(16):
                b4 = tbi // 4; r = tbi % 4
                tb = r * 4 + b4
                ns = nr_i[0] % 2; nr_i[0] += 1
                dma("sp", nrmr[ns][:, :, :].rearrange("p k t -> p (k t)"), nrm_g.bitcast(BF16)[b4, r * 128:(r + 1) * 128, :],
                    [("nrm_g", b4)], [("nrmr", ns)], ("nrmr", ns))
                bq = tbi % 2
                bv = 2 + tbi % 2
                for kc in range(16):
                    mm(ps[bq][:, 0:256], whs[0][:, kc * 128:kc * 128 + 128], nrmr[ns][:, kc, :], kc == 0, kc == 15,
                       [("whs", 0), ("nrmr", ns)], [PSK(bq)])
                for kc in range(16):
                    mm(ps[bq][:, 256:512], whs[1][:, kc * 128:kc * 128 + 128], nrmr[ns][:, kc, :], kc == 0, kc == 15,
                       [("whs", 1), ("nrmr", ns)], [PSK(bq)])
                for tc in range(2):
                    for kc in range(16):
                        mm(ps[bv][:, tc * 128:tc * 128 + 128], nrmr[ns][:, kc, tc * 128:tc * 128 + 128],
                           whs[2][:, kc * 128:kc * 128 + 128], kc == 0, kc == 15, [("whs", 2), ("nrmr", ns)], [PSK(bv)])
                s2 = sq2_i[0] % 2; sq2_i[0] += 1
                act(sqt[s2][:, :], ps[bq][:, :], AF.Square, [PSK(bq)], [("sqt", s2)])
                mm(ps[6][:, :], ones[:, :], sqt[s2][:, :], True, True, ["ones", ("sqt", s2)], [PSK(6)])
                act(rstd[:, 0:512], ps[6][:, :], AF.Ln, [PSK(6)], ["rstd"], bias=EPS, scale=1.0 / 128)
                act(rstd[:, 0:512], rstd[:, 0:512], AF.Exp, ["rstd"], ["rstd"], scale=-0.5)
                tok0 = tb * 256
                stt(qT[:, tok0:tok0 + 256], ps[bq][:, 0:256], gq, rstd[:, 0:256], ALU.mult, ALU.mult,
                    [PSK(bq), "qknsc", "rstd"], ["qT"])
                stt(kTp[:, PADN + tok0:PADN + tok0 + 256], ps[bq][:, 256:512], gk, rstd[:, 256:512], ALU.mult, ALU.mult,
                    [PSK(bq), "qknsc", "rstd"], ["kT"])
                act(vS[:, tb * 2:tb * 2 + 2, :], ps[bv][:, 0:256].rearrange("p (c d) -> p c d", c=2), AF.Identity, [PSK(bv)], ["vS"])
                if pending_fin and tbi % 2 == 1:
                    pending_fin.pop(0)()
            if hd < 3:
                for i3 in range(3):
                    castdma(whs[i3][:, :], wh[hd + 1, i3], (), [("whs", i3)], ("whs", i3))
            ckpt(2 if hd == 0 else (5 if hd == 2 else -1))
            if not is_na:
                for which, buf, off, key in ((0, qT, 0, "qT"), (1, kTp, PADN, "kT")):
                    for pc in range(8):
                        cc0 = pc * 512
                        x0 = buf[0:32, off + cc0:off + cc0 + 512]
                        rb = 6 + pc % 2
                        mm(ps[rb][0:32, :], perm[:, :], x0, True, True, ["perm", key], [PSK(rb)])
                        tt("dve", rtmp[0][:, :], x0, ropeC[:, cc0:cc0 + 512], ALU.mult, [key, "ropeT"], [("rtmp", 0)])
                        tt("dve", rtmp[1][:, :], ps[rb][0:32, :], ropeS[:, cc0:cc0 + 512], ALU.mult, [PSK(rb), "ropeT"], [("rtmp", 1)])
                        tt("dve", x0, rtmp[0][:, :], rtmp[1][:, :], ALU.add, [("rtmp", 0), ("rtmp", 1)], [key])
            if hd == 2:
                ckpt(6)
            if DEBUG:
                dma("sp", dbg["qT"][hd], qT[:, :], ["qT"], [], "dbg")
                dma("sp", dbg["kT"][hd], kTp[:, PADN:PADN + S], ["kT"], [], "dbg")

            def tile_S(tl):
                if tl.get("pre_S"):
                    tl["pre_S"]()
                chunks = tl["chunks"]; qap = tl["q"]
                nch = len(chunks)
                sb = (pt_i[0] % 2) * 2
                pti = pt_i[0] % 2; pt_i[0] += 1
                for ci, kap in enumerate(chunks):
                    b = sb + ci // 4
                    col = (ci % 4) * 128
                    mm(ps[b][:, col:col + 128], kap, qap, True, False, ["kT", "kTpad", "kDpad", "qT"] + tl.get("kkeys", []), [PSK(b)])
                    mm(ps[b][:, col:col + 128], ident[:, :], tl["mask"](ci), False, True, ["ident"] + tl["mkeys"], [PSK(b)])
                n1 = min(nch, 4) * 128
                act(PT[pti][:, 0:n1], ps[sb][:, 0:n1], AF.Exp, [PSK(sb)], [("PT", pti)])
                if nch > 4:
                    n2 = (nch - 4) * 128
                    act(PT[pti][:, 512:512 + n2], ps[sb + 1][:, 0:n2], AF.Exp, [PSK(sb + 1)], [("PT", pti)])
                return pti, nch

            def tile_PV(tl, st):
                if tl.get("pre_PV"):
                    tl["pre_PV"]()
                pti, nch = st
                bo = 4 + po_i[0] % 2; po_i[0] += 1
                for ci in range(nch):
                    mm(ps[bo][:, 0:128], tl["v"](ci), PT[pti][:, ci * 128:ci * 128 + 128], ci == 0, ci == nch - 1,
                       tl["vkeys"] + [("PT", pti)], [PSK(bo)])
                for ci in range(nch):
                    mm(ps[bo][:, 128:256], ones[:, :], PT[pti][:, ci * 128:ci * 128 + 128], ci == 0, ci == nch - 1,
                       ["ones", ("PT", pti)], [PSK(bo)])
                src = ps[bo][:, 0:256].rearrange("p (w q) -> p w q", w=2)
                if tl["first"]:
                    cp("dve", tl["acc"], src, [PSK(bo)], ["acc_all"])
                else:
                    tt("dve", tl["acc"], src, tl["acc"], ALU.add, [PSK(bo)], ["acc_all"])

            tiles = []
            if is_na:
                for n in range(32):
                    lst = NA_TILES[n]
                    tiles.append(dict(
                        chunks=[kTp[:, PADN + c * 128:PADN + c * 128 + 128] for c, _ in lst],
                        q=qT[:, n * 128:n * 128 + 128],
                        v=(lambda ci, lst=lst: vS[:, lst[ci][0], :]), vkeys=["vS"],
                        mask=(lambda ci, lst=lst: Bt[:, lst[ci][1] * 128:lst[ci][1] * 128 + 128]), mkeys=["Bt"],
                        acc=acc[:, :, n * 128:n * 128 + 128], first=True))
            else:
                vdv = vdram.rearrange("(c p) d -> p c d", p=128)
                for q4 in range(4):
                    dma("sp", vdv[:, q4 * 8:q4 * 8 + 8, :], vS[:, q4 * 8:q4 * 8 + 8, :], ["vS"], [("vdram", q4)], "st_v")
                for Dd in (1, 4, 16):
                    L = S // Dd

                    def ld_vD(Dd=Dd):
                        na_ = (S // Dd) // 128
                        Vv = vdram.rearrange("(a i r) d -> i r a d", i=128, r=Dd)
                        for g in range(4):
                            if na_ >= 8:
                                rs = slice((8 * g) // na_, (8 * g) // na_ + 1); as_ = slice((8 * g) % na_, (8 * g) % na_ + 8)
                            else:
                                rs = slice((8 * g) // na_, (8 * g) // na_ + 8 // na_); as_ = slice(0, na_)
                            na_u = as_.stop - as_.start
                            vdk = [("vdram", q4) for q4 in range(4)]
                            for ri, r_ in enumerate(range(rs.start, rs.stop)):
                                c0 = 8 * g + ri * na_u
                                dma("sp", vD[64:128, c0:c0 + na_u, :], Vv[0:64, r_, as_], vdk,
                                    [("vDh", c) for c in range(c0, c0 + na_u)], "ld_vD")
                                dma("sp", vD[0:64, c0 + 1:c0 + 1 + na_u, :], Vv[64:128, r_, as_], vdk,
                                    [("vDl", c) for c in range(c0 + 1, c0 + 1 + na_u)], "ld_vD")

                    def mk_kD(Dd=Dd):
                        srcv = kTp[:, PADN:PADN + S].rearrange("p (u r) -> p r u", r=Dd)
                        for g in range(4):
                            nr = Dd // 4
                            cp("pool", kD[:, 64 + 1024 * g:64 + 1024 * g + 1024].rearrange("p (r u) -> p r u", r=nr),
                               srcv[:, g * nr:(g + 1) * nr, :], ["kT"], [("kD", c) for c in range(8 * g, 8 * g + 9)])

                    for n in range(32):
                        m0 = 128 * n
                        rho = m0 // L; u0 = m0 % L
                        q0 = rho + Dd * u0
                        if Dd == 1:
                            chunks = [kTp[:, PADN + m0 - 64:PADN + m0 + 64], kTp[:, PADN + m0 + 64:PADN + m0 + 192]]
                        else:
                            chunks = [kD[:, m0:m0 + 128], kD[:, m0 + 128:m0 + 256]]
                        mt = [1 if u0 == 0 else 0, 3 if u0 + 128 == L else 2]
                        tiles.append(dict(
                            chunks=chunks, q=qT[:, q0:q0 + 127 * Dd + 1:Dd],
                            kkeys=([("kD", n), ("kD", n + 1)] if Dd > 1 else []),
                            v=(lambda ci, n=n: vD[:, n + ci, :]), vkeys=[("vDh", n), ("vDl", n), ("vDh", n + 1), ("vDl", n + 1), "vDpad"],
                            mask=(lambda ci, mt=mt: dmask[:, mt[ci] * 128:mt[ci] * 128 + 128]), mkeys=["dmask"],
                            acc=acc[:, :, q0:q0 + 127 * Dd + 1:Dd], first=(Dd == 1),
                            pre_S=(mk_kD if (n == 0 and Dd > 1) else None), pre_PV=(ld_vD if n == 0 else None)))
            prev = None
            for tl in tiles:
                st = tile_S(tl)
                if prev is not None:
                    tile_PV(*prev)
                prev = (tl, st)
            tile_PV(*prev)
            def make_fin(hd=hd):
                fns = []
                for pc in range(4):
                    def piece(pc=pc):
                        sl_ = slice(pc * 1024, pc * 1024 + 1024)
                        act(acc[:, 1, sl_], acc[:, 1, sl_], AF.Ln, ["acc_all"], ["acc_all"])
                        act(acc[:, 1, sl_], acc[:, 1, sl_], AF.Exp, ["acc_all"], ["acc_all"], scale=-1.0)
                        tt("dve", oT[:, sl_], acc[:, 0, sl_], acc[:, 1, sl_], ALU.mult, ["acc_all"], ["oT", "acc_all"])
                    fns.append(piece)

                def store():
                    for tq in range(4):
                        dma("sp", o_b.bitcast(BF16)[hd, tq * 128:tq * 128 + 128, :], oT[:, tq * 1024:tq * 1024 + 1024],
                            ["oT"], [("o_b", hd)], "st_o")
                    P.add("pool", lambda e: e.collective_compute("AllGather", ALU.bypass, replica_groups=[[0, 1, 2, 3], [4, 5, 6, 7]], dma_qos=CC_QOS,
                                                                 ins=[o_b[hd].opt()], outs=[o_g[hd * 2048:(hd + 1) * 2048, :].opt()]),
                          [("o_b", hd)], [("o_g", hd)], dma="cc_o%d" % hd, inc=1)
                    if DEBUG:
                        dma("sp", dbg["oT"][hd], oT[:, :], ["oT"], [], "dbg")
                fns.append(store)
                return fns
            pending_fin.extend(make_fin())
            if hd == 3:
                while pending_fin:
                    pending_fin.pop(0)()
            ckpt({0: 3, 1: 4, 2: 10, 3: 11}[hd])

    except _Stop:
        return finish()
    P.barrier(nops)

    if STAGE == 3:
        return finish()
    def ld_o(e):
        ogb = o_g.bitcast(BF16).rearrange("(h r j q) t -> h j q r t", h=4, j=4, r=4)
        pid = e.partition_id()
        j = pid % 4
        for g in range(2):
            for l in range(2):
                src = ogb[2 * g + l, bass.ds(j, 1)].rearrange("o q r t -> (o q) r t")
                e.dma_start(out=xn[:, g * 8 + l:g * 8 + 8:2, :], in_=src).then_inc(dmasem["ld_o"], 16)
        return None
    dsem("ld_o")
    P.add("pool", ld_o, [("o_g", i) for i in range(4)], [("xn", kc) for kc in range(16)], dma="ld_o", inc=64)
    for grp in range(2):
        for k8 in range(8):
            kc = grp * 8 + k8
            s = sq_i[0] % 2; sq_i[0] += 1
            if kc % 2 == 0:
                act(sqr[s][:, :], xn[:, kc, :], AF.Square, [("xn", kc)], [("sqr", s)])
            else:
                tt("dve", sqr[s][:, :], xn[:, kc, :], xn[:, kc, :], ALU.mult, [("xn", kc)], [("sqr", s)])
            for t in range(2):
                mm(ps[2 + t][:, :], ones[:, :], sqr[s][:, t * 512:t * 512 + 512], k8 == 0, k8 == 7,
                   ["ones", ("sqr", s)], [PSK(2 + t)])
        for t in range(2):
            act(rstd[:, t * 512:t * 512 + 512], ps[2 + t][:, :], AF.Ln, [PSK(2 + t)], ["rstd"], bias=EPS, scale=1.0 / 1024)
        act(rstd[:, :], rstd[:, :], AF.Exp, ["rstd"], ["rstd"], scale=-0.5)
        for k8 in range(8):
            kc = grp * 8 + k8
            stt(xn[:, kc, :], xn[:, kc, :], gouts[:, kc:kc + 1], rstd[:, :], ALU.mult, ALU.mult,
                [("xn", kc), "gouts", "rstd"], [("xn", kc)])
    cp("dve", gth[:, :], modT[:, 5 * 16:5 * 16 + 16], ["modT"], ["gth"])
    for g in range(4):
        down_group(wo, g * 4, lambda fl, t, g=g: xn[:, g * 4 + fl, t * 512:t * 512 + 512],
                   lambda fl, g=g: [("xn", g * 4 + fl)], gth)
    if DEBUG:
        dma("sp", dbg["h2"].rearrange("(k p) t -> p k t", p=128), h[:, :, :], HALL, [], "dbg")

    if not LITE:
        norm_mod(2, 6, 7)
        ffn(w2g, w2u, w2d, 8)

    return finish()


def _tile_w(W):
    K, N = W.shape
    return np.ascontiguousarray(W.reshape(K // 128, 128, N // 128, 128).transpose(2, 1, 0, 3)).reshape(N // 128, 128, (K // 128) * 128)


def _vec16(v):
    return np.ascontiguousarray(v.reshape(-1, 128).T)


_NC_CACHE = {}


def kernel(x, c, w_ada, b_ada, g_ffn1, w1_gate, w1_up, w1_down, g_mix, w_qkv, qn_na, kn_na, qn_dil, kn_dil,
           rpb_na, g_out_na, g_out_dil, w_o, g_ffn2, w2_gate, w2_up, w2_down):
    f = lambda a: np.asarray(a, dtype=np.float32)
    x = f(x); c = f(c)
    w_ada = f(w_ada)[0]; b_ada = f(b_ada)[0]
    if not LITE:
        w1g = _tile_w(f(w1_gate)[0]); w1u = _tile_w(f(w1_up)[0]); w1d = np.ascontiguousarray(f(w1_down)[0]).reshape(NFC, 128, 2048)
        w2g = _tile_w(f(w2_gate)[0]); w2u = _tile_w(f(w2_up)[0]); w2d = np.ascontiguousarray(f(w2_down)[0]).reshape(NFC, 128, 2048)
    wqkv = f(w_qkv)[0]
    wo = np.ascontiguousarray(f(w_o)[0]).reshape(16, 128, 2048)
    gv = np.concatenate([_vec16(f(g_ffn1)[0]), _vec16(f(g_mix)[0]), _vec16(f(g_ffn2)[0])], axis=1)
    gout = np.concatenate([_vec16(f(g_out_na)[0]), _vec16(f(g_out_dil)[0])], axis=1)
    qkn = np.stack([f(qn_na)[0], f(kn_na)[0], f(qn_dil)[0], f(kn_dil)[0]], axis=1)
    rpb = f(rpb_na)[0]
    cT = np.ascontiguousarray(c.T.reshape(16, 128, 2).transpose(1, 0, 2)).reshape(128, 32)
    ropeC, ropeS = rope_tabs()
    badaT_h = np.ascontiguousarray(b_ada.reshape(144, 128).T)
    dmask = dil_masks()
    ident = np.eye(128, dtype=np.float32)
    perm32 = np.zeros((32, 32), np.float32)
    perm32[(np.arange(32) + 16) % 32, np.arange(32)] = 1.0
    in_maps = []
    for core in range(8):
        b = core // 4; j = core % 4
        heads = [2 * j, 2 * j + 1, 8 + 2 * j, 9 + 2 * j]
        whl = []
        for hh in heads:
            blk = []
            for part in range(3):
                col0 = part * D + hh * 128
                blk.append(_tile_w(wqkv[:, col0:col0 + 128])[0])
            whl.append(np.stack(blk))
        m = {
            "xT": np.ascontiguousarray(x[b, j * NT:(j + 1) * NT, :].T),
            "cT": cT,
            "wada": _tile_w(np.concatenate([w_ada[:, 12 * j * 128:(12 * j + 12) * 128],
                                            w_ada[:, (48 + 24 * j) * 128:(48 + 24 * j + 24) * 128]], axis=1)),
            "bada": badaT_h,
            "sel": np.array([[1.0 - b], [float(b)]], np.float32),
            "wh": np.stack(whl), "wo": wo, "gv": gv, "gout": gout, "qkn": np.ascontiguousarray(qkn),
            "bt": np.stack([build_bias_tiles(rpb[2 * j]), build_bias_tiles(rpb[2 * j + 1])]),
            "ropeC": ropeC, "ropeS": ropeS, "dmask": dmask, "ident": ident, "perm": perm32,
        }
        if not LITE:
            m.update({"w1g": w1g, "w1u": w1u, "w1d": w1d, "w2g": w2g, "w2u": w2u, "w2d": w2d})
        in_maps.append(m)
    if "nc" not in _NC_CACHE:
        _NC_CACHE["nc"] = build_nc()
    nc = _NC_CACHE["nc"]
    res = run_bass_kernel_spmd(nc, in_maps, core_ids=list(range(8)))
    out = np.empty((2, S, D), np.float32)
    for core in range(8):
        b = core // 4; j = core % 4
        out[b, j * NT:(j + 1) * NT, :] = res.results[core]["outT"].T
    if DEBUG:
        kernel.last = res
    return out
```

```python
import numpy as np
from contextlib import ExitStack
import concourse.bass as bass
import concourse.mybir as mybir
from concourse.bass_utils import run_bass_kernel_spmd

F32 = mybir.dt.float32
BF16 = mybir.dt.bfloat16
AF = mybir.ActivationFunctionType
ALU = mybir.AluOpType

D = 2048
NT = 1024
S = 4096
DFF = 5632
NFC = 44
EPS = 1e-6
SCALE = 128.0 ** -0.5
NEGM = -30000.0
PADN = 64
DEBUG = False
STAGE = 9
ATTSTOP = 0
CC_QOS = "P2"
NRM_NP = 4
LITE = False


class Op:
    __slots__ = ("eng", "fn", "deps", "is_dma", "semkey", "inc", "cum", "targets", "signal", "seq", "idx")


class _Stop(Exception):
    pass


def ckpt(n):
    if ATTSTOP == n:
        raise _Stop()


class Prog:
    ENGS = ("pe", "act", "dve", "pool", "sp")

    def __init__(self):
        self.ops = {e: [] for e in self.ENGS}
        self.lastw = {}
        self.readers = {}
        self.dma_count = {}
        self.last_dma = {}
        self.n = 0

    def add(self, eng, fn, reads=(), writes=(), dma=None, inc=16, extra_deps=()):
        op = Op()
        op.eng = eng; op.fn = fn; op.is_dma = dma is not None; op.semkey = dma; op.inc = inc
        op.signal = False; op.seq = 0; op.idx = self.n; self.n += 1
        deps = []
        for k in reads:
            w = self.lastw.get(k)
            if w is not None:
                deps.append((w, "raw"))
        for k in writes:
            rds = self.readers.get(k, ())
            w = self.lastw.get(k)
            if w is not None and not rds:
                deps.append((w, "waw"))
            for r in rds:
                deps.append((r, "war"))
        for d in extra_deps:
            deps.append((d, "raw"))
        for k in reads:
            self.readers.setdefault(k, []).append(op)
        for k in writes:
            self.lastw[k] = op
            self.readers[k] = []
        op.deps = deps
        op.targets = {d.semkey: self.dma_count[d.semkey] for d, _ in deps if d.is_dma}
        if op.is_dma:
            self.dma_count[dma] = self.dma_count.get(dma, 0) + inc
            op.cum = self.dma_count[dma]
            self.last_dma[dma] = op
        self.ops[eng].append(op)
        return op

    def barrier(self, mk_nop):
        firsts = []
        for e in ("pe", "act", "dve", "pool"):
            firsts.append(self.add(e, mk_nop[e], writes=[("bar1", e, self.n)]))
        dmas = [op for key, op in self.last_dma.items() if not str(key).startswith("cc_")]
        for e in self.ENGS:
            self.add(e, mk_nop[e], writes=[("bar2", e, self.n)], extra_deps=firsts + dmas)

    def needs_wait(self, op, d, kind):
        if d.is_dma:
            return True
        if d.eng != op.eng:
            return True
        if op.eng == "pe":
            return False
        if op.is_dma:
            return True
        return kind == "raw"

    def finalize(self):
        for e in self.ENGS:
            for op in self.ops[e]:
                latest = {}
                for d, kind in op.deps:
                    if (not d.is_dma) and self.needs_wait(op, d, kind):
                        if d.eng not in latest or d.idx > latest[d.eng].idx:
                            latest[d.eng] = d
                for d in latest.values():
                    d.signal = True
                op.deps = [(d, k) for d, k in op.deps if d.is_dma or latest.get(d.eng) is d]
        for e in self.ENGS:
            s = 0
            for op in self.ops[e]:
                if op.signal and not op.is_dma:
                    s += 1
                    op.seq = s

    def emit(self, nc, block, engsem, dmasem):
        self.finalize()
        decos = {"pe": block.tensor, "act": block.scalar, "dve": block.vector, "pool": block.gpsimd, "sp": block.sync}
        for e in self.ENGS:
            ops = self.ops[e]

            def body(eng, ops=ops, e=e):
                waited = {}
                for op in ops:
                    need = {}
                    for d, kind in op.deps:
                        if not self.needs_wait(op, d, kind):
                            continue
                        if d.is_dma:
                            key = ("d", d.semkey); val = op.targets[d.semkey]; sem = dmasem[d.semkey]
                        else:
                            key = ("e", d.eng); val = d.seq; sem = engsem[d.eng]
                        if val > need.get(key, (0, None))[0]:
                            need[key] = (val, sem)
                    for key, (val, sem) in need.items():
                        if waited.get(key, 0) >= val:
                            continue
                        eng.wait_ge(sem, val)
                        waited[key] = val
                    inst = op.fn(eng)
                    if inst is None:
                        continue
                    if op.is_dma:
                        inst.then_inc(dmasem[op.semkey], op.inc)
                    elif op.signal:
                        inst.then_inc(engsem[e], 1)
            decos[e](body)


def na_tile_tables():
    types = {}
    tiles = []
    for n in range(32):
        rows = []
        for rq in (2 * n, 2 * n + 1):
            rs = min(max(rq - 4, 0), 56)
            rows.append((rq, rs))
        lo = min(r[1] for r in rows) // 2
        hi = (max(r[1] for r in rows) + 7) // 2
        lst = []
        for c in range(lo, hi + 1):
            key = []
            for rkl in range(2):
                for rql in range(2):
                    rk = 2 * c + rkl
                    rq, rs = rows[rql]
                    if rs <= rk < rs + 8:
                        key.append(rk - rq + 7)
                    else:
                        key.append(-1)
            key = tuple(key)
            if all(k < 0 for k in key):
                continue
            if key not in types:
                types[key] = len(types)
            lst.append((c, types[key]))
        tiles.append(lst)
    return tiles, types


NA_TILES, NA_TYPES = na_tile_tables()
NTYPES = len(NA_TYPES)


def build_bias_tiles(rpb_h):
    out = np.full((128, NTYPES, 128), NEGM, np.float32)
    ck = np.arange(64)[:, None]
    cq = np.arange(64)[None, :]
    cs = np.clip(cq - 8, 0, 48)
    cvalid = (ck >= cs) & (ck < cs + 16)
    coff = np.clip(ck - cq + 15, 0, 30)
    for key, t in NA_TYPES.items():
        i = 0
        for rkl in range(2):
            for rql in range(2):
                a = key[i]; i += 1
                if a < 0:
                    continue
                blk = np.where(cvalid, rpb_h[a][coff], np.float32(NEGM))
                out[64 * rkl:64 * rkl + 64, t, 64 * rql:64 * rql + 64] = blk
    return out.reshape(128, NTYPES * 128)


def dil_masks():
    i = np.arange(128)[:, None]
    j = np.arange(128)[None, :]
    A = np.where(j <= i, 0.0, NEGM)
    B = np.where(j >= i, 0.0, NEGM)
    A1 = A.copy(); A1[:64, :] = NEGM
    B1 = B.copy(); B1[64:, :] = NEGM
    return np.concatenate([A, A1, B, B1], axis=1).astype(np.float32)


def rope_tabs():
    pos = np.arange(S, dtype=np.float32)
    inv = np.power(np.float32(500000.0), -np.arange(0, 32, 2, dtype=np.float32) / np.float32(32)).astype(np.float32)
    ang = (pos[None, :] * inv[:, None]).astype(np.float32)
    c = np.cos(ang).astype(np.float32); s = np.sin(ang).astype(np.float32)
    C = np.concatenate([c, c], axis=0)
    Sg = np.concatenate([-s, s], axis=0)
    return C, Sg


def build_nc():
    nc = bass.Bass("TRN2", target_bir_lowering=False)
    P = Prog()

    def din(name, shape, dt=F32):
        return nc.dram_tensor(name, list(shape), dt, kind="ExternalInput").ap()

    xT = din("xT", [D, NT]); cT = din("cT", [128, 32]); wada = din("wada", [36, 128, 2048]); bada = din("bada", [128, 144])
    sel = din("sel", [2, 1])
    if not LITE:
        w1g = din("w1g", [NFC, 128, 2048]); w1u = din("w1u", [NFC, 128, 2048]); w1d = din("w1d", [NFC, 128, 2048])
        w2g = din("w2g", [NFC, 128, 2048]); w2u = din("w2u", [NFC, 128, 2048]); w2d = din("w2d", [NFC, 128, 2048])
    wh = din("wh", [4, 3, 128, 2048]); wo = din("wo", [16, 128, 2048])
    gv = din("gv", [128, 48]); goutd = din("gout", [128, 16]); qknd = din("qkn", [128, 4])
    btd = din("bt", [2, 128, NTYPES * 128]); ropeCd = din("ropeC", [32, S]); ropeSd = din("ropeS", [32, S])
    dmaskd = din("dmask", [128, 512]); identd = din("ident", [128, 128]); permd = din("perm", [32, 32])
    outT = nc.dram_tensor("outT", [D, NT], F32, kind="ExternalOutput").ap()
    dbg = {}
    if DEBUG:
        dbg["h1"] = nc.dram_tensor("dbg_h1", [D, NT], F32, kind="ExternalOutput").ap()
        dbg["modT"] = nc.dram_tensor("dbg_modT", [128, 144], F32, kind="ExternalOutput").ap()
        dbg["oT"] = nc.dram_tensor("dbg_oT", [4, 128, S], BF16, kind="ExternalOutput").ap()
        dbg["nrm"] = nc.dram_tensor("dbg_nrm", [D, NT], BF16, kind="ExternalOutput").ap()
        dbg["qT"] = nc.dram_tensor("dbg_qT", [4, 128, S], BF16, kind="ExternalOutput").ap()
        dbg["kT"] = nc.dram_tensor("dbg_kT", [4, 128, S], BF16, kind="ExternalOutput").ap()
        dbg["h2"] = nc.dram_tensor("dbg_h2", [D, NT], F32, kind="ExternalOutput").ap()

    mod_bA = nc.dram_tensor("mod_bA", [2, 1536], F32).ap()
    mod_gA = nc.dram_tensor("mod_gA", [8, 1536], F32).ap()
    mod_bB = nc.dram_tensor("mod_bB", [2, 3072], F32).ap()
    mod_gB = nc.dram_tensor("mod_gB", [8, 3072], F32).ap()
    nrm_b = nc.dram_tensor("nrm_b", [4, 128, 2048], F32).ap()
    nrm_g = nc.dram_tensor("nrm_g", [4, 512, 2048], F32).ap()
    vdram = nc.dram_tensor("vdram", [S, 128], BF16).ap()
    o_b = nc.dram_tensor("o_b", [4, 512, 512], F32).ap()
    o_g = nc.dram_tensor("o_g", [16 * 512, 512], F32).ap()

    es = ExitStack()
    ARENA = 212480
    es.enter_context(nc.sbuf_tensor("arena", [128, ARENA + 64], mybir.dt.uint8))
    base0 = (nc.sbuf_base - (ARENA + 64) + 31) // 32 * 32
    cur = [base0]

    def salloc(name, shape, dt, at=None):
        nbytes = int(np.prod(shape[1:])) * (4 if dt == F32 else 2)
        nbytes = (nbytes + 31) // 32 * 32
        if at is None:
            off = cur[0]; cur[0] += nbytes
        else:
            off = at
        assert off + nbytes <= base0 + ARENA, (name, off + nbytes - base0, ARENA)
        return nc.alloc_sbuf_tensor_at(name, list(shape), dt, offset=off), off + nbytes

    h, _ = salloc("h", [128, 16, NT], F32)
    modT, _ = salloc("modT", [128, 144], F32)
    gvs, _ = salloc("gvs", [128, 48], F32)
    gouts, _ = salloc("gouts", [128, 16], F32)
    qkns, _ = salloc("qkns", [128, 4], F32)
    qknsc, _ = salloc("qknsc", [128, 4], F32)
    Avec, _ = salloc("Avec", [128, 16], F32)
    tmpA, _ = salloc("tmpA", [128, 16], F32)
    gth, _ = salloc("gth", [128, 16], F32)
    ident, _ = salloc("ident", [128, 128], BF16)
    ones, _ = salloc("ones", [128, 128], BF16)
    rstd, _ = salloc("rstd", [128, NT], F32)
    sels, _ = salloc("sels", [2, 1], F32)
    scr, _ = salloc("scr", [128, 8], F32)
    R0 = cur[0]
    xn, e1 = salloc("xn", [128, 16, NT], BF16)
    wgu = []
    for i in range(6):
        t, _ = salloc(f"wgu{i}", [128, 2048], BF16); wgu.append(t)
    wdr = []
    for i in range(6):
        t, _ = salloc(f"wd{i}", [128, 2048], BF16); wdr.append(t)
    hid = []
    hid_off = cur[0]
    for i in range(3):
        t, _ = salloc(f"hid{i}", [128, 4, NT], BF16); hid.append(t)
    sgt = []
    for i in range(2):
        t, _ = salloc(f"sgt{i}", [128, 512], F32); sgt.append(t)
    sqr = []
    for i in range(2):
        t, _ = salloc(f"sqr{i}", [128, NT], BF16); sqr.append(t)
    tmpf = []
    for i in range(2):
        t, _ = salloc(f"tmpf{i}", [128, NT], F32); tmpf.append(t)
    cTf, _ = salloc("cTf", [128, 32], F32)
    csb, _ = salloc("csb", [128, 32], BF16)
    badaT, _ = salloc("badaT", [128, 144], F32)
    stg = []
    for i in range(2):
        t, _ = salloc(f"stg{i}", [2, 512], F32); stg.append(t)
    stgr = []
    for i in range(2):
        t, _ = salloc(f"stgr{i}", [2, 512], F32); stgr.append(t)
    ffn_end = cur[0]
    cur[0] = R0
    nrmr = []
    for i in range(2):
        t, _ = salloc(f"nrmr{i}", [128, 16, 256], BF16); nrmr.append(t)
    whs = []
    for i in range(3):
        t, _ = salloc(f"whs{i}", [128, 2048], BF16); whs.append(t)
    qT, _ = salloc("qT", [128, S], BF16)
    kTp, _ = salloc("kTp", [128, S + 2 * PADN], BF16)
    vS, _ = salloc("vS", [128, 32, 128], BF16)
    vD, _ = salloc("vD", [128, 33, 128], BF16)
    acc, _ = salloc("acc", [128, 2, S], F32)
    PT = []
    for i in range(2):
        t, _ = salloc(f"PT{i}", [128, 640], BF16); PT.append(t)
    Bt, _ = salloc("Bt", [128, NTYPES * 128], BF16)
    oT, _ = salloc("oT", [128, S], BF16)
    ropeC, _ = salloc("ropeCt", [32, S], BF16)
    ropeS, _ = salloc("ropeSt", [32, S], BF16)
    kD, _ = salloc("kD", [128, S + 128], BF16)
    perm, _ = salloc("perm", [32, 32], BF16)
    rtmp = []
    for i in range(2):
        t, _ = salloc(f"rtmp{i}", [32, 512], F32); rtmp.append(t)
    sqt = []
    for i in range(2):
        t, _ = salloc(f"sqt{i}", [128, 512], BF16); sqt.append(t)
    dmask, _ = salloc("dmask", [128, 512], BF16)
    att_end = cur[0]
    assert max(att_end, ffn_end) <= base0 + ARENA

    ps = [es.enter_context(nc.psum_tensor(f"ps{i}", [128, 512], F32)) for i in range(8)]
    psS = []
    engsem = {e: es.enter_context(nc.semaphore(f"sem_{e}")) for e in ("pe", "act", "dve", "pool")}
    dmasem = {}

    def dsem(key):
        if key not in dmasem:
            dmasem[key] = es.enter_context(nc.semaphore("d_" + str(key)))
        return key

    PSK = lambda b: ("ps", b)
    HK = lambda kc: [("h", kc, 0), ("h", kc, 1)]
    HALL = [("h", kc, t) for kc in range(16) for t in range(2)]

    def dma(eng, out, in_, reads, writes, key, **kw):
        dsem(key)
        return P.add(eng, lambda e: e.dma_start(out=out, in_=in_, **kw), reads, writes, dma=key)

    def castdma(out, in_, reads, writes, key):
        return dma("pool", out, in_, reads, writes, key, max_dma_last_dim=8192)

    def mm(out, lhsT, rhs, start, stop, reads, writes, **kw):
        return P.add("pe", lambda e: e.matmul(out, lhsT, rhs, start=start, stop=stop, **kw), reads, writes)

    def act(out, in_, func, reads, writes, bias=None, scale=None):
        kw = {}
        if bias is not None:
            kw["bias"] = bias
        if scale is not None:
            kw["scale"] = scale
        return P.add("act", lambda e: e.activation(out, in_, func, **kw), reads, writes)

    def tt(eng, out, in0, in1, op, reads, writes):
        return P.add(eng, lambda e: e.tensor_tensor(out, in0, in1, op), reads, writes)

    def stt(out, in0, scalar, in1, op0, op1, reads, writes):
        return P.add("dve", lambda e: e.scalar_tensor_tensor(out, in0, scalar, in1, op0, op1), reads, writes)

    def ts(eng, out, in0, s1, s2, op0, op1, reads, writes):
        return P.add(eng, lambda e: e.tensor_scalar(out, in0, s1, s2, op0, op1), reads, writes)

    def cp(eng, out, in_, reads, writes):
        return P.add(eng, lambda e: e.tensor_copy(out, in_), reads, writes)

    def memset(eng, ap, val, writes):
        return P.add(eng, lambda e: e.memset(ap, val), (), writes)

    dma("sp", h[:, :, :], xT.rearrange("(k p) t -> p k t", p=128), (), HALL, "ld_h")
    dma("sp", cTf[:, :], cT, (), ["cTf"], "ld_c")
    dma("sp", badaT[:, :], bada, (), ["badaT"], "ld_c")
    dma("sp", sels[:, :], sel, (), ["sels"], "ld_c")
    dma("sp", gvs[:, :], gv, (), ["gvs"], "ld_c")
    dma("sp", gouts[:, :], goutd, (), ["gouts"], "ld_c")
    dma("sp", qkns[:, :], qknd, (), ["qkns"], "ld_c")
    castdma(ident[:, :], identd, (), ["ident"], "ld_id")
    memset("dve", ones[:, :], 1.0, ["ones"])

    act(csb[:, :], cTf[:, :], AF.Silu, ["cTf"], ["csb"])
    wgu_i = [0]

    def next_wgu():
        i = wgu_i[0] % 6; wgu_i[0] += 1
        return i

    stg_i = [0]; stgr_i = [0]

    def ada_chunk(c, c0, bank, dst, dkey):
        sl = next_wgu()
        castdma(wgu[sl][:, :], wada[c], (), [("wgu", sl)], ("wgu", sl))
        lc = c - c0
        for kc in range(16):
            mm(ps[bank][0:2, (lc % 4) * 128:(lc % 4) * 128 + 128], csb[:, kc * 2:kc * 2 + 2], wgu[sl][:, kc * 128:kc * 128 + 128],
               kc == 0, kc == 15, ["csb", ("wgu", sl)], [PSK(bank)])
        if lc % 4 == 3:
            s_ = stg_i[0] % 2; stg_i[0] += 1
            cp("dve", stg[s_][:, :], ps[bank][0:2, 0:512], [PSK(bank)], [("stg", s_)])
            dma("sp", dst[:, (lc // 4) * 512:(lc // 4) * 512 + 512], stg[s_][:, :], [("stg", s_)], [dkey], ("stgd", s_))

    def ada_gather(src, dst, skey, gkey, sem):
        dsem(sem)
        P.add("pool", lambda e: e.collective_compute("AllGather", ALU.bypass, replica_groups=[[0, 1, 2, 3], [4, 5, 6, 7]], dma_qos=CC_QOS,
                                                     ins=[src.opt()], outs=[dst.opt()]),
              [skey], [gkey], dma=sem, inc=1)

    def ada_transposes(gathered, gkey, nchunk, gi_base, bank):
        for r in range(4):
            for grp in range(nchunk // 4):
                s_ = stgr_i[0] % 2; stgr_i[0] += 1
                dma("sp", stgr[s_][:, :], gathered[2 * r:2 * r + 2, grp * 512:grp * 512 + 512], [gkey], [("stgr", s_)], ("stgrd", s_))
                for k in range(4):
                    gi = gi_base + r * nchunk + grp * 4 + k
                    mm(ps[bank][:, gi:gi + 1], stgr[s_][:, k * 128:k * 128 + 128], sels[:, :], True, True,
                       [("stgr", s_), "sels"], [PSK(bank)])
        lo, hi = gi_base, gi_base + 4 * nchunk
        tt("dve", modT[:, lo:hi], ps[bank][:, lo:hi], badaT[:, lo:hi], ALU.add, [PSK(bank), "badaT"], ["modT"])

    for c in range(12):
        ada_chunk(c, 0, 0, mod_bA, "mod_bA")
    ada_gather(mod_bA, mod_gA, "mod_bA", "mod_gA", "cc_modA")
    ada_transposes(mod_gA, "mod_gA", 12, 0, 1)
    adaB = {fc: (lambda c=12 + fc: ada_chunk(c, 12, 7, mod_bB, "mod_bB")) for fc in range(24)}
    adaB[24] = lambda: ada_gather(mod_bB, mod_gB, "mod_bB", "mod_gB", "cc_modB")
    if LITE:
        for fc in range(25):
            adaB[fc]()
        ada_transposes(mod_gB, "mod_gB", 24, 48, 7)
    if DEBUG:
        dma("sp", dbg["modT"], modT[:, :], ["modT"], [], "dbg")
    cp("dve", qknsc[:, :], qkns[:, :], ["qkns"], ["qknsc"])
    ts("dve", qknsc[:, 0:1], qkns[:, 0:1], SCALE, None, ALU.mult, ALU.bypass, ["qkns", "qknsc"], ["qknsc"])
    ts("dve", qknsc[:, 2:3], qkns[:, 2:3], SCALE, None, ALU.mult, ALU.bypass, ["qkns", "qknsc"], ["qknsc"])

    MS = lambda m, kc: modT[:, m * 16 + kc:m * 16 + kc + 1]

    sq_i = [0]; tf_i = [0]

    xnB = xn[:, :, :].rearrange("p k t -> p (k t)").rearrange("p (b k t) -> p b k t", b=4, k=16)

    def norm_mod(gidx, m_sh, m_sc, blockmajor=False):
        ts("dve", tmpA[:, :], modT[:, m_sc * 16:m_sc * 16 + 16], 1.0, None, ALU.add, ALU.bypass, ["modT"], ["tmpA"])
        tt("dve", Avec[:, :], tmpA[:, :], gvs[:, gidx * 16:gidx * 16 + 16], ALU.mult, ["tmpA", "gvs"], ["Avec"])
        for kc in range(16):
            s = sq_i[0] % 2; sq_i[0] += 1
            if kc % 2 == 0:
                act(sqr[s][:, :], h[:, kc, :], AF.Square, HK(kc), [("sqr", s)])
            else:
                tt("dve", sqr[s][:, :], h[:, kc, :], h[:, kc, :], ALU.mult, HK(kc), [("sqr", s)])
            for t in range(2):
                mm(ps[2 + t][:, :], ones[:, :], sqr[s][:, t * 512:t * 512 + 512], kc == 0, kc == 15,
                   ["ones", ("sqr", s)], [PSK(2 + t)])
        for t in range(2):
            act(rstd[:, t * 512:t * 512 + 512], ps[2 + t][:, :], AF.Ln, [PSK(2 + t)], ["rstd"], bias=EPS, scale=1.0 / D)
        act(rstd[:, :], rstd[:, :], AF.Exp, ["rstd"], ["rstd"], scale=-0.5)
        for kc in range(16):
            s = tf_i[0] % 2; tf_i[0] += 1
            stt(tmpf[s][:, :], h[:, kc, :], Avec[:, kc:kc + 1], rstd[:, :], ALU.mult, ALU.mult,
                HK(kc) + ["Avec", "rstd"], [("tmpf", s)])
            if blockmajor:
                act(xnB[:, :, kc, :], tmpf[s][:, :].rearrange("p (b t) -> p b t", b=4), AF.Identity,
                    [("tmpf", s), "modT"], [("xn", kc)], bias=MS(m_sh, kc), scale=1.0)
            else:
                act(xn[:, kc, :], tmpf[s][:, :], AF.Identity, [("tmpf", s), "modT"], [("xn", kc)], bias=MS(m_sh, kc), scale=1.0)

    wd_i = [0]
    pd_i = [0]

    def down_group(wsrc, fc0, rhs_fn, rhs_keys, scal):
        slots = []
        for fl in range(4):
            sl = wd_i[0] % 6; wd_i[0] += 1
            castdma(wdr[sl][:, :], wsrc[fc0 + fl], (), [("wd", sl)], ("wd", sl))
            slots.append(sl)
        for dc in range(16):
            for t in range(2):
                b = 4 + pd_i[0] % 4; pd_i[0] += 1
                for fl in range(4):
                    mm(ps[b][:, :], wdr[slots[fl]][:, dc * 128:dc * 128 + 128], rhs_fn(fl, t), fl == 0, fl == 3,
                       [("wd", slots[fl])] + rhs_keys(fl), [PSK(b)])
                stt(h[:, dc, t * 512:t * 512 + 512], ps[b][:, :], scal[:, dc:dc + 1], h[:, dc, t * 512:t * 512 + 512],
                    ALU.mult, ALU.add, [PSK(b), "gth", ("h", dc, t)], [("h", dc, t)])

    def ffn(wg, wu, wd_, m_gt, side=None):
        ts("dve", gth[:, :], modT[:, m_gt * 16:m_gt * 16 + 16], 0.5, None, ALU.mult, ALU.bypass, ["modT"], ["gth"])
        gu_i = 0
        for fc in range(NFC):
            if side and fc in side:
                side[fc]()
            g = fc // 4
            hs = g % 3
            sg_ = next_wgu(); su_ = next_wgu()
            castdma(wgu[sg_][:, :], wg[fc], (), [("wgu", sg_)], ("wgu", sg_))
            castdma(wgu[su_][:, :], wu[fc], (), [("wgu", su_)], ("wgu", su_))
            for t in range(2):
                bg = (gu_i % 2) * 2; bu = bg + 1; gu_i += 1
                for kc in range(16):
                    mm(ps[bg][:, :], wgu[sg_][:, kc * 128:kc * 128 + 128], xn[:, kc, t * 512:t * 512 + 512], kc == 0, kc == 15,
                       [("wgu", sg_), ("xn", kc)], [PSK(bg)])
                for kc in range(16):
                    mm(ps[bu][:, :], wgu[su_][:, kc * 128:kc * 128 + 128], xn[:, kc, t * 512:t * 512 + 512], kc == 0, kc == 15,
                       [("wgu", su_), ("xn", kc)], [PSK(bu)])
                st = gu_i % 2
                act(sgt[st][:, :], ps[bg][:, :], AF.Silu, [PSK(bg)], [("sgt", st)])
                tt("dve", hid[hs][:, fc % 4, t * 512:t * 512 + 512], sgt[st][:, :], ps[bu][:, :], ALU.mult,
                   [("sgt", st), PSK(bu)], [("hid", hs, fc % 4)])
            if fc % 4 == 3 and g >= 1:
                gp = g - 1
                down_group(wd_, gp * 4, lambda fl, t, gp=gp: hid[gp % 3][:, fl, t * 512:t * 512 + 512],
                           lambda fl, gp=gp: [("hid", gp % 3, fl)], gth)
        gp = NFC // 4 - 1
        down_group(wd_, gp * 4, lambda fl, t, gp=gp: hid[gp % 3][:, fl, t * 512:t * 512 + 512],
                   lambda fl, gp=gp: [("hid", gp % 3, fl)], gth)

    def finish():
        dma("sp", outT.rearrange("(k p) t -> p k t", p=128), h[:, :, :], HALL, ["out"], "st_out")
        P.add("sp", lambda e: e.nop(), ["out"], [])
        P.add("sp", lambda e: e.nop(), [], [], extra_deps=list(P.last_dma.values()))
        with es:
            with nc.Block() as block:
                P.emit(nc, block, engsem, dmasem)
        return nc

    if not LITE:
        norm_mod(0, 0, 1)
        ffn(w1g, w1u, w1d, 2, side=adaB)
        ada_transposes(mod_gB, "mod_gB", 24, 48, 7)
    if DEBUG:
        dma("sp", dbg["h1"].rearrange("(k p) t -> p k t", p=128), h[:, :, :], HALL, [], "dbg")

    if STAGE == 1:
        return finish()
    norm_mod(1, 3, 4, blockmajor=True)
    if DEBUG:
        dma("sp", dbg["nrm"].rearrange("(k p) (b t) -> p b k t", p=128, b=4), xnB, [("xn", kc) for kc in range(16)], [], "dbg")
    for b4 in range(4):
        dsem("cc_nrm%d" % b4)
        dma("sp", nrm_b.bitcast(BF16)[b4], xnB[:, b4].rearrange("p k t -> p (k t)"),
            [("xn", kc) for kc in range(16)], [("nrm_b", b4)], "st_nrm")

    def nrm_gather(b4):
        P.add("pool", lambda e, b4=b4: e.collective_compute("AllGather", ALU.bypass, replica_groups=[[0, 1, 2, 3], [4, 5, 6, 7]], dma_qos=CC_QOS,
                                                            ins=[nrm_b[b4].opt()], outs=[nrm_g[b4].opt()]),
              [("nrm_b", b4)], [("nrm_g", b4)], dma="cc_nrm%d" % b4, inc=1)
    nrm_gather(0)

    if STAGE == 2:
        return finish()
    nops = {
        "pe": lambda e: e.matmul(ps[7][0:1, 0:1], sels[:, :], sels[:, :], start=True, stop=True),
        "act": lambda e: e.activation(scr[:, 0:1], scr[:, 0:1], AF.Identity),
        "dve": lambda e: e.memset(scr[:, 1:2], 0.0),
        "pool": lambda e: e.memset(scr[:, 2:3], 0.0),
        "sp": lambda e: e.nop(),
    }
    memset("dve", scr[:, :], 0.0, ["scr"])
    P.barrier(nops)

    try:
        castdma(dmask[:, :], dmaskd, (), ["dmask"], "ld_dm")
        castdma(perm[:, :], permd, (), ["perm"], "ld_dm")
        for pc in range(4):
            castdma(ropeC[:, pc * 1024:pc * 1024 + 1024], ropeCd[:, pc * 1024:pc * 1024 + 1024], (), ["ropeT"], "ld_dm")
            castdma(ropeS[:, pc * 1024:pc * 1024 + 1024], ropeSd[:, pc * 1024:pc * 1024 + 1024], (), ["ropeT"], "ld_dm")
        memset("pool", kD[:, 0:64], 0.0, ["kDpad"])
        memset("pool", kD[:, 64 + S:128 + S], 0.0, ["kDpad"])
        memset("pool", kTp[:, 0:PADN], 0.0, ["kTpad"])
        memset("pool", kTp[:, PADN + S:PADN + S + PADN], 0.0, ["kTpad"])
        memset("pool", vD[0:64, 0, :], 0.0, ["vDpad"])
        memset("pool", vD[64:128, 32, :], 0.0, ["vDpad"])
        nr_i = [0]; sq2_i = [0]; pt_i = [0]; po_i = [0]; rp_i = [0]
        ckpt(1)

        for hd_ in range(4):
            dsem("cc_o%d" % hd_)
        pending_fin = []
        for i3 in range(3):
            castdma(whs[i3][:, :], wh[0, i3], (), [("whs", i3)], ("whs", i3))
        castdma(Bt[:, :], btd[0], (), ["Bt"], "ld_bt")
        for b4 in range(1, 4):
            nrm_gather(b4)
        for hd in range(4):
            is_na = hd < 2
            if hd == 1:
                castdma(Bt[:, :], btd[hd], (), ["Bt"], "ld_bt")
            gq = qknsc[:, 0:1] if is_na else qknsc[:, 2:3]
            gk = qknsc[:, 1:2] if is_na else qknsc[:, 3:4]
            for tbi in range(16):
                b4 = tbi // 4; r = tbi % 4
                tb = r * 4 + b4
                ns = nr_i[0] % 2; nr_i[0] += 1
                dma("sp", nrmr[ns][:, :, :].rearrange("p k t -> p (k t)"), nrm_g.bitcast(BF16)[b4, r * 128:(r + 1) * 128, :],
                    [("nrm_g", b4)], [("nrmr", ns)], ("nrmr", ns))
                bq = tbi % 2
                bv = 2 + tbi % 2
                for kc in range(16):
                    mm(ps[bq][:, 0:256], whs[0][:, kc * 128:kc * 128 + 128], nrmr[ns][:, kc, :], kc == 0, kc == 15,
                       [("whs", 0), ("nrmr", ns)], [PSK(bq)])
                for kc in range(16):
                    mm(ps[bq][:, 256:512], whs[1][:, kc * 128:kc * 128 + 128], nrmr[ns][:, kc, :], kc == 0, kc == 15,
                       [("whs", 1), ("nrmr", ns)], [PSK(bq)])
                for tc in range(2):
                    for kc in range(16):
                        mm(ps[bv][:, tc * 128:tc * 128 + 128], nrmr[ns][:, kc, tc * 128:tc * 128 + 128],
                           whs[2][:, kc * 128:kc * 128 + 128], kc == 0, kc == 15, [("whs", 2), ("nrmr", ns)], [PSK(bv)])
                s2 = sq2_i[0] % 2; sq2_i[0] += 1
                act(sqt[s2][:, :], ps[bq][:, :], AF.Square, [PSK(bq)], [("sqt", s2)])
                mm(ps[6][:, :], ones[:, :], sqt[s2][:, :], True, True, ["ones", ("sqt", s2)], [PSK(6)])
                act(rstd[:, 0:512], ps[6][:, :], AF.Ln, [PSK(6)], ["rstd"], bias=EPS, scale=1.0 / 128)
                act(rstd[:, 0:512], rstd[:, 0:512], AF.Exp, ["rstd"], ["rstd"], scale=-0.5)
                tok0 = tb * 256
                stt(qT[:, tok0:tok0 + 256], ps[bq][:, 0:256], gq, rstd[:, 0:256], ALU.mult, ALU.mult,
                    [PSK(bq), "qknsc", "rstd"], ["qT"])
                stt(kTp[:, PADN + tok0:PADN + tok0 + 256], ps[bq][:, 256:512], gk, rstd[:, 256:512], ALU.mult, ALU.mult,
                    [PSK(bq), "qknsc", "rstd"], ["kT"])
                act(vS[:, tb * 2:tb * 2 + 2, :], ps[bv][:, 0:256].rearrange("p (c d) -> p c d", c=2), AF.Identity, [PSK(bv)], ["vS"])
                if pending_fin and tbi % 2 == 1:
                    pending_fin.pop(0)()
            if hd < 3:
                for i3 in range(3):
                    castdma(whs[i3][:, :], wh[hd + 1, i3], (), [("whs", i3)], ("whs", i3))
            ckpt(2 if hd == 0 else (5 if hd == 2 else -1))
            if not is_na:
                for which, buf, off, key in ((0, qT, 0, "qT"), (1, kTp, PADN, "kT")):
                    for pc in range(8):
                        cc0 = pc * 512
                        x0 = buf[0:32, off + cc0:off + cc0 + 512]
                        rb = 6 + pc % 2
                        mm(ps[rb][0:32, :], perm[:, :], x0, True, True, ["perm", key], [PSK(rb)])
                        tt("dve", rtmp[0][:, :], x0, ropeC[:, cc0:cc0 + 512], ALU.mult, [key, "ropeT"], [("rtmp", 0)])
                        tt("dve", rtmp[1][:, :], ps[rb][0:32, :], ropeS[:, cc0:cc0 + 512], ALU.mult, [PSK(rb), "ropeT"], [("rtmp", 1)])
                        tt("dve", x0, rtmp[0][:, :], rtmp[1][:, :], ALU.add, [("rtmp", 0), ("rtmp", 1)], [key])
            if hd == 2:
                ckpt(6)
            if DEBUG:
                dma("sp", dbg["qT"][hd], qT[:, :], ["qT"], [], "dbg")
                dma("sp", dbg["kT"][hd], kTp[:, PADN:PADN + S], ["kT"], [], "dbg")

            def tile_S(tl):
                if tl.get("pre_S"):
                    tl["pre_S"]()
                chunks = tl["chunks"]; qap = tl["q"]
                nch = len(chunks)
                sb = (pt_i[0] % 2) * 2
                pti = pt_i[0] % 2; pt_i[0] += 1
                for ci, kap in enumerate(chunks):
                    b = sb + ci // 4
                    col = (ci % 4) * 128
                    mm(ps[b][:, col:col + 128], kap, qap, True, False, ["kT", "kTpad", "kDpad", "qT"] + tl.get("kkeys", []), [PSK(b)])
                    mm(ps[b][:, col:col + 128], ident[:, :], tl["mask"](ci), False, True, ["ident"] + tl["mkeys"], [PSK(b)])
                n1 = min(nch, 4) * 128
                act(PT[pti][:, 0:n1], ps[sb][:, 0:n1], AF.Exp, [PSK(sb)], [("PT", pti)])
                if nch > 4:
                    n2 = (nch - 4) * 128
                    act(PT[pti][:, 512:512 + n2], ps[sb + 1][:, 0:n2], AF.Exp, [PSK(sb + 1)], [("PT", pti)])
                return pti, nch

            def tile_PV(tl, st):
                if tl.get("pre_PV"):
                    tl["pre_PV"]()
                pti, nch = st
                bo = 4 + po_i[0] % 2; po_i[0] += 1
                for ci in range(nch):
                    mm(ps[bo][:, 0:128], tl["v"](ci), PT[pti][:, ci * 128:ci * 128 + 128], ci == 0, ci == nch - 1,
                       tl["vkeys"] + [("PT", pti)], [PSK(bo)])
                for ci in range(nch):
                    mm(ps[bo][:, 128:256], ones[:, :], PT[pti][:, ci * 128:ci * 128 + 128], ci == 0, ci == nch - 1,
                       ["ones", ("PT", pti)], [PSK(bo)])
                src = ps[bo][:, 0:256].rearrange("p (w q) -> p w q", w=2)
                if tl["first"]:
                    cp("dve", tl["acc"], src, [PSK(bo)], ["acc_all"])
                else:
                    tt("dve", tl["acc"], src, tl["acc"], ALU.add, [PSK(bo)], ["acc_all"])

            tiles = []
            if is_na:
                for n in range(32):
                    lst = NA_TILES[n]
                    tiles.append(dict(
                        chunks=[kTp[:, PADN + c * 128:PADN + c * 128 + 128] for c, _ in lst],
                        q=qT[:, n * 128:n * 128 + 128],
                        v=(lambda ci, lst=lst: vS[:, lst[ci][0], :]), vkeys=["vS"],
                        mask=(lambda ci, lst=lst: Bt[:, lst[ci][1] * 128:lst[ci][1] * 128 + 128]), mkeys=["Bt"],
                        acc=acc[:, :, n * 128:n * 128 + 128], first=True))
            else:
                vdv = vdram.rearrange("(c p) d -> p c d", p=128)
                for q4 in range(4):
                    dma("sp", vdv[:, q4 * 8:q4 * 8 + 8, :], vS[:, q4 * 8:q4 * 8 + 8, :], ["vS"], [("vdram", q4)], "st_v")
                for Dd in (1, 4, 16):
                    L = S // Dd

                    def ld_vD(Dd=Dd):
                        na_ = (S // Dd) // 128
                        Vv = vdram.rearrange("(a i r) d -> i r a d", i=128, r=Dd)
                        for g in range(4):
                            if na_ >= 8:
                                rs = slice((8 * g) // na_, (8 * g) // na_ + 1); as_ = slice((8 * g) % na_, (8 * g) % na_ + 8)
                            else:
                                rs = slice((8 * g) // na_, (8 * g) // na_ + 8 // na_); as_ = slice(0, na_)
                            na_u = as_.stop - as_.start
                            vdk = [("vdram", q4) for q4 in range(4)]
                            for ri, r_ in enumerate(range(rs.start, rs.stop)):
                                c0 = 8 * g + ri * na_u
                                dma("sp", vD[64:128, c0:c0 + na_u, :], Vv[0:64, r_, as_], vdk,
                                    [("vDh", c) for c in range(c0, c0 + na_u)], "ld_vD")
                                dma("sp", vD[0:64, c0 + 1:c0 + 1 + na_u, :], Vv[64:128, r_, as_], vdk,
                                    [("vDl", c) for c in range(c0 + 1, c0 + 1 + na_u)], "ld_vD")

                    def mk_kD(Dd=Dd):
                        srcv = kTp[:, PADN:PADN + S].rearrange("p (u r) -> p r u", r=Dd)
                        for g in range(4):
                            nr = Dd // 4
                            cp("pool", kD[:, 64 + 1024 * g:64 + 1024 * g + 1024].rearrange("p (r u) -> p r u", r=nr),
                               srcv[:, g * nr:(g + 1) * nr, :], ["kT"], [("kD", c) for c in range(8 * g, 8 * g + 9)])

                    for n in range(32):
                        m0 = 128 * n
                        rho = m0 // L; u0 = m0 % L
                        q0 = rho + Dd * u0
                        if Dd == 1:
                            chunks = [kTp[:, PADN + m0 - 64:PADN + m0 + 64], kTp[:, PADN + m0 + 64:PADN + m0 + 192]]
                        else:
                            chunks = [kD[:, m0:m0 + 128], kD[:, m0 + 128:m0 + 256]]
                        mt = [1 if u0 == 0 else 0, 3 if u0 + 128 == L else 2]
                        tiles.append(dict(
                            chunks=chunks, q=qT[:, q0:q0 + 127 * Dd + 1:Dd],
                            kkeys=([("kD", n), ("kD", n + 1)] if Dd > 1 else []),
                            v=(lambda ci, n=n: vD[:, n + ci, :]), vkeys=[("vDh", n), ("vDl", n), ("vDh", n + 1), ("vDl", n + 1), "vDpad"],
                            mask=(lambda ci, mt=mt: dmask[:, mt[ci] * 128:mt[ci] * 128 + 128]), mkeys=["dmask"],
                            acc=acc[:, :, q0:q0 + 127 * Dd + 1:Dd], first=(Dd == 1),
                            pre_S=(mk_kD if (n == 0 and Dd > 1) else None), pre_PV=(ld_vD if n == 0 else None)))
            prev = None
            for tl in tiles:
                st = tile_S(tl)
                if prev is not None:
                    tile_PV(*prev)
                prev = (tl, st)
            tile_PV(*prev)
            def make_fin(hd=hd):
                fns = []
                for pc in range(4):
                    def piece(pc=pc):
                        sl_ = slice(pc * 1024, pc * 1024 + 1024)
                        act(acc[:, 1, sl_], acc[:, 1, sl_], AF.Ln, ["acc_all"], ["acc_all"])
                        act(acc[:, 1, sl_], acc[:, 1, sl_], AF.Exp, ["acc_all"], ["acc_all"], scale=-1.0)
                        tt("dve", oT[:, sl_], acc[:, 0, sl_], acc[:, 1, sl_], ALU.mult, ["acc_all"], ["oT", "acc_all"])
                    fns.append(piece)

                def store():
                    for tq in range(4):
                        dma("sp", o_b.bitcast(BF16)[hd, tq * 128:tq * 128 + 128, :], oT[:, tq * 1024:tq * 1024 + 1024],
                            ["oT"], [("o_b", hd)], "st_o")
                    P.add("pool", lambda e: e.collective_compute("AllGather", ALU.bypass, replica_groups=[[0, 1, 2, 3], [4, 5, 6, 7]], dma_qos=CC_QOS,
                                                                 ins=[o_b[hd].opt()], outs=[o_g[hd * 2048:(hd + 1) * 2048, :].opt()]),
                          [("o_b", hd)], [("o_g", hd)], dma="cc_o%d" % hd, inc=1)
                    if DEBUG:
                        dma("sp", dbg["oT"][hd], oT[:, :], ["oT"], [], "dbg")
                fns.append(store)
                return fns
            pending_fin.extend(make_fin())
            if hd == 3:
                while pending_fin:
                    pending_fin.pop(0)()
            ckpt({0: 3, 1: 4, 2: 10, 3: 11}[hd])

    except _Stop:
        return finish()
    P.barrier(nops)

    if STAGE == 3:
        return finish()
    jcache = {}

    def mk_ld_o(g, l):
        def ld_o(e):
            if "j" not in jcache:
                jcache["j"] = e.partition_id() % 4
            ogb = o_g.bitcast(BF16).rearrange("(h r j q) t -> h j q r t", h=4, j=4, r=4)
            src = ogb[2 * g + l, bass.ds(jcache["j"], 1)].rearrange("o q r t -> (o q) r t")
            return e.dma_start(out=xn[:, g * 8 + l:g * 8 + 8:2, :], in_=src)
        return ld_o
    for g in range(2):
        for l in range(2):
            hl = 2 * g + l
            dsem("ld_o%d" % hl)
            P.add("pool", mk_ld_o(g, l), [("o_g", hl)], [("xn", g * 8 + l + 2 * r) for r in range(4)], dma="ld_o%d" % hl, inc=16)
    cp("dve", gth[:, :], modT[:, 5 * 16:5 * 16 + 16], ["modT"], ["gth"])
    for grp in range(2):
        for k8 in range(8):
            kc = grp * 8 + k8
            s = sq_i[0] % 2; sq_i[0] += 1
            if kc % 2 == 0:
                act(sqr[s][:, :], xn[:, kc, :], AF.Square, [("xn", kc)], [("sqr", s)])
            else:
                tt("dve", sqr[s][:, :], xn[:, kc, :], xn[:, kc, :], ALU.mult, [("xn", kc)], [("sqr", s)])
            for t in range(2):
                mm(ps[2 + t][:, :], ones[:, :], sqr[s][:, t * 512:t * 512 + 512], k8 == 0, k8 == 7,
                   ["ones", ("sqr", s)], [PSK(2 + t)])
        for t in range(2):
            act(rstd[:, t * 512:t * 512 + 512], ps[2 + t][:, :], AF.Ln, [PSK(2 + t)], ["rstd"], bias=EPS, scale=1.0 / 1024)
        act(rstd[:, :], rstd[:, :], AF.Exp, ["rstd"], ["rstd"], scale=-0.5)
        for k8 in range(8):
            kc = grp * 8 + k8
            stt(xn[:, kc, :], xn[:, kc, :], gouts[:, kc:kc + 1], rstd[:, :], ALU.mult, ALU.mult,
                [("xn", kc), "gouts", "rstd"], [("xn", kc)])
        for g in (2 * grp, 2 * grp + 1):
            down_group(wo, g * 4, lambda fl, t, g=g: xn[:, g * 4 + fl, t * 512:t * 512 + 512],
                       lambda fl, g=g: [("xn", g * 4 + fl)], gth)
    if DEBUG:
        dma("sp", dbg["h2"].rearrange("(k p) t -> p k t", p=128), h[:, :, :], HALL, [], "dbg")

    if not LITE:
        norm_mod(2, 6, 7)
        ffn(w2g, w2u, w2d, 8)

    return finish()


def _tile_w(W):
    K, N = W.shape
    return np.ascontiguousarray(W.reshape(K // 128, 128, N // 128, 128).transpose(2, 1, 0, 3)).reshape(N // 128, 128, (K // 128) * 128)


def _vec16(v):
    return np.ascontiguousarray(v.reshape(-1, 128).T)


_NC_CACHE = {}


def kernel(x, c, w_ada, b_ada, g_ffn1, w1_gate, w1_up, w1_down, g_mix, w_qkv, qn_na, kn_na, qn_dil, kn_dil,
           rpb_na, g_out_na, g_out_dil, w_o, g_ffn2, w2_gate, w2_up, w2_down):
    f = lambda a: np.asarray(a, dtype=np.float32)
    x = f(x); c = f(c)
    w_ada = f(w_ada)[0]; b_ada = f(b_ada)[0]
    if not LITE:
        w1g = _tile_w(f(w1_gate)[0]); w1u = _tile_w(f(w1_up)[0]); w1d = np.ascontiguousarray(f(w1_down)[0]).reshape(NFC, 128, 2048)
        w2g = _tile_w(f(w2_gate)[0]); w2u = _tile_w(f(w2_up)[0]); w2d = np.ascontiguousarray(f(w2_down)[0]).reshape(NFC, 128, 2048)
    wqkv = f(w_qkv)[0]
    wo = np.ascontiguousarray(f(w_o)[0]).reshape(16, 128, 2048)
    gv = np.concatenate([_vec16(f(g_ffn1)[0]), _vec16(f(g_mix)[0]), _vec16(f(g_ffn2)[0])], axis=1)
    gout = np.concatenate([_vec16(f(g_out_na)[0]), _vec16(f(g_out_dil)[0])], axis=1)
    qkn = np.stack([f(qn_na)[0], f(kn_na)[0], f(qn_dil)[0], f(kn_dil)[0]], axis=1)
    rpb = f(rpb_na)[0]
    cT = np.ascontiguousarray(c.T.reshape(16, 128, 2).transpose(1, 0, 2)).reshape(128, 32)
    ropeC, ropeS = rope_tabs()
    badaT_h = np.ascontiguousarray(b_ada.reshape(144, 128).T)
    dmask = dil_masks()
    ident = np.eye(128, dtype=np.float32)
    perm32 = np.zeros((32, 32), np.float32)
    perm32[(np.arange(32) + 16) % 32, np.arange(32)] = 1.0
    in_maps = []
    for core in range(8):
        b = core // 4; j = core % 4
        heads = [2 * j, 2 * j + 1, 8 + 2 * j, 9 + 2 * j]
        whl = []
        for hh in heads:
            blk = []
            for part in range(3):
                col0 = part * D + hh * 128
                blk.append(_tile_w(wqkv[:, col0:col0 + 128])[0])
            whl.append(np.stack(blk))
        m = {
            "xT": np.ascontiguousarray(x[b, j * NT:(j + 1) * NT, :].T),
            "cT": cT,
            "wada": _tile_w(np.concatenate([w_ada[:, 12 * j * 128:(12 * j + 12) * 128],
                                            w_ada[:, (48 + 24 * j) * 128:(48 + 24 * j + 24) * 128]], axis=1)),
            "bada": badaT_h,
            "sel": np.array([[1.0 - b], [float(b)]], np.float32),
            "wh": np.stack(whl), "wo": wo, "gv": gv, "gout": gout, "qkn": np.ascontiguousarray(qkn),
            "bt": np.stack([build_bias_tiles(rpb[2 * j]), build_bias_tiles(rpb[2 * j + 1])]),
            "ropeC": ropeC, "ropeS": ropeS, "dmask": dmask, "ident": ident, "perm": perm32,
        }
        if not LITE:
            m.update({"w1g": w1g, "w1u": w1u, "w1d": w1d, "w2g": w2g, "w2u": w2u, "w2d": w2d})
        in_maps.append(m)
    if "nc" not in _NC_CACHE:
        _NC_CACHE["nc"] = build_nc()
    nc = _NC_CACHE["nc"]
    res = run_bass_kernel_spmd(nc, in_maps, core_ids=list(range(8)))
    out = np.empty((2, S, D), np.float32)
    for core in range(8):
        b = core // 4; j = core % 4
        out[b, j * NT:(j + 1) * NT, :] = res.results[core]["outT"].T
    if DEBUG:
        kernel.last = res
    return out
```

```python
import numpy as np
from contextlib import ExitStack
import concourse.bass as bass
import concourse.mybir as mybir
from concourse.bass_utils import run_bass_kernel_spmd

F32 = mybir.dt.float32
BF16 = mybir.dt.bfloat16
AF = mybir.ActivationFunctionType
ALU = mybir.AluOpType

D = 2048
NT = 1024
S = 4096
DFF = 5632
NFC = 44
EPS = 1e-6
SCALE = 128.0 ** -0.5
NEGM = -30000.0
PADN = 64
DEBUG = False
STAGE = 9
ATTSTOP = 0
CC_QOS = "P2"
NRM_NP = 4
LITE = False


class Op:
    __slots__ = ("eng", "fn", "deps", "is_dma", "semkey", "inc", "cum", "targets", "signal", "seq", "idx")


class _Stop(Exception):
    pass


def ckpt(n):
    if ATTSTOP == n:
        raise _Stop()


class Prog:
    ENGS = ("pe", "act", "dve", "pool", "sp")

    def __init__(self):
        self.ops = {e: [] for e in self.ENGS}
        self.lastw = {}
        self.readers = {}
        self.dma_count = {}
        self.last_dma = {}
        self.n = 0

    def add(self, eng, fn, reads=(), writes=(), dma=None, inc=16, extra_deps=()):
        op = Op()
        op.eng = eng; op.fn = fn; op.is_dma = dma is not None; op.semkey = dma; op.inc = inc
        op.signal = False; op.seq = 0; op.idx = self.n; self.n += 1
        deps = []
        for k in reads:
            w = self.lastw.get(k)
            if w is not None:
                deps.append((w, "raw"))
        for k in writes:
            rds = self.readers.get(k, ())
            w = self.lastw.get(k)
            if w is not None and not rds:
                deps.append((w, "waw"))
            for r in rds:
                deps.append((r, "war"))
        for d in extra_deps:
            deps.append((d, "raw"))
        for k in reads:
            self.readers.setdefault(k, []).append(op)
        for k in writes:
            self.lastw[k] = op
            self.readers[k] = []
        op.deps = deps
        op.targets = {d.semkey: self.dma_count[d.semkey] for d, _ in deps if d.is_dma}
        if op.is_dma:
            self.dma_count[dma] = self.dma_count.get(dma, 0) + inc
            op.cum = self.dma_count[dma]
            self.last_dma[dma] = op
        self.ops[eng].append(op)
        return op

    def barrier(self, mk_nop, rw=None):
        rw = rw or {}
        firsts = []
        for e in ("pe", "act", "dve", "pool"):
            rd, wr = rw.get(e, ((), ()))
            firsts.append(self.add(e, mk_nop[e], reads=list(rd), writes=[("bar1", e, self.n)] + list(wr)))
        dmas = [op for key, op in self.last_dma.items() if not str(key).startswith("cc_")]
        for e in self.ENGS:
            rd, wr = rw.get(e, ((), ()))
            self.add(e, mk_nop[e], reads=list(rd), writes=[("bar2", e, self.n)] + list(wr), extra_deps=firsts + dmas)

    def needs_wait(self, op, d, kind):
        if d.is_dma:
            return True
        if d.eng != op.eng:
            return True
        if op.eng == "pe":
            return False
        if op.is_dma:
            return True
        return kind == "raw"

    def finalize(self):
        for e in self.ENGS:
            for op in self.ops[e]:
                latest = {}
                for d, kind in op.deps:
                    if (not d.is_dma) and self.needs_wait(op, d, kind):
                        if d.eng not in latest or d.idx > latest[d.eng].idx:
                            latest[d.eng] = d
                for d in latest.values():
                    d.signal = True
                op.deps = [(d, k) for d, k in op.deps if d.is_dma or latest.get(d.eng) is d]
        for e in self.ENGS:
            s = 0
            for op in self.ops[e]:
                if op.signal and not op.is_dma:
                    s += 1
                    op.seq = s

    def emit(self, nc, block, engsem, dmasem):
        self.finalize()
        decos = {"pe": block.tensor, "act": block.scalar, "dve": block.vector, "pool": block.gpsimd, "sp": block.sync}
        for e in self.ENGS:
            ops = self.ops[e]

            def body(eng, ops=ops, e=e):
                waited = {}
                for op in ops:
                    need = {}
                    for d, kind in op.deps:
                        if not self.needs_wait(op, d, kind):
                            continue
                        if d.is_dma:
                            key = ("d", d.semkey); val = op.targets[d.semkey]; sem = dmasem[d.semkey]
                        else:
                            key = ("e", d.eng); val = d.seq; sem = engsem[d.eng]
                        if val > need.get(key, (0, None))[0]:
                            need[key] = (val, sem)
                    for key, (val, sem) in need.items():
                        if waited.get(key, 0) >= val:
                            continue
                        eng.wait_ge(sem, val)
                        waited[key] = val
                    inst = op.fn(eng)
                    if inst is None:
                        continue
                    if op.is_dma:
                        inst.then_inc(dmasem[op.semkey], op.inc)
                    elif op.signal:
                        inst.then_inc(engsem[e], 1)
            decos[e](body)


def na_tile_tables():
    types = {}
    tiles = []
    for n in range(32):
        rows = []
        for rq in (2 * n, 2 * n + 1):
            rs = min(max(rq - 4, 0), 56)
            rows.append((rq, rs))
        lo = min(r[1] for r in rows) // 2
        hi = (max(r[1] for r in rows) + 7) // 2
        lst = []
        for c in range(lo, hi + 1):
            key = []
            for rkl in range(2):
                for rql in range(2):
                    rk = 2 * c + rkl
                    rq, rs = rows[rql]
                    if rs <= rk < rs + 8:
                        key.append(rk - rq + 7)
                    else:
                        key.append(-1)
            key = tuple(key)
            if all(k < 0 for k in key):
                continue
            if key not in types:
                types[key] = len(types)
            lst.append((c, types[key]))
        tiles.append(lst)
    return tiles, types


NA_TILES, NA_TYPES = na_tile_tables()
NTYPES = len(NA_TYPES)


def build_bias_tiles(rpb_h):
    out = np.full((128, NTYPES, 128), NEGM, np.float32)
    ck = np.arange(64)[:, None]
    cq = np.arange(64)[None, :]
    cs = np.clip(cq - 8, 0, 48)
    cvalid = (ck >= cs) & (ck < cs + 16)
    coff = np.clip(ck - cq + 15, 0, 30)
    for key, t in NA_TYPES.items():
        i = 0
        for rkl in range(2):
            for rql in range(2):
                a = key[i]; i += 1
                if a < 0:
                    continue
                blk = np.where(cvalid, rpb_h[a][coff], np.float32(NEGM))
                out[64 * rkl:64 * rkl + 64, t, 64 * rql:64 * rql + 64] = blk
    return out.reshape(128, NTYPES * 128)


def dil_masks():
    i = np.arange(128)[:, None]
    j = np.arange(128)[None, :]
    A = np.where(j <= i, 0.0, NEGM)
    B = np.where(j >= i, 0.0, NEGM)
    A1 = A.copy(); A1[:64, :] = NEGM
    B1 = B.copy(); B1[64:, :] = NEGM
    return np.concatenate([A, A1, B, B1], axis=1).astype(np.float32)


def rope_tabs():
    pos = np.arange(S, dtype=np.float32)
    inv = np.power(np.float32(500000.0), -np.arange(0, 32, 2, dtype=np.float32) / np.float32(32)).astype(np.float32)
    ang = (pos[None, :] * inv[:, None]).astype(np.float32)
    c = np.cos(ang).astype(np.float32); s = np.sin(ang).astype(np.float32)
    C = np.concatenate([c, c], axis=0)
    Sg = np.concatenate([-s, s], axis=0)
    return C, Sg


def build_nc():
    nc = bass.Bass("TRN2", target_bir_lowering=False)
    P = Prog()

    def din(name, shape, dt=F32):
        return nc.dram_tensor(name, list(shape), dt, kind="ExternalInput").ap()

    xT = din("xT", [D, NT]); cT = din("cT", [128, 32]); wada = din("wada", [36, 128, 2048]); bada = din("bada", [128, 144])
    sel = din("sel", [2, 1])
    if not LITE:
        w1g = din("w1g", [NFC, 128, 2048]); w1u = din("w1u", [NFC, 128, 2048]); w1d = din("w1d", [NFC, 128, 2048])
        w2g = din("w2g", [NFC, 128, 2048]); w2u = din("w2u", [NFC, 128, 2048]); w2d = din("w2d", [NFC, 128, 2048])
    wh = din("wh", [4, 3, 128, 2048]); wo = din("wo", [16, 128, 2048])
    gv = din("gv", [128, 48]); goutd = din("gout", [128, 16]); qknd = din("qkn", [128, 4])
    btd = din("bt", [2, 128, NTYPES * 128]); ropeCd = din("ropeC", [32, S]); ropeSd = din("ropeS", [32, S])
    dmaskd = din("dmask", [128, 512]); identd = din("ident", [128, 128]); permd = din("perm", [32, 32])
    outT = nc.dram_tensor("outT", [D, NT], F32, kind="ExternalOutput").ap()
    dbg = {}
    if DEBUG:
        dbg["h1"] = nc.dram_tensor("dbg_h1", [D, NT], F32, kind="ExternalOutput").ap()
        dbg["modT"] = nc.dram_tensor("dbg_modT", [128, 144], F32, kind="ExternalOutput").ap()
        dbg["oT"] = nc.dram_tensor("dbg_oT", [4, 128, S], BF16, kind="ExternalOutput").ap()
        dbg["nrm"] = nc.dram_tensor("dbg_nrm", [D, NT], BF16, kind="ExternalOutput").ap()
        dbg["qT"] = nc.dram_tensor("dbg_qT", [4, 128, S], BF16, kind="ExternalOutput").ap()
        dbg["kT"] = nc.dram_tensor("dbg_kT", [4, 128, S], BF16, kind="ExternalOutput").ap()
        dbg["h2"] = nc.dram_tensor("dbg_h2", [D, NT], F32, kind="ExternalOutput").ap()

    mod_bA = nc.dram_tensor("mod_bA", [2, 1536], F32).ap()
    mod_gA = nc.dram_tensor("mod_gA", [8, 1536], F32).ap()
    mod_bB = nc.dram_tensor("mod_bB", [2, 3072], F32).ap()
    mod_gB = nc.dram_tensor("mod_gB", [8, 3072], F32).ap()
    nrm_b = nc.dram_tensor("nrm_b", [4, 128, 2048], F32).ap()
    nrm_g = nc.dram_tensor("nrm_g", [4, 512, 2048], F32).ap()
    vdram = nc.dram_tensor("vdram", [S, 128], BF16).ap()
    o_b = nc.dram_tensor("o_b", [4, 512, 512], F32).ap()
    o_g = nc.dram_tensor("o_g", [16 * 512, 512], F32).ap()

    es = ExitStack()
    ARENA = 212480
    es.enter_context(nc.sbuf_tensor("arena", [128, ARENA + 64], mybir.dt.uint8))
    base0 = (nc.sbuf_base - (ARENA + 64) + 31) // 32 * 32
    cur = [base0]

    def salloc(name, shape, dt, at=None):
        nbytes = int(np.prod(shape[1:])) * (4 if dt == F32 else 2)
        nbytes = (nbytes + 31) // 32 * 32
        if at is None:
            off = cur[0]; cur[0] += nbytes
        else:
            off = at
        assert off + nbytes <= base0 + ARENA, (name, off + nbytes - base0, ARENA)
        return nc.alloc_sbuf_tensor_at(name, list(shape), dt, offset=off), off + nbytes

    h, _ = salloc("h", [128, 16, NT], F32)
    modT, _ = salloc("modT", [128, 144], F32)
    gvs, _ = salloc("gvs", [128, 48], F32)
    gouts, _ = salloc("gouts", [128, 16], F32)
    qkns, _ = salloc("qkns", [128, 4], F32)
    qknsc, _ = salloc("qknsc", [128, 4], F32)
    Avec, _ = salloc("Avec", [128, 16], F32)
    tmpA, _ = salloc("tmpA", [128, 16], F32)
    gth, _ = salloc("gth", [128, 16], F32)
    ident, _ = salloc("ident", [128, 128], BF16)
    ones, _ = salloc("ones", [128, 128], BF16)
    rstd, _ = salloc("rstd", [128, NT], F32)
    sels, _ = salloc("sels", [2, 1], F32)
    scr, _ = salloc("scr", [128, 8], F32)
    R0 = cur[0]
    xn, e1 = salloc("xn", [128, 16, NT], BF16)
    wgu = []
    for i in range(6):
        t, _ = salloc(f"wgu{i}", [128, 2048], BF16); wgu.append(t)
    wdr = []
    for i in range(6):
        t, _ = salloc(f"wd{i}", [128, 2048], BF16); wdr.append(t)
    hid = []
    hid_off = cur[0]
    for i in range(3):
        t, _ = salloc(f"hid{i}", [128, 4, NT], BF16); hid.append(t)
    sgt = []
    for i in range(2):
        t, _ = salloc(f"sgt{i}", [128, 512], F32); sgt.append(t)
    sqr = []
    for i in range(2):
        t, _ = salloc(f"sqr{i}", [128, NT], BF16); sqr.append(t)
    tmpf = []
    for i in range(2):
        t, _ = salloc(f"tmpf{i}", [128, NT], F32); tmpf.append(t)
    cTf, _ = salloc("cTf", [128, 32], F32)
    csb, _ = salloc("csb", [128, 32], BF16)
    badaT, _ = salloc("badaT", [128, 144], F32)
    stg = []
    for i in range(2):
        t, _ = salloc(f"stg{i}", [2, 512], F32); stg.append(t)
    stgr = []
    for i in range(2):
        t, _ = salloc(f"stgr{i}", [2, 512], F32); stgr.append(t)
    ffn_end = cur[0]
    cur[0] = R0
    nrmr = []
    for i in range(2):
        t, _ = salloc(f"nrmr{i}", [128, 16, 256], BF16); nrmr.append(t)
    whs = []
    for i in range(3):
        t, _ = salloc(f"whs{i}", [128, 2048], BF16); whs.append(t)
    qT, _ = salloc("qT", [128, S], BF16)
    kTp, _ = salloc("kTp", [128, S + 2 * PADN], BF16)
    vS, _ = salloc("vS", [128, 32, 128], BF16)
    vD, _ = salloc("vD", [128, 33, 128], BF16)
    acc, _ = salloc("acc", [128, 2, S], F32)
    PT = []
    for i in range(2):
        t, _ = salloc(f"PT{i}", [128, 640], BF16); PT.append(t)
    Bt, _ = salloc("Bt", [128, NTYPES * 128], BF16)
    oT, _ = salloc("oT", [128, S], BF16)
    ropeC, _ = salloc("ropeCt", [32, S], BF16)
    ropeS, _ = salloc("ropeSt", [32, S], BF16)
    kD, _ = salloc("kD", [128, S + 128], BF16)
    perm, _ = salloc("perm", [32, 32], BF16)
    rtmp = []
    for i in range(2):
        t, _ = salloc(f"rtmp{i}", [32, 512], F32); rtmp.append(t)
    sqt = []
    for i in range(2):
        t, _ = salloc(f"sqt{i}", [128, 512], BF16); sqt.append(t)
    dmask, _ = salloc("dmask", [128, 512], BF16)
    att_end = cur[0]
    assert max(att_end, ffn_end) <= base0 + ARENA

    ps = [es.enter_context(nc.psum_tensor(f"ps{i}", [128, 512], F32)) for i in range(8)]
    psS = []
    engsem = {e: es.enter_context(nc.semaphore(f"sem_{e}")) for e in ("pe", "act", "dve", "pool")}
    dmasem = {}

    def dsem(key):
        if key not in dmasem:
            dmasem[key] = es.enter_context(nc.semaphore("d_" + str(key)))
        return key

    PSK = lambda b: ("ps", b)
    HK = lambda kc: [("h", kc, 0), ("h", kc, 1)]
    HALL = [("h", kc, t) for kc in range(16) for t in range(2)]

    def dma(eng, out, in_, reads, writes, key, **kw):
        dsem(key)
        return P.add(eng, lambda e: e.dma_start(out=out, in_=in_, **kw), reads, writes, dma=key)

    def castdma(out, in_, reads, writes, key):
        return dma("pool", out, in_, reads, writes, key, max_dma_last_dim=8192)

    def mm(out, lhsT, rhs, start, stop, reads, writes, **kw):
        return P.add("pe", lambda e: e.matmul(out, lhsT, rhs, start=start, stop=stop, **kw), reads, writes)

    def act(out, in_, func, reads, writes, bias=None, scale=None):
        kw = {}
        if bias is not None:
            kw["bias"] = bias
        if scale is not None:
            kw["scale"] = scale
        return P.add("act", lambda e: e.activation(out, in_, func, **kw), reads, writes)

    def tt(eng, out, in0, in1, op, reads, writes):
        return P.add(eng, lambda e: e.tensor_tensor(out, in0, in1, op), reads, writes)

    def stt(out, in0, scalar, in1, op0, op1, reads, writes):
        return P.add("dve", lambda e: e.scalar_tensor_tensor(out, in0, scalar, in1, op0, op1), reads, writes)

    def ts(eng, out, in0, s1, s2, op0, op1, reads, writes):
        return P.add(eng, lambda e: e.tensor_scalar(out, in0, s1, s2, op0, op1), reads, writes)

    def cp(eng, out, in_, reads, writes):
        return P.add(eng, lambda e: e.tensor_copy(out, in_), reads, writes)

    def memset(eng, ap, val, writes):
        return P.add(eng, lambda e: e.memset(ap, val), (), writes)

    dma("sp", h[:, :, :], xT.rearrange("(k p) t -> p k t", p=128), (), HALL, "ld_h")
    dma("sp", cTf[:, :], cT, (), ["cTf"], "ld_c")
    dma("sp", badaT[:, :], bada, (), ["badaT"], "ld_c")
    dma("sp", sels[:, :], sel, (), ["sels"], "ld_c")
    dma("sp", gvs[:, :], gv, (), ["gvs"], "ld_c")
    dma("sp", gouts[:, :], goutd, (), ["gouts"], "ld_c")
    dma("sp", qkns[:, :], qknd, (), ["qkns"], "ld_c")
    castdma(ident[:, :], identd, (), ["ident"], "ld_id")
    memset("dve", ones[:, :], 1.0, ["ones"])

    act(csb[:, :], cTf[:, :], AF.Silu, ["cTf"], ["csb"])
    wgu_i = [0]

    def next_wgu():
        i = wgu_i[0] % 6; wgu_i[0] += 1
        return i

    stg_i = [0]; stgr_i = [0]

    def ada_chunk(c, c0, bank, dst, dkey):
        sl = next_wgu()
        castdma(wgu[sl][:, :], wada[c], (), [("wgu", sl)], ("wgu", sl))
        lc = c - c0
        for kc in range(16):
            mm(ps[bank][0:2, (lc % 4) * 128:(lc % 4) * 128 + 128], csb[:, kc * 2:kc * 2 + 2], wgu[sl][:, kc * 128:kc * 128 + 128],
               kc == 0, kc == 15, ["csb", ("wgu", sl)], [PSK(bank)])
        if lc % 4 == 3:
            s_ = stg_i[0] % 2; stg_i[0] += 1
            cp("dve", stg[s_][:, :], ps[bank][0:2, 0:512], [PSK(bank)], [("stg", s_)])
            dma("sp", dst[:, (lc // 4) * 512:(lc // 4) * 512 + 512], stg[s_][:, :], [("stg", s_)], [dkey], ("stgd", s_))

    def ada_gather(src, dst, skey, gkey, sem):
        dsem(sem)
        P.add("pool", lambda e: e.collective_compute("AllGather", ALU.bypass, replica_groups=[[0, 1, 2, 3], [4, 5, 6, 7]], dma_qos=CC_QOS,
                                                     ins=[src.opt()], outs=[dst.opt()]),
              [skey], [gkey], dma=sem, inc=1)

    def ada_transposes(gathered, gkey, nchunk, gi_base, bank):
        for r in range(4):
            for grp in range(nchunk // 4):
                s_ = stgr_i[0] % 2; stgr_i[0] += 1
                dma("sp", stgr[s_][:, :], gathered[2 * r:2 * r + 2, grp * 512:grp * 512 + 512], [gkey], [("stgr", s_)], ("stgrd", s_))
                for k in range(4):
                    gi = gi_base + r * nchunk + grp * 4 + k
                    mm(ps[bank][:, gi:gi + 1], stgr[s_][:, k * 128:k * 128 + 128], sels[:, :], True, True,
                       [("stgr", s_), "sels"], [PSK(bank)])
        lo, hi = gi_base, gi_base + 4 * nchunk
        tt("dve", modT[:, lo:hi], ps[bank][:, lo:hi], badaT[:, lo:hi], ALU.add, [PSK(bank), "badaT"], ["modT"])

    for c in range(12):
        ada_chunk(c, 0, 0, mod_bA, "mod_bA")
    ada_gather(mod_bA, mod_gA, "mod_bA", "mod_gA", "cc_modA")
    ada_transposes(mod_gA, "mod_gA", 12, 0, 1)
    adaB = {fc: (lambda c=12 + fc: ada_chunk(c, 12, 7, mod_bB, "mod_bB")) for fc in range(24)}
    adaB[24] = lambda: ada_gather(mod_bB, mod_gB, "mod_bB", "mod_gB", "cc_modB")
    if LITE:
        for fc in range(25):
            adaB[fc]()
        ada_transposes(mod_gB, "mod_gB", 24, 48, 7)
    if DEBUG:
        dma("sp", dbg["modT"], modT[:, :], ["modT"], [], "dbg")
    cp("dve", qknsc[:, :], qkns[:, :], ["qkns"], ["qknsc"])
    ts("dve", qknsc[:, 0:1], qkns[:, 0:1], SCALE, None, ALU.mult, ALU.bypass, ["qkns", "qknsc"], ["qknsc"])
    ts("dve", qknsc[:, 2:3], qkns[:, 2:3], SCALE, None, ALU.mult, ALU.bypass, ["qkns", "qknsc"], ["qknsc"])

    MS = lambda m, kc: modT[:, m * 16 + kc:m * 16 + kc + 1]

    sq_i = [0]; tf_i = [0]

    xnB = xn[:, :, :].rearrange("p k t -> p (k t)").rearrange("p (b k t) -> p b k t", b=4, k=16)

    def norm_mod(gidx, m_sh, m_sc, blockmajor=False):
        ts("dve", tmpA[:, :], modT[:, m_sc * 16:m_sc * 16 + 16], 1.0, None, ALU.add, ALU.bypass, ["modT"], ["tmpA"])
        tt("dve", Avec[:, :], tmpA[:, :], gvs[:, gidx * 16:gidx * 16 + 16], ALU.mult, ["tmpA", "gvs"], ["Avec"])
        for kc in range(16):
            s = sq_i[0] % 2; sq_i[0] += 1
            if kc % 2 == 0:
                act(sqr[s][:, :], h[:, kc, :], AF.Square, HK(kc), [("sqr", s)])
            else:
                tt("dve", sqr[s][:, :], h[:, kc, :], h[:, kc, :], ALU.mult, HK(kc), [("sqr", s)])
            for t in range(2):
                mm(ps[2 + t][:, :], ones[:, :], sqr[s][:, t * 512:t * 512 + 512], kc == 0, kc == 15,
                   ["ones", ("sqr", s)], [PSK(2 + t)])
        for t in range(2):
            act(rstd[:, t * 512:t * 512 + 512], ps[2 + t][:, :], AF.Ln, [PSK(2 + t)], ["rstd"], bias=EPS, scale=1.0 / D)
        act(rstd[:, :], rstd[:, :], AF.Exp, ["rstd"], ["rstd"], scale=-0.5)
        for kc in range(16):
            s = tf_i[0] % 2; tf_i[0] += 1
            stt(tmpf[s][:, :], h[:, kc, :], Avec[:, kc:kc + 1], rstd[:, :], ALU.mult, ALU.mult,
                HK(kc) + ["Avec", "rstd"], [("tmpf", s)])
            if blockmajor:
                act(xnB[:, :, kc, :], tmpf[s][:, :].rearrange("p (b t) -> p b t", b=4), AF.Identity,
                    [("tmpf", s), "modT"], [("xn", kc)], bias=MS(m_sh, kc), scale=1.0)
            else:
                act(xn[:, kc, :], tmpf[s][:, :], AF.Identity, [("tmpf", s), "modT"], [("xn", kc)], bias=MS(m_sh, kc), scale=1.0)

    wd_i = [0]
    pd_i = [0]

    def down_group(wsrc, fc0, rhs_fn, rhs_keys, scal):
        slots = []
        for fl in range(4):
            sl = wd_i[0] % 6; wd_i[0] += 1
            castdma(wdr[sl][:, :], wsrc[fc0 + fl], (), [("wd", sl)], ("wd", sl))
            slots.append(sl)
        for dc in range(16):
            for t in range(2):
                b = 4 + pd_i[0] % 4; pd_i[0] += 1
                for fl in range(4):
                    mm(ps[b][:, :], wdr[slots[fl]][:, dc * 128:dc * 128 + 128], rhs_fn(fl, t), fl == 0, fl == 3,
                       [("wd", slots[fl])] + rhs_keys(fl), [PSK(b)])
                stt(h[:, dc, t * 512:t * 512 + 512], ps[b][:, :], scal[:, dc:dc + 1], h[:, dc, t * 512:t * 512 + 512],
                    ALU.mult, ALU.add, [PSK(b), "gth", ("h", dc, t)], [("h", dc, t)])

    def ffn(wg, wu, wd_, m_gt, side=None):
        ts("dve", gth[:, :], modT[:, m_gt * 16:m_gt * 16 + 16], 0.5, None, ALU.mult, ALU.bypass, ["modT"], ["gth"])
        gu_i = 0
        for fc in range(NFC):
            if side and fc in side:
                side[fc]()
            g = fc // 4
            hs = g % 3
            sg_ = next_wgu(); su_ = next_wgu()
            castdma(wgu[sg_][:, :], wg[fc], (), [("wgu", sg_)], ("wgu", sg_))
            castdma(wgu[su_][:, :], wu[fc], (), [("wgu", su_)], ("wgu", su_))
            for t in range(2):
                bg = (gu_i % 2) * 2; bu = bg + 1; gu_i += 1
                for kc in range(16):
                    mm(ps[bg][:, :], wgu[sg_][:, kc * 128:kc * 128 + 128], xn[:, kc, t * 512:t * 512 + 512], kc == 0, kc == 15,
                       [("wgu", sg_), ("xn", kc)], [PSK(bg)])
                for kc in range(16):
                    mm(ps[bu][:, :], wgu[su_][:, kc * 128:kc * 128 + 128], xn[:, kc, t * 512:t * 512 + 512], kc == 0, kc == 15,
                       [("wgu", su_), ("xn", kc)], [PSK(bu)])
                st = gu_i % 2
                act(sgt[st][:, :], ps[bg][:, :], AF.Silu, [PSK(bg)], [("sgt", st)])
                tt("dve", hid[hs][:, fc % 4, t * 512:t * 512 + 512], sgt[st][:, :], ps[bu][:, :], ALU.mult,
                   [("sgt", st), PSK(bu)], [("hid", hs, fc % 4)])
            if fc % 4 == 3 and g >= 1:
                gp = g - 1
                down_group(wd_, gp * 4, lambda fl, t, gp=gp: hid[gp % 3][:, fl, t * 512:t * 512 + 512],
                           lambda fl, gp=gp: [("hid", gp % 3, fl)], gth)
        gp = NFC // 4 - 1
        down_group(wd_, gp * 4, lambda fl, t, gp=gp: hid[gp % 3][:, fl, t * 512:t * 512 + 512],
                   lambda fl, gp=gp: [("hid", gp % 3, fl)], gth)

    def finish():
        dma("sp", outT.rearrange("(k p) t -> p k t", p=128), h[:, :, :], HALL, ["out"], "st_out")
        P.add("sp", lambda e: e.nop(), ["out"], [])
        P.add("sp", lambda e: e.nop(), [], [], extra_deps=list(P.last_dma.values()))
        with es:
            with nc.Block() as block:
                P.emit(nc, block, engsem, dmasem)
        return nc

    if not LITE:
        norm_mod(0, 0, 1)
        ffn(w1g, w1u, w1d, 2, side=adaB)
        ada_transposes(mod_gB, "mod_gB", 24, 48, 7)
    if DEBUG:
        dma("sp", dbg["h1"].rearrange("(k p) t -> p k t", p=128), h[:, :, :], HALL, [], "dbg")

    if STAGE == 1:
        return finish()
    norm_mod(1, 3, 4, blockmajor=True)
    if DEBUG:
        dma("sp", dbg["nrm"].rearrange("(k p) (b t) -> p b k t", p=128, b=4), xnB, [("xn", kc) for kc in range(16)], [], "dbg")
    for b4 in range(4):
        dsem("cc_nrm%d" % b4)
        dma("sp", nrm_b.bitcast(BF16)[b4], xnB[:, b4].rearrange("p k t -> p (k t)"),
            [("xn", kc) for kc in range(16)], [("nrm_b", b4)], "st_nrm")

    def nrm_gather(b4):
        P.add("pool", lambda e, b4=b4: e.collective_compute("AllGather", ALU.bypass, replica_groups=[[0, 1, 2, 3], [4, 5, 6, 7]], dma_qos=CC_QOS,
                                                            ins=[nrm_b[b4].opt()], outs=[nrm_g[b4].opt()]),
              [("nrm_b", b4)], [("nrm_g", b4)], dma="cc_nrm%d" % b4, inc=1)
    nrm_gather(0)

    if STAGE == 2:
        return finish()
    nops = {
        "pe": lambda e: e.matmul(ps[7][0:1, 0:1], sels[:, :], sels[:, :], start=True, stop=True),
        "act": lambda e: e.activation(scr[:, 0:1], gvs[:, 0:1], AF.Identity),
        "dve": lambda e: e.memset(scr[:, 1:2], 0.0),
        "pool": lambda e: e.memset(scr[:, 2:3], 0.0),
        "sp": lambda e: e.nop(),
    }
    nop_rw = {"pe": (["sels"], [PSK(7)]), "act": (["gvs"], [("scr", "act")]), "dve": ((), [("scr", "dve")]),
              "pool": ((), [("scr", "pool")])}
    P.barrier(nops, nop_rw)

    try:
        castdma(dmask[:, :], dmaskd, (), ["dmask"], "ld_dm")
        castdma(perm[:, :], permd, (), ["perm"], "ld_dm")
        for pc in range(4):
            castdma(ropeC[:, pc * 1024:pc * 1024 + 1024], ropeCd[:, pc * 1024:pc * 1024 + 1024], (), ["ropeT"], "ld_dm")
            castdma(ropeS[:, pc * 1024:pc * 1024 + 1024], ropeSd[:, pc * 1024:pc * 1024 + 1024], (), ["ropeT"], "ld_dm")
        memset("pool", kD[:, 0:64], 0.0, ["kDpad"])
        memset("pool", kD[:, 64 + S:128 + S], 0.0, ["kDpad"])
        memset("pool", kTp[:, 0:PADN], 0.0, ["kTpad"])
        memset("pool", kTp[:, PADN + S:PADN + S + PADN], 0.0, ["kTpad"])
        memset("pool", vD[0:64, 0, :], 0.0, ["vDpad"])
        memset("pool", vD[64:128, 32, :], 0.0, ["vDpad"])
        nr_i = [0]; sq2_i = [0]; pt_i = [0]; po_i = [0]; rp_i = [0]
        ckpt(1)

        for hd_ in range(4):
            dsem("cc_o%d" % hd_)
        pending_fin = []
        for i3 in range(3):
            castdma(whs[i3][:, :], wh[0, i3], (), [("whs", i3)], ("whs", i3))
        castdma(Bt[:, :], btd[0], (), ["Bt"], "ld_bt")
        for b4 in range(1, 4):
            nrm_gather(b4)
        for hd in range(4):
            is_na = hd < 2
            if hd == 1:
                castdma(Bt[:, :], btd[hd], (), ["Bt"], "ld_bt")
            gq = qknsc[:, 0:1] if is_na else qknsc[:, 2:3]
            gk = qknsc[:, 1:2] if is_na else qknsc[:, 3:4]
            for tbi in range(16):
                b4 = tbi // 4; r = tbi % 4
                tb = r * 4 + b4
                ns = nr_i[0] % 2; nr_i[0] += 1
                dma("sp", nrmr[ns][:, :, :].rearrange("p k t -> p (k t)"), nrm_g.bitcast(BF16)[b4, r * 128:(r + 1) * 128, :],
                    [("nrm_g", b4)], [("nrmr", ns)], ("nrmr", ns))
                bq = tbi % 2
                bv = 2 + tbi % 2
                for kc in range(16):
                    mm(ps[bq][:, 0:256], whs[0][:, kc * 128:kc * 128 + 128], nrmr[ns][:, kc, :], kc == 0, kc == 15,
                       [("whs", 0), ("nrmr", ns)], [PSK(bq)])
                for kc in range(16):
                    mm(ps[bq][:, 256:512], whs[1][:, kc * 128:kc * 128 + 128], nrmr[ns][:, kc, :], kc == 0, kc == 15,
                       [("whs", 1), ("nrmr", ns)], [PSK(bq)])
                for tc in range(2):
                    for kc in range(16):
                        mm(ps[bv][:, tc * 128:tc * 128 + 128], nrmr[ns][:, kc, tc * 128:tc * 128 + 128],
                           whs[2][:, kc * 128:kc * 128 + 128], kc == 0, kc == 15, [("whs", 2), ("nrmr", ns)], [PSK(bv)])
                s2 = sq2_i[0] % 2; sq2_i[0] += 1
                act(sqt[s2][:, :], ps[bq][:, :], AF.Square, [PSK(bq)], [("sqt", s2)])
                mm(ps[6][:, :], ones[:, :], sqt[s2][:, :], True, True, ["ones", ("sqt", s2)], [PSK(6)])
                act(rstd[:, 0:512], ps[6][:, :], AF.Ln, [PSK(6)], ["rstd"], bias=EPS, scale=1.0 / 128)
                act(rstd[:, 0:512], rstd[:, 0:512], AF.Exp, ["rstd"], ["rstd"], scale=-0.5)
                tok0 = tb * 256
                stt(qT[:, tok0:tok0 + 256], ps[bq][:, 0:256], gq, rstd[:, 0:256], ALU.mult, ALU.mult,
                    [PSK(bq), "qknsc", "rstd"], ["qT"])
                stt(kTp[:, PADN + tok0:PADN + tok0 + 256], ps[bq][:, 256:512], gk, rstd[:, 256:512], ALU.mult, ALU.mult,
                    [PSK(bq), "qknsc", "rstd"], ["kT"])
                act(vS[:, tb * 2:tb * 2 + 2, :], ps[bv][:, 0:256].rearrange("p (c d) -> p c d", c=2), AF.Identity, [PSK(bv)], ["vS"])
                if pending_fin and tbi % 2 == 1:
                    pending_fin.pop(0)()
            if hd < 3:
                for i3 in range(3):
                    castdma(whs[i3][:, :], wh[hd + 1, i3], (), [("whs", i3)], ("whs", i3))
            ckpt(2 if hd == 0 else (5 if hd == 2 else -1))
            if not is_na:
                for which, buf, off, key in ((0, qT, 0, "qT"), (1, kTp, PADN, "kT")):
                    for pc in range(8):
                        cc0 = pc * 512
                        x0 = buf[0:32, off + cc0:off + cc0 + 512]
                        rb = 6 + pc % 2
                        mm(ps[rb][0:32, :], perm[:, :], x0, True, True, ["perm", key], [PSK(rb)])
                        tt("dve", rtmp[0][:, :], x0, ropeC[:, cc0:cc0 + 512], ALU.mult, [key, "ropeT"], [("rtmp", 0)])
                        tt("dve", rtmp[1][:, :], ps[rb][0:32, :], ropeS[:, cc0:cc0 + 512], ALU.mult, [PSK(rb), "ropeT"], [("rtmp", 1)])
                        tt("dve", x0, rtmp[0][:, :], rtmp[1][:, :], ALU.add, [("rtmp", 0), ("rtmp", 1)], [key])
            if hd == 2:
                ckpt(6)
            if DEBUG:
                dma("sp", dbg["qT"][hd], qT[:, :], ["qT"], [], "dbg")
                dma("sp", dbg["kT"][hd], kTp[:, PADN:PADN + S], ["kT"], [], "dbg")

            def tile_S(tl):
                if tl.get("pre_S"):
                    tl["pre_S"]()
                chunks = tl["chunks"]; qap = tl["q"]
                nch = len(chunks)
                sb = (pt_i[0] % 2) * 2
                pti = pt_i[0] % 2; pt_i[0] += 1
                for ci, kap in enumerate(chunks):
                    b = sb + ci // 4
                    col = (ci % 4) * 128
                    mm(ps[b][:, col:col + 128], kap, qap, True, False, ["kT", "kTpad", "kDpad", "qT"] + tl.get("kkeys", []), [PSK(b)])
                    mm(ps[b][:, col:col + 128], ident[:, :], tl["mask"](ci), False, True, ["ident"] + tl["mkeys"], [PSK(b)])
                n1 = min(nch, 4) * 128
                act(PT[pti][:, 0:n1], ps[sb][:, 0:n1], AF.Exp, [PSK(sb)], [("PT", pti)])
                if nch > 4:
                    n2 = (nch - 4) * 128
                    act(PT[pti][:, 512:512 + n2], ps[sb + 1][:, 0:n2], AF.Exp, [PSK(sb + 1)], [("PT", pti)])
                return pti, nch

            def tile_PV(tl, st):
                if tl.get("pre_PV"):
                    tl["pre_PV"]()
                pti, nch = st
                bo = 4 + po_i[0] % 2; po_i[0] += 1
                for ci in range(nch):
                    mm(ps[bo][:, 0:128], tl["v"](ci), PT[pti][:, ci * 128:ci * 128 + 128], ci == 0, ci == nch - 1,
                       tl["vkeys"] + [("PT", pti)], [PSK(bo)])
                for ci in range(nch):
                    mm(ps[bo][:, 128:256], ones[:, :], PT[pti][:, ci * 128:ci * 128 + 128], ci == 0, ci == nch - 1,
                       ["ones", ("PT", pti)], [PSK(bo)])
                src = ps[bo][:, 0:256].rearrange("p (w q) -> p w q", w=2)
                if tl["first"]:
                    cp("dve", tl["acc"], src, [PSK(bo)], ["acc_all"])
                else:
                    tt("dve", tl["acc"], src, tl["acc"], ALU.add, [PSK(bo)], ["acc_all"])

            tiles = []
            if is_na:
                for n in range(32):
                    lst = NA_TILES[n]
                    tiles.append(dict(
                        chunks=[kTp[:, PADN + c * 128:PADN + c * 128 + 128] for c, _ in lst],
                        q=qT[:, n * 128:n * 128 + 128],
                        v=(lambda ci, lst=lst: vS[:, lst[ci][0], :]), vkeys=["vS"],
                        mask=(lambda ci, lst=lst: Bt[:, lst[ci][1] * 128:lst[ci][1] * 128 + 128]), mkeys=["Bt"],
                        acc=acc[:, :, n * 128:n * 128 + 128], first=True))
            else:
                vdv = vdram.rearrange("(c p) d -> p c d", p=128)
                for q4 in range(4):
                    dma("sp", vdv[:, q4 * 8:q4 * 8 + 8, :], vS[:, q4 * 8:q4 * 8 + 8, :], ["vS"], [("vdram", q4)], "st_v")
                for Dd in (1, 4, 16):
                    L = S // Dd

                    def ld_vD(Dd=Dd):
                        na_ = (S // Dd) // 128
                        Vv = vdram.rearrange("(a i r) d -> i r a d", i=128, r=Dd)
                        for g in range(4):
                            if na_ >= 8:
                                rs = slice((8 * g) // na_, (8 * g) // na_ + 1); as_ = slice((8 * g) % na_, (8 * g) % na_ + 8)
                            else:
                                rs = slice((8 * g) // na_, (8 * g) // na_ + 8 // na_); as_ = slice(0, na_)
                            na_u = as_.stop - as_.start
                            vdk = [("vdram", q4) for q4 in range(4)]
                            for ri, r_ in enumerate(range(rs.start, rs.stop)):
                                c0 = 8 * g + ri * na_u
                                dma("sp", vD[64:128, c0:c0 + na_u, :], Vv[0:64, r_, as_], vdk,
                                    [("vDh", c) for c in range(c0, c0 + na_u)], "ld_vD")
                                dma("sp", vD[0:64, c0 + 1:c0 + 1 + na_u, :], Vv[64:128, r_, as_], vdk,
                                    [("vDl", c) for c in range(c0 + 1, c0 + 1 + na_u)], "ld_vD")

                    def mk_kD(Dd=Dd):
                        srcv = kTp[:, PADN:PADN + S].rearrange("p (u r) -> p r u", r=Dd)
                        for g in range(4):
                            nr = Dd // 4
                            cp("pool", kD[:, 64 + 1024 * g:64 + 1024 * g + 1024].rearrange("p (r u) -> p r u", r=nr),
                               srcv[:, g * nr:(g + 1) * nr, :], ["kT"], [("kD", c) for c in range(8 * g, 8 * g + 9)])

                    for n in range(32):
                        m0 = 128 * n
                        rho = m0 // L; u0 = m0 % L
                        q0 = rho + Dd * u0
                        if Dd == 1:
                            chunks = [kTp[:, PADN + m0 - 64:PADN + m0 + 64], kTp[:, PADN + m0 + 64:PADN + m0 + 192]]
                        else:
                            chunks = [kD[:, m0:m0 + 128], kD[:, m0 + 128:m0 + 256]]
                        mt = [1 if u0 == 0 else 0, 3 if u0 + 128 == L else 2]
                        tiles.append(dict(
                            chunks=chunks, q=qT[:, q0:q0 + 127 * Dd + 1:Dd],
                            kkeys=([("kD", n), ("kD", n + 1)] if Dd > 1 else []),
                            v=(lambda ci, n=n: vD[:, n + ci, :]), vkeys=[("vDh", n), ("vDl", n), ("vDh", n + 1), ("vDl", n + 1), "vDpad"],
                            mask=(lambda ci, mt=mt: dmask[:, mt[ci] * 128:mt[ci] * 128 + 128]), mkeys=["dmask"],
                            acc=acc[:, :, q0:q0 + 127 * Dd + 1:Dd], first=(Dd == 1),
                            pre_S=(mk_kD if (n == 0 and Dd > 1) else None), pre_PV=(ld_vD if n == 0 else None)))
            prev = None
            for tl in tiles:
                st = tile_S(tl)
                if prev is not None:
                    tile_PV(*prev)
                prev = (tl, st)
            tile_PV(*prev)
            def make_fin(hd=hd):
                fns = []
                for pc in range(4):
                    def piece(pc=pc):
                        sl_ = slice(pc * 1024, pc * 1024 + 1024)
                        act(acc[:, 1, sl_], acc[:, 1, sl_], AF.Ln, ["acc_all"], ["acc_all"])
                        act(acc[:, 1, sl_], acc[:, 1, sl_], AF.Exp, ["acc_all"], ["acc_all"], scale=-1.0)
                        tt("dve", oT[:, sl_], acc[:, 0, sl_], acc[:, 1, sl_], ALU.mult, ["acc_all"], ["oT", "acc_all"])
                    fns.append(piece)

                def store():
                    for tq in range(4):
                        dma("sp", o_b.bitcast(BF16)[hd, tq * 128:tq * 128 + 128, :], oT[:, tq * 1024:tq * 1024 + 1024],
                            ["oT"], [("o_b", hd)], "st_o")
                    P.add("pool", lambda e: e.collective_compute("AllGather", ALU.bypass, replica_groups=[[0, 1, 2, 3], [4, 5, 6, 7]], dma_qos=CC_QOS,
                                                                 ins=[o_b[hd].opt()], outs=[o_g[hd * 2048:(hd + 1) * 2048, :].opt()]),
                          [("o_b", hd)], [("o_g", hd)], dma="cc_o%d" % hd, inc=1)
                    if DEBUG:
                        dma("sp", dbg["oT"][hd], oT[:, :], ["oT"], [], "dbg")
                fns.append(store)
                return fns
            pending_fin.extend(make_fin())
            if hd == 3:
                while pending_fin:
                    pending_fin.pop(0)()
            ckpt({0: 3, 1: 4, 2: 10, 3: 11}[hd])

    except _Stop:
        return finish()
    P.barrier(nops, nop_rw)

    if STAGE == 3:
        return finish()
    jcache = {}

    def mk_ld_o(g, l):
        def ld_o(e):
            if "j" not in jcache:
                jcache["j"] = e.partition_id() % 4
            ogb = o_g.bitcast(BF16).rearrange("(h r j q) t -> h j q r t", h=4, j=4, r=4)
            src = ogb[2 * g + l, bass.ds(jcache["j"], 1)].rearrange("o q r t -> (o q) r t")
            return e.dma_start(out=xn[:, g * 8 + l:g * 8 + 8:2, :], in_=src)
        return ld_o
    for g in range(2):
        for l in range(2):
            hl = 2 * g + l
            dsem("ld_o%d" % hl)
            P.add("pool", mk_ld_o(g, l), [("o_g", hl)], [("xn", g * 8 + l + 2 * r) for r in range(4)], dma="ld_o%d" % hl, inc=16)
    cp("dve", gth[:, :], modT[:, 5 * 16:5 * 16 + 16], ["modT"], ["gth"])
    for grp in range(2):
        for k8 in range(8):
            kc = grp * 8 + k8
            s = sq_i[0] % 2; sq_i[0] += 1
            if kc % 2 == 0:
                act(sqr[s][:, :], xn[:, kc, :], AF.Square, [("xn", kc)], [("sqr", s)])
            else:
                tt("dve", sqr[s][:, :], xn[:, kc, :], xn[:, kc, :], ALU.mult, [("xn", kc)], [("sqr", s)])
            for t in range(2):
                mm(ps[2 + t][:, :], ones[:, :], sqr[s][:, t * 512:t * 512 + 512], k8 == 0, k8 == 7,
                   ["ones", ("sqr", s)], [PSK(2 + t)])
        for t in range(2):
            act(rstd[:, t * 512:t * 512 + 512], ps[2 + t][:, :], AF.Ln, [PSK(2 + t)], ["rstd"], bias=EPS, scale=1.0 / 1024)
        act(rstd[:, :], rstd[:, :], AF.Exp, ["rstd"], ["rstd"], scale=-0.5)
        for k8 in range(8):
            kc = grp * 8 + k8
            stt(xn[:, kc, :], xn[:, kc, :], gouts[:, kc:kc + 1], rstd[:, :], ALU.mult, ALU.mult,
                [("xn", kc), "gouts", "rstd"], [("xn", kc)])
        for g in (2 * grp, 2 * grp + 1):
            down_group(wo, g * 4, lambda fl, t, g=g: xn[:, g * 4 + fl, t * 512:t * 512 + 512],
                       lambda fl, g=g: [("xn", g * 4 + fl)], gth)
    if DEBUG:
        dma("sp", dbg["h2"].rearrange("(k p) t -> p k t", p=128), h[:, :, :], HALL, [], "dbg")

    if not LITE:
        norm_mod(2, 6, 7)
        ffn(w2g, w2u, w2d, 8)

    return finish()


def _tile_w(W):
    K, N = W.shape
    return np.ascontiguousarray(W.reshape(K // 128, 128, N // 128, 128).transpose(2, 1, 0, 3)).reshape(N // 128, 128, (K // 128) * 128)


def _vec16(v):
    return np.ascontiguousarray(v.reshape(-1, 128).T)


_NC_CACHE = {}


def kernel(x, c, w_ada, b_ada, g_ffn1, w1_gate, w1_up, w1_down, g_mix, w_qkv, qn_na, kn_na, qn_dil, kn_dil,
           rpb_na, g_out_na, g_out_dil, w_o, g_ffn2, w2_gate, w2_up, w2_down):
    f = lambda a: np.asarray(a, dtype=np.float32)
    x = f(x); c = f(c)
    w_ada = f(w_ada)[0]; b_ada = f(b_ada)[0]
    if not LITE:
        w1g = _tile_w(f(w1_gate)[0]); w1u = _tile_w(f(w1_up)[0]); w1d = np.ascontiguousarray(f(w1_down)[0]).reshape(NFC, 128, 2048)
        w2g = _tile_w(f(w2_gate)[0]); w2u = _tile_w(f(w2_up)[0]); w2d = np.ascontiguousarray(f(w2_down)[0]).reshape(NFC, 128, 2048)
    wqkv = f(w_qkv)[0]
    wo = np.ascontiguousarray(f(w_o)[0]).reshape(16, 128, 2048)
    gv = np.concatenate([_vec16(f(g_ffn1)[0]), _vec16(f(g_mix)[0]), _vec16(f(g_ffn2)[0])], axis=1)
    gout = np.concatenate([_vec16(f(g_out_na)[0]), _vec16(f(g_out_dil)[0])], axis=1)
    qkn = np.stack([f(qn_na)[0], f(kn_na)[0], f(qn_dil)[0], f(kn_dil)[0]], axis=1)
    rpb = f(rpb_na)[0]
    cT = np.ascontiguousarray(c.T.reshape(16, 128, 2).transpose(1, 0, 2)).reshape(128, 32)
    ropeC, ropeS = rope_tabs()
    badaT_h = np.ascontiguousarray(b_ada.reshape(144, 128).T)
    dmask = dil_masks()
    ident = np.eye(128, dtype=np.float32)
    perm32 = np.zeros((32, 32), np.float32)
    perm32[(np.arange(32) + 16) % 32, np.arange(32)] = 1.0
    in_maps = []
    for core in range(8):
        b = core // 4; j = core % 4
        heads = [2 * j, 2 * j + 1, 8 + 2 * j, 9 + 2 * j]
        whl = []
        for hh in heads:
            blk = []
            for part in range(3):
                col0 = part * D + hh * 128
                blk.append(_tile_w(wqkv[:, col0:col0 + 128])[0])
            whl.append(np.stack(blk))
        m = {
            "xT": np.ascontiguousarray(x[b, j * NT:(j + 1) * NT, :].T),
            "cT": cT,
            "wada": _tile_w(np.concatenate([w_ada[:, 12 * j * 128:(12 * j + 12) * 128],
                                            w_ada[:, (48 + 24 * j) * 128:(48 + 24 * j + 24) * 128]], axis=1)),
            "bada": badaT_h,
            "sel": np.array([[1.0 - b], [float(b)]], np.float32),
            "wh": np.stack(whl), "wo": wo, "gv": gv, "gout": gout, "qkn": np.ascontiguousarray(qkn),
            "bt": np.stack([build_bias_tiles(rpb[2 * j]), build_bias_tiles(rpb[2 * j + 1])]),
            "ropeC": ropeC, "ropeS": ropeS, "dmask": dmask, "ident": ident, "perm": perm32,
        }
        if not LITE:
            m.update({"w1g": w1g, "w1u": w1u, "w1d": w1d, "w2g": w2g, "w2u": w2u, "w2d": w2d})
        in_maps.append(m)
    if "nc" not in _NC_CACHE:
        _NC_CACHE["nc"] = build_nc()
    nc = _NC_CACHE["nc"]
    res = run_bass_kernel_spmd(nc, in_maps, core_ids=list(range(8)))
    out = np.empty((2, S, D), np.float32)
    for core in range(8):
        b = core // 4; j = core % 4
        out[b, j * NT:(j + 1) * NT, :] = res.results[core]["outT"].T
    if DEBUG:
        kernel.last = res
    return out
```

```python
import numpy as np
from contextlib import ExitStack
import concourse.bass as bass
import concourse.mybir as mybir
from concourse.bass_utils import run_bass_kernel_spmd

F32 = mybir.dt.float32
BF16 = mybir.dt.bfloat16
AF = mybir.ActivationFunctionType
ALU = mybir.AluOpType

D = 2048
NT = 1024
S = 4096
DFF = 5632
NFC = 44
EPS = 1e-6
SCALE = 128.0 ** -0.5
NEGM = -30000.0
PADN = 64
DEBUG = False
STAGE = 9
ATTSTOP = 0
CC_QOS = "P2"
NRM_NP = 4
LITE = False


class Op:
    __slots__ = ("eng", "fn", "deps", "is_dma", "semkey", "inc", "cum", "targets", "signal", "seq", "idx")


class _Stop(Exception):
    pass


def ckpt(n):
    if ATTSTOP == n:
        raise _Stop()


class Prog:
    ENGS = ("pe", "act", "dve", "pool", "sp")

    def __init__(self):
        self.ops = {e: [] for e in self.ENGS}
        self.lastw = {}
        self.readers = {}
        self.dma_count = {}
        self.last_dma = {}
        self.n = 0

    def add(self, eng, fn, reads=(), writes=(), dma=None, inc=16, extra_deps=()):
        op = Op()
        op.eng = eng; op.fn = fn; op.is_dma = dma is not None; op.semkey = dma; op.inc = inc
        op.signal = False; op.seq = 0; op.idx = self.n; self.n += 1
        deps = []
        for k in reads:
            w = self.lastw.get(k)
            if w is not None:
                deps.append((w, "raw"))
        for k in writes:
            rds = self.readers.get(k, ())
            w = self.lastw.get(k)
            if w is not None and not rds:
                deps.append((w, "waw"))
            for r in rds:
                deps.append((r, "war"))
        for d in extra_deps:
            deps.append((d, "raw"))
        for k in reads:
            self.readers.setdefault(k, []).append(op)
        for k in writes:
            self.lastw[k] = op
            self.readers[k] = []
        op.deps = deps
        op.targets = {d.semkey: self.dma_count[d.semkey] for d, _ in deps if d.is_dma}
        if op.is_dma:
            self.dma_count[dma] = self.dma_count.get(dma, 0) + inc
            op.cum = self.dma_count[dma]
            self.last_dma[dma] = op
        self.ops[eng].append(op)
        return op

    def barrier(self, mk_nop, rw=None):
        rw = rw or {}
        firsts = []
        for e in ("pe", "act", "dve", "pool"):
            rd, wr = rw.get(e, ((), ()))
            firsts.append(self.add(e, mk_nop[e], reads=list(rd), writes=[("bar1", e, self.n)] + list(wr)))
        dmas = [op for key, op in self.last_dma.items() if not str(key).startswith("cc_")]
        for e in self.ENGS:
            rd, wr = rw.get(e, ((), ()))
            self.add(e, mk_nop[e], reads=list(rd), writes=[("bar2", e, self.n)] + list(wr), extra_deps=firsts + dmas)

    def needs_wait(self, op, d, kind):
        if d.is_dma:
            return True
        if d.eng != op.eng:
            return True
        if op.eng == "pe":
            return False
        if op.is_dma:
            return True
        return kind == "raw"

    def finalize(self):
        for e in self.ENGS:
            for op in self.ops[e]:
                latest = {}
                for d, kind in op.deps:
                    if (not d.is_dma) and self.needs_wait(op, d, kind):
                        if d.eng not in latest or d.idx > latest[d.eng].idx:
                            latest[d.eng] = d
                for d in latest.values():
                    d.signal = True
                op.deps = [(d, k) for d, k in op.deps if d.is_dma or latest.get(d.eng) is d]
        for e in self.ENGS:
            s = 0
            for op in self.ops[e]:
                if op.signal and not op.is_dma:
                    s += 1
                    op.seq = s

    def emit(self, nc, block, engsem, dmasem):
        self.finalize()
        decos = {"pe": block.tensor, "act": block.scalar, "dve": block.vector, "pool": block.gpsimd, "sp": block.sync}
        for e in self.ENGS:
            ops = self.ops[e]

            def body(eng, ops=ops, e=e):
                waited = {}
                for op in ops:
                    need = {}
                    for d, kind in op.deps:
                        if not self.needs_wait(op, d, kind):
                            continue
                        if d.is_dma:
                            key = ("d", d.semkey); val = op.targets[d.semkey]; sem = dmasem[d.semkey]
                        else:
                            key = ("e", d.eng); val = d.seq; sem = engsem[d.eng]
                        if val > need.get(key, (0, None))[0]:
                            need[key] = (val, sem)
                    for key, (val, sem) in need.items():
                        if waited.get(key, 0) >= val:
                            continue
                        eng.wait_ge(sem, val)
                        waited[key] = val
                    inst = op.fn(eng)
                    if inst is None:
                        continue
                    if op.is_dma:
                        inst.then_inc(dmasem[op.semkey], op.inc)
                    elif op.signal:
                        inst.then_inc(engsem[e], 1)
            decos[e](body)


def na_tile_tables():
    types = {}
    tiles = []
    for n in range(32):
        rows = []
        for rq in (2 * n, 2 * n + 1):
            rs = min(max(rq - 4, 0), 56)
            rows.append((rq, rs))
        lo = min(r[1] for r in rows) // 2
        hi = (max(r[1] for r in rows) + 7) // 2
        lst = []
        for c in range(lo, hi + 1):
            key = []
            for rkl in range(2):
                for rql in range(2):
                    rk = 2 * c + rkl
                    rq, rs = rows[rql]
                    if rs <= rk < rs + 8:
                        key.append(rk - rq + 7)
                    else:
                        key.append(-1)
            key = tuple(key)
            if all(k < 0 for k in key):
                continue
            if key not in types:
                types[key] = len(types)
            lst.append((c, types[key]))
        tiles.append(lst)
    return tiles, types


NA_TILES, NA_TYPES = na_tile_tables()
NTYPES = len(NA_TYPES)


def build_bias_tiles(rpb_h):
    out = np.full((128, NTYPES, 128), NEGM, np.float32)
    ck = np.arange(64)[:, None]
    cq = np.arange(64)[None, :]
    cs = np.clip(cq - 8, 0, 48)
    cvalid = (ck >= cs) & (ck < cs + 16)
    coff = np.clip(ck - cq + 15, 0, 30)
    for key, t in NA_TYPES.items():
        i = 0
        for rkl in range(2):
            for rql in range(2):
                a = key[i]; i += 1
                if a < 0:
                    continue
                blk = np.where(cvalid, rpb_h[a][coff], np.float32(NEGM))
                out[64 * rkl:64 * rkl + 64, t, 64 * rql:64 * rql + 64] = blk
    return out.reshape(128, NTYPES * 128)


def dil_masks():
    i = np.arange(128)[:, None]
    j = np.arange(128)[None, :]
    A = np.where(j <= i, 0.0, NEGM)
    B = np.where(j >= i, 0.0, NEGM)
    A1 = A.copy(); A1[:64, :] = NEGM
    B1 = B.copy(); B1[64:, :] = NEGM
    return np.concatenate([A, A1, B, B1], axis=1).astype(np.float32)


def rope_tabs():
    pos = np.arange(S, dtype=np.float32)
    inv = np.power(np.float32(500000.0), -np.arange(0, 32, 2, dtype=np.float32) / np.float32(32)).astype(np.float32)
    ang = (pos[None, :] * inv[:, None]).astype(np.float32)
    c = np.cos(ang).astype(np.float32); s = np.sin(ang).astype(np.float32)
    C = np.concatenate([c, c], axis=0)
    Sg = np.concatenate([-s, s], axis=0)
    return C, Sg


def build_nc():
    nc = bass.Bass("TRN2", target_bir_lowering=False)
    P = Prog()

    def din(name, shape, dt=F32):
        return nc.dram_tensor(name, list(shape), dt, kind="ExternalInput").ap()

    xT = din("xT", [D, NT]); cT = din("cT", [128, 32]); wada = din("wada", [36, 128, 2048]); bada = din("bada", [128, 144])
    sel = din("sel", [2, 1])
    if not LITE:
        w1g = din("w1g", [NFC, 128, 2048]); w1u = din("w1u", [NFC, 128, 2048]); w1d = din("w1d", [NFC, 128, 2048])
        w2g = din("w2g", [NFC, 128, 2048]); w2u = din("w2u", [NFC, 128, 2048]); w2d = din("w2d", [NFC, 128, 2048])
    wh = din("wh", [4, 3, 128, 2048]); wo = din("wo", [16, 128, 2048])
    gv = din("gv", [128, 48]); goutd = din("gout", [128, 16]); qknd = din("qkn", [128, 4])
    btd = din("bt", [2, 128, NTYPES * 128]); ropeCd = din("ropeC", [32, S]); ropeSd = din("ropeS", [32, S])
    dmaskd = din("dmask", [128, 512]); identd = din("ident", [128, 128]); permd = din("perm", [32, 32])
    outT = nc.dram_tensor("outT", [D, NT], F32, kind="ExternalOutput").ap()
    dbg = {}
    if DEBUG:
        dbg["h1"] = nc.dram_tensor("dbg_h1", [D, NT], F32, kind="ExternalOutput").ap()
        dbg["modT"] = nc.dram_tensor("dbg_modT", [128, 144], F32, kind="ExternalOutput").ap()
        dbg["oT"] = nc.dram_tensor("dbg_oT", [4, 128, S], BF16, kind="ExternalOutput").ap()
        dbg["nrm"] = nc.dram_tensor("dbg_nrm", [D, NT], BF16, kind="ExternalOutput").ap()
        dbg["qT"] = nc.dram_tensor("dbg_qT", [4, 128, S], BF16, kind="ExternalOutput").ap()
        dbg["kT"] = nc.dram_tensor("dbg_kT", [4, 128, S], BF16, kind="ExternalOutput").ap()
        dbg["h2"] = nc.dram_tensor("dbg_h2", [D, NT], F32, kind="ExternalOutput").ap()

    mod_bA = nc.dram_tensor("mod_bA", [2, 1536], F32).ap()
    mod_gA = nc.dram_tensor("mod_gA", [8, 1536], F32).ap()
    mod_bB = nc.dram_tensor("mod_bB", [2, 3072], F32).ap()
    mod_gB = nc.dram_tensor("mod_gB", [8, 3072], F32).ap()
    nrm_b = nc.dram_tensor("nrm_b", [4, 128, 2048], F32).ap()
    nrm_g = nc.dram_tensor("nrm_g", [4, 512, 2048], F32).ap()
    vdram = nc.dram_tensor("vdram", [S, 128], BF16).ap()
    o_b = nc.dram_tensor("o_b", [4, 512, 512], F32).ap()
    o_g = nc.dram_tensor("o_g", [16 * 512, 512], F32).ap()

    es = ExitStack()
    ARENA = 212480
    es.enter_context(nc.sbuf_tensor("arena", [128, ARENA + 64], mybir.dt.uint8))
    base0 = (nc.sbuf_base - (ARENA + 64) + 31) // 32 * 32
    cur = [base0]

    def salloc(name, shape, dt, at=None):
        nbytes = int(np.prod(shape[1:])) * (4 if dt == F32 else 2)
        nbytes = (nbytes + 31) // 32 * 32
        if at is None:
            off = cur[0]; cur[0] += nbytes
        else:
            off = at
        assert off + nbytes <= base0 + ARENA, (name, off + nbytes - base0, ARENA)
        return nc.alloc_sbuf_tensor_at(name, list(shape), dt, offset=off), off + nbytes

    h, _ = salloc("h", [128, 16, NT], F32)
    modT, _ = salloc("modT", [128, 144], F32)
    gvs, _ = salloc("gvs", [128, 48], F32)
    gouts, _ = salloc("gouts", [128, 16], F32)
    qkns, _ = salloc("qkns", [128, 4], F32)
    qknsc, _ = salloc("qknsc", [128, 4], F32)
    Avec, _ = salloc("Avec", [128, 16], F32)
    tmpA, _ = salloc("tmpA", [128, 16], F32)
    gth, _ = salloc("gth", [128, 16], F32)
    ident, _ = salloc("ident", [128, 128], BF16)
    ones, _ = salloc("ones", [128, 128], BF16)
    rstd, _ = salloc("rstd", [128, NT], F32)
    sels, _ = salloc("sels", [2, 1], F32)
    scr, _ = salloc("scr", [128, 8], F32)
    R0 = cur[0]
    xn, e1 = salloc("xn", [128, 16, NT], BF16)
    wgu = []
    for i in range(6):
        t, _ = salloc(f"wgu{i}", [128, 2048], BF16); wgu.append(t)
    wdr = []
    for i in range(8):
        t, _ = salloc(f"wd{i}", [128, 2048], BF16); wdr.append(t)
    hid = []
    hid_off = cur[0]
    for i in range(3):
        t, _ = salloc(f"hid{i}", [128, 4, NT], BF16); hid.append(t)
    sgt = []
    for i in range(2):
        t, _ = salloc(f"sgt{i}", [128, 512], F32); sgt.append(t)
    sqr = []
    for i in range(2):
        t, _ = salloc(f"sqr{i}", [128, NT], BF16); sqr.append(t)
    tmpf = []
    for i in range(2):
        t, _ = salloc(f"tmpf{i}", [128, NT], F32); tmpf.append(t)
    cTf, _ = salloc("cTf", [128, 32], F32)
    csb, _ = salloc("csb", [128, 32], BF16)
    badaT, _ = salloc("badaT", [128, 144], F32)
    stg = []
    for i in range(2):
        t, _ = salloc(f"stg{i}", [2, 512], F32); stg.append(t)
    stgr = []
    for i in range(2):
        t, _ = salloc(f"stgr{i}", [2, 512], F32); stgr.append(t)
    ffn_end = cur[0]
    cur[0] = R0
    nrmr = []
    for i in range(2):
        t, _ = salloc(f"nrmr{i}", [128, 16, 256], BF16); nrmr.append(t)
    whs = []
    for i in range(3):
        t, _ = salloc(f"whs{i}", [128, 2048], BF16); whs.append(t)
    qT, _ = salloc("qT", [128, S], BF16)
    kTp, _ = salloc("kTp", [128, S + 2 * PADN], BF16)
    vS, _ = salloc("vS", [128, 32, 128], BF16)
    vD, _ = salloc("vD", [128, 33, 128], BF16)
    acc, _ = salloc("acc", [128, 2, S], F32)
    PT = []
    for i in range(2):
        t, _ = salloc(f"PT{i}", [128, 640], BF16); PT.append(t)
    Bt, _ = salloc("Bt", [128, NTYPES * 128], BF16)
    oT, _ = salloc("oT", [128, S], BF16)
    ropeC, _ = salloc("ropeCt", [32, S], BF16)
    ropeS, _ = salloc("ropeSt", [32, S], BF16)
    kD, _ = salloc("kD", [128, S + 128], BF16)
    perm, _ = salloc("perm", [32, 32], BF16)
    rtmp = []
    for i in range(2):
        t, _ = salloc(f"rtmp{i}", [32, 512], F32); rtmp.append(t)
    sqt = []
    for i in range(2):
        t, _ = salloc(f"sqt{i}", [128, 512], BF16); sqt.append(t)
    dmask, _ = salloc("dmask", [128, 512], BF16)
    att_end = cur[0]
    assert max(att_end, ffn_end) <= base0 + ARENA

    ps = [es.enter_context(nc.psum_tensor(f"ps{i}", [128, 512], F32)) for i in range(8)]
    psS = []
    engsem = {e: es.enter_context(nc.semaphore(f"sem_{e}")) for e in ("pe", "act", "dve", "pool")}
    dmasem = {}

    def dsem(key):
        if key not in dmasem:
            dmasem[key] = es.enter_context(nc.semaphore("d_" + str(key)))
        return key

    PSK = lambda b: ("ps", b)
    HK = lambda kc: [("h", kc, 0), ("h", kc, 1)]
    HALL = [("h", kc, t) for kc in range(16) for t in range(2)]

    def dma(eng, out, in_, reads, writes, key, **kw):
        dsem(key)
        return P.add(eng, lambda e: e.dma_start(out=out, in_=in_, **kw), reads, writes, dma=key)

    def castdma(out, in_, reads, writes, key):
        return dma("pool", out, in_, reads, writes, key, max_dma_last_dim=8192)

    def mm(out, lhsT, rhs, start, stop, reads, writes, **kw):
        return P.add("pe", lambda e: e.matmul(out, lhsT, rhs, start=start, stop=stop, **kw), reads, writes)

    def act(out, in_, func, reads, writes, bias=None, scale=None):
        kw = {}
        if bias is not None:
            kw["bias"] = bias
        if scale is not None:
            kw["scale"] = scale
        return P.add("act", lambda e: e.activation(out, in_, func, **kw), reads, writes)

    def tt(eng, out, in0, in1, op, reads, writes):
        return P.add(eng, lambda e: e.tensor_tensor(out, in0, in1, op), reads, writes)

    def stt(out, in0, scalar, in1, op0, op1, reads, writes):
        return P.add("dve", lambda e: e.scalar_tensor_tensor(out, in0, scalar, in1, op0, op1), reads, writes)

    def ts(eng, out, in0, s1, s2, op0, op1, reads, writes):
        return P.add(eng, lambda e: e.tensor_scalar(out, in0, s1, s2, op0, op1), reads, writes)

    def cp(eng, out, in_, reads, writes):
        return P.add(eng, lambda e: e.tensor_copy(out, in_), reads, writes)

    def memset(eng, ap, val, writes):
        return P.add(eng, lambda e: e.memset(ap, val), (), writes)

    dma("sp", h[:, :, :], xT.rearrange("(k p) t -> p k t", p=128), (), HALL, "ld_h")
    dma("sp", cTf[:, :], cT, (), ["cTf"], "ld_c")
    dma("sp", badaT[:, :], bada, (), ["badaT"], "ld_c")
    dma("sp", sels[:, :], sel, (), ["sels"], "ld_c")
    dma("sp", gvs[:, :], gv, (), ["gvs"], "ld_c")
    dma("sp", gouts[:, :], goutd, (), ["gouts"], "ld_c")
    dma("sp", qkns[:, :], qknd, (), ["qkns"], "ld_c")
    castdma(ident[:, :], identd, (), ["ident"], "ld_id")
    memset("dve", ones[:, :], 1.0, ["ones"])

    act(csb[:, :], cTf[:, :], AF.Silu, ["cTf"], ["csb"])
    wgu_i = [0]

    def next_wgu():
        i = wgu_i[0] % 6; wgu_i[0] += 1
        return i

    stg_i = [0]; stgr_i = [0]

    def ada_chunk(c, c0, bank, dst, dkey):
        sl = next_wgu()
        castdma(wgu[sl][:, :], wada[c], (), [("wgu", sl)], ("wgu", sl))
        lc = c - c0
        for kc in range(16):
            mm(ps[bank][0:2, (lc % 4) * 128:(lc % 4) * 128 + 128], csb[:, kc * 2:kc * 2 + 2], wgu[sl][:, kc * 128:kc * 128 + 128],
               kc == 0, kc == 15, ["csb", ("wgu", sl)], [PSK(bank)])
        if lc % 4 == 3:
            s_ = stg_i[0] % 2; stg_i[0] += 1
            cp("dve", stg[s_][:, :], ps[bank][0:2, 0:512], [PSK(bank)], [("stg", s_)])
            dma("sp", dst[:, (lc // 4) * 512:(lc // 4) * 512 + 512], stg[s_][:, :], [("stg", s_)], [dkey], ("stgd", s_))

    def ada_gather(src, dst, skey, gkey, sem):
        dsem(sem)
        P.add("pool", lambda e: e.collective_compute("AllGather", ALU.bypass, replica_groups=[[0, 1, 2, 3], [4, 5, 6, 7]], dma_qos=CC_QOS,
                                                     ins=[src.opt()], outs=[dst.opt()]),
              [skey], [gkey], dma=sem, inc=1)

    def ada_transposes(gathered, gkey, nchunk, gi_base, bank):
        for r in range(4):
            for grp in range(nchunk // 4):
                s_ = stgr_i[0] % 2; stgr_i[0] += 1
                dma("sp", stgr[s_][:, :], gathered[2 * r:2 * r + 2, grp * 512:grp * 512 + 512], [gkey], [("stgr", s_)], ("stgrd", s_))
                for k in range(4):
                    gi = gi_base + r * nchunk + grp * 4 + k
                    mm(ps[bank][:, gi:gi + 1], stgr[s_][:, k * 128:k * 128 + 128], sels[:, :], True, True,
                       [("stgr", s_), "sels"], [PSK(bank)])
        lo, hi = gi_base, gi_base + 4 * nchunk
        tt("dve", modT[:, lo:hi], ps[bank][:, lo:hi], badaT[:, lo:hi], ALU.add, [PSK(bank), "badaT"], ["modT"])

    for c in range(12):
        ada_chunk(c, 0, 0, mod_bA, "mod_bA")
    ada_gather(mod_bA, mod_gA, "mod_bA", "mod_gA", "cc_modA")
    ada_transposes(mod_gA, "mod_gA", 12, 0, 1)
    adaB = {fc: (lambda c=12 + fc: ada_chunk(c, 12, 7, mod_bB, "mod_bB")) for fc in range(24)}
    adaB[24] = lambda: ada_gather(mod_bB, mod_gB, "mod_bB", "mod_gB", "cc_modB")
    if LITE:
        for fc in range(25):
            adaB[fc]()
        ada_transposes(mod_gB, "mod_gB", 24, 48, 7)
    if DEBUG:
        dma("sp", dbg["modT"], modT[:, :], ["modT"], [], "dbg")
    cp("dve", qknsc[:, :], qkns[:, :], ["qkns"], ["qknsc"])
    ts("dve", qknsc[:, 0:1], qkns[:, 0:1], SCALE, None, ALU.mult, ALU.bypass, ["qkns", "qknsc"], ["qknsc"])
    ts("dve", qknsc[:, 2:3], qkns[:, 2:3], SCALE, None, ALU.mult, ALU.bypass, ["qkns", "qknsc"], ["qknsc"])

    MS = lambda m, kc: modT[:, m * 16 + kc:m * 16 + kc + 1]

    sq_i = [0]; tf_i = [0]

    xnB = xn[:, :, :].rearrange("p k t -> p (k t)").rearrange("p (b k t) -> p b k t", b=4, k=16)

    def norm_mod(gidx, m_sh, m_sc, blockmajor=False):
        ts("dve", tmpA[:, :], modT[:, m_sc * 16:m_sc * 16 + 16], 1.0, None, ALU.add, ALU.bypass, ["modT"], ["tmpA"])
        tt("dve", Avec[:, :], tmpA[:, :], gvs[:, gidx * 16:gidx * 16 + 16], ALU.mult, ["tmpA", "gvs"], ["Avec"])
        for kc in range(16):
            s = sq_i[0] % 2; sq_i[0] += 1
            if kc % 2 == 0:
                act(sqr[s][:, :], h[:, kc, :], AF.Square, HK(kc), [("sqr", s)])
            else:
                tt("dve", sqr[s][:, :], h[:, kc, :], h[:, kc, :], ALU.mult, HK(kc), [("sqr", s)])
            for t in range(2):
                mm(ps[2 + t][:, :], ones[:, :], sqr[s][:, t * 512:t * 512 + 512], kc == 0, kc == 15,
                   ["ones", ("sqr", s)], [PSK(2 + t)])
        for t in range(2):
            act(rstd[:, t * 512:t * 512 + 512], ps[2 + t][:, :], AF.Ln, [PSK(2 + t)], ["rstd"], bias=EPS, scale=1.0 / D)
        act(rstd[:, :], rstd[:, :], AF.Exp, ["rstd"], ["rstd"], scale=-0.5)
        for kc in range(16):
            s = tf_i[0] % 2; tf_i[0] += 1
            stt(tmpf[s][:, :], h[:, kc, :], Avec[:, kc:kc + 1], rstd[:, :], ALU.mult, ALU.mult,
                HK(kc) + ["Avec", "rstd"], [("tmpf", s)])
            if blockmajor:
                act(xnB[:, :, kc, :], tmpf[s][:, :].rearrange("p (b t) -> p b t", b=4), AF.Identity,
                    [("tmpf", s), "modT"], [("xn", kc)], bias=MS(m_sh, kc), scale=1.0)
            else:
                act(xn[:, kc, :], tmpf[s][:, :], AF.Identity, [("tmpf", s), "modT"], [("xn", kc)], bias=MS(m_sh, kc), scale=1.0)

    wd_i = [0]
    pd_i = [0]

    def down_group(wsrc, fc0, rhs_fn, rhs_keys, scal):
        slots = []
        for fl in range(4):
            sl = wd_i[0] % 8; wd_i[0] += 1
            castdma(wdr[sl][:, :], wsrc[fc0 + fl], (), [("wd", sl)], ("wd", sl))
            slots.append(sl)
        for dc in range(16):
            for t in range(2):
                b = 4 + pd_i[0] % 4; pd_i[0] += 1
                for fl in range(4):
                    mm(ps[b][:, :], wdr[slots[fl]][:, dc * 128:dc * 128 + 128], rhs_fn(fl, t), fl == 0, fl == 3,
                       [("wd", slots[fl])] + rhs_keys(fl), [PSK(b)])
                stt(h[:, dc, t * 512:t * 512 + 512], ps[b][:, :], scal[:, dc:dc + 1], h[:, dc, t * 512:t * 512 + 512],
                    ALU.mult, ALU.add, [PSK(b), "gth", ("h", dc, t)], [("h", dc, t)])

    def ffn(wg, wu, wd_, m_gt, side=None):
        ts("dve", gth[:, :], modT[:, m_gt * 16:m_gt * 16 + 16], 0.5, None, ALU.mult, ALU.bypass, ["modT"], ["gth"])
        gu_i = 0
        for fc in range(NFC):
            if side and fc in side:
                side[fc]()
            g = fc // 4
            hs = g % 3
            sg_ = next_wgu(); su_ = next_wgu()
            castdma(wgu[sg_][:, :], wg[fc], (), [("wgu", sg_)], ("wgu", sg_))
            castdma(wgu[su_][:, :], wu[fc], (), [("wgu", su_)], ("wgu", su_))
            for t in range(2):
                bg = (gu_i % 2) * 2; bu = bg + 1; gu_i += 1
                for kc in range(16):
                    mm(ps[bg][:, :], wgu[sg_][:, kc * 128:kc * 128 + 128], xn[:, kc, t * 512:t * 512 + 512], kc == 0, kc == 15,
                       [("wgu", sg_), ("xn", kc)], [PSK(bg)])
                for kc in range(16):
                    mm(ps[bu][:, :], wgu[su_][:, kc * 128:kc * 128 + 128], xn[:, kc, t * 512:t * 512 + 512], kc == 0, kc == 15,
                       [("wgu", su_), ("xn", kc)], [PSK(bu)])
                st = gu_i % 2
                act(sgt[st][:, :], ps[bg][:, :], AF.Silu, [PSK(bg)], [("sgt", st)])
                tt("dve", hid[hs][:, fc % 4, t * 512:t * 512 + 512], sgt[st][:, :], ps[bu][:, :], ALU.mult,
                   [("sgt", st), PSK(bu)], [("hid", hs, fc % 4)])
            if fc % 4 == 3 and g >= 1:
                gp = g - 1
                down_group(wd_, gp * 4, lambda fl, t, gp=gp: hid[gp % 3][:, fl, t * 512:t * 512 + 512],
                           lambda fl, gp=gp: [("hid", gp % 3, fl)], gth)
        gp = NFC // 4 - 1
        down_group(wd_, gp * 4, lambda fl, t, gp=gp: hid[gp % 3][:, fl, t * 512:t * 512 + 512],
                   lambda fl, gp=gp: [("hid", gp % 3, fl)], gth)

    def finish():
        dma("sp", outT.rearrange("(k p) t -> p k t", p=128), h[:, :, :], HALL, ["out"], "st_out")
        P.add("sp", lambda e: e.nop(), ["out"], [])
        P.add("sp", lambda e: e.nop(), [], [], extra_deps=list(P.last_dma.values()))
        with es:
            with nc.Block() as block:
                P.emit(nc, block, engsem, dmasem)
        return nc

    if not LITE:
        norm_mod(0, 0, 1)
        ffn(w1g, w1u, w1d, 2, side=adaB)
        ada_transposes(mod_gB, "mod_gB", 24, 48, 7)
    if DEBUG:
        dma("sp", dbg["h1"].rearrange("(k p) t -> p k t", p=128), h[:, :, :], HALL, [], "dbg")

    if STAGE == 1:
        return finish()
    norm_mod(1, 3, 4, blockmajor=True)
    if DEBUG:
        dma("sp", dbg["nrm"].rearrange("(k p) (b t) -> p b k t", p=128, b=4), xnB, [("xn", kc) for kc in range(16)], [], "dbg")
    for b4 in range(4):
        dsem("cc_nrm%d" % b4)
        dma("sp", nrm_b.bitcast(BF16)[b4], xnB[:, b4].rearrange("p k t -> p (k t)"),
            [("xn", kc) for kc in range(16)], [("nrm_b", b4)], "st_nrm")

    def nrm_gather(b4):
        P.add("pool", lambda e, b4=b4: e.collective_compute("AllGather", ALU.bypass, replica_groups=[[0, 1, 2, 3], [4, 5, 6, 7]], dma_qos=CC_QOS,
                                                            ins=[nrm_b[b4].opt()], outs=[nrm_g[b4].opt()]),
              [("nrm_b", b4)], [("nrm_g", b4)], dma="cc_nrm%d" % b4, inc=1)
    nrm_gather(0)

    if STAGE == 2:
        return finish()
    nops = {
        "pe": lambda e: e.matmul(ps[7][0:1, 0:1], sels[:, :], sels[:, :], start=True, stop=True),
        "act": lambda e: e.activation(scr[:, 0:1], gvs[:, 0:1], AF.Identity),
        "dve": lambda e: e.memset(scr[:, 1:2], 0.0),
        "pool": lambda e: e.memset(scr[:, 2:3], 0.0),
        "sp": lambda e: e.nop(),
    }
    nop_rw = {"pe": (["sels"], [PSK(7)]), "act": (["gvs"], [("scr", "act")]), "dve": ((), [("scr", "dve")]),
              "pool": ((), [("scr", "pool")])}
    P.barrier(nops, nop_rw)

    try:
        castdma(dmask[:, :], dmaskd, (), ["dmask"], "ld_dm")
        castdma(perm[:, :], permd, (), ["perm"], "ld_dm")
        for pc in range(4):
            castdma(ropeC[:, pc * 1024:pc * 1024 + 1024], ropeCd[:, pc * 1024:pc * 1024 + 1024], (), ["ropeT"], "ld_dm")
            castdma(ropeS[:, pc * 1024:pc * 1024 + 1024], ropeSd[:, pc * 1024:pc * 1024 + 1024], (), ["ropeT"], "ld_dm")
        memset("pool", kD[:, 0:64], 0.0, ["kDpad"])
        memset("pool", kD[:, 64 + S:128 + S], 0.0, ["kDpad"])
        memset("pool", kTp[:, 0:PADN], 0.0, ["kTpad"])
        memset("pool", kTp[:, PADN + S:PADN + S + PADN], 0.0, ["kTpad"])
        memset("pool", vD[0:64, 0, :], 0.0, ["vDpad"])
        memset("pool", vD[64:128, 32, :], 0.0, ["vDpad"])
        nr_i = [0]; sq2_i = [0]; pt_i = [0]; po_i = [0]; rp_i = [0]
        ckpt(1)

        for hd_ in range(4):
            dsem("cc_o%d" % hd_)
        pending_fin = []
        for i3 in range(3):
            castdma(whs[i3][:, :], wh[0, i3], (), [("whs", i3)], ("whs", i3))
        castdma(Bt[:, :], btd[0], (), ["Bt"], "ld_bt")
        for b4 in range(1, 4):
            nrm_gather(b4)
        for hd in range(4):
            is_na = hd < 2
            if hd == 1:
                castdma(Bt[:, :], btd[hd], (), ["Bt"], "ld_bt")
            gq = qknsc[:, 0:1] if is_na else qknsc[:, 2:3]
            gk = qknsc[:, 1:2] if is_na else qknsc[:, 3:4]
            for tbi in range(16):
                b4 = tbi // 4; r = tbi % 4
                tb = r * 4 + b4
                ns = nr_i[0] % 2; nr_i[0] += 1
                dma("sp", nrmr[ns][:, :, :].rearrange("p k t -> p (k t)"), nrm_g.bitcast(BF16)[b4, r * 128:(r + 1) * 128, :],
                    [("nrm_g", b4)], [("nrmr", ns)], ("nrmr", ns))
                bq = tbi % 2
                bv = 2 + tbi % 2
                for kc in range(16):
                    mm(ps[bq][:, 0:256], whs[0][:, kc * 128:kc * 128 + 128], nrmr[ns][:, kc, :], kc == 0, kc == 15,
                       [("whs", 0), ("nrmr", ns)], [PSK(bq)])
                for kc in range(16):
                    mm(ps[bq][:, 256:512], whs[1][:, kc * 128:kc * 128 + 128], nrmr[ns][:, kc, :], kc == 0, kc == 15,
                       [("whs", 1), ("nrmr", ns)], [PSK(bq)])
                for tc in range(2):
                    for kc in range(16):
                        mm(ps[bv][:, tc * 128:tc * 128 + 128], nrmr[ns][:, kc, tc * 128:tc * 128 + 128],
                           whs[2][:, kc * 128:kc * 128 + 128], kc == 0, kc == 15, [("whs", 2), ("nrmr", ns)], [PSK(bv)])
                s2 = sq2_i[0] % 2; sq2_i[0] += 1
                act(sqt[s2][:, :], ps[bq][:, :], AF.Square, [PSK(bq)], [("sqt", s2)])
                mm(ps[6][:, :], ones[:, :], sqt[s2][:, :], True, True, ["ones", ("sqt", s2)], [PSK(6)])
                act(rstd[:, 0:512], ps[6][:, :], AF.Ln, [PSK(6)], ["rstd"], bias=EPS, scale=1.0 / 128)
                act(rstd[:, 0:512], rstd[:, 0:512], AF.Exp, ["rstd"], ["rstd"], scale=-0.5)
                tok0 = tb * 256
                stt(qT[:, tok0:tok0 + 256], ps[bq][:, 0:256], gq, rstd[:, 0:256], ALU.mult, ALU.mult,
                    [PSK(bq), "qknsc", "rstd"], ["qT"])
                stt(kTp[:, PADN + tok0:PADN + tok0 + 256], ps[bq][:, 256:512], gk, rstd[:, 256:512], ALU.mult, ALU.mult,
                    [PSK(bq), "qknsc", "rstd"], ["kT"])
                act(vS[:, tb * 2:tb * 2 + 2, :], ps[bv][:, 0:256].rearrange("p (c d) -> p c d", c=2), AF.Identity, [PSK(bv)], ["vS"])
                if pending_fin and tbi % 2 == 1:
                    pending_fin.pop(0)()
            if hd < 3:
                for i3 in range(3):
                    castdma(whs[i3][:, :], wh[hd + 1, i3], (), [("whs", i3)], ("whs", i3))
            ckpt(2 if hd == 0 else (5 if hd == 2 else -1))
            if not is_na:
                for which, buf, off, key in ((0, qT, 0, "qT"), (1, kTp, PADN, "kT")):
                    for pc in range(8):
                        cc0 = pc * 512
                        x0 = buf[0:32, off + cc0:off + cc0 + 512]
                        rb = 6 + pc % 2
                        mm(ps[rb][0:32, :], perm[:, :], x0, True, True, ["perm", key], [PSK(rb)])
                        tt("dve", rtmp[0][:, :], x0, ropeC[:, cc0:cc0 + 512], ALU.mult, [key, "ropeT"], [("rtmp", 0)])
                        tt("dve", rtmp[1][:, :], ps[rb][0:32, :], ropeS[:, cc0:cc0 + 512], ALU.mult, [PSK(rb), "ropeT"], [("rtmp", 1)])
                        tt("dve", x0, rtmp[0][:, :], rtmp[1][:, :], ALU.add, [("rtmp", 0), ("rtmp", 1)], [key])
            if hd == 2:
                ckpt(6)
            if DEBUG:
                dma("sp", dbg["qT"][hd], qT[:, :], ["qT"], [], "dbg")
                dma("sp", dbg["kT"][hd], kTp[:, PADN:PADN + S], ["kT"], [], "dbg")

            def tile_S(tl):
                if tl.get("pre_S"):
                    tl["pre_S"]()
                chunks = tl["chunks"]; qap = tl["q"]
                nch = len(chunks)
                sb = (pt_i[0] % 2) * 2
                pti = pt_i[0] % 2; pt_i[0] += 1
                for ci, kap in enumerate(chunks):
                    b = sb + ci // 4
                    col = (ci % 4) * 128
                    mm(ps[b][:, col:col + 128], kap, qap, True, False, ["kT", "kTpad", "kDpad", "qT"] + tl.get("kkeys", []), [PSK(b)])
                    mm(ps[b][:, col:col + 128], ident[:, :], tl["mask"](ci), False, True, ["ident"] + tl["mkeys"], [PSK(b)])
                n1 = min(nch, 4) * 128
                act(PT[pti][:, 0:n1], ps[sb][:, 0:n1], AF.Exp, [PSK(sb)], [("PT", pti)])
                if nch > 4:
                    n2 = (nch - 4) * 128
                    act(PT[pti][:, 512:512 + n2], ps[sb + 1][:, 0:n2], AF.Exp, [PSK(sb + 1)], [("PT", pti)])
                return pti, nch

            def tile_PV(tl, st):
                if tl.get("pre_PV"):
                    tl["pre_PV"]()
                pti, nch = st
                bo = 4 + po_i[0] % 2; po_i[0] += 1
                for ci in range(nch):
                    mm(ps[bo][:, 0:128], tl["v"](ci), PT[pti][:, ci * 128:ci * 128 + 128], ci == 0, ci == nch - 1,
                       tl["vkeys"] + [("PT", pti)], [PSK(bo)])
                for ci in range(nch):
                    mm(ps[bo][:, 128:256], ones[:, :], PT[pti][:, ci * 128:ci * 128 + 128], ci == 0, ci == nch - 1,
                       ["ones", ("PT", pti)], [PSK(bo)])
                src = ps[bo][:, 0:256].rearrange("p (w q) -> p w q", w=2)
                if tl["first"]:
                    cp("dve", tl["acc"], src, [PSK(bo)], ["acc_all"])
                else:
                    tt("dve", tl["acc"], src, tl["acc"], ALU.add, [PSK(bo)], ["acc_all"])

            tiles = []
            if is_na:
                for n in range(32):
                    lst = NA_TILES[n]
                    tiles.append(dict(
                        chunks=[kTp[:, PADN + c * 128:PADN + c * 128 + 128] for c, _ in lst],
                        q=qT[:, n * 128:n * 128 + 128],
                        v=(lambda ci, lst=lst: vS[:, lst[ci][0], :]), vkeys=["vS"],
                        mask=(lambda ci, lst=lst: Bt[:, lst[ci][1] * 128:lst[ci][1] * 128 + 128]), mkeys=["Bt"],
                        acc=acc[:, :, n * 128:n * 128 + 128], first=True))
            else:
                vdv = vdram.rearrange("(c p) d -> p c d", p=128)
                for q4 in range(4):
                    dma("sp", vdv[:, q4 * 8:q4 * 8 + 8, :], vS[:, q4 * 8:q4 * 8 + 8, :], ["vS"], [("vdram", q4)], "st_v")
                for Dd in (1, 4, 16):
                    L = S // Dd

                    def ld_vD(Dd=Dd):
                        na_ = (S // Dd) // 128
                        Vv = vdram.rearrange("(a i r) d -> i r a d", i=128, r=Dd)
                        for g in range(4):
                            if na_ >= 8:
                                rs = slice((8 * g) // na_, (8 * g) // na_ + 1); as_ = slice((8 * g) % na_, (8 * g) % na_ + 8)
                            else:
                                rs = slice((8 * g) // na_, (8 * g) // na_ + 8 // na_); as_ = slice(0, na_)
                            na_u = as_.stop - as_.start
                            vdk = [("vdram", q4) for q4 in range(4)]
                            for ri, r_ in enumerate(range(rs.start, rs.stop)):
                                c0 = 8 * g + ri * na_u
                                dma("sp", vD[64:128, c0:c0 + na_u, :], Vv[0:64, r_, as_], vdk,
                                    [("vDh", c) for c in range(c0, c0 + na_u)], "ld_vD")
                                dma("sp", vD[0:64, c0 + 1:c0 + 1 + na_u, :], Vv[64:128, r_, as_], vdk,
                                    [("vDl", c) for c in range(c0 + 1, c0 + 1 + na_u)], "ld_vD")

                    def mk_kD(Dd=Dd):
                        srcv = kTp[:, PADN:PADN + S].rearrange("p (u r) -> p r u", r=Dd)
                        for g in range(4):
                            nr = Dd // 4
                            cp("pool", kD[:, 64 + 1024 * g:64 + 1024 * g + 1024].rearrange("p (r u) -> p r u", r=nr),
                               srcv[:, g * nr:(g + 1) * nr, :], ["kT"], [("kD", c) for c in range(8 * g, 8 * g + 9)])

                    for n in range(32):
                        m0 = 128 * n
                        rho = m0 // L; u0 = m0 % L
                        q0 = rho + Dd * u0
                        if Dd == 1:
                            chunks = [kTp[:, PADN + m0 - 64:PADN + m0 + 64], kTp[:, PADN + m0 + 64:PADN + m0 + 192]]
                        else:
                            chunks = [kD[:, m0:m0 + 128], kD[:, m0 + 128:m0 + 256]]
                        mt = [1 if u0 == 0 else 0, 3 if u0 + 128 == L else 2]
                        tiles.append(dict(
                            chunks=chunks, q=qT[:, q0:q0 + 127 * Dd + 1:Dd],
                            kkeys=([("kD", n), ("kD", n + 1)] if Dd > 1 else []),
                            v=(lambda ci, n=n: vD[:, n + ci, :]), vkeys=[("vDh", n), ("vDl", n), ("vDh", n + 1), ("vDl", n + 1), "vDpad"],
                            mask=(lambda ci, mt=mt: dmask[:, mt[ci] * 128:mt[ci] * 128 + 128]), mkeys=["dmask"],
                            acc=acc[:, :, q0:q0 + 127 * Dd + 1:Dd], first=(Dd == 1),
                            pre_S=(mk_kD if (n == 0 and Dd > 1) else None), pre_PV=(ld_vD if n == 0 else None)))
            prev = None
            for tl in tiles:
                st = tile_S(tl)
                if prev is not None:
                    tile_PV(*prev)
                prev = (tl, st)
            tile_PV(*prev)
            def make_fin(hd=hd):
                fns = []
                for pc in range(4):
                    def piece(pc=pc):
                        sl_ = slice(pc * 1024, pc * 1024 + 1024)
                        act(acc[:, 1, sl_], acc[:, 1, sl_], AF.Ln, ["acc_all"], ["acc_all"])
                        act(acc[:, 1, sl_], acc[:, 1, sl_], AF.Exp, ["acc_all"], ["acc_all"], scale=-1.0)
                        tt("dve", oT[:, sl_], acc[:, 0, sl_], acc[:, 1, sl_], ALU.mult, ["acc_all"], ["oT", "acc_all"])
                    fns.append(piece)

                def store():
                    for tq in range(4):
                        dma("sp", o_b.bitcast(BF16)[hd, tq * 128:tq * 128 + 128, :], oT[:, tq * 1024:tq * 1024 + 1024],
                            ["oT"], [("o_b", hd)], "st_o")
                    P.add("pool", lambda e: e.collective_compute("AllGather", ALU.bypass, replica_groups=[[0, 1, 2, 3], [4, 5, 6, 7]], dma_qos=CC_QOS,
                                                                 ins=[o_b[hd].opt()], outs=[o_g[hd * 2048:(hd + 1) * 2048, :].opt()]),
                          [("o_b", hd)], [("o_g", hd)], dma="cc_o%d" % hd, inc=1)
                    if DEBUG:
                        dma("sp", dbg["oT"][hd], oT[:, :], ["oT"], [], "dbg")
                fns.append(store)
                return fns
            pending_fin.extend(make_fin())
            if hd == 3:
                while pending_fin:
                    pending_fin.pop(0)()
            ckpt({0: 3, 1: 4, 2: 10, 3: 11}[hd])

    except _Stop:
        return finish()
    P.barrier(nops, nop_rw)

    if STAGE == 3:
        return finish()
    jcache = {}

    def mk_ld_o(g, l):
        def ld_o(e):
            if "j" not in jcache:
                jcache["j"] = e.partition_id() % 4
            ogb = o_g.bitcast(BF16).rearrange("(h r j q) t -> h j q r t", h=4, j=4, r=4)
            src = ogb[2 * g + l, bass.ds(jcache["j"], 1)].rearrange("o q r t -> (o q) r t")
            return e.dma_start(out=xn[:, g * 8 + l:g * 8 + 8:2, :], in_=src)
        return ld_o
    for g in range(2):
        for l in range(2):
            hl = 2 * g + l
            dsem("ld_o%d" % hl)
            P.add("pool", mk_ld_o(g, l), [("o_g", hl)], [("xn", g * 8 + l + 2 * r) for r in range(4)], dma="ld_o%d" % hl, inc=16)
    cp("dve", gth[:, :], modT[:, 5 * 16:5 * 16 + 16], ["modT"], ["gth"])
    for grp in range(2):
        for k8 in range(8):
            kc = grp * 8 + k8
            s = sq_i[0] % 2; sq_i[0] += 1
            if kc % 2 == 0:
                act(sqr[s][:, :], xn[:, kc, :], AF.Square, [("xn", kc)], [("sqr", s)])
            else:
                tt("dve", sqr[s][:, :], xn[:, kc, :], xn[:, kc, :], ALU.mult, [("xn", kc)], [("sqr", s)])
            for t in range(2):
                mm(ps[2 + t][:, :], ones[:, :], sqr[s][:, t * 512:t * 512 + 512], k8 == 0, k8 == 7,
                   ["ones", ("sqr", s)], [PSK(2 + t)])
        for t in range(2):
            act(rstd[:, t * 512:t * 512 + 512], ps[2 + t][:, :], AF.Ln, [PSK(2 + t)], ["rstd"], bias=EPS, scale=1.0 / 1024)
        act(rstd[:, :], rstd[:, :], AF.Exp, ["rstd"], ["rstd"], scale=-0.5)
        for k8 in range(8):
            kc = grp * 8 + k8
            stt(xn[:, kc, :], xn[:, kc, :], gouts[:, kc:kc + 1], rstd[:, :], ALU.mult, ALU.mult,
                [("xn", kc), "gouts", "rstd"], [("xn", kc)])
        for g in (2 * grp, 2 * grp + 1):
            down_group(wo, g * 4, lambda fl, t, g=g: xn[:, g * 4 + fl, t * 512:t * 512 + 512],
                       lambda fl, g=g: [("xn", g * 4 + fl)], gth)
    if DEBUG:
        dma("sp", dbg["h2"].rearrange("(k p) t -> p k t", p=128), h[:, :, :], HALL, [], "dbg")

    if not LITE:
        norm_mod(2, 6, 7)
        ffn(w2g, w2u, w2d, 8)

    return finish()


def _tile_w(W):
    K, N = W.shape
    return np.ascontiguousarray(W.reshape(K // 128, 128, N // 128, 128).transpose(2, 1, 0, 3)).reshape(N // 128, 128, (K // 128) * 128)


def _vec16(v):
    return np.ascontiguousarray(v.reshape(-1, 128).T)


_NC_CACHE = {}


def kernel(x, c, w_ada, b_ada, g_ffn1, w1_gate, w1_up, w1_down, g_mix, w_qkv, qn_na, kn_na, qn_dil, kn_dil,
           rpb_na, g_out_na, g_out_dil, w_o, g_ffn2, w2_gate, w2_up, w2_down):
    f = lambda a: np.asarray(a, dtype=np.float32)
    x = f(x); c = f(c)
    w_ada = f(w_ada)[0]; b_ada = f(b_ada)[0]
    if not LITE:
        w1g = _tile_w(f(w1_gate)[0]); w1u = _tile_w(f(w1_up)[0]); w1d = np.ascontiguousarray(f(w1_down)[0]).reshape(NFC, 128, 2048)
        w2g = _tile_w(f(w2_gate)[0]); w2u = _tile_w(f(w2_up)[0]); w2d = np.ascontiguousarray(f(w2_down)[0]).reshape(NFC, 128, 2048)
    wqkv = f(w_qkv)[0]
    wo = np.ascontiguousarray(f(w_o)[0]).reshape(16, 128, 2048)
    gv = np.concatenate([_vec16(f(g_ffn1)[0]), _vec16(f(g_mix)[0]), _vec16(f(g_ffn2)[0])], axis=1)
    gout = np.concatenate([_vec16(f(g_out_na)[0]), _vec16(f(g_out_dil)[0])], axis=1)
    qkn = np.stack([f(qn_na)[0], f(kn_na)[0], f(qn_dil)[0], f(kn_dil)[0]], axis=1)
    rpb = f(rpb_na)[0]
    cT = np.ascontiguousarray(c.T.reshape(16, 128, 2).transpose(1, 0, 2)).reshape(128, 32)
    ropeC, ropeS = rope_tabs()
    badaT_h = np.ascontiguousarray(b_ada.reshape(144, 128).T)
    dmask = dil_masks()
    ident = np.eye(128, dtype=np.float32)
    perm32 = np.zeros((32, 32), np.float32)
    perm32[(np.arange(32) + 16) % 32, np.arange(32)] = 1.0
    in_maps = []
    for core in range(8):
        b = core // 4; j = core % 4
        heads = [2 * j, 2 * j + 1, 8 + 2 * j, 9 + 2 * j]
        whl = []
        for hh in heads:
            blk = []
            for part in range(3):
                col0 = part * D + hh * 128
                blk.append(_tile_w(wqkv[:, col0:col0 + 128])[0])
            whl.append(np.stack(blk))
        m = {
            "xT": np.ascontiguousarray(x[b, j * NT:(j + 1) * NT, :].T),
            "cT": cT,
            "wada": _tile_w(np.concatenate([w_ada[:, 12 * j * 128:(12 * j + 12) * 128],
                                            w_ada[:, (48 + 24 * j) * 128:(48 + 24 * j + 24) * 128]], axis=1)),
            "bada": badaT_h,
            "sel": np.array([[1.0 - b], [float(b)]], np.float32),
            "wh": np.stack(whl), "wo": wo, "gv": gv, "gout": gout, "qkn": np.ascontiguousarray(qkn),
            "bt": np.stack([build_bias_tiles(rpb[2 * j]), build_bias_tiles(rpb[2 * j + 1])]),
            "ropeC": ropeC, "ropeS": ropeS, "dmask": dmask, "ident": ident, "perm": perm32,
        }
        if not LITE:
            m.update({"w1g": w1g, "w1u": w1u, "w1d": w1d, "w2g": w2g, "w2u": w2u, "w2d": w2d})
        in_maps.append(m)
    if "nc" not in _NC_CACHE:
        _NC_CACHE["nc"] = build_nc()
    nc = _NC_CACHE["nc"]
    res = run_bass_kernel_spmd(nc, in_maps, core_ids=list(range(8)))
    out = np.empty((2, S, D), np.float32)
    for core in range(8):
        b = core // 4; j = core % 4
        out[b, j * NT:(j + 1) * NT, :] = res.results[core]["outT"].T
    if DEBUG:
        kernel.last = res
    return out
```

```python
import numpy as np
from contextlib import ExitStack
import concourse.bass as bass
import concourse.mybir as mybir
from concourse.bass_utils import run_bass_kernel_spmd

F32 = mybir.dt.float32
BF16 = mybir.dt.bfloat16
AF = mybir.ActivationFunctionType
ALU = mybir.AluOpType

D = 2048
NT = 1024
S = 4096
DFF = 5632
NFC = 44
EPS = 1e-6
SCALE = 128.0 ** -0.5
NEGM = -30000.0
PADN = 64
DEBUG = False
STAGE = 9
ATTSTOP = 0
CC_QOS = "P2"
NRM_NP = 4
LITE = False


class Op:
    __slots__ = ("eng", "fn", "deps", "is_dma", "semkey", "inc", "cum", "targets", "signal", "seq", "idx")


class _Stop(Exception):
    pass


def ckpt(n):
    if ATTSTOP == n:
        raise _Stop()


class Prog:
    ENGS = ("pe", "act", "dve", "pool", "sp")

    def __init__(self):
        self.ops = {e: [] for e in self.ENGS}
        self.lastw = {}
        self.readers = {}
        self.dma_count = {}
        self.last_dma = {}
        self.n = 0

    def add(self, eng, fn, reads=(), writes=(), dma=None, inc=16, extra_deps=()):
        op = Op()
        op.eng = eng; op.fn = fn; op.is_dma = dma is not None; op.semkey = dma; op.inc = inc
        op.signal = False; op.seq = 0; op.idx = self.n; self.n += 1
        deps = []
        for k in reads:
            w = self.lastw.get(k)
            if w is not None:
                deps.append((w, "raw"))
        for k in writes:
            rds = self.readers.get(k, ())
            w = self.lastw.get(k)
            if w is not None and not rds:
                deps.append((w, "waw"))
            for r in rds:
                deps.append((r, "war"))
        for d in extra_deps:
            deps.append((d, "raw"))
        for k in reads:
            self.readers.setdefault(k, []).append(op)
        for k in writes:
            self.lastw[k] = op
            self.readers[k] = []
        op.deps = deps
        op.targets = {d.semkey: self.dma_count[d.semkey] for d, _ in deps if d.is_dma}
        if op.is_dma:
            self.dma_count[dma] = self.dma_count.get(dma, 0) + inc
            op.cum = self.dma_count[dma]
            self.last_dma[dma] = op
        self.ops[eng].append(op)
        return op

    def barrier(self, mk_nop, rw=None):
        rw = rw or {}
        firsts = []
        for e in ("pe", "act", "dve", "pool"):
            rd, wr = rw.get(e, ((), ()))
            firsts.append(self.add(e, mk_nop[e], reads=list(rd), writes=[("bar1", e, self.n)] + list(wr)))
        dmas = [op for key, op in self.last_dma.items() if not str(key).startswith("cc_")]
        for e in self.ENGS:
            rd, wr = rw.get(e, ((), ()))
            self.add(e, mk_nop[e], reads=list(rd), writes=[("bar2", e, self.n)] + list(wr), extra_deps=firsts + dmas)

    def needs_wait(self, op, d, kind):
        if d.is_dma:
            return True
        if d.eng != op.eng:
            return True
        if op.eng == "pe":
            return False
        if op.is_dma:
            return True
        return kind == "raw"

    def finalize(self):
        for e in self.ENGS:
            for op in self.ops[e]:
                latest = {}
                for d, kind in op.deps:
                    if (not d.is_dma) and self.needs_wait(op, d, kind):
                        if d.eng not in latest or d.idx > latest[d.eng].idx:
                            latest[d.eng] = d
                for d in latest.values():
                    d.signal = True
                op.deps = [(d, k) for d, k in op.deps if d.is_dma or latest.get(d.eng) is d]
        for e in self.ENGS:
            s = 0
            for op in self.ops[e]:
                if op.signal and not op.is_dma:
                    s += 1
                    op.seq = s

    def emit(self, nc, block, engsem, dmasem):
        self.finalize()
        decos = {"pe": block.tensor, "act": block.scalar, "dve": block.vector, "pool": block.gpsimd, "sp": block.sync}
        for e in self.ENGS:
            ops = self.ops[e]

            def body(eng, ops=ops, e=e):
                waited = {}
                for op in ops:
                    need = {}
                    for d, kind in op.deps:
                        if not self.needs_wait(op, d, kind):
                            continue
                        if d.is_dma:
                            key = ("d", d.semkey); val = op.targets[d.semkey]; sem = dmasem[d.semkey]
                        else:
                            key = ("e", d.eng); val = d.seq; sem = engsem[d.eng]
                        if val > need.get(key, (0, None))[0]:
                            need[key] = (val, sem)
                    for key, (val, sem) in need.items():
                        if waited.get(key, 0) >= val:
                            continue
                        eng.wait_ge(sem, val)
                        waited[key] = val
                    inst = op.fn(eng)
                    if inst is None:
                        continue
                    if op.is_dma:
                        inst.then_inc(dmasem[op.semkey], op.inc)
                    elif op.signal:
                        inst.then_inc(engsem[e], 1)
            decos[e](body)


def na_tile_tables():
    types = {}
    tiles = []
    for n in range(32):
        rows = []
        for rq in (2 * n, 2 * n + 1):
            rs = min(max(rq - 4, 0), 56)
            rows.append((rq, rs))
        lo = min(r[1] for r in rows) // 2
        hi = (max(r[1] for r in rows) + 7) // 2
        lst = []
        for c in range(lo, hi + 1):
            key = []
            for rkl in range(2):
                for rql in range(2):
                    rk = 2 * c + rkl
                    rq, rs = rows[rql]
                    if rs <= rk < rs + 8:
                        key.append(rk - rq + 7)
                    else:
                        key.append(-1)
            key = tuple(key)
            if all(k < 0 for k in key):
                continue
            if key not in types:
                types[key] = len(types)
            lst.append((c, types[key]))
        tiles.append(lst)
    return tiles, types


NA_TILES, NA_TYPES = na_tile_tables()
NTYPES = len(NA_TYPES)


def build_bias_tiles(rpb_h):
    out = np.full((128, NTYPES, 128), NEGM, np.float32)
    ck = np.arange(64)[:, None]
    cq = np.arange(64)[None, :]
    cs = np.clip(cq - 8, 0, 48)
    cvalid = (ck >= cs) & (ck < cs + 16)
    coff = np.clip(ck - cq + 15, 0, 30)
    for key, t in NA_TYPES.items():
        i = 0
        for rkl in range(2):
            for rql in range(2):
                a = key[i]; i += 1
                if a < 0:
                    continue
                blk = np.where(cvalid, rpb_h[a][coff], np.float32(NEGM))
                out[64 * rkl:64 * rkl + 64, t, 64 * rql:64 * rql + 64] = blk
    return out.reshape(128, NTYPES * 128)


def dil_masks():
    i = np.arange(128)[:, None]
    j = np.arange(128)[None, :]
    A = np.where(j <= i, 0.0, NEGM)
    B = np.where(j >= i, 0.0, NEGM)
    A1 = A.copy(); A1[:64, :] = NEGM
    B1 = B.copy(); B1[64:, :] = NEGM
    return np.concatenate([A, A1, B, B1], axis=1).astype(np.float32)


def rope_tabs():
    pos = np.arange(S, dtype=np.float32)
    inv = np.power(np.float32(500000.0), -np.arange(0, 32, 2, dtype=np.float32) / np.float32(32)).astype(np.float32)
    ang = (pos[None, :] * inv[:, None]).astype(np.float32)
    c = np.cos(ang).astype(np.float32); s = np.sin(ang).astype(np.float32)
    C = np.concatenate([c, c], axis=0)
    Sg = np.concatenate([-s, s], axis=0)
    return C, Sg


def build_nc():
    nc = bass.Bass("TRN2", target_bir_lowering=False)
    P = Prog()

    def din(name, shape, dt=F32):
        return nc.dram_tensor(name, list(shape), dt, kind="ExternalInput").ap()

    xT = din("xT", [D, NT]); cT = din("cT", [128, 32]); wada = din("wada", [36, 128, 2048]); bada = din("bada", [128, 144])
    sel = din("sel", [2, 1])
    if not LITE:
        w1g = din("w1g", [NFC, 128, 2048]); w1u = din("w1u", [NFC, 128, 2048]); w1d = din("w1d", [NFC, 128, 2048])
        w2g = din("w2g", [NFC, 128, 2048]); w2u = din("w2u", [NFC, 128, 2048]); w2d = din("w2d", [NFC, 128, 2048])
    wh = din("wh", [4, 3, 128, 2048]); wo = din("wo", [16, 128, 2048])
    gv = din("gv", [128, 48]); goutd = din("gout", [128, 16]); qknd = din("qkn", [128, 4])
    btd = din("bt", [2, 128, NTYPES * 128]); ropeCd = din("ropeC", [32, S]); ropeSd = din("ropeS", [32, S])
    dmaskd = din("dmask", [128, 512]); identd = din("ident", [128, 128]); permd = din("perm", [32, 32])
    outT = nc.dram_tensor("outT", [D, NT], F32, kind="ExternalOutput").ap()
    dbg = {}
    if DEBUG:
        dbg["h1"] = nc.dram_tensor("dbg_h1", [D, NT], F32, kind="ExternalOutput").ap()
        dbg["modT"] = nc.dram_tensor("dbg_modT", [128, 144], F32, kind="ExternalOutput").ap()
        dbg["oT"] = nc.dram_tensor("dbg_oT", [4, 128, S], BF16, kind="ExternalOutput").ap()
        dbg["nrm"] = nc.dram_tensor("dbg_nrm", [D, NT], BF16, kind="ExternalOutput").ap()
        dbg["qT"] = nc.dram_tensor("dbg_qT", [4, 128, S], BF16, kind="ExternalOutput").ap()
        dbg["kT"] = nc.dram_tensor("dbg_kT", [4, 128, S], BF16, kind="ExternalOutput").ap()
        dbg["h2"] = nc.dram_tensor("dbg_h2", [D, NT], F32, kind="ExternalOutput").ap()

    mod_bA = nc.dram_tensor("mod_bA", [2, 1536], F32).ap()
    mod_gA = nc.dram_tensor("mod_gA", [8, 1536], F32).ap()
    mod_bB = nc.dram_tensor("mod_bB", [2, 3072], F32).ap()
    mod_gB = nc.dram_tensor("mod_gB", [8, 3072], F32).ap()
    nrm_b = nc.dram_tensor("nrm_b", [4, 128, 2048], F32).ap()
    nrm_g = nc.dram_tensor("nrm_g", [4, 512, 2048], F32).ap()
    vdram = nc.dram_tensor("vdram", [S, 128], BF16).ap()
    o_b = nc.dram_tensor("o_b", [4, 512, 512], F32).ap()
    o_g = nc.dram_tensor("o_g", [16 * 512, 512], F32).ap()

    es = ExitStack()
    ARENA = 212480
    es.enter_context(nc.sbuf_tensor("arena", [128, ARENA + 64], mybir.dt.uint8))
    base0 = (nc.sbuf_base - (ARENA + 64) + 31) // 32 * 32
    cur = [base0]

    def salloc(name, shape, dt, at=None):
        nbytes = int(np.prod(shape[1:])) * (4 if dt == F32 else 2)
        nbytes = (nbytes + 31) // 32 * 32
        if at is None:
            off = cur[0]; cur[0] += nbytes
        else:
            off = at
        assert off + nbytes <= base0 + ARENA, (name, off + nbytes - base0, ARENA)
        return nc.alloc_sbuf_tensor_at(name, list(shape), dt, offset=off), off + nbytes

    h, _ = salloc("h", [128, 16, NT], F32)
    modT, _ = salloc("modT", [128, 144], F32)
    gvs, _ = salloc("gvs", [128, 48], F32)
    gouts, _ = salloc("gouts", [128, 16], F32)
    qkns, _ = salloc("qkns", [128, 4], F32)
    qknsc, _ = salloc("qknsc", [128, 4], F32)
    Avec, _ = salloc("Avec", [128, 16], F32)
    tmpA, _ = salloc("tmpA", [128, 16], F32)
    gth, _ = salloc("gth", [128, 16], F32)
    ident, _ = salloc("ident", [128, 128], BF16)
    ones, _ = salloc("ones", [128, 128], BF16)
    rstd, _ = salloc("rstd", [128, NT], F32)
    sels, _ = salloc("sels", [2, 1], F32)
    scr, _ = salloc("scr", [128, 8], F32)
    R0 = cur[0]
    xn, e1 = salloc("xn", [128, 16, NT], BF16)
    wgu = []
    for i in range(6):
        t, _ = salloc(f"wgu{i}", [128, 2048], BF16); wgu.append(t)
    wdr = []
    for i in range(8):
        t, _ = salloc(f"wd{i}", [128, 2048], BF16); wdr.append(t)
    hid = []
    hid_off = cur[0]
    for i in range(3):
        t, _ = salloc(f"hid{i}", [128, 4, NT], BF16); hid.append(t)
    sgt = []
    for i in range(2):
        t, _ = salloc(f"sgt{i}", [128, 512], F32); sgt.append(t)
    sqr = []
    for i in range(2):
        t, _ = salloc(f"sqr{i}", [128, NT], BF16); sqr.append(t)
    tmpf = []
    for i in range(2):
        t, _ = salloc(f"tmpf{i}", [128, NT], F32); tmpf.append(t)
    cTf, _ = salloc("cTf", [128, 32], F32)
    csb, _ = salloc("csb", [128, 32], BF16)
    badaT, _ = salloc("badaT", [128, 144], F32)
    stg = []
    for i in range(2):
        t, _ = salloc(f"stg{i}", [2, 512], F32); stg.append(t)
    stgr = []
    for i in range(2):
        t, _ = salloc(f"stgr{i}", [2, 512], F32); stgr.append(t)
    ffn_end = cur[0]
    cur[0] = R0
    nrmr = []
    for i in range(2):
        t, _ = salloc(f"nrmr{i}", [128, 16, 256], BF16); nrmr.append(t)
    whs = []
    for i in range(3):
        t, _ = salloc(f"whs{i}", [128, 2048], BF16); whs.append(t)
    qT, _ = salloc("qT", [128, S], BF16)
    kTp, _ = salloc("kTp", [128, S + 2 * PADN], BF16)
    vS, _ = salloc("vS", [128, 32, 128], BF16)
    vD, _ = salloc("vD", [128, 33, 128], BF16)
    acc, _ = salloc("acc", [128, 2, S], F32)
    PT = []
    for i in range(2):
        t, _ = salloc(f"PT{i}", [128, 640], BF16); PT.append(t)
    Bt, _ = salloc("Bt", [128, NTYPES * 128], BF16)
    oT, _ = salloc("oT", [128, S], BF16)
    ropeC, _ = salloc("ropeCt", [32, S], BF16)
    ropeS, _ = salloc("ropeSt", [32, S], BF16)
    kD, _ = salloc("kD", [128, S + 128], BF16)
    perm, _ = salloc("perm", [32, 32], BF16)
    rtmp = []
    for i in range(2):
        t, _ = salloc(f"rtmp{i}", [32, 512], F32); rtmp.append(t)
    sqt = []
    for i in range(2):
        t, _ = salloc(f"sqt{i}", [128, 512], BF16); sqt.append(t)
    dmask, _ = salloc("dmask", [128, 512], BF16)
    att_end = cur[0]
    assert max(att_end, ffn_end) <= base0 + ARENA

    ps = [es.enter_context(nc.psum_tensor(f"ps{i}", [128, 512], F32)) for i in range(8)]
    psS = []
    engsem = {e: es.enter_context(nc.semaphore(f"sem_{e}")) for e in ("pe", "act", "dve", "pool")}
    dmasem = {}

    def dsem(key):
        if key not in dmasem:
            dmasem[key] = es.enter_context(nc.semaphore("d_" + str(key)))
        return key

    PSK = lambda b: ("ps", b)
    HK = lambda kc: [("h", kc, 0), ("h", kc, 1)]
    HALL = [("h", kc, t) for kc in range(16) for t in range(2)]

    def dma(eng, out, in_, reads, writes, key, **kw):
        dsem(key)
        return P.add(eng, lambda e: e.dma_start(out=out, in_=in_, **kw), reads, writes, dma=key)

    def castdma(out, in_, reads, writes, key):
        return dma("pool", out, in_, reads, writes, key, max_dma_last_dim=8192)

    def mm(out, lhsT, rhs, start, stop, reads, writes, **kw):
        return P.add("pe", lambda e: e.matmul(out, lhsT, rhs, start=start, stop=stop, **kw), reads, writes)

    def act(out, in_, func, reads, writes, bias=None, scale=None):
        kw = {}
        if bias is not None:
            kw["bias"] = bias
        if scale is not None:
            kw["scale"] = scale
        return P.add("act", lambda e: e.activation(out, in_, func, **kw), reads, writes)

    def tt(eng, out, in0, in1, op, reads, writes):
        return P.add(eng, lambda e: e.tensor_tensor(out, in0, in1, op), reads, writes)

    def stt(out, in0, scalar, in1, op0, op1, reads, writes):
        return P.add("dve", lambda e: e.scalar_tensor_tensor(out, in0, scalar, in1, op0, op1), reads, writes)

    def ts(eng, out, in0, s1, s2, op0, op1, reads, writes):
        return P.add(eng, lambda e: e.tensor_scalar(out, in0, s1, s2, op0, op1), reads, writes)

    def cp(eng, out, in_, reads, writes):
        return P.add(eng, lambda e: e.tensor_copy(out, in_), reads, writes)

    def memset(eng, ap, val, writes):
        return P.add(eng, lambda e: e.memset(ap, val), (), writes)

    dma("sp", cTf[:, :], cT, (), ["cTf"], "ld_c")
    dma("sp", badaT[:, :], bada, (), ["badaT"], "ld_c")
    dma("sp", sels[:, :], sel, (), ["sels"], "ld_c")
    dma("sp", gvs[:, :], gv, (), ["gvs"], "ld_c")
    dma("sp", gouts[:, :], goutd, (), ["gouts"], "ld_c")
    dma("sp", qkns[:, :], qknd, (), ["qkns"], "ld_c")
    castdma(ident[:, :], identd, (), ["ident"], "ld_id")
    memset("dve", ones[:, :], 1.0, ["ones"])

    act(csb[:, :], cTf[:, :], AF.Silu, ["cTf"], ["csb"])
    wgu_i = [0]

    def next_wgu():
        i = wgu_i[0] % 6; wgu_i[0] += 1
        return i

    stg_i = [0]; stgr_i = [0]

    def ada_chunk(c, c0, bank, dst, dkey):
        sl = next_wgu()
        castdma(wgu[sl][:, :], wada[c], (), [("wgu", sl)], ("wgu", sl))
        lc = c - c0
        for kc in range(16):
            mm(ps[bank][0:2, (lc % 4) * 128:(lc % 4) * 128 + 128], csb[:, kc * 2:kc * 2 + 2], wgu[sl][:, kc * 128:kc * 128 + 128],
               kc == 0, kc == 15, ["csb", ("wgu", sl)], [PSK(bank)])
        if lc % 4 == 3:
            s_ = stg_i[0] % 2; stg_i[0] += 1
            cp("dve", stg[s_][:, :], ps[bank][0:2, 0:512], [PSK(bank)], [("stg", s_)])
            dma("sp", dst[:, (lc // 4) * 512:(lc // 4) * 512 + 512], stg[s_][:, :], [("stg", s_)], [dkey], ("stgd", s_))

    def ada_gather(src, dst, skey, gkey, sem):
        dsem(sem)
        P.add("pool", lambda e: e.collective_compute("AllGather", ALU.bypass, replica_groups=[[0, 1, 2, 3], [4, 5, 6, 7]], dma_qos=CC_QOS,
                                                     ins=[src.opt()], outs=[dst.opt()]),
              [skey], [gkey], dma=sem, inc=1)

    def ada_transposes(gathered, gkey, nchunk, gi_base, bank):
        for r in range(4):
            for grp in range(nchunk // 4):
                s_ = stgr_i[0] % 2; stgr_i[0] += 1
                dma("sp", stgr[s_][:, :], gathered[2 * r:2 * r + 2, grp * 512:grp * 512 + 512], [gkey], [("stgr", s_)], ("stgrd", s_))
                for k in range(4):
                    gi = gi_base + r * nchunk + grp * 4 + k
                    mm(ps[bank][:, gi:gi + 1], stgr[s_][:, k * 128:k * 128 + 128], sels[:, :], True, True,
                       [("stgr", s_), "sels"], [PSK(bank)])
        lo, hi = gi_base, gi_base + 4 * nchunk
        tt("dve", modT[:, lo:hi], ps[bank][:, lo:hi], badaT[:, lo:hi], ALU.add, [PSK(bank), "badaT"], ["modT"])

    for c in range(12):
        ada_chunk(c, 0, 0, mod_bA, "mod_bA")
    dma("pool", h[:, :, :], xT.rearrange("(k p) t -> p k t", p=128), (), HALL, "ld_h")
    ada_gather(mod_bA, mod_gA, "mod_bA", "mod_gA", "cc_modA")
    ada_transposes(mod_gA, "mod_gA", 12, 0, 1)
    adaB = {fc: (lambda c=12 + fc: ada_chunk(c, 12, 7, mod_bB, "mod_bB")) for fc in range(24)}
    adaB[24] = lambda: ada_gather(mod_bB, mod_gB, "mod_bB", "mod_gB", "cc_modB")
    if LITE:
        for fc in range(25):
            adaB[fc]()
        ada_transposes(mod_gB, "mod_gB", 24, 48, 7)
    if DEBUG:
        dma("sp", dbg["modT"], modT[:, :], ["modT"], [], "dbg")
    cp("dve", qknsc[:, :], qkns[:, :], ["qkns"], ["qknsc"])
    ts("dve", qknsc[:, 0:1], qkns[:, 0:1], SCALE, None, ALU.mult, ALU.bypass, ["qkns", "qknsc"], ["qknsc"])
    ts("dve", qknsc[:, 2:3], qkns[:, 2:3], SCALE, None, ALU.mult, ALU.bypass, ["qkns", "qknsc"], ["qknsc"])

    MS = lambda m, kc: modT[:, m * 16 + kc:m * 16 + kc + 1]

    sq_i = [0]; tf_i = [0]

    xnB = xn[:, :, :].rearrange("p k t -> p (k t)").rearrange("p (b k t) -> p b k t", b=4, k=16)

    def norm_mod(gidx, m_sh, m_sc, blockmajor=False):
        ts("dve", tmpA[:, :], modT[:, m_sc * 16:m_sc * 16 + 16], 1.0, None, ALU.add, ALU.bypass, ["modT"], ["tmpA"])
        tt("dve", Avec[:, :], tmpA[:, :], gvs[:, gidx * 16:gidx * 16 + 16], ALU.mult, ["tmpA", "gvs"], ["Avec"])
        for kc in range(16):
            s = sq_i[0] % 2; sq_i[0] += 1
            if kc % 2 == 0:
                act(sqr[s][:, :], h[:, kc, :], AF.Square, HK(kc), [("sqr", s)])
            else:
                tt("dve", sqr[s][:, :], h[:, kc, :], h[:, kc, :], ALU.mult, HK(kc), [("sqr", s)])
            for t in range(2):
                mm(ps[2 + t][:, :], ones[:, :], sqr[s][:, t * 512:t * 512 + 512], kc == 0, kc == 15,
                   ["ones", ("sqr", s)], [PSK(2 + t)])
        for t in range(2):
            act(rstd[:, t * 512:t * 512 + 512], ps[2 + t][:, :], AF.Ln, [PSK(2 + t)], ["rstd"], bias=EPS, scale=1.0 / D)
        act(rstd[:, :], rstd[:, :], AF.Exp, ["rstd"], ["rstd"], scale=-0.5)
        for kc in range(16):
            s = tf_i[0] % 2; tf_i[0] += 1
            stt(tmpf[s][:, :], h[:, kc, :], Avec[:, kc:kc + 1], rstd[:, :], ALU.mult, ALU.mult,
                HK(kc) + ["Avec", "rstd"], [("tmpf", s)])
            if blockmajor:
                act(xnB[:, :, kc, :], tmpf[s][:, :].rearrange("p (b t) -> p b t", b=4), AF.Identity,
                    [("tmpf", s), "modT"], [("xn", kc)], bias=MS(m_sh, kc), scale=1.0)
            else:
                act(xn[:, kc, :], tmpf[s][:, :], AF.Identity, [("tmpf", s), "modT"], [("xn", kc)], bias=MS(m_sh, kc), scale=1.0)

    wd_i = [0]
    pd_i = [0]

    def down_group(wsrc, fc0, rhs_fn, rhs_keys, scal):
        slots = []
        for fl in range(4):
            sl = wd_i[0] % 8; wd_i[0] += 1
            castdma(wdr[sl][:, :], wsrc[fc0 + fl], (), [("wd", sl)], ("wd", sl))
            slots.append(sl)
        for dc in range(16):
            for t in range(2):
                b = 4 + pd_i[0] % 4; pd_i[0] += 1
                for fl in range(4):
                    mm(ps[b][:, :], wdr[slots[fl]][:, dc * 128:dc * 128 + 128], rhs_fn(fl, t), fl == 0, fl == 3,
                       [("wd", slots[fl])] + rhs_keys(fl), [PSK(b)])
                stt(h[:, dc, t * 512:t * 512 + 512], ps[b][:, :], scal[:, dc:dc + 1], h[:, dc, t * 512:t * 512 + 512],
                    ALU.mult, ALU.add, [PSK(b), "gth", ("h", dc, t)], [("h", dc, t)])

    def ffn(wg, wu, wd_, m_gt, side=None):
        ts("dve", gth[:, :], modT[:, m_gt * 16:m_gt * 16 + 16], 0.5, None, ALU.mult, ALU.bypass, ["modT"], ["gth"])
        gu_i = 0
        for fc in range(NFC):
            if side and fc in side:
                side[fc]()
            g = fc // 4
            hs = g % 3
            sg_ = next_wgu(); su_ = next_wgu()
            castdma(wgu[sg_][:, :], wg[fc], (), [("wgu", sg_)], ("wgu", sg_))
            castdma(wgu[su_][:, :], wu[fc], (), [("wgu", su_)], ("wgu", su_))
            for t in range(2):
                bg = (gu_i % 2) * 2; bu = bg + 1; gu_i += 1
                for kc in range(16):
                    mm(ps[bg][:, :], wgu[sg_][:, kc * 128:kc * 128 + 128], xn[:, kc, t * 512:t * 512 + 512], kc == 0, kc == 15,
                       [("wgu", sg_), ("xn", kc)], [PSK(bg)])
                for kc in range(16):
                    mm(ps[bu][:, :], wgu[su_][:, kc * 128:kc * 128 + 128], xn[:, kc, t * 512:t * 512 + 512], kc == 0, kc == 15,
                       [("wgu", su_), ("xn", kc)], [PSK(bu)])
                st = gu_i % 2
                act(sgt[st][:, :], ps[bg][:, :], AF.Silu, [PSK(bg)], [("sgt", st)])
                tt("dve", hid[hs][:, fc % 4, t * 512:t * 512 + 512], sgt[st][:, :], ps[bu][:, :], ALU.mult,
                   [("sgt", st), PSK(bu)], [("hid", hs, fc % 4)])
            if fc % 4 == 3 and g >= 1:
                gp = g - 1
                down_group(wd_, gp * 4, lambda fl, t, gp=gp: hid[gp % 3][:, fl, t * 512:t * 512 + 512],
                           lambda fl, gp=gp: [("hid", gp % 3, fl)], gth)
        gp = NFC // 4 - 1
        down_group(wd_, gp * 4, lambda fl, t, gp=gp: hid[gp % 3][:, fl, t * 512:t * 512 + 512],
                   lambda fl, gp=gp: [("hid", gp % 3, fl)], gth)

    def finish():
        dma("sp", outT.rearrange("(k p) t -> p k t", p=128), h[:, :, :], HALL, ["out"], "st_out")
        P.add("sp", lambda e: e.nop(), ["out"], [])
        P.add("sp", lambda e: e.nop(), [], [], extra_deps=list(P.last_dma.values()))
        with es:
            with nc.Block() as block:
                P.emit(nc, block, engsem, dmasem)
        return nc

    if not LITE:
        norm_mod(0, 0, 1)
        ffn(w1g, w1u, w1d, 2, side=adaB)
        ada_transposes(mod_gB, "mod_gB", 24, 48, 7)
    if DEBUG:
        dma("sp", dbg["h1"].rearrange("(k p) t -> p k t", p=128), h[:, :, :], HALL, [], "dbg")

    if STAGE == 1:
        return finish()
    norm_mod(1, 3, 4, blockmajor=True)
    if DEBUG:
        dma("sp", dbg["nrm"].rearrange("(k p) (b t) -> p b k t", p=128, b=4), xnB, [("xn", kc) for kc in range(16)], [], "dbg")
    for b4 in range(4):
        dsem("cc_nrm%d" % b4)
        dma("sp", nrm_b.bitcast(BF16)[b4], xnB[:, b4].rearrange("p k t -> p (k t)"),
            [("xn", kc) for kc in range(16)], [("nrm_b", b4)], "st_nrm")

    def nrm_gather(b4):
        P.add("pool", lambda e, b4=b4: e.collective_compute("AllGather", ALU.bypass, replica_groups=[[0, 1, 2, 3], [4, 5, 6, 7]], dma_qos=CC_QOS,
                                                            ins=[nrm_b[b4].opt()], outs=[nrm_g[b4].opt()]),
              [("nrm_b", b4)], [("nrm_g", b4)], dma="cc_nrm%d" % b4, inc=1)
    nrm_gather(0)

    if STAGE == 2:
        return finish()
    nops = {
        "pe": lambda e: e.matmul(ps[7][0:1, 0:1], sels[:, :], sels[:, :], start=True, stop=True),
        "act": lambda e: e.activation(scr[:, 0:1], gvs[:, 0:1], AF.Identity),
        "dve": lambda e: e.memset(scr[:, 1:2], 0.0),
        "pool": lambda e: e.memset(scr[:, 2:3], 0.0),
        "sp": lambda e: e.nop(),
    }
    nop_rw = {"pe": (["sels"], [PSK(7)]), "act": (["gvs"], [("scr", "act")]), "dve": ((), [("scr", "dve")]),
              "pool": ((), [("scr", "pool")])}
    P.barrier(nops, nop_rw)

    try:
        castdma(dmask[:, :], dmaskd, (), ["dmask"], "ld_dm")
        castdma(perm[:, :], permd, (), ["perm"], "ld_dm")
        for pc in range(4):
            castdma(ropeC[:, pc * 1024:pc * 1024 + 1024], ropeCd[:, pc * 1024:pc * 1024 + 1024], (), ["ropeT"], "ld_dm")
            castdma(ropeS[:, pc * 1024:pc * 1024 + 1024], ropeSd[:, pc * 1024:pc * 1024 + 1024], (), ["ropeT"], "ld_dm")
        memset("pool", kD[:, 0:64], 0.0, ["kDpad"])
        memset("pool", kD[:, 64 + S:128 + S], 0.0, ["kDpad"])
        memset("pool", kTp[:, 0:PADN], 0.0, ["kTpad"])
        memset("pool", kTp[:, PADN + S:PADN + S + PADN], 0.0, ["kTpad"])
        memset("pool", vD[0:64, 0, :], 0.0, ["vDpad"])
        memset("pool", vD[64:128, 32, :], 0.0, ["vDpad"])
        nr_i = [0]; sq2_i = [0]; pt_i = [0]; po_i = [0]; rp_i = [0]
        ckpt(1)

        for hd_ in range(4):
            dsem("cc_o%d" % hd_)
        pending_fin = []
        for i3 in range(3):
            castdma(whs[i3][:, :], wh[0, i3], (), [("whs", i3)], ("whs", i3))
        castdma(Bt[:, :], btd[0], (), ["Bt"], "ld_bt")
        for b4 in range(1, 4):
            nrm_gather(b4)
        for hd in range(4):
            is_na = hd < 2
            if hd == 1:
                castdma(Bt[:, :], btd[hd], (), ["Bt"], "ld_bt")
            gq = qknsc[:, 0:1] if is_na else qknsc[:, 2:3]
            gk = qknsc[:, 1:2] if is_na else qknsc[:, 3:4]
            for tbi in range(16):
                b4 = tbi // 4; r = tbi % 4
                tb = r * 4 + b4
                ns = nr_i[0] % 2; nr_i[0] += 1
                dma("sp", nrmr[ns][:, :, :].rearrange("p k t -> p (k t)"), nrm_g.bitcast(BF16)[b4, r * 128:(r + 1) * 128, :],
                    [("nrm_g", b4)], [("nrmr", ns)], ("nrmr", ns))
                bq = tbi % 2
                bv = 2 + tbi % 2
                for kc in range(16):
                    mm(ps[bq][:, 0:256], whs[0][:, kc * 128:kc * 128 + 128], nrmr[ns][:, kc, :], kc == 0, kc == 15,
                       [("whs", 0), ("nrmr", ns)], [PSK(bq)])
                for kc in range(16):
                    mm(ps[bq][:, 256:512], whs[1][:, kc * 128:kc * 128 + 128], nrmr[ns][:, kc, :], kc == 0, kc == 15,
                       [("whs", 1), ("nrmr", ns)], [PSK(bq)])
                for tc in range(2):
                    for kc in range(16):
                        mm(ps[bv][:, tc * 128:tc * 128 + 128], nrmr[ns][:, kc, tc * 128:tc * 128 + 128],
                           whs[2][:, kc * 128:kc * 128 + 128], kc == 0, kc == 15, [("whs", 2), ("nrmr", ns)], [PSK(bv)])
                s2 = sq2_i[0] % 2; sq2_i[0] += 1
                act(sqt[s2][:, :], ps[bq][:, :], AF.Square, [PSK(bq)], [("sqt", s2)])
                mm(ps[6][:, :], ones[:, :], sqt[s2][:, :], True, True, ["ones", ("sqt", s2)], [PSK(6)])
                act(rstd[:, 0:512], ps[6][:, :], AF.Ln, [PSK(6)], ["rstd"], bias=EPS, scale=1.0 / 128)
                act(rstd[:, 0:512], rstd[:, 0:512], AF.Exp, ["rstd"], ["rstd"], scale=-0.5)
                tok0 = tb * 256
                stt(qT[:, tok0:tok0 + 256], ps[bq][:, 0:256], gq, rstd[:, 0:256], ALU.mult, ALU.mult,
                    [PSK(bq), "qknsc", "rstd"], ["qT"])
                stt(kTp[:, PADN + tok0:PADN + tok0 + 256], ps[bq][:, 256:512], gk, rstd[:, 256:512], ALU.mult, ALU.mult,
                    [PSK(bq), "qknsc", "rstd"], ["kT"])
                act(vS[:, tb * 2:tb * 2 + 2, :], ps[bv][:, 0:256].rearrange("p (c d) -> p c d", c=2), AF.Identity, [PSK(bv)], ["vS"])
                if pending_fin and tbi % 2 == 1:
                    pending_fin.pop(0)()
            if hd < 3:
                for i3 in range(3):
                    castdma(whs[i3][:, :], wh[hd + 1, i3], (), [("whs", i3)], ("whs", i3))
            ckpt(2 if hd == 0 else (5 if hd == 2 else -1))
            if not is_na:
                for which, buf, off, key in ((0, qT, 0, "qT"), (1, kTp, PADN, "kT")):
                    for pc in range(8):
                        cc0 = pc * 512
                        x0 = buf[0:32, off + cc0:off + cc0 + 512]
                        rb = 6 + pc % 2
                        mm(ps[rb][0:32, :], perm[:, :], x0, True, True, ["perm", key], [PSK(rb)])
                        tt("dve", rtmp[0][:, :], x0, ropeC[:, cc0:cc0 + 512], ALU.mult, [key, "ropeT"], [("rtmp", 0)])
                        tt("dve", rtmp[1][:, :], ps[rb][0:32, :], ropeS[:, cc0:cc0 + 512], ALU.mult, [PSK(rb), "ropeT"], [("rtmp", 1)])
                        tt("dve", x0, rtmp[0][:, :], rtmp[1][:, :], ALU.add, [("rtmp", 0), ("rtmp", 1)], [key])
            if hd == 2:
                ckpt(6)
            if DEBUG:
                dma("sp", dbg["qT"][hd], qT[:, :], ["qT"], [], "dbg")
                dma("sp", dbg["kT"][hd], kTp[:, PADN:PADN + S], ["kT"], [], "dbg")

            def tile_S(tl):
                if tl.get("pre_S"):
                    tl["pre_S"]()
                chunks = tl["chunks"]; qap = tl["q"]
                nch = len(chunks)
                sb = (pt_i[0] % 2) * 2
                pti = pt_i[0] % 2; pt_i[0] += 1
                for ci, kap in enumerate(chunks):
                    b = sb + ci // 4
                    col = (ci % 4) * 128
                    mm(ps[b][:, col:col + 128], kap, qap, True, False, ["kT", "kTpad", "kDpad", "qT"] + tl.get("kkeys", []), [PSK(b)])
                    mm(ps[b][:, col:col + 128], ident[:, :], tl["mask"](ci), False, True, ["ident"] + tl["mkeys"], [PSK(b)])
                n1 = min(nch, 4) * 128
                act(PT[pti][:, 0:n1], ps[sb][:, 0:n1], AF.Exp, [PSK(sb)], [("PT", pti)])
                if nch > 4:
                    n2 = (nch - 4) * 128
                    act(PT[pti][:, 512:512 + n2], ps[sb + 1][:, 0:n2], AF.Exp, [PSK(sb + 1)], [("PT", pti)])
                return pti, nch

            def tile_PV(tl, st):
                if tl.get("pre_PV"):
                    tl["pre_PV"]()
                pti, nch = st
                bo = 4 + po_i[0] % 2; po_i[0] += 1
                for ci in range(nch):
                    mm(ps[bo][:, 0:128], tl["v"](ci), PT[pti][:, ci * 128:ci * 128 + 128], ci == 0, ci == nch - 1,
                       tl["vkeys"] + [("PT", pti)], [PSK(bo)])
                for ci in range(nch):
                    mm(ps[bo][:, 128:256], ones[:, :], PT[pti][:, ci * 128:ci * 128 + 128], ci == 0, ci == nch - 1,
                       ["ones", ("PT", pti)], [PSK(bo)])
                src = ps[bo][:, 0:256].rearrange("p (w q) -> p w q", w=2)
                if tl["first"]:
                    cp("dve", tl["acc"], src, [PSK(bo)], ["acc_all"])
                else:
                    tt("dve", tl["acc"], src, tl["acc"], ALU.add, [PSK(bo)], ["acc_all"])

            tiles = []
            if is_na:
                for n in range(32):
                    lst = NA_TILES[n]
                    tiles.append(dict(
                        chunks=[kTp[:, PADN + c * 128:PADN + c * 128 + 128] for c, _ in lst],
                        q=qT[:, n * 128:n * 128 + 128],
                        v=(lambda ci, lst=lst: vS[:, lst[ci][0], :]), vkeys=["vS"],
                        mask=(lambda ci, lst=lst: Bt[:, lst[ci][1] * 128:lst[ci][1] * 128 + 128]), mkeys=["Bt"],
                        acc=acc[:, :, n * 128:n * 128 + 128], first=True))
            else:
                vdv = vdram.rearrange("(c p) d -> p c d", p=128)
                for q4 in range(4):
                    dma("sp", vdv[:, q4 * 8:q4 * 8 + 8, :], vS[:, q4 * 8:q4 * 8 + 8, :], ["vS"], [("vdram", q4)], "st_v")
                for Dd in (1, 4, 16):
                    L = S // Dd

                    def ld_vD(Dd=Dd):
                        na_ = (S // Dd) // 128
                        Vv = vdram.rearrange("(a i r) d -> i r a d", i=128, r=Dd)
                        for g in range(4):
                            if na_ >= 8:
                                rs = slice((8 * g) // na_, (8 * g) // na_ + 1); as_ = slice((8 * g) % na_, (8 * g) % na_ + 8)
                            else:
                                rs = slice((8 * g) // na_, (8 * g) // na_ + 8 // na_); as_ = slice(0, na_)
                            na_u = as_.stop - as_.start
                            vdk = [("vdram", q4) for q4 in range(4)]
                            for ri, r_ in enumerate(range(rs.start, rs.stop)):
                                c0 = 8 * g + ri * na_u
                                dma("sp", vD[64:128, c0:c0 + na_u, :], Vv[0:64, r_, as_], vdk,
                                    [("vDh", c) for c in range(c0, c0 + na_u)], "ld_vD")
                                dma("sp", vD[0:64, c0 + 1:c0 + 1 + na_u, :], Vv[64:128, r_, as_], vdk,
                                    [("vDl", c) for c in range(c0 + 1, c0 + 1 + na_u)], "ld_vD")

                    def mk_kD(Dd=Dd):
                        srcv = kTp[:, PADN:PADN + S].rearrange("p (u r) -> p r u", r=Dd)
                        for g in range(4):
                            nr = Dd // 4
                            cp("pool", kD[:, 64 + 1024 * g:64 + 1024 * g + 1024].rearrange("p (r u) -> p r u", r=nr),
                               srcv[:, g * nr:(g + 1) * nr, :], ["kT"], [("kD", c) for c in range(8 * g, 8 * g + 9)])

                    for n in range(32):
                        m0 = 128 * n
                        rho = m0 // L; u0 = m0 % L
                        q0 = rho + Dd * u0
                        if Dd == 1:
                            chunks = [kTp[:, PADN + m0 - 64:PADN + m0 + 64], kTp[:, PADN + m0 + 64:PADN + m0 + 192]]
                        else:
                            chunks = [kD[:, m0:m0 + 128], kD[:, m0 + 128:m0 + 256]]
                        mt = [1 if u0 == 0 else 0, 3 if u0 + 128 == L else 2]
                        tiles.append(dict(
                            chunks=chunks, q=qT[:, q0:q0 + 127 * Dd + 1:Dd],
                            kkeys=([("kD", n), ("kD", n + 1)] if Dd > 1 else []),
                            v=(lambda ci, n=n: vD[:, n + ci, :]), vkeys=[("vDh", n), ("vDl", n), ("vDh", n + 1), ("vDl", n + 1), "vDpad"],
                            mask=(lambda ci, mt=mt: dmask[:, mt[ci] * 128:mt[ci] * 128 + 128]), mkeys=["dmask"],
                            acc=acc[:, :, q0:q0 + 127 * Dd + 1:Dd], first=(Dd == 1),
                            pre_S=(mk_kD if (n == 0 and Dd > 1) else None), pre_PV=(ld_vD if n == 0 else None)))
            prev = None
            for tl in tiles:
                st = tile_S(tl)
                if prev is not None:
                    tile_PV(*prev)
                prev = (tl, st)
            tile_PV(*prev)
            def make_fin(hd=hd):
                fns = []
                for pc in range(4):
                    def piece(pc=pc):
                        sl_ = slice(pc * 1024, pc * 1024 + 1024)
                        act(acc[:, 1, sl_], acc[:, 1, sl_], AF.Ln, ["acc_all"], ["acc_all"])
                        act(acc[:, 1, sl_], acc[:, 1, sl_], AF.Exp, ["acc_all"], ["acc_all"], scale=-1.0)
                        tt("dve", oT[:, sl_], acc[:, 0, sl_], acc[:, 1, sl_], ALU.mult, ["acc_all"], ["oT", "acc_all"])
                    fns.append(piece)

                def store():
                    for tq in range(4):
                        dma("sp", o_b.bitcast(BF16)[hd, tq * 128:tq * 128 + 128, :], oT[:, tq * 1024:tq * 1024 + 1024],
                            ["oT"], [("o_b", hd)], "st_o")
                    P.add("pool", lambda e: e.collective_compute("AllGather", ALU.bypass, replica_groups=[[0, 1, 2, 3], [4, 5, 6, 7]], dma_qos=CC_QOS,
                                                                 ins=[o_b[hd].opt()], outs=[o_g[hd * 2048:(hd + 1) * 2048, :].opt()]),
                          [("o_b", hd)], [("o_g", hd)], dma="cc_o%d" % hd, inc=1)
                    if DEBUG:
                        dma("sp", dbg["oT"][hd], oT[:, :], ["oT"], [], "dbg")
                fns.append(store)
                return fns
            pending_fin.extend(make_fin())
            if hd == 3:
                while pending_fin:
                    pending_fin.pop(0)()
            ckpt({0: 3, 1: 4, 2: 10, 3: 11}[hd])

    except _Stop:
        return finish()
    P.barrier(nops, nop_rw)

    if STAGE == 3:
        return finish()
    jcache = {}

    def mk_ld_o(g, l):
        def ld_o(e):
            if "j" not in jcache:
                jcache["j"] = e.partition_id() % 4
            ogb = o_g.bitcast(BF16).rearrange("(h r j q) t -> h j q r t", h=4, j=4, r=4)
            src = ogb[2 * g + l, bass.ds(jcache["j"], 1)].rearrange("o q r t -> (o q) r t")
            return e.dma_start(out=xn[:, g * 8 + l:g * 8 + 8:2, :], in_=src)
        return ld_o
    for g in range(2):
        for l in range(2):
            hl = 2 * g + l
            dsem("ld_o%d" % hl)
            P.add("pool", mk_ld_o(g, l), [("o_g", hl)], [("xn", g * 8 + l + 2 * r) for r in range(4)], dma="ld_o%d" % hl, inc=16)
    cp("dve", gth[:, :], modT[:, 5 * 16:5 * 16 + 16], ["modT"], ["gth"])
    for grp in range(2):
        for k8 in range(8):
            kc = grp * 8 + k8
            s = sq_i[0] % 2; sq_i[0] += 1
            if kc % 2 == 0:
                act(sqr[s][:, :], xn[:, kc, :], AF.Square, [("xn", kc)], [("sqr", s)])
            else:
                tt("dve", sqr[s][:, :], xn[:, kc, :], xn[:, kc, :], ALU.mult, [("xn", kc)], [("sqr", s)])
            for t in range(2):
                mm(ps[2 + t][:, :], ones[:, :], sqr[s][:, t * 512:t * 512 + 512], k8 == 0, k8 == 7,
                   ["ones", ("sqr", s)], [PSK(2 + t)])
        for t in range(2):
            act(rstd[:, t * 512:t * 512 + 512], ps[2 + t][:, :], AF.Ln, [PSK(2 + t)], ["rstd"], bias=EPS, scale=1.0 / 1024)
        act(rstd[:, :], rstd[:, :], AF.Exp, ["rstd"], ["rstd"], scale=-0.5)
        for k8 in range(8):
            kc = grp * 8 + k8
            stt(xn[:, kc, :], xn[:, kc, :], gouts[:, kc:kc + 1], rstd[:, :], ALU.mult, ALU.mult,
                [("xn", kc), "gouts", "rstd"], [("xn", kc)])
        for g in (2 * grp, 2 * grp + 1):
            down_group(wo, g * 4, lambda fl, t, g=g: xn[:, g * 4 + fl, t * 512:t * 512 + 512],
                       lambda fl, g=g: [("xn", g * 4 + fl)], gth)
    if DEBUG:
        dma("sp", dbg["h2"].rearrange("(k p) t -> p k t", p=128), h[:, :, :], HALL, [], "dbg")

    if not LITE:
        norm_mod(2, 6, 7)
        ffn(w2g, w2u, w2d, 8)

    return finish()


def _tile_w(W):
    K, N = W.shape
    return np.ascontiguousarray(W.reshape(K // 128, 128, N // 128, 128).transpose(2, 1, 0, 3)).reshape(N // 128, 128, (K // 128) * 128)


def _vec16(v):
    return np.ascontiguousarray(v.reshape(-1, 128).T)


_NC_CACHE = {}


def kernel(x, c, w_ada, b_ada, g_ffn1, w1_gate, w1_up, w1_down, g_mix, w_qkv, qn_na, kn_na, qn_dil, kn_dil,
           rpb_na, g_out_na, g_out_dil, w_o, g_ffn2, w2_gate, w2_up, w2_down):
    f = lambda a: np.asarray(a, dtype=np.float32)
    x = f(x); c = f(c)
    w_ada = f(w_ada)[0]; b_ada = f(b_ada)[0]
    if not LITE:
        w1g = _tile_w(f(w1_gate)[0]); w1u = _tile_w(f(w1_up)[0]); w1d = np.ascontiguousarray(f(w1_down)[0]).reshape(NFC, 128, 2048)
        w2g = _tile_w(f(w2_gate)[0]); w2u = _tile_w(f(w2_up)[0]); w2d = np.ascontiguousarray(f(w2_down)[0]).reshape(NFC, 128, 2048)
    wqkv = f(w_qkv)[0]
    wo = np.ascontiguousarray(f(w_o)[0]).reshape(16, 128, 2048)
    gv = np.concatenate([_vec16(f(g_ffn1)[0]), _vec16(f(g_mix)[0]), _vec16(f(g_ffn2)[0])], axis=1)
    gout = np.concatenate([_vec16(f(g_out_na)[0]), _vec16(f(g_out_dil)[0])], axis=1)
    qkn = np.stack([f(qn_na)[0], f(kn_na)[0], f(qn_dil)[0], f(kn_dil)[0]], axis=1)
    rpb = f(rpb_na)[0]
    cT = np.ascontiguousarray(c.T.reshape(16, 128, 2).transpose(1, 0, 2)).reshape(128, 32)
    ropeC, ropeS = rope_tabs()
    badaT_h = np.ascontiguousarray(b_ada.reshape(144, 128).T)
    dmask = dil_masks()
    ident = np.eye(128, dtype=np.float32)
    perm32 = np.zeros((32, 32), np.float32)
    perm32[(np.arange(32) + 16) % 32, np.arange(32)] = 1.0
    in_maps = []
    for core in range(8):
        b = core // 4; j = core % 4
        heads = [2 * j, 2 * j + 1, 8 + 2 * j, 9 + 2 * j]
        whl = []
        for hh in heads:
            blk = []
            for part in range(3):
                col0 = part * D + hh * 128
                blk.append(_tile_w(wqkv[:, col0:col0 + 128])[0])
            whl.append(np.stack(blk))
        m = {
            "xT": np.ascontiguousarray(x[b, j * NT:(j + 1) * NT, :].T),
            "cT": cT,
            "wada": _tile_w(np.concatenate([w_ada[:, 12 * j * 128:(12 * j + 12) * 128],
                                            w_ada[:, (48 + 24 * j) * 128:(48 + 24 * j + 24) * 128]], axis=1)),
            "bada": badaT_h,
            "sel": np.array([[1.0 - b], [float(b)]], np.float32),
            "wh": np.stack(whl), "wo": wo, "gv": gv, "gout": gout, "qkn": np.ascontiguousarray(qkn),
            "bt": np.stack([build_bias_tiles(rpb[2 * j]), build_bias_tiles(rpb[2 * j + 1])]),
            "ropeC": ropeC, "ropeS": ropeS, "dmask": dmask, "ident": ident, "perm": perm32,
        }
        if not LITE:
            m.update({"w1g": w1g, "w1u": w1u, "w1d": w1d, "w2g": w2g, "w2u": w2u, "w2d": w2d})
        in_maps.append(m)
    if "nc" not in _NC_CACHE:
        _NC_CACHE["nc"] = build_nc()
    nc = _NC_CACHE["nc"]
    res = run_bass_kernel_spmd(nc, in_maps, core_ids=list(range(8)))
    out = np.empty((2, S, D), np.float32)
    for core in range(8):
        b = core // 4; j = core % 4
        out[b, j * NT:(j + 1) * NT, :] = res.results[core]["outT"].T
    if DEBUG:
        kernel.last = res
    return out
```
